# Optimizing a Trainium2 kernel written in Bass

```python
import math
import jax
import jax.numpy as jnp
from jax import lax
import numpy as np

D_MODEL = 1024
BATCH = 8
SEQ = 4096
DEPTH = 1

D_INNER = 2 * D_MODEL
D_ATTN = D_INNER // 2
D_SSM = D_INNER - D_ATTN
SB_HEAD_DIM = 64
SB_HEADS = D_ATTN // SB_HEAD_DIM
SSM_HEAD_DIM = 64
SSM_HEADS = D_SSM // SSM_HEAD_DIM
SSM_GROUPS = 2
SSM_STATE = 128
CONV_WIDTH = 4
SSD_CHUNK = 128
Q_BLOCK = 128
NORM_EPS = 1e-6
D_XBC = D_SSM + 2 * SSM_GROUPS * SSM_STATE
D_PROJ = 4 * D_ATTN + D_XBC + SSM_HEADS + D_SSM
DT_MIN = 1e-3
DT_MAX = 1e-1

kernel_name = "hybrid_stickbreak_ssd_layer"


def rms_norm(x, gain):
    xf = x.astype(jnp.float32)
    y = xf * lax.rsqrt(jnp.mean(xf * xf, axis=-1, keepdims=True) + NORM_EPS)
    return (y * gain.astype(jnp.float32)).astype(x.dtype)


def stick_breaking_attention(q, k, v):
    seq = q.shape[1]
    scale = q.shape[-1] ** -0.5
    outs = []
    for blk in range(seq // Q_BLOCK):
        start = blk * Q_BLOCK
        end = start + Q_BLOCK
        z = jnp.einsum("bqhd,bkhd->bhqk", q[:, start:end], k[:, :end]).astype(jnp.float32) * scale
        t_idx = start + jnp.arange(Q_BLOCK)[:, None]
        s_idx = jnp.arange(end)[None, :]
        mask = s_idx < t_idx
        log_beta = jax.nn.log_sigmoid(z)
        log_one_minus = jnp.where(mask, jax.nn.log_sigmoid(-z), 0.0)
        tail = lax.cumsum(log_one_minus, axis=3, reverse=True) - log_one_minus
        weights = jnp.exp(jnp.where(mask, log_beta + tail, -jnp.inf))
        outs.append(jnp.einsum("bhqk,bkhd->bqhd", weights.astype(v.dtype), v[:, :end]))
    return jnp.concatenate(outs, axis=1)


def causal_depthwise_conv(u, w, b):
    ch = u.shape[-1]
    y = lax.conv_general_dilated(
        u, w[:, None, :].astype(u.dtype), window_strides=(1,),
        padding=[(CONV_WIDTH - 1, 0)], dimension_numbers=("NWC", "WIO", "NWC"),
        feature_group_count=ch)
    return y + b.astype(u.dtype)


def ssd_scan(xs, dt, a, b_in, c_in, d_skip):
    bsz, seq, nh, hp = xs.shape
    ng, ns = b_in.shape[-2:]
    hpg = nh // ng
    nc = seq // SSD_CHUNK
    cl = SSD_CHUNK
    xf = xs.astype(jnp.float32)
    xdt = (xf * dt[..., None]).reshape(bsz, nc, cl, ng, hpg, hp)
    bc = b_in.astype(jnp.float32).reshape(bsz, nc, cl, ng, ns)
    cc = c_in.astype(jnp.float32).reshape(bsz, nc, cl, ng, ns)
    log_decay = (dt * a).reshape(bsz, nc, cl, ng, hpg).transpose(0, 1, 3, 4, 2)
    a_cum = jnp.cumsum(log_decay, axis=-1)
    causal = jnp.tril(jnp.ones((cl, cl), dtype=bool))
    seg = a_cum[..., :, None] - a_cum[..., None, :]
    decay = jnp.exp(jnp.where(causal, seg, -jnp.inf))
    cb = jnp.einsum("bclgn,bcsgn->bcgls", cc, bc)
    y_diag = jnp.einsum("bcgrls,bcsgrp->bclgrp", cb[:, :, :, None] * decay, xdt)
    decay_to_end = jnp.exp(a_cum[..., -1:] - a_cum)
    chunk_states = jnp.einsum("bclgn,bcgrl,bclgrp->bcgrpn", bc, decay_to_end, xdt)
    chunk_decay = jnp.exp(a_cum[..., -1])

    def step(h, inp):
        st, dec = inp
        return h * dec[..., None, None] + st, h

    h0 = jnp.zeros((bsz, ng, hpg, hp, ns), jnp.float32)
    _, prev = lax.scan(step, h0, (jnp.moveaxis(chunk_states, 1, 0), jnp.moveaxis(chunk_decay, 1, 0)))
    prev = jnp.moveaxis(prev, 0, 1)
    y_off = jnp.einsum("bclgn,bcgrpn,bcgrl->bclgrp", cc, prev, jnp.exp(a_cum))
    y = (y_diag + y_off).reshape(bsz, seq, nh, hp)
    return y + xf * d_skip.astype(jnp.float32)[:, None]


def setup_inputs(seed: int = 0) -> dict:
    key = jax.random.key(seed)
    ks = jax.random.split(key, 16)
    f = jnp.float32
    nrm = jax.random.normal
    x = nrm(ks[0], (BATCH, SEQ, D_MODEL), f)
    c = nrm(ks[1], (BATCH, D_MODEL), f)
    w_ada = nrm(ks[2], (DEPTH, D_MODEL, 3 * D_MODEL), f) * D_MODEL ** -0.5
    b_ada = 0.02 * nrm(ks[3], (DEPTH, 3 * D_MODEL), f)
    norm_in_gain = 1.0 + 0.1 * nrm(ks[4], (DEPTH, D_MODEL), f)
    w_in = nrm(ks[5], (DEPTH, D_MODEL, D_PROJ), f) * D_MODEL ** -0.5
    conv_w = nrm(ks[6], (DEPTH, CONV_WIDTH, D_XBC), f) * CONV_WIDTH ** -0.5
    conv_b = 0.02 * nrm(ks[7], (DEPTH, D_XBC), f)
    dt0 = jnp.exp(jax.random.uniform(ks[8], (DEPTH, SSM_HEADS), f, math.log(DT_MIN), math.log(DT_MAX)))
    dt_bias = dt0 + jnp.log(-jnp.expm1(-dt0))
    a_log = jnp.log(jax.random.uniform(ks[9], (DEPTH, SSM_HEADS), f, 1.0, 16.0))
    d_skip = 1.0 + 0.1 * nrm(ks[10], (DEPTH, SSM_HEADS), f)
    sb_norm_gain = 1.0 + 0.1 * nrm(ks[11], (DEPTH, D_ATTN), f)
    ssm_norm_gain = 1.0 + 0.1 * nrm(ks[12], (DEPTH, D_SSM), f)
    w_out = nrm(ks[13], (DEPTH, D_INNER, D_MODEL), f) * D_INNER ** -0.5
    norm_f_gain = 1.0 + 0.1 * nrm(ks[14], (D_MODEL,), f)
    return {"x": x, "c": c, "w_ada": w_ada, "b_ada": b_ada, "norm_in_gain": norm_in_gain,
            "w_in": w_in, "conv_w": conv_w, "conv_b": conv_b, "dt_bias": dt_bias,
            "a_log": a_log, "d_skip": d_skip, "sb_norm_gain": sb_norm_gain,
            "ssm_norm_gain": ssm_norm_gain, "w_out": w_out, "norm_f_gain": norm_f_gain}


def reference(x, c, w_ada, b_ada, norm_in_gain, w_in, conv_w, conv_b, dt_bias, a_log,
              d_skip, sb_norm_gain, ssm_norm_gain, w_out, norm_f_gain):
    bsz, seq, _ = x.shape
    splits = [D_ATTN, 2 * D_ATTN, 3 * D_ATTN, 4 * D_ATTN, 4 * D_ATTN + D_XBC,
              4 * D_ATTN + D_XBC + SSM_HEADS]
    c_act = jax.nn.silu(c)
    for layer in range(DEPTH):
        mod = c_act @ w_ada[layer] + b_ada[layer]
        shift, scale, gate = jnp.split(mod, 3, axis=-1)
        h = rms_norm(x, norm_in_gain[layer]) * (1.0 + scale[:, None, :]) + shift[:, None, :]

        proj = h @ w_in[layer]
        q, k, v, z_attn, xbc, dt_raw, z_ssm = jnp.split(proj, splits, axis=-1)

        o = stick_breaking_attention(
            q.reshape(bsz, seq, SB_HEADS, SB_HEAD_DIM),
            k.reshape(bsz, seq, SB_HEADS, SB_HEAD_DIM),
            v.reshape(bsz, seq, SB_HEADS, SB_HEAD_DIM)).reshape(bsz, seq, D_ATTN)
        y_attn = rms_norm(o, sb_norm_gain[layer]) * jax.nn.silu(z_attn)

        xbc = jax.nn.silu(causal_depthwise_conv(xbc, conv_w[layer], conv_b[layer]))
        xs, b_ssm, c_ssm = jnp.split(xbc, [D_SSM, D_SSM + SSM_GROUPS * SSM_STATE], axis=-1)
        dt = jax.nn.softplus((dt_raw + dt_bias[layer]).astype(jnp.float32))
        a = -jnp.exp(a_log[layer].astype(jnp.float32))
        y = ssd_scan(xs.reshape(bsz, seq, SSM_HEADS, SSM_HEAD_DIM), dt, a,
                     b_ssm.reshape(bsz, seq, SSM_GROUPS, SSM_STATE),
                     c_ssm.reshape(bsz, seq, SSM_GROUPS, SSM_STATE), d_skip[layer])
        y = y.reshape(bsz, seq, D_SSM).astype(x.dtype)
        y_ssm = rms_norm(y * jax.nn.silu(z_ssm), ssm_norm_gain[layer])

        mixed = jnp.concatenate([y_attn, y_ssm], axis=-1) @ w_out[layer]
        x = x + gate[:, None, :] * mixed
    return rms_norm(x, norm_f_gain)
```

```python
import numpy as np
from contextlib import ExitStack
import concourse.bass as bass
import concourse.mybir as mybir
from concourse.bass_utils import run_bass_kernel_spmd

F32 = mybir.dt.float32
BF16 = mybir.dt.bfloat16
AF = mybir.ActivationFunctionType
ALU = mybir.AluOpType
AX = mybir.AxisListType

ENGS = ("pe", "act", "dve", "pool", "sp")
EPS = 1e-6
NEGV = -30000.0


class Buf:
    __slots__ = ("name", "writers", "readers", "const")

    def __init__(self, name, const=False):
        self.name = name
        self.writers = []
        self.readers = []
        self.const = const


class Op:
    __slots__ = ("eng", "fn", "deps", "dma", "idx", "need_sig", "val", "key", "real")

    def __init__(self, eng, fn, dma, key, real):
        self.eng = eng
        self.fn = fn
        self.dma = dma
        self.key = key
        self.real = real
        self.deps = ()
        self.idx = -1
        self.need_sig = False
        self.val = 0


class Sched:
    def __init__(self, nc):
        self.nc = nc
        self.ops = {e: [] for e in ENGS}
        self.last_dma = {}
        self.groups = {e: {} for e in ENGS}
        self._gstart = {}

    def group_begin(self, eng):
        self._gstart[eng] = len(self.ops[eng])

    def group_end(self, eng):
        g0 = self._gstart.pop(eng)
        g1 = len(self.ops[eng])
        if g1 > g0 + 1:
            self.groups[eng][g0] = g1

    def emit(self, eng, fn, reads=(), writes=(), pwrites=(), dma=False, key=None, extra=(), real=True):
        if dma and key is None:
            key = writes[0].name if writes else (pwrites[0].name if pwrites else reads[0].name)
        op = Op(eng, fn, dma, key, real)
        deps = set(extra)
        for b in reads:
            deps.update(b.writers)
        for b in writes:
            deps.update(b.writers)
            deps.update(b.readers)
        for b in pwrites:
            deps.update(b.readers)
            if b.writers:
                deps.add(b.writers[0])
        op.deps = deps
        for b in reads:
            if not b.const:
                b.readers.append(op)
        for b in writes:
            b.writers = [op]
            b.readers = []
        for b in pwrites:
            b.writers.append(op)
        op.idx = len(self.ops[eng])
        self.ops[eng].append(op)
        if dma:
            self.last_dma[key] = op
        return op

    def barrier(self):
        lasts = []
        for e in ENGS:
            for op in reversed(self.ops[e]):
                if op.real and not op.dma:
                    lasts.append(op)
                    break
        deps = set(lasts) | set(self.last_dma.values())
        for e in ENGS:
            self.emit(e, lambda g: g.nop(), extra=deps, real=False)

    def _needed(self, op):
        comp = {}
        dmas = []
        for d in op.deps:
            if d.dma:
                dmas.append(d)
                continue
            if d.eng == op.eng and not op.dma:
                if d.eng == "pe":
                    continue
                if d.eng in ("act", "dve") and d.idx < op.idx - 2:
                    continue
            cur = comp.get(d.eng)
            if cur is None or d.idx > cur.idx:
                comp[d.eng] = d
        return comp, dmas

    def run(self):
        nc = self.nc
        for e in ENGS:
            for op in self.ops[e]:
                comp, _ = self._needed(op)
                for d in comp.values():
                    d.need_sig = True
        for e in ENGS:
            cnt = 0
            for op in self.ops[e]:
                if op.dma:
                    continue
                if op.need_sig:
                    cnt += 1
                    op.val = cnt
        keycnt = {}
        for e in ENGS:
            for op in self.ops[e]:
                if op.dma:
                    keycnt[op.key] = keycnt.get(op.key, 0) + 1
                    op.val = 16 * keycnt[op.key]
        keys = sorted(keycnt.keys())
        with ExitStack() as st:
            esem = {e: st.enter_context(nc.semaphore("s_" + e)) for e in ENGS}
            ksem = {k: st.enter_context(nc.semaphore("d%d" % i)) for i, k in enumerate(keys)}
            block = st.enter_context(nc.Block())
            sched = self

            def replay(ename, eng):
                seen = {}

                def do_waits(op):
                    comp, dmas = sched._needed(op)
                    for d in comp.values():
                        if seen.get(("e", d.eng), 0) < d.val:
                            eng.wait_ge(esem[d.eng], d.val)
                            seen[("e", d.eng)] = d.val
                    kmax = {}
                    for d in dmas:
                        if kmax.get(d.key, 0) < d.val:
                            kmax[d.key] = d.val
                    for kk_, vv_ in kmax.items():
                        if seen.get(("k", kk_), 0) < vv_:
                            eng.wait_ge(ksem[kk_], vv_)
                            seen[("k", kk_)] = vv_

                oplist = sched.ops[ename]
                for op in oplist:
                    g1 = sched.groups[ename].get(op.idx)
                    if g1 is not None:
                        for op2 in oplist[op.idx:g1]:
                            do_waits(op2)
                    comp, dmas = sched._needed(op)
                    for d in comp.values():
                        if seen.get(("e", d.eng), 0) < d.val:
                            eng.wait_ge(esem[d.eng], d.val)
                            seen[("e", d.eng)] = d.val
                    kmax = {}
                    for d in dmas:
                        if kmax.get(d.key, 0) < d.val:
                            kmax[d.key] = d.val
                    for kk_, vv_ in kmax.items():
                        if seen.get(("k", kk_), 0) < vv_:
                            eng.wait_ge(ksem[kk_], vv_)
                            seen[("k", kk_)] = vv_
                    ins = op.fn(eng)
                    if op.dma:
                        ins.then_inc(ksem[op.key], 16)
                    elif op.need_sig:
                        ins.then_inc(esem[ename], 1)

            @block.tensor
            def _(e):
                replay("pe", e)

            @block.scalar
            def _(e):
                replay("act", e)

            @block.vector
            def _(e):
                replay("dve", e)

            @block.gpsimd
            def _(e):
                replay("pool", e)

            @block.sync
            def _(e):
                replay("sp", e)


class Arena:
    def __init__(self, big, nw):
        self.big = big
        self.nw = nw
        self.top = 0

    def f32(self, n):
        off = self.top
        self.top += n
        assert self.top <= self.nw, ("sbuf arena overflow", self.top, self.nw)
        return self.big[:, off:off + n]

    def bf16(self, n):
        w = (n + 1) // 2
        off = self.top
        self.top += w
        assert self.top <= self.nw, ("sbuf arena overflow", self.top, self.nw)
        return self.big[:, off:off + w].bitcast(BF16)


C_ID, C_GE, C_LT, C_LE, C_GT, C_NEG, C_ONE = 0, 128, 256, 384, 512, 640, 768
NCONST = 896
PC_C, PC_BADA, PC_GIN, PC_GA, PC_GS, PC_D, PC_CW, PC_CB = 0, 8, 32, 40, 48, 56, 64, 112
NPCOL = 124
PR_BG, PR_GF, PR_DTB, PR_AL = 0, 1024, 2048, 2064
NPROW = 2080
NW_SBUF = 52224
NWARM = 0
NFILL = 0


def build(S, debug=False):
    NT = S // 512
    NCH = S // 128
    nc = bass.Bass("TRN2", target_bir_lowering=False)
    dk = "ExternalOutput" if debug else "Internal"
    x_d = nc.dram_tensor("x", [S, 1024], F32, kind="ExternalInput").ap()
    wada_d = nc.dram_tensor("w_ada", [1024, 3072], F32, kind="ExternalInput").ap()
    win_d = nc.dram_tensor("w_in", [1024, 6672], F32, kind="ExternalInput").ap()
    wout_d = nc.dram_tensor("w_out", [2048, 1024], F32, kind="ExternalInput").ap()
    const_d = nc.dram_tensor("consts", [128, NCONST], F32, kind="ExternalInput").ap()
    pcol_d = nc.dram_tensor("pcol", [128, NPCOL], F32, kind="ExternalInput").ap()
    prow_d = nc.dram_tensor("prow", [1, NPROW], F32, kind="ExternalInput").ap()
    out_d = nc.dram_tensor("out", [S, 1024], F32, kind="ExternalOutput").ap()
    HT_d = nc.dram_tensor("s_ht", [8, 128, S], BF16, kind=dk).ap()
    QT_d = nc.dram_tensor("s_qt", [1024, S], BF16, kind=dk).ap()
    KT_d = nc.dram_tensor("s_kt", [1024, S], BF16, kind=dk).ap()
    ZA_d = nc.dram_tensor("s_za", [1024, S], BF16, kind=dk).ap()
    V_d = nc.dram_tensor("s_v", [S, 1024], BF16, kind=dk).ap()
    AT_d = nc.dram_tensor("s_at", [2048, S], BF16, kind=dk).ap()
    XS_d = nc.dram_tensor("s_xs", [1024, S], F32, kind=dk).ap()
    BC_d = nc.dram_tensor("s_bc", [512, S], BF16, kind=dk).ap()
    ZS_d = nc.dram_tensor("s_zs", [1024, S], BF16, kind=dk).ap()
    DT_d = nc.dram_tensor("s_dt", [S, 32], F32, kind=dk).ap()

    S_ = Sched(nc)
    E = S_.emit

    with ExitStack() as st:
        big = st.enter_context(nc.sbuf_tensor("big", [128, NW_SBUF], F32))
        psum = st.enter_context(nc.psum_tensor("psum", [128, 4096], F32))
        A = Arena(big, NW_SBUF)

        def bank(i):
            return psum[:, 512 * i:512 * (i + 1)]

        PB = [Buf("ps%d" % i) for i in range(8)]

        cf = A.f32(NCONST); b_cf = Buf("cf", const=True)
        cb = A.bf16(NCONST); b_cb = Buf("cb", const=True)
        zb = A.bf16(512); b_zb = Buf("zb", const=True)
        neg4 = A.bf16(512); b_neg4 = Buf("neg4", const=True)
        pcol = A.f32(NPCOL); b_pcol = Buf("pcol", const=True)
        prow = A.f32(NPROW); b_prow = Buf("prow", const=True)
        modT = A.f32(24); b_modT = Buf("modT")
        g1 = A.f32(8); b_g1 = Buf("g1")
        gate_bc = A.f32(1024); b_gate = Buf("gate_bc")
        a_bc = A.f32(16); b_abc = Buf("a_bc")
        ssq_a = A.f32(NCH); b_ssqa = Buf("ssq_a")
        ssq_s = A.f32(NCH); b_ssqs = Buf("ssq_s")
        rstd_a = A.f32(NCH); rstd_s = A.f32(NCH); b_rstd = Buf("rstd_as")
        ssqf = A.f32(NCH); b_ssqf = Buf("ssqf")
        rstdf = A.f32(NCH); b_rstdf = Buf("rstdf")
        tmp_small = A.f32(64); b_tmps = Buf("tmps")

        idf = cf[:, C_ID:C_ID + 128]
        GEb = cb[:, C_GE:C_GE + 128]
        LTb = cb[:, C_LT:C_LT + 128]
        LTf = cf[:, C_LT:C_LT + 128]
        LEf = cf[:, C_LE:C_LE + 128]
        GTf = cf[:, C_GT:C_GT + 128]
        idb = cb[:, C_ID:C_ID + 128]
        onesf = cf[:, C_ONE:C_ONE + 128]

        E("sp", lambda e: e.dma_start(out=cf, in_=const_d), writes=[b_cf], dma=True)
        E("pool", lambda e: e.dma_start(out=cb, in_=const_d), writes=[b_cb], dma=True)
        E("sp", lambda e: e.dma_start(out=pcol, in_=pcol_d), writes=[b_pcol], dma=True)
        E("sp", lambda e: e.dma_start(out=prow, in_=prow_d.partition_broadcast(128)), writes=[b_prow], dma=True)
        E("dve", lambda e: e.memset(zb, 0.0), writes=[b_zb])
        for q in range(4):
            E("dve", lambda e, q=q: e.tensor_copy(out=neg4[:, q * 128:(q + 1) * 128], in_=cf[:, C_NEG:C_NEG + 128]),
              reads=[b_cf], pwrites=[b_neg4])

        base_mark = A.top
        Wa = A.bf16(8 * 4096); b_Wa = Buf("Wa")
        win_v = win_d.rearrange("(k p) c -> p k c", p=128)
        Wa3 = Wa.rearrange("p (k c) -> p k c", k=8)
        for ci in range(8):
            E("pool", lambda e, ci=ci: e.dma_start(out=Wa3[:, :, ci * 512:(ci + 1) * 512],
                                                  in_=win_v[:, :, ci * 512:(ci + 1) * 512]),
              pwrites=[b_Wa], dma=True, key="Wa")
        p0_mark = A.top

        cact = A.f32(8); b_cact = Buf("cact")
        cact_rep = A.f32(8 * 128); b_crep = Buf("crep")
        wa = [A.f32(8 * 512) for _ in range(2)]
        b_wa = [Buf("wa%d" % i) for i in range(2)]
        E("act", lambda e: e.activation(out=cact, in_=pcol[:, PC_C:PC_C + 8], func=AF.Silu),
          reads=[b_pcol], writes=[b_cact])
        E("dve", lambda e: e.tensor_copy(out=cact_rep.rearrange("p (k m) -> p k m", k=8),
                                         in_=cact.unsqueeze(2).to_broadcast([128, 8, 128])),
          reads=[b_cact], writes=[b_crep])
        E("act", lambda e: e.activation(out=a_bc, in_=prow[:, PR_AL:PR_AL + 16], func=AF.Exp),
          reads=[b_prow], writes=[b_abc])
        E("dve", lambda e: e.tensor_scalar(out=a_bc, in0=a_bc, scalar1=-1.0, scalar2=None, op0=ALU.mult),
          reads=[b_abc], writes=[b_abc])
        wada_v = wada_d.rearrange("(k p) c -> p k c", p=128)
        for ci in range(6):
            r = ci % 2
            E("sp", lambda e, ci=ci, r=r: e.dma_start(out=wa[r].rearrange("p (k c) -> p k c", k=8),
                                                     in_=wada_v[:, :, ci * 512:(ci + 1) * 512]),
              writes=[b_wa[r]], dma=True)
            if ci < 4:
                for mm in range(4):
                    m = ci * 4 + mm
                    for k in range(8):
                        E("pe", lambda e, r=r, mm=mm, m=m, k=k: e.matmul(
                            bank(0)[:, m:m + 1], lhsT=wa[r][:, k * 512 + mm * 128:k * 512 + (mm + 1) * 128],
                            rhs=cact[:, k:k + 1], start=(k == 0), stop=(k == 7)),
                          reads=[b_wa[r], b_cact], pwrites=[PB[0]])
            else:
                n = ci - 4
                for k in range(8):
                    E("pe", lambda e, r=r, n=n, k=k: e.matmul(
                        bank(1 + n), lhsT=cact_rep[:, k * 128:(k + 1) * 128],
                        rhs=wa[r][:, k * 512:(k + 1) * 512], start=(k == 0), stop=(k == 7)),
                      reads=[b_wa[r], b_crep], pwrites=[PB[1 + n]])
        E("dve", lambda e: e.tensor_tensor(out=modT[:, 0:16], in0=bank(0)[:, 0:16], in1=pcol[:, PC_BADA:PC_BADA + 16],
                                           op=ALU.add), reads=[PB[0], b_pcol], writes=[b_modT])
        E("dve", lambda e: e.scalar_tensor_tensor(out=g1, in0=modT[:, 8:16], scalar=1.0, in1=pcol[:, PC_GIN:PC_GIN + 8],
                                                  op0=ALU.add, op1=ALU.mult), reads=[b_modT, b_pcol], writes=[b_g1])
        E("dve", lambda e: e.tensor_tensor(out=gate_bc, in0=psum[:, 512:1536], in1=prow[:, PR_BG:PR_BG + 1024], op=ALU.add),
          reads=[PB[1], PB[2], b_prow], writes=[b_gate])
        shiftc = modT
        S_.barrier()
        A.top = p0_mark

        xt = [A.f32(4096) for _ in range(2)]; b_xt = [Buf("xt%d" % i) for i in range(2)]
        hT = [A.bf16(8 * 512) for _ in range(2)]; b_hT = [Buf("hT%d" % i) for i in range(2)]
        junk = A.bf16(1024); b_junk = Buf("junk")
        ssq1 = A.f32(4 * NT); b_ssq1 = Buf("ssq1")
        rs1 = A.f32(4 * NT); b_rs1 = Buf("rs1")
        stq = A.bf16(8 * 512); b_stq = Buf("stq")
        stk = A.bf16(8 * 512); b_stk = Buf("stk")
        stz = A.bf16(8 * 512); b_stz = Buf("stz")
        stv = A.bf16(4 * 1024); b_stv = Buf("stv")
        stages = [(stq, b_stq, QT_d), (stk, b_stk, KT_d), (stz, b_stz, ZA_d)]
        pcnt = [0]

        def next_bank(lo, n):
            i = lo + (pcnt[0] % n)
            pcnt[0] += 1
            return i

        evq = [0]
        def p1a_front(tt):
            r = tt % 2
            E("sp", lambda e, tt=tt, r=r: e.dma_start(
                out=xt[r].rearrange("p (j d) -> p j d", j=4),
                in_=x_d[tt * 512:(tt + 1) * 512, :].rearrange("(j p) d -> p j d", p=128)),
              writes=[b_xt[r]], dma=True)
            for j in range(4):
                E("act", lambda e, r=r, j=j, tt=tt: e.activation(
                    out=junk, in_=xt[r][:, j * 1024:(j + 1) * 1024], func=AF.Square,
                    accum_out=ssq1[:, tt * 4 + j:tt * 4 + j + 1]),
                  reads=[b_xt[r]], writes=[b_junk], pwrites=[b_ssq1])
            E("act", lambda e, tt=tt: e.activation(out=rs1[:, tt * 4:tt * 4 + 4], in_=ssq1[:, tt * 4:tt * 4 + 4],
                                                   func=AF.Ln, scale=1.0 / 1024, bias=EPS),
              reads=[b_ssq1], pwrites=[b_rs1])
            E("act", lambda e, tt=tt: e.activation(out=rs1[:, tt * 4:tt * 4 + 4], in_=rs1[:, tt * 4:tt * 4 + 4],
                                                   func=AF.Exp, scale=-0.5),
              reads=[b_rs1], pwrites=[b_rs1])
            for j in range(4):
                E("dve", lambda e, r=r, j=j, tt=tt: e.tensor_scalar(
                    out=xt[r][:, j * 1024:(j + 1) * 1024], in0=xt[r][:, j * 1024:(j + 1) * 1024],
                    scalar1=rs1[:, tt * 4 + j:tt * 4 + j + 1], scalar2=None, op0=ALU.mult),
                  reads=[b_rs1, b_xt[r]], writes=[b_xt[r]])
            for k in range(8):
                bi = next_bank(0, 4)
                for j in range(4):
                    E("pe", lambda e, r=r, j=j, k=k, bi=bi: e.transpose(
                        bank(bi)[:, j * 128:(j + 1) * 128], xt[r][:, j * 1024 + k * 128:j * 1024 + (k + 1) * 128], idf),
                      reads=[b_xt[r], b_cf], writes=[PB[bi]] if j == 0 else (), pwrites=() if j == 0 else [PB[bi]])
                if k % 2 == 0:
                    E("dve", lambda e, r=r, k=k, bi=bi: e.tensor_scalar(
                        out=hT[r][:, k * 512:(k + 1) * 512], in0=bank(bi), scalar1=g1[:, k:k + 1],
                        scalar2=shiftc[:, k:k + 1], op0=ALU.mult, op1=ALU.add),
                      reads=[PB[bi], b_g1, b_modT], pwrites=[b_hT[r]] if k else (), writes=() if k else [b_hT[r]])
                else:
                    E("act", lambda e, r=r, k=k, bi=bi: e.activation(
                        out=hT[r][:, k * 512:(k + 1) * 512], in_=bank(bi), func=AF.Identity,
                        scale=g1[:, k:k + 1], bias=shiftc[:, k:k + 1]),
                      reads=[PB[bi], b_g1, b_modT], pwrites=[b_hT[r]])
            E("pool", lambda e, r=r, tt=tt: e.dma_start(
                out=HT_d[:, :, tt * 512:(tt + 1) * 512].rearrange("k p t -> p k t"),
                in_=hT[r].rearrange("p (k t) -> p k t", k=8)), reads=[b_hT[r]], dma=True, key="hTo%d" % r)

        def p1a_back1(tt):
            r = tt % 2
            for grp in range(3):
                stg, b_stg, dst = stages[grp]
                c_base = [0, 1024, 3072][grp]
                for m in range(8):
                    bi = next_bank(4, 4)
                    for k in range(8):
                        E("pe", lambda e, r=r, k=k, bi=bi, c0=c_base + m * 128: e.matmul(
                            bank(bi), lhsT=Wa[:, k * 4096 + c0:k * 4096 + c0 + 128],
                            rhs=hT[r][:, k * 512:(k + 1) * 512], start=(k == 0), stop=(k == 7)),
                          reads=[b_Wa, b_hT[r]], writes=[PB[bi]] if k == 0 else (), pwrites=() if k == 0 else [PB[bi]])
                    wkw = dict(writes=[b_stg]) if m == 0 else dict(pwrites=[b_stg])
                    osl = stg[:, m * 512:(m + 1) * 512]
                    if grp == 0:
                        if evq[0] % 2 == 0:
                            E("act", lambda e, osl=osl, bi=bi: e.mul(out=osl, in_=bank(bi), mul=0.125),
                              reads=[PB[bi]], **wkw)
                        else:
                            E("dve", lambda e, osl=osl, bi=bi: e.tensor_scalar(out=osl, in0=bank(bi), scalar1=0.125,
                                                                               scalar2=None, op0=ALU.mult),
                              reads=[PB[bi]], **wkw)
                        evq[0] += 1
                    elif grp == 1:
                        if evq[0] % 2 == 0:
                            E("act", lambda e, osl=osl, bi=bi: e.copy(out=osl, in_=bank(bi)), reads=[PB[bi]], **wkw)
                        else:
                            E("dve", lambda e, osl=osl, bi=bi: e.tensor_copy(out=osl, in_=bank(bi)), reads=[PB[bi]], **wkw)
                        evq[0] += 1
                    else:
                        E("act", lambda e, osl=osl, bi=bi: e.activation(out=osl, in_=bank(bi), func=AF.Silu),
                          reads=[PB[bi]], **wkw)
                E("pool", lambda e, stg=stg, dst=dst, tt=tt: e.dma_start(
                    out=dst[:, tt * 512:(tt + 1) * 512].rearrange("(m p) t -> p m t", p=128),
                    in_=stg.rearrange("p (m t) -> p m t", m=8)), reads=[b_stg], dma=True)

        def p1a_back2(tt):
            r = tt % 2
            for j in range(4):
                for n in range(2):
                    bi = next_bank(4, 4)
                    for k in range(8):
                        E("pe", lambda e, r=r, k=k, bi=bi, j=j, n=n: e.matmul(
                            bank(bi), lhsT=hT[r][:, k * 512 + j * 128:k * 512 + (j + 1) * 128],
                            rhs=Wa[:, k * 4096 + 2048 + n * 512:k * 4096 + 2048 + (n + 1) * 512],
                            start=(k == 0), stop=(k == 7)),
                          reads=[b_Wa, b_hT[r]], writes=[PB[bi]] if k == 0 else (), pwrites=() if k == 0 else [PB[bi]])
                    wkw = dict(writes=[b_stv]) if (j == 0 and n == 0) else dict(pwrites=[b_stv])
                    osl = stv[:, j * 1024 + n * 512:j * 1024 + (n + 1) * 512]
                    if evq[0] % 2 == 0:
                        E("act", lambda e, osl=osl, bi=bi: e.copy(out=osl, in_=bank(bi)), reads=[PB[bi]], **wkw)
                    else:
                        E("dve", lambda e, osl=osl, bi=bi: e.tensor_copy(out=osl, in_=bank(bi)), reads=[PB[bi]], **wkw)
                    evq[0] += 1
            E("pool", lambda e, tt=tt: e.dma_start(
                out=V_d[tt * 512:(tt + 1) * 512, :].rearrange("(j p) c -> p j c", p=128),
                in_=stv.rearrange("p (j c) -> p j c", j=4)), reads=[b_stv], dma=True)

        p1a_front(0)
        for tt in range(NT):
            p1a_back1(tt)
            if tt + 1 < NT:
                p1a_front(tt + 1)
            p1a_back2(tt)
        S_.barrier()
        A.top = base_mark

        NWS = 2576
        Ws = A.bf16(8 * NWS); b_Ws = Buf("Ws")
        Ws3 = Ws.rearrange("p (k c) -> p k c", k=8)
        for ci in range(6):
            c0 = ci * 512
            c1 = min(NWS, c0 + 512)
            E("pool", lambda e, c0=c0, c1=c1: e.dma_start(out=Ws3[:, :, c0:c1], in_=win_v[:, :, 4096 + c0:4096 + c1]),
              pwrites=[b_Ws], dma=True, key="Ws")
        hSr = [A.bf16(8 * 512) for _ in range(2)]; b_hSr = [Buf("hS%d" % i) for i in range(2)]
        XW_ = 515
        xin = A.f32(12 * XW_); b_xin = [Buf("xin%d" % m) for m in range(12)]
        cv = A.f32(12 * 512); b_cv = [Buf("cv%d" % m) for m in range(12)]
        BCs = A.bf16(4 * 512); b_BCs = Buf("BCs")
        ZS = A.bf16(8 * 512); b_ZS = Buf("ZSst")
        dtv = A.f32(64); b_dtv = Buf("dtv")
        dts = A.f32(64); b_dts = Buf("dts")
        lds = A.f32(64); b_lds = Buf("lds")
        cw = pcol[:, PC_CW:PC_CW + 48]
        cbias = pcol[:, PC_CB:PC_CB + 12]
        E("dve", lambda e: e.memset(xin, 0.0), writes=b_xin)
        for tt in range(NT):
            hS = hSr[tt % 2]; b_hS = b_hSr[tt % 2]
            E("sp", lambda e, tt=tt, hS=hS: e.dma_start(
                out=hS.rearrange("p (k t) -> p k t", k=8),
                in_=HT_d[:, :, tt * 512:(tt + 1) * 512].rearrange("k p t -> p k t")), writes=[b_hS], dma=True)
            for m in range(12):
                if tt > 0:
                    E("dve", lambda e, m=m: e.tensor_copy(out=xin[:, m * XW_:m * XW_ + 3], in_=xin[:, m * XW_ + 512:m * XW_ + 515]),
                      reads=[b_xin[m]], writes=[b_xin[m]])
                bi = next_bank(0, 4)
                for k in range(8):
                    E("pe", lambda e, k=k, bi=bi, c0=m * 128, hS=hS: e.matmul(
                        bank(bi), lhsT=Ws[:, k * NWS + c0:k * NWS + c0 + 128], rhs=hS[:, k * 512:(k + 1) * 512],
                        start=(k == 0), stop=(k == 7)),
                      reads=[b_Ws, b_hS], writes=[PB[bi]] if k == 0 else (), pwrites=() if k == 0 else [PB[bi]])
                E("act", lambda e, m=m, bi=bi: e.copy(out=xin[:, m * XW_ + 3:m * XW_ + 515], in_=bank(bi)),
                  reads=[PB[bi]], writes=[b_xin[m]])
                acc = cv[:, m * 512:(m + 1) * 512]
                E("dve", lambda e, m=m, acc=acc: e.tensor_scalar(
                    out=acc, in0=xin[:, m * XW_ + 3:m * XW_ + 515], scalar1=cw[:, m * 4 + 3:m * 4 + 4],
                    scalar2=cbias[:, m:m + 1], op0=ALU.mult, op1=ALU.add),
                  reads=[b_xin[m], b_pcol], writes=[b_cv[m]])
                for kk in range(3):
                    E("dve", lambda e, m=m, acc=acc, kk=kk: e.scalar_tensor_tensor(
                        out=acc, in0=xin[:, m * XW_ + kk:m * XW_ + kk + 512], scalar=cw[:, m * 4 + kk:m * 4 + kk + 1],
                        in1=acc, op0=ALU.mult, op1=ALU.add),
                      reads=[b_xin[m], b_pcol, b_cv[m]], writes=[b_cv[m]])
                if m < 8:
                    E("act", lambda e, acc=acc: e.activation(out=acc, in_=acc, func=AF.Silu),
                      reads=[b_cv[m]], writes=[b_cv[m]])
                else:
                    dst = BCs[:, (m - 8) * 512:(m - 7) * 512]
                    E("act", lambda e, acc=acc, dst=dst: e.activation(out=dst, in_=acc, func=AF.Silu),
                      reads=[b_cv[m]], writes=[b_BCs] if m == 8 else (), pwrites=() if m == 8 else [b_BCs])
            E("pool", lambda e, tt=tt: e.dma_start(
                out=XS_d[:, tt * 512:(tt + 1) * 512].rearrange("(m p) t -> p m t", p=128),
                in_=cv[:, 0:8 * 512].rearrange("p (m t) -> p m t", m=8)), reads=b_cv[0:8], dma=True, key="cvo")
            E("pool", lambda e, tt=tt: e.dma_start(
                out=BC_d[:, tt * 512:(tt + 1) * 512].rearrange("(m p) t -> p m t", p=128),
                in_=BCs.rearrange("p (m t) -> p m t", m=4)), reads=[b_BCs], dma=True, key="bco")
            for m in range(8):
                bi = next_bank(0, 4)
                for k in range(8):
                    E("pe", lambda e, k=k, bi=bi, c0=1552 + m * 128, hS=hS: e.matmul(
                        bank(bi), lhsT=Ws[:, k * NWS + c0:k * NWS + c0 + 128], rhs=hS[:, k * 512:(k + 1) * 512],
                        start=(k == 0), stop=(k == 7)),
                      reads=[b_Ws, b_hS], writes=[PB[bi]] if k == 0 else (), pwrites=() if k == 0 else [PB[bi]])
                E("act", lambda e, m=m, bi=bi: e.activation(out=ZS[:, m * 512:(m + 1) * 512], in_=bank(bi), func=AF.Silu),
                  reads=[PB[bi]], writes=[b_ZS] if m == 0 else (), pwrites=() if m == 0 else [b_ZS])
            E("pool", lambda e, tt=tt: e.dma_start(
                out=ZS_d[:, tt * 512:(tt + 1) * 512].rearrange("(m p) t -> p m t", p=128),
                in_=ZS.rearrange("p (m t) -> p m t", m=8)), reads=[b_ZS], dma=True, key="zso")
            for j in range(4):
                for k in range(8):
                    E("pe", lambda e, j=j, k=k, hS=hS: e.matmul(
                        bank(7)[:, j * 16:(j + 1) * 16], lhsT=hS[:, k * 512 + j * 128:k * 512 + (j + 1) * 128],
                        rhs=Ws[:, k * NWS + 1536:k * NWS + 1552], start=(k == 0), stop=(k == 7)),
                      reads=[b_Ws, b_hS], writes=[PB[7]] if (j == 0 and k == 0) else (),
                      pwrites=() if (j == 0 and k == 0) else [PB[7]])
            E("dve", lambda e: e.tensor_tensor(
                out=dtv.rearrange("p (j h) -> p j h", j=4), in0=bank(7)[:, 0:64].rearrange("p (j h) -> p j h", j=4),
                in1=prow[:, PR_DTB:PR_DTB + 16].unsqueeze(1).to_broadcast([128, 4, 16]), op=ALU.add),
              reads=[PB[7], b_prow], writes=[b_dtv])
            E("act", lambda e: e.activation(out=dtv, in_=dtv, func=AF.Exp), reads=[b_dtv], writes=[b_dtv])
            E("act", lambda e: e.activation(out=dts, in_=dtv, func=AF.Ln, bias=1.0), reads=[b_dtv], writes=[b_dts])
            E("dve", lambda e: e.tensor_tensor(
                out=lds.rearrange("p (j h) -> p j h", j=4), in0=dts.rearrange("p (j h) -> p j h", j=4),
                in1=a_bc.unsqueeze(1).to_broadcast([128, 4, 16]), op=ALU.mult),
              reads=[b_dts, b_abc], writes=[b_lds])
            E("pool", lambda e, tt=tt: e.dma_start(
                out=DT_d[tt * 512:(tt + 1) * 512, 0:16].rearrange("(j p) h -> p j h", p=128),
                in_=dts.rearrange("p (j h) -> p j h", j=4)), reads=[b_dts], dma=True, key="dto")
            E("pool", lambda e, tt=tt: e.dma_start(
                out=DT_d[tt * 512:(tt + 1) * 512, 16:32].rearrange("(j p) h -> p j h", p=128),
                in_=lds.rearrange("p (j h) -> p j h", j=4)), reads=[b_lds], dma=True, key="ldo")
        S_.barrier()
        A.top = base_mark

        NB = NCH
        QT = [A.bf16(S) for _ in range(1)] * 2; KT = [A.bf16(S) for _ in range(1)] * 2
        Vall = A.bf16(NB * 1024); b_Vall = Buf("Vall")
        ZAq = [A.bf16(512) for _ in range(2)]; b_ZAq = [Buf("ZAq%d" % i) for i in range(2)]
        b_QT = [Buf("QT0")] * 2; b_KT = [Buf("KT0")] * 2
        NE, NL, NX, NWB = 3, 4, 2, 3
        eb = [A.f32(1024) for _ in range(NE)]; b_eb = [Buf("e%d" % i) for i in range(NE)]
        lb = [A.bf16(1024) for _ in range(NL)]; b_lb = [Buf("l%d" % i) for i in range(NL)]
        xb = [A.f32(1024) for _ in range(NX)]; b_xb = [Buf("xx%d" % i) for i in range(NX)]
        wb = [A.bf16(1024) for _ in range(NWB)]; b_wb = [Buf("w%d" % i) for i in range(NWB)]
        sqa = A.f32(512); b_sqa = Buf("sqa")
        ats = [A.bf16(512) for _ in range(2)]; b_ats = [Buf("ats%d" % i) for i in range(2)]
        atq = [0]

        SB7 = 7
        b7 = bank(SB7)
        xsI = [A.f32(1024) for _ in range(2)]; b_xsI = [Buf("xsI%d" % i) for i in range(2)]
        bcI = [A.bf16(512) for _ in range(2)]; b_bcI = [Buf("bcI%d" % i) for i in range(2)]
        zsI = [A.bf16(1024) for _ in range(2)]; b_zsI = [Buf("zsI%d" % i) for i in range(2)]
        dtI = [A.f32(32) for _ in range(2)]; b_dtI = [Buf("dtI%d" % i) for i in range(2)]
        Rr = A.f32(16 * 128); b_R = Buf("R")
        decay = A.f32(16 * 128); b_decay = Buf("decay")
        Mm = A.bf16(16 * 128); b_M = Buf("M")
        CE = A.bf16(16 * 128); b_CE = Buf("CE")
        xtf = A.f32(1024); b_xtf = Buf("xtf")
        xtb = A.bf16(1024); b_xtb = Buf("xtb")
        Btk = A.bf16(256); b_Btk = Buf("Btk")
        XWt = A.bf16(1024); b_XW = Buf("XW")
        cbs = A.f32(256); b_cbs = Buf("cbs")
        dcd = A.f32(32); b_dcd = Buf("dcd")
        wgt = A.f32(16); b_wgt = Buf("wgt")
        state = A.f32(1024); b_state = Buf("state")
        stateb = A.bf16(1024); b_stateb = Buf("stateb")
        stsT = A.bf16(8 * 512); b_stsT = Buf("stsT")
        ub = Rr[:, 0:1024]
        sqs = Rr[:, 1024:2048]
        E("dve", lambda e: e.memset(state, 0.0), writes=[b_state])
        E("dve", lambda e: e.memset(stateb, 0.0), writes=[b_stateb])

        def ssd_load(c):
            sl = c % 2
            E("sp", lambda e: e.dma_start(out=xsI[sl].rearrange("p (m t) -> p m t", m=8),
                                          in_=XS_d[:, c * 128:(c + 1) * 128].rearrange("(m p) t -> p m t", p=128)),
              writes=[b_xsI[sl]], dma=True)
            E("sp", lambda e: e.dma_start(out=bcI[sl].rearrange("p (m t) -> p m t", m=4),
                                          in_=BC_d[:, c * 128:(c + 1) * 128].rearrange("(m p) t -> p m t", p=128)),
              writes=[b_bcI[sl]], dma=True)
            E("sp", lambda e: e.dma_start(out=zsI[sl].rearrange("p (m t) -> p m t", m=8),
                                          in_=ZS_d[:, c * 128:(c + 1) * 128].rearrange("(m p) t -> p m t", p=128)),
              writes=[b_zsI[sl]], dma=True)
            E("sp", lambda e: e.dma_start(out=dtI[sl], in_=DT_d[c * 128:(c + 1) * 128, :]), writes=[b_dtI[sl]], dma=True)

        def ssd_chunk(c):
            sl = c % 2
            xsT, bc_, zs_, dt_ = xsI[sl], bcI[sl], zsI[sl], dtI[sl]
            bx, bb, bz, bd = b_xsI[sl], b_bcI[sl], b_zsI[sl], b_dtI[sl]
            dtj = dt_[:, 0:16]
            ldj = dt_[:, 16:32]
            jq = c % 4
            if c + 1 < NCH:
                ssd_load(c + 1)
            E("dve", lambda e: e.tensor_tensor(
                out=Rr.rearrange("p (h l) -> p h l", h=16), in0=LEf.unsqueeze(1).to_broadcast([128, 16, 128]),
                in1=ldj.unsqueeze(2).to_broadcast([128, 16, 128]), op=ALU.mult),
              reads=[b_cf, bd], writes=[b_R])
            for hf in range(2):
                for m4 in range(4):
                    m = hf * 4 + m4
                    E("pe", lambda e, m=m, m4=m4: e.transpose(b7[:, m4 * 128:(m4 + 1) * 128], xsT[:, m * 128:(m + 1) * 128], idf),
                      reads=[bx, b_cf], writes=[PB[SB7]] if m4 == 0 else (), pwrites=() if m4 == 0 else [PB[SB7]])
                E("dve", lambda e, hf=hf: e.tensor_copy(out=xtf[:, hf * 512:(hf + 1) * 512], in_=b7), reads=[PB[SB7]],
                  writes=[b_xtf] if hf == 0 else (), pwrites=() if hf == 0 else [b_xtf])
                yield
            E("dve", lambda e: e.tensor_copy(out=xtb, in_=xtf), reads=[b_xtf], writes=[b_xtb])
            b7b = b7[:, 0:128].bitcast(BF16)
            for g in range(2):
                E("pe", lambda e, g=g: e.transpose(b7b[:, g * 128:(g + 1) * 128], bc_[:, g * 128:(g + 1) * 128], idb),
                  reads=[bb, b_cb], writes=[PB[SB7]] if g == 0 else (), pwrites=() if g == 0 else [PB[SB7]])
            for g in range(2):
                E("pe", lambda e, g=g: e.matmul(b7[:, 256 + g * 128:256 + (g + 1) * 128], lhsT=bc_[:, g * 128:(g + 1) * 128],
                                                rhs=bc_[:, 256 + g * 128:256 + (g + 1) * 128], start=True, stop=True),
                  reads=[bb], pwrites=[PB[SB7]])
            E("dve", lambda e: e.tensor_copy(out=Btk, in_=b7b), reads=[PB[SB7]], writes=[b_Btk])
            E("dve", lambda e: e.tensor_copy(out=cbs, in_=b7[:, 256:512]), reads=[PB[SB7]], writes=[b_cbs])
            yield
            for q in range(4):
                E("pe", lambda e, q=q: e.matmul(b7, lhsT=GTf, rhs=Rr[:, q * 512:(q + 1) * 512], start=True, stop=False,
                                                skip_group_check=True), reads=[b_R, b_cf], writes=[PB[SB7]])
                E("pe", lambda e: e.matmul(b7, lhsT=idb, rhs=neg4, start=False, stop=True, skip_group_check=True),
                  reads=[b_cb, b_neg4], pwrites=[PB[SB7]])
                E("act", lambda e, q=q: e.activation(out=decay[:, q * 512:(q + 1) * 512], in_=b7, func=AF.Exp),
                  reads=[PB[SB7]], writes=[b_decay] if q == 0 else (), pwrites=() if q == 0 else [b_decay])
                yield
            for hq in range(2):
                for h in range(hq * 8, hq * 8 + 8):
                    g = h // 8
                    E("dve", lambda e, h=h, g=g: e.scalar_tensor_tensor(
                        out=Mm[:, h * 128:(h + 1) * 128], in0=decay[:, h * 128:(h + 1) * 128], scalar=dtj[:, h:h + 1],
                        in1=cbs[:, g * 128:(g + 1) * 128], op0=ALU.mult, op1=ALU.mult),
                      reads=[b_decay, bd, b_cbs], writes=[b_M] if h == 0 else (), pwrites=() if h == 0 else [b_M])
                if hq == 0:
                    E("pe", lambda e: e.matmul(b7[:, 0:16], lhsT=GTf, rhs=ldj, start=True, stop=True),
                      reads=[bd, b_cf], writes=[PB[SB7]])
                    E("pe", lambda e: e.matmul(b7[:, 16:32], lhsT=onesf, rhs=ldj, start=True, stop=True),
                      reads=[bd, b_cf], pwrites=[PB[SB7]])
                    E("act", lambda e: e.activation(out=dcd, in_=b7[:, 0:32], func=AF.Exp), reads=[PB[SB7]], writes=[b_dcd])
                yield
            E("dve", lambda e: e.tensor_tensor(out=wgt, in0=dtj, in1=dcd[:, 0:16], op=ALU.mult),
              reads=[bd, b_dcd], writes=[b_wgt])
            E("dve", lambda e: e.tensor_tensor(
                out=XWt.rearrange("p (h q) -> p h q", h=16), in0=xtf.rearrange("p (h q) -> p h q", h=16),
                in1=wgt.unsqueeze(2).to_broadcast([128, 16, 64]), op=ALU.mult),
              reads=[b_xtf, b_wgt], writes=[b_XW])
            for q in range(4):
                E("pe", lambda e, q=q: e.matmul(b7, lhsT=onesf, rhs=Rr[:, q * 512:(q + 1) * 512], start=True, stop=True),
                  reads=[b_R, b_cf], writes=[PB[SB7]])
                E("act", lambda e, q=q: e.activation(out=decay[:, q * 512:(q + 1) * 512], in_=b7, func=AF.Exp),
                  reads=[PB[SB7], b_M], writes=[b_decay] if q == 0 else (), pwrites=() if q == 0 else [b_decay])
                yield
            for g in range(2):
                E("dve", lambda e, g=g: e.tensor_tensor(
                    out=CE[:, g * 1024:(g + 1) * 1024].rearrange("p (h l) -> p h l", h=8),
                    in0=decay[:, g * 1024:(g + 1) * 1024].rearrange("p (h l) -> p h l", h=8),
                    in1=bc_[:, 256 + g * 128:256 + (g + 1) * 128].unsqueeze(1).to_broadcast([128, 8, 128]), op=ALU.mult),
                  reads=[b_decay, bb], writes=[b_CE] if g == 0 else (), pwrites=() if g == 0 else [b_CE])
            yield
            for hb in range(2):
                for p4 in range(4):
                    pr = hb * 4 + p4
                    csl = slice(p4 * 128, (p4 + 1) * 128)
                    for half in range(2):
                        h = 2 * pr + half
                        first = (p4 == 0 and half == 0)
                        E("pe", lambda e, csl=csl, half=half, h=h: e.matmul(
                            b7[64 * half:64 * half + 64, csl], lhsT=xtb[:, h * 64:(h + 1) * 64],
                            rhs=Mm[:, h * 128:(h + 1) * 128], start=True, stop=False, skip_group_check=True),
                          reads=[b_xtb, b_M], writes=[PB[SB7]] if first else (), pwrites=() if first else [PB[SB7]])
                        E("pe", lambda e, csl=csl, half=half, h=h: e.matmul(
                            b7[64 * half:64 * half + 64, csl], lhsT=stateb[:, h * 64:(h + 1) * 64],
                            rhs=CE[:, h * 128:(h + 1) * 128], start=False, stop=True, skip_group_check=True),
                          reads=[b_stateb, b_CE], pwrites=[PB[SB7]])
                usl = ub[:, hb * 512:(hb + 1) * 512]
                u3 = usl.rearrange("p (m t) -> p m t", m=4)
                E("dve", lambda e, hb=hb, u3=u3: e.tensor_tensor(
                    out=u3, in0=xsT[:, hb * 512:(hb + 1) * 512].rearrange("p (m t) -> p m t", m=4),
                    in1=pcol[:, PC_D + hb * 4:PC_D + hb * 4 + 4].unsqueeze(2).to_broadcast([128, 4, 128]), op=ALU.mult),
                  reads=[bx, b_pcol], writes=[b_R] if hb == 0 else (), pwrites=() if hb == 0 else [b_R])
                E("dve", lambda e, usl=usl: e.tensor_tensor(out=usl, in0=usl, in1=b7, op=ALU.add),
                  reads=[b_R, PB[SB7]], pwrites=[b_R])
                E("dve", lambda e, hb=hb, usl=usl: e.tensor_tensor(out=usl, in0=usl, in1=zs_[:, hb * 512:(hb + 1) * 512],
                                                                   op=ALU.mult), reads=[b_R, bz], pwrites=[b_R])
                o3 = stsT.rearrange("p (m t) -> p m t", m=8)[:, hb * 4:hb * 4 + 4, jq * 128:(jq + 1) * 128]
                E("dve", lambda e, hb=hb, u3=u3, o3=o3: e.tensor_tensor(
                    out=o3, in0=u3,
                    in1=pcol[:, PC_GS + hb * 4:PC_GS + hb * 4 + 4].unsqueeze(2).to_broadcast([128, 4, 128]), op=ALU.mult),
                  reads=[b_R, b_pcol], pwrites=[b_stsT])
                yield
            E("dve", lambda e: e.tensor_tensor(out=sqs, in0=ub, in1=ub, op=ALU.mult), reads=[b_R], pwrites=[b_R])
            yield
            for pr in range(8):
                E("pe", lambda e, pr=pr: e.matmul(b7[:, 0:1], lhsT=sqs[:, pr * 128:(pr + 1) * 128], rhs=onesf[:, 0:1],
                                                 start=(pr == 0), stop=(pr == 7)),
                  reads=[b_R, b_cf], writes=[PB[SB7]] if pr == 0 else (), pwrites=() if pr == 0 else [PB[SB7]])
            E("dve", lambda e: e.tensor_copy(out=ssq_s[:, c:c + 1], in_=b7[:, 0:1]), reads=[PB[SB7]], pwrites=[b_ssqs])
            yield
            for g in range(2):
                E("pe", lambda e, g=g: e.matmul(b7, lhsT=Btk[:, g * 128:(g + 1) * 128], rhs=XWt[:, g * 512:(g + 1) * 512],
                                                start=True, stop=True), reads=[b_Btk, b_XW], writes=[PB[SB7]])
                ssl = state[:, g * 512:(g + 1) * 512]
                E("dve", lambda e, g=g, ssl=ssl: e.tensor_tensor(
                    out=ssl.rearrange("p (h q) -> p h q", h=8), in0=ssl.rearrange("p (h q) -> p h q", h=8),
                    in1=dcd[:, 16 + g * 8:16 + g * 8 + 8].unsqueeze(2).to_broadcast([128, 8, 64]), op=ALU.mult),
                  reads=[b_state, b_dcd], writes=[b_state] if g == 0 else (), pwrites=() if g == 0 else [b_state])
                E("dve", lambda e, ssl=ssl: e.tensor_tensor(out=ssl, in0=ssl, in1=b7, op=ALU.add),
                  reads=[b_state, PB[SB7]], pwrites=[b_state])
                yield
            E("dve", lambda e: e.tensor_copy(out=stateb, in_=state), reads=[b_state], writes=[b_stateb])
            if jq == 3:
                tt_ = c // 4
                E("sp", lambda e: e.dma_start(
                    out=AT_d[1024:2048, tt_ * 512:(tt_ + 1) * 512].rearrange("(m p) t -> p m t", p=128),
                    in_=stsT.rearrange("p (m t) -> p m t", m=8)), reads=[b_stsT], dma=True)
            yield

        def ssd_all():
            for c in range(NCH):
                for _ in ssd_chunk(c):
                    yield

        ssd_load(0)
        ssd_gen = ssd_all()

        def ssd_step():
            try:
                next(ssd_gen)
                return True
            except StopIteration:
                return False

        def v2(ap, c0):
            return ap.rearrange("p (b c) -> p b c", b=2)[:, :, c0:512]

        def load_hp(hp):
            r = hp % 2
            E("sp", lambda e: e.dma_start(out=QT[r], in_=QT_d[hp * 128:(hp + 1) * 128, :]), writes=[b_QT[r]], dma=True)
            E("sp", lambda e: e.dma_start(out=KT[r], in_=KT_d[hp * 128:(hp + 1) * 128, :]), writes=[b_KT[r]], dma=True)

        Vall3 = Vall.rearrange("p (n c) -> p n c", c=1024)
        V_v = V_d.rearrange("(n p) c -> p n c", p=128)
        nvq = max(1, NB // 8)
        for vq in range(0, NB, nvq):
            E("sp", lambda e, vq=vq: e.dma_start(out=Vall3[:, vq:vq + nvq, :], in_=V_v[:, vq:vq + nvq, :]),
              pwrites=[b_Vall], dma=True, key="Vall")
        load_hp(0)
        ucount = [0]

        def do_hp(hp, r):
            if hp > 0:
                load_hp(hp)
            steps = []
            for qt in range(NT):
                for jj in range(4 * qt + 3, -1, -1):
                    steps.append((qt, jj))
            nu = len(steps)
            info = {}

            def st_qk(n):
                qt, jj = steps[n]
                u = ucount[0] + n
                kk = jj - 4 * qt
                c0 = 128 * kk if kk > 0 else 0
                ap_ = 2 * (u % 2)
                info[n] = (qt, jj, u, kk, c0, ap_)
                if jj == 4 * qt + 3:
                    zi = (hp * NT + qt) % 2
                    E("sp", lambda e, zi=zi: e.dma_start(out=ZAq[zi], in_=ZA_d[hp * 128:(hp + 1) * 128, qt * 512:(qt + 1) * 512]),
                      writes=[b_ZAq[zi]], dma=True)
                for half in range(2):
                    pb = 64 * half
                    E("pe", lambda e, pb=pb, half=half: e.matmul(
                        bank(ap_ + half)[:, c0:512], lhsT=KT[r][pb:pb + 64, jj * 128:(jj + 1) * 128],
                        rhs=QT[r][pb:pb + 64, qt * 512 + c0:(qt + 1) * 512], start=True, stop=True),
                      reads=[b_KT[r], b_QT[r]], writes=[PB[ap_ + half]])

            def st_act1(n):
                qt, jj, u, kk, c0, ap_ = info[n]
                ei, li = u % NE, u % NL
                E("act", lambda e: e.activation(out=v2(eb[ei], c0), in_=v2(psum[:, ap_ * 512:(ap_ + 2) * 512], c0), func=AF.Exp),
                  reads=[PB[ap_], PB[ap_ + 1]], writes=[b_eb[ei]])
                if kk >= 0:
                    ev = eb[ei].rearrange("p (b c) -> p b c", b=2)[:, :, c0:c0 + 128]
                    E("pool", lambda e: e.tensor_tensor(out=ev, in0=ev, in1=LTf.unsqueeze(1).to_broadcast([128, 2, 128]),
                                                        op=ALU.mult), reads=[b_eb[ei], b_cf], writes=[b_eb[ei]])

            def st_act2(n):
                qt, jj, u, kk, c0, ap_ = info[n]
                ei, li = u % NE, u % NL
                E("act", lambda e: e.activation(out=v2(lb[li], c0), in_=v2(eb[ei], c0), func=AF.Ln, bias=1.0),
                  reads=[b_eb[ei]], writes=[b_lb[li]])

            def st_ge(n):
                qt, jj, u, kk, c0, ap_ = info[n]
                li, xi = u % NL, u % NX
                for half in range(2):
                    cbk = 4 + half
                    if jj == 4 * qt + 3:
                        E("pe", lambda e, cbk=cbk: e.matmul(bank(cbk), lhsT=zb[:, 0:128], rhs=zb, start=True, stop=False,
                                                            skip_group_check=True), reads=[b_zb], writes=[PB[cbk]])
                    E("pe", lambda e, cbk=cbk, half=half: e.matmul(
                        bank(cbk)[:, c0:512], lhsT=GEb, rhs=lb[li][:, half * 512 + c0:(half + 1) * 512],
                        start=False, stop=False, skip_group_check=True), reads=[b_lb[li], b_cb], writes=[PB[cbk]])
                E("act", lambda e: e.activation(out=v2(xb[xi], c0), in_=v2(psum[:, 2048:3072], c0), func=AF.Exp, scale=-1.0),
                  reads=[PB[4], PB[5]], writes=[b_xb[xi]])

            def st_lt(n):
                qt, jj, u, kk, c0, ap_ = info[n]
                if jj == 0:
                    return
                li = u % NL
                for half in range(2):
                    cbk = 4 + half
                    E("pe", lambda e, cbk=cbk, half=half: e.matmul(
                        bank(cbk)[:, c0:512], lhsT=LTb, rhs=lb[li][:, half * 512 + c0:(half + 1) * 512],
                        start=False, stop=False, skip_group_check=True), reads=[b_lb[li], b_cb], writes=[PB[cbk]])

            def st_w(n):
                qt, jj, u, kk, c0, ap_ = info[n]
                ei, xi, wi = u % NE, u % NX, u % NWB
                E("dve", lambda e: e.tensor_tensor(out=v2(wb[wi], c0), in0=v2(eb[ei], c0), in1=v2(xb[xi], c0), op=ALU.mult),
                  reads=[b_eb[ei], b_xb[xi]], writes=[b_wb[wi]])

            def st_pv(n):
                qt, jj, u, kk, c0, ap_ = info[n]
                wi = u % NWB
                ob = 6
                for half in range(2):
                    pb = 64 * half
                    hh = 2 * hp + half
                    if jj == 4 * qt + 3:
                        E("pe", lambda e, pb=pb: e.matmul(bank(ob)[pb:pb + 64, :], lhsT=zb[:, 0:64], rhs=zb, start=True,
                                                          stop=False, skip_group_check=True), reads=[b_zb],
                          writes=[PB[ob]] if half == 0 else (), pwrites=() if half == 0 else [PB[ob]])
                    E("pe", lambda e, pb=pb, half=half, hh=hh: e.matmul(
                        bank(ob)[pb:pb + 64, c0:512], lhsT=Vall[:, jj * 1024 + hh * 64:jj * 1024 + hh * 64 + 64],
                        rhs=wb[wi][:, half * 512 + c0:(half + 1) * 512], start=False, stop=(jj == 0), skip_group_check=True),
                      reads=[b_Vall, b_wb[wi]], pwrites=[PB[ob]])

            def st_epi(n):
                qt, jj, u, kk, c0, ap_ = info[n]
                ob = 6
                if jj == 0:
                    E("act", lambda e: e.activation(out=sqa, in_=bank(ob), func=AF.Square), reads=[PB[ob]], writes=[b_sqa])
                    for sub in range(4):
                        E("pe", lambda e, sub=sub: e.matmul(bank(0)[:, sub:sub + 1], lhsT=sqa[:, sub * 128:(sub + 1) * 128],
                                                            rhs=onesf[:, 0:1], start=True, stop=True),
                          reads=[b_sqa, b_cf], writes=[PB[0]] if sub == 0 else (), pwrites=() if sub == 0 else [PB[0]])
                    if hp == 0:
                        E("dve", lambda e: e.tensor_copy(out=ssq_a[:, 4 * qt:4 * qt + 4], in_=bank(0)[:, 0:4]),
                          reads=[PB[0]], pwrites=[b_ssqa])
                    else:
                        E("dve", lambda e: e.tensor_tensor(out=ssq_a[:, 4 * qt:4 * qt + 4], in0=ssq_a[:, 4 * qt:4 * qt + 4],
                                                           in1=bank(0)[:, 0:4], op=ALU.add),
                          reads=[PB[0], b_ssqa], writes=[b_ssqa])
                    ai = atq[0] % 2
                    atq[0] += 1
                    E("dve", lambda e: e.scalar_tensor_tensor(
                        out=ats[ai], in0=bank(ob), scalar=pcol[:, PC_GA + hp:PC_GA + hp + 1],
                        in1=ZAq[(hp * NT + qt) % 2], op0=ALU.mult, op1=ALU.mult),
                      reads=[PB[ob], b_pcol, b_ZAq[(hp * NT + qt) % 2]], writes=[b_ats[ai]])
                    E("sp", lambda e: e.dma_start(out=AT_d[hp * 128:(hp + 1) * 128, qt * 512:(qt + 1) * 512], in_=ats[ai]),
                      reads=[b_ats[ai]], dma=True)

            for n in range(nu + 3):
                ssd_step()
                if n < nu:
                    st_qk(n)
                if 0 <= n - 2 < nu:
                    st_lt(n - 2)
                if n < nu:
                    st_act1(n)
                if 0 <= n - 1 < nu:
                    st_ge(n - 1)
                if n < nu:
                    st_act2(n)
                if 0 <= n - 1 < nu:
                    st_w(n - 1)
                if 0 <= n - 2 < nu:
                    st_pv(n - 2)
                if 0 <= n - 2 < nu:
                    st_epi(n - 2)
            ucount[0] += nu

        for hp_ in range(8):
            do_hp(hp_, hp_ % 2)
        while ssd_step():
            pass
        S_.barrier()
        A.top = base_mark

        Wg = A.bf16(16 * 1024); b_Wg = Buf("Wg")
        wst = [A.f32(4096) for _ in range(2)]; b_wst = [Buf("wst%d" % i) for i in range(2)]
        wout_v = wout_d.rearrange("(m p) c -> p m c", p=128)
        for ch in range(4):
            rr = ch % 2
            E("sp", lambda e, ch=ch, rr=rr: e.dma_start(out=wst[rr].rearrange("p (m c) -> p m c", m=4),
                                                      in_=wout_v[:, ch * 4:(ch + 1) * 4, :]), writes=[b_wst[rr]], dma=True)
            E("dve" if ch % 2 == 0 else "pool", lambda e, ch=ch, rr=rr: e.tensor_tensor(
                out=Wg[:, ch * 4096:(ch + 1) * 4096].rearrange("p (m c) -> p m c", m=4),
                in0=wst[rr].rearrange("p (m c) -> p m c", m=4),
                in1=gate_bc.unsqueeze(1).to_broadcast([128, 4, 1024]), op=ALU.mult),
              reads=[b_wst[rr], b_gate], pwrites=[b_Wg])
        for (src, dst) in ((ssq_a, rstd_a), (ssq_s, rstd_s)):
            E("act", lambda e, src=src, dst=dst: e.activation(out=dst, in_=src, func=AF.Ln, scale=1.0 / 1024, bias=EPS),
              reads=[b_ssqa, b_ssqs], pwrites=[b_rstd])
            E("act", lambda e, dst=dst: e.activation(out=dst, in_=dst, func=AF.Exp, scale=-0.5),
              reads=[b_rstd], pwrites=[b_rstd])
        ATt = [A.bf16(16 * 512) for _ in range(2)]; b_ATt = [Buf("ATt%d" % i) for i in range(2)]
        xo = [A.f32(4096) for _ in range(2)]; b_xo = [Buf("xo%d" % i) for i in range(2)]
        yb = [A.f32(1024) for _ in range(2)]; b_yb = [Buf("yb%d" % i) for i in range(2)]
        junk3 = A.bf16(1024); b_junk3 = Buf("junk3")
        b_out = Buf("outd")
        yq = [0]
        gf_bc = prow[:, PR_GF:PR_GF + 1024]
        for tt in range(NT):
            r = tt % 2
            E("sp", lambda e, tt=tt, r=r: e.dma_start(
                out=ATt[r].rearrange("p (m t) -> p m t", m=16),
                in_=AT_d[:, tt * 512:(tt + 1) * 512].rearrange("(m p) t -> p m t", p=128)), writes=[b_ATt[r]], dma=True)
            E("sp", lambda e, tt=tt, r=r: e.dma_start(
                out=xo[r].rearrange("p (j d) -> p j d", j=4),
                in_=x_d[tt * 512:(tt + 1) * 512, :].rearrange("(j p) d -> p j d", p=128)), writes=[b_xo[r]], dma=True)
            for j in range(4):
                c = tt * 4 + j
                yi = yq[0] % 2
                yq[0] += 1
                for n in range(2):
                    ba = 4 * (c % 2) + 2 * n
                    bs = ba + 1
                    for m in range(8):
                        E("pe", lambda e, r=r, m=m, j=j, n=n, ba=ba: e.matmul(
                            bank(ba), lhsT=ATt[r][:, m * 512 + j * 128:m * 512 + (j + 1) * 128],
                            rhs=Wg[:, m * 1024 + n * 512:m * 1024 + (n + 1) * 512], start=(m == 0), stop=(m == 7)),
                          reads=[b_ATt[r], b_Wg], writes=[PB[ba]] if m == 0 else (), pwrites=() if m == 0 else [PB[ba]])
                    for m in range(8, 16):
                        E("pe", lambda e, r=r, m=m, j=j, n=n, bs=bs: e.matmul(
                            bank(bs), lhsT=ATt[r][:, m * 512 + j * 128:m * 512 + (j + 1) * 128],
                            rhs=Wg[:, m * 1024 + n * 512:m * 1024 + (n + 1) * 512], start=(m == 8), stop=(m == 15)),
                          reads=[b_ATt[r], b_Wg], writes=[PB[bs]] if m == 8 else (), pwrites=() if m == 8 else [PB[bs]])
                    ysl = yb[yi][:, n * 512:(n + 1) * 512]
                    E("dve", lambda e, r=r, j=j, n=n, ba=ba, ysl=ysl, c=c: e.scalar_tensor_tensor(
                        out=ysl, in0=bank(ba), scalar=rstd_a[:, c:c + 1],
                        in1=xo[r][:, j * 1024 + n * 512:j * 1024 + (n + 1) * 512], op0=ALU.mult, op1=ALU.add),
                      reads=[PB[ba], b_rstd, b_xo[r]], writes=[b_yb[yi]] if n == 0 else (), pwrites=() if n == 0 else [b_yb[yi]])
                    E("dve", lambda e, bs=bs, ysl=ysl, c=c: e.scalar_tensor_tensor(
                        out=ysl, in0=bank(bs), scalar=rstd_s[:, c:c + 1], in1=ysl, op0=ALU.mult, op1=ALU.add),
                      reads=[PB[bs], b_rstd, b_yb[yi]], pwrites=[b_yb[yi]])
                E("act", lambda e, yi=yi, c=c: e.activation(out=junk3, in_=yb[yi], func=AF.Square,
                                                            accum_out=ssqf[:, c:c + 1]),
                  reads=[b_yb[yi]], writes=[b_junk3], pwrites=[b_ssqf])
                E("act", lambda e, c=c: e.activation(out=rstdf[:, c:c + 1], in_=ssqf[:, c:c + 1], func=AF.Ln,
                                                     scale=1.0 / 1024, bias=EPS), reads=[b_ssqf], pwrites=[b_rstdf])
                E("act", lambda e, c=c: e.activation(out=rstdf[:, c:c + 1], in_=rstdf[:, c:c + 1], func=AF.Exp, scale=-0.5),
                  reads=[b_rstdf], pwrites=[b_rstdf])
                E("dve", lambda e, yi=yi, c=c: e.scalar_tensor_tensor(
                    out=yb[yi], in0=yb[yi], scalar=rstdf[:, c:c + 1], in1=gf_bc, op0=ALU.mult, op1=ALU.mult),
                  reads=[b_yb[yi], b_rstdf, b_prow], writes=[b_yb[yi]])
                E("pool", lambda e, yi=yi, c=c: e.dma_start(out=out_d[c * 128:(c + 1) * 128, :], in_=yb[yi]),
                  reads=[b_yb[yi]], pwrites=[b_out], dma=True, key="yo%d" % yi)
        E("sp", lambda e: e.nop(), reads=[b_out], real=False)
        S_.run()
    return nc


def _consts():
    a = np.arange(128)[:, None]
    b = np.arange(128)[None, :]
    c = np.zeros((128, NCONST), np.float32)
    c[:, C_ID:C_ID + 128] = (a == b)
    c[:, C_GE:C_GE + 128] = (a >= b)
    c[:, C_LT:C_LT + 128] = (a < b)
    c[:, C_LE:C_LE + 128] = (a <= b)
    c[:, C_GT:C_GT + 128] = (a > b)
    c[:, C_NEG:C_NEG + 128] = NEGV * (a > b)
    c[:, C_ONE:C_ONE + 128] = 1.0
    return c


def _col(v, n):
    return np.ascontiguousarray(np.asarray(v, np.float32).reshape(n, 128).T)


def make_in_maps(x, c, w_ada, b_ada, norm_in_gain, w_in, conv_w, conv_b, dt_bias, a_log, d_skip,
                 sb_norm_gain, ssm_norm_gain, w_out, norm_f_gain, cores=None):
    B = x.shape[0]
    consts = _consts()
    maps = []
    w_ada0 = np.ascontiguousarray(w_ada[0], np.float32)
    w_in0 = np.ascontiguousarray(w_in[0], np.float32)
    w_out0 = np.ascontiguousarray(w_out[0], np.float32)
    prow = np.zeros((1, NPROW), np.float32)
    prow[0, PR_BG:PR_BG + 1024] = b_ada[0, 2048:3072]
    prow[0, PR_GF:PR_GF + 1024] = norm_f_gain
    prow[0, PR_DTB:PR_DTB + 16] = dt_bias[0]
    prow[0, PR_AL:PR_AL + 16] = a_log[0]
    for b in (range(B) if cores is None else cores):
        pcol = np.zeros((128, NPCOL), np.float32)
        pcol[:, PC_C:PC_C + 8] = _col(c[b], 8)
        pcol[:, PC_BADA:PC_BADA + 24] = _col(b_ada[0], 24)
        pcol[:, PC_GIN:PC_GIN + 8] = _col(norm_in_gain[0], 8)
        pcol[:, PC_GA:PC_GA + 8] = _col(sb_norm_gain[0], 8)
        pcol[:, PC_GS:PC_GS + 8] = _col(ssm_norm_gain[0], 8)
        pcol[:, PC_D:PC_D + 8] = _col(np.repeat(np.asarray(d_skip[0], np.float32), 64), 8)
        cw = np.asarray(conv_w[0], np.float32)
        pcol[:, PC_CW:PC_CW + 48] = cw.T.reshape(12, 128, 4).transpose(1, 0, 2).reshape(128, 48)
        pcol[:, PC_CB:PC_CB + 12] = _col(conv_b[0], 12)
        maps.append({"x": np.ascontiguousarray(x[b], np.float32), "w_ada": w_ada0, "w_in": w_in0, "w_out": w_out0,
                     "consts": consts, "pcol": pcol, "prow": prow})
    return maps


_NC_CACHE = {}


def kernel(x, c, w_ada, b_ada, norm_in_gain, w_in, conv_w, conv_b, dt_bias, a_log, d_skip,
           sb_norm_gain, ssm_norm_gain, w_out, norm_f_gain):
    x = np.asarray(x)
    B, S, _ = x.shape
    if S not in _NC_CACHE:
        _NC_CACHE[S] = build(S)
    nc = _NC_CACHE[S]
    maps = make_in_maps(x, np.asarray(c), np.asarray(w_ada), np.asarray(b_ada), np.asarray(norm_in_gain),
                        np.asarray(w_in), np.asarray(conv_w), np.asarray(conv_b), np.asarray(dt_bias),
                        np.asarray(a_log), np.asarray(d_skip), np.asarray(sb_norm_gain),
                        np.asarray(ssm_norm_gain), np.asarray(w_out), np.asarray(norm_f_gain))
    res = run_bass_kernel_spmd(nc, maps, core_ids=list(range(B)))
    return np.stack([np.asarray(r["out"], np.float32) for r in res.results], axis=0)
```

```python
import numpy as np
from contextlib import ExitStack
import concourse.bass as bass
import concourse.mybir as mybir
from concourse.bass_utils import run_bass_kernel_spmd

F32 = mybir.dt.float32
BF16 = mybir.dt.bfloat16
AF = mybir.ActivationFunctionType
ALU = mybir.AluOpType
AX = mybir.AxisListType

ENGS = ("pe", "act", "dve", "pool", "sp")
EPS = 1e-6
NEGV = -30000.0


class Buf:
    __slots__ = ("name", "writers", "readers", "const")

    def __init__(self, name, const=False):
        self.name = name
        self.writers = []
        self.readers = []
        self.const = const


class Op:
    __slots__ = ("eng", "fn", "deps", "dma", "idx", "need_sig", "val", "key", "real")

    def __init__(self, eng, fn, dma, key, real):
        self.eng = eng
        self.fn = fn
        self.dma = dma
        self.key = key
        self.real = real
        self.deps = ()
        self.idx = -1
        self.need_sig = False
        self.val = 0


class Sched:
    def __init__(self, nc):
        self.nc = nc
        self.ops = {e: [] for e in ENGS}
        self.last_dma = {}
        self.groups = {e: {} for e in ENGS}
        self._gstart = {}

    def group_begin(self, eng):
        self._gstart[eng] = len(self.ops[eng])

    def group_end(self, eng):
        g0 = self._gstart.pop(eng)
        g1 = len(self.ops[eng])
        if g1 > g0 + 1:
            self.groups[eng][g0] = g1

    def emit(self, eng, fn, reads=(), writes=(), pwrites=(), dma=False, key=None, extra=(), real=True):
        if dma and key is None:
            key = writes[0].name if writes else (pwrites[0].name if pwrites else reads[0].name)
        op = Op(eng, fn, dma, key, real)
        deps = set(extra)
        for b in reads:
            deps.update(b.writers)
        for b in writes:
            deps.update(b.writers)
            deps.update(b.readers)
        for b in pwrites:
            deps.update(b.readers)
            if b.writers:
                deps.add(b.writers[0])
        op.deps = deps
        for b in reads:
            if not b.const:
                b.readers.append(op)
        for b in writes:
            b.writers = [op]
            b.readers = []
        for b in pwrites:
            b.writers.append(op)
        op.idx = len(self.ops[eng])
        self.ops[eng].append(op)
        if dma:
            self.last_dma[key] = op
        return op

    def barrier(self):
        lasts = []
        for e in ENGS:
            for op in reversed(self.ops[e]):
                if op.real and not op.dma:
                    lasts.append(op)
                    break
        deps = set(lasts) | set(self.last_dma.values())
        for e in ENGS:
            self.emit(e, lambda g: g.nop(), extra=deps, real=False)

    def _needed(self, op):
        comp = {}
        dmas = []
        for d in op.deps:
            if d.dma:
                dmas.append(d)
                continue
            if d.eng == op.eng and not op.dma:
                if d.eng == "pe":
                    continue
                if d.eng in ("act", "dve") and d.idx < op.idx - 2:
                    continue
            cur = comp.get(d.eng)
            if cur is None or d.idx > cur.idx:
                comp[d.eng] = d
        return comp, dmas

    def run(self):
        nc = self.nc
        for e in ENGS:
            for op in self.ops[e]:
                comp, _ = self._needed(op)
                for d in comp.values():
                    d.need_sig = True
        for e in ENGS:
            cnt = 0
            for op in self.ops[e]:
                if op.dma:
                    continue
                if op.need_sig:
                    cnt += 1
                    op.val = cnt
        keycnt = {}
        for e in ENGS:
            for op in self.ops[e]:
                if op.dma:
                    keycnt[op.key] = keycnt.get(op.key, 0) + 1
                    op.val = 16 * keycnt[op.key]
        keys = sorted(keycnt.keys())
        with ExitStack() as st:
            esem = {e: st.enter_context(nc.semaphore("s_" + e)) for e in ENGS}
            ksem = {k: st.enter_context(nc.semaphore("d%d" % i)) for i, k in enumerate(keys)}
            block = st.enter_context(nc.Block())
            sched = self

            def replay(ename, eng):
                seen = {}

                def do_waits(op):
                    comp, dmas = sched._needed(op)
                    for d in comp.values():
                        if seen.get(("e", d.eng), 0) < d.val:
                            eng.wait_ge(esem[d.eng], d.val)
                            seen[("e", d.eng)] = d.val
                    kmax = {}
                    for d in dmas:
                        if kmax.get(d.key, 0) < d.val:
                            kmax[d.key] = d.val
                    for kk_, vv_ in kmax.items():
                        if seen.get(("k", kk_), 0) < vv_:
                            eng.wait_ge(ksem[kk_], vv_)
                            seen[("k", kk_)] = vv_

                oplist = sched.ops[ename]
                for op in oplist:
                    g1 = sched.groups[ename].get(op.idx)
                    if g1 is not None:
                        for op2 in oplist[op.idx:g1]:
                            do_waits(op2)
                    comp, dmas = sched._needed(op)
                    for d in comp.values():
                        if seen.get(("e", d.eng), 0) < d.val:
                            eng.wait_ge(esem[d.eng], d.val)
                            seen[("e", d.eng)] = d.val
                    kmax = {}
                    for d in dmas:
                        if kmax.get(d.key, 0) < d.val:
                            kmax[d.key] = d.val
                    for kk_, vv_ in kmax.items():
                        if seen.get(("k", kk_), 0) < vv_:
                            eng.wait_ge(ksem[kk_], vv_)
                            seen[("k", kk_)] = vv_
                    ins = op.fn(eng)
                    if op.dma:
                        ins.then_inc(ksem[op.key], 16)
                    elif op.need_sig:
                        ins.then_inc(esem[ename], 1)

            @block.tensor
            def _(e):
                replay("pe", e)

            @block.scalar
            def _(e):
                replay("act", e)

            @block.vector
            def _(e):
                replay("dve", e)

            @block.gpsimd
            def _(e):
                replay("pool", e)

            @block.sync
            def _(e):
                replay("sp", e)


class Arena:
    def __init__(self, big, nw):
        self.big = big
        self.nw = nw
        self.top = 0

    def f32(self, n):
        off = self.top
        self.top += n
        assert self.top <= self.nw, ("sbuf arena overflow", self.top, self.nw)
        return self.big[:, off:off + n]

    def bf16(self, n):
        w = (n + 1) // 2
        off = self.top
        self.top += w
        assert self.top <= self.nw, ("sbuf arena overflow", self.top, self.nw)
        return self.big[:, off:off + w].bitcast(BF16)


C_ID, C_GE, C_LT, C_LE, C_GT, C_NEG, C_ONE = 0, 128, 256, 384, 512, 640, 768
NCONST = 896
PC_C, PC_BADA, PC_GIN, PC_GA, PC_GS, PC_D, PC_CW, PC_CB = 0, 8, 32, 40, 48, 56, 64, 112
NPCOL = 124
PR_BG, PR_GF, PR_DTB, PR_AL = 0, 1024, 2048, 2064
NPROW = 2080
NW_SBUF = 52224
NWARM = 0
NFILL = 0


def build(S, debug=False):
    NT = S // 512
    NCH = S // 128
    nc = bass.Bass("TRN2", target_bir_lowering=False)
    dk = "ExternalOutput" if debug else "Internal"
    x_d = nc.dram_tensor("x", [S, 1024], F32, kind="ExternalInput").ap()
    wada_d = nc.dram_tensor("w_ada", [1024, 3072], F32, kind="ExternalInput").ap()
    win_d = nc.dram_tensor("w_in", [1024, 6672], F32, kind="ExternalInput").ap()
    wout_d = nc.dram_tensor("w_out", [2048, 1024], F32, kind="ExternalInput").ap()
    const_d = nc.dram_tensor("consts", [128, NCONST], F32, kind="ExternalInput").ap()
    pcol_d = nc.dram_tensor("pcol", [128, NPCOL], F32, kind="ExternalInput").ap()
    prow_d = nc.dram_tensor("prow", [1, NPROW], F32, kind="ExternalInput").ap()
    out_d = nc.dram_tensor("out", [S, 1024], F32, kind="ExternalOutput").ap()
    HT_d = nc.dram_tensor("s_ht", [8, 128, S], BF16, kind=dk).ap()
    QT_d = nc.dram_tensor("s_qt", [1024, S], BF16, kind=dk).ap()
    KT_d = nc.dram_tensor("s_kt", [1024, S], BF16, kind=dk).ap()
    ZA_d = nc.dram_tensor("s_za", [1024, S], BF16, kind=dk).ap()
    V_d = nc.dram_tensor("s_v", [S, 1024], BF16, kind=dk).ap()
    AT_d = nc.dram_tensor("s_at", [2048, S], BF16, kind=dk).ap()
    XS_d = nc.dram_tensor("s_xs", [1024, S], F32, kind=dk).ap()
    BC_d = nc.dram_tensor("s_bc", [512, S], BF16, kind=dk).ap()
    ZS_d = nc.dram_tensor("s_zs", [1024, S], BF16, kind=dk).ap()
    DT_d = nc.dram_tensor("s_dt", [S, 32], F32, kind=dk).ap()

    S_ = Sched(nc)
    E = S_.emit

    with ExitStack() as st:
        big = st.enter_context(nc.sbuf_tensor("big", [128, NW_SBUF], F32))
        psum = st.enter_context(nc.psum_tensor("psum", [128, 4096], F32))
        A = Arena(big, NW_SBUF)

        def bank(i):
            return psum[:, 512 * i:512 * (i + 1)]

        PB = [Buf("ps%d" % i) for i in range(8)]

        cf = A.f32(NCONST); b_cf = Buf("cf", const=True)
        cb = A.bf16(NCONST); b_cb = Buf("cb", const=True)
        zb = A.bf16(512); b_zb = Buf("zb", const=True)
        neg4 = A.bf16(512); b_neg4 = Buf("neg4", const=True)
        pcol = A.f32(NPCOL); b_pcol = Buf("pcol", const=True)
        prow = A.f32(NPROW); b_prow = Buf("prow", const=True)
        modT = A.f32(24); b_modT = Buf("modT")
        g1 = A.f32(8); b_g1 = Buf("g1")
        gate_bc = A.f32(1024); b_gate = Buf("gate_bc")
        a_bc = A.f32(16); b_abc = Buf("a_bc")
        ssq_a = A.f32(NCH); b_ssqa = Buf("ssq_a")
        ssq_s = A.f32(NCH); b_ssqs = Buf("ssq_s")
        rstd_a = A.f32(NCH); rstd_s = A.f32(NCH); b_rstd = Buf("rstd_as")
        ssqf = A.f32(NCH); b_ssqf = Buf("ssqf")
        rstdf = A.f32(NCH); b_rstdf = Buf("rstdf")
        tmp_small = A.f32(64); b_tmps = Buf("tmps")

        idf = cf[:, C_ID:C_ID + 128]
        GEb = cb[:, C_GE:C_GE + 128]
        LTb = cb[:, C_LT:C_LT + 128]
        LTf = cf[:, C_LT:C_LT + 128]
        LEf = cf[:, C_LE:C_LE + 128]
        GTf = cf[:, C_GT:C_GT + 128]
        idb = cb[:, C_ID:C_ID + 128]
        onesf = cf[:, C_ONE:C_ONE + 128]

        E("sp", lambda e: e.dma_start(out=cf, in_=const_d), writes=[b_cf], dma=True)
        E("pool", lambda e: e.dma_start(out=cb, in_=const_d), writes=[b_cb], dma=True)
        E("sp", lambda e: e.dma_start(out=pcol, in_=pcol_d), writes=[b_pcol], dma=True)
        E("sp", lambda e: e.dma_start(out=prow, in_=prow_d.partition_broadcast(128)), writes=[b_prow], dma=True)
        E("dve", lambda e: e.memset(zb, 0.0), writes=[b_zb])
        for q in range(4):
            E("dve", lambda e, q=q: e.tensor_copy(out=neg4[:, q * 128:(q + 1) * 128], in_=cf[:, C_NEG:C_NEG + 128]),
              reads=[b_cf], pwrites=[b_neg4])

        base_mark = A.top
        Wa = A.bf16(8 * 4096); b_Wa = Buf("Wa")
        win_v = win_d.rearrange("(k p) c -> p k c", p=128)
        Wa3 = Wa.rearrange("p (k c) -> p k c", k=8)
        for ci in range(8):
            E("pool", lambda e, ci=ci: e.dma_start(out=Wa3[:, :, ci * 512:(ci + 1) * 512],
                                                  in_=win_v[:, :, ci * 512:(ci + 1) * 512]),
              pwrites=[b_Wa], dma=True, key="Wa")
        p0_mark = A.top

        cact = A.f32(8); b_cact = Buf("cact")
        cact_rep = A.f32(8 * 128); b_crep = Buf("crep")
        wa = [A.f32(8 * 512) for _ in range(2)]
        b_wa = [Buf("wa%d" % i) for i in range(2)]
        E("act", lambda e: e.activation(out=cact, in_=pcol[:, PC_C:PC_C + 8], func=AF.Silu),
          reads=[b_pcol], writes=[b_cact])
        E("dve", lambda e: e.tensor_copy(out=cact_rep.rearrange("p (k m) -> p k m", k=8),
                                         in_=cact.unsqueeze(2).to_broadcast([128, 8, 128])),
          reads=[b_cact], writes=[b_crep])
        E("act", lambda e: e.activation(out=a_bc, in_=prow[:, PR_AL:PR_AL + 16], func=AF.Exp),
          reads=[b_prow], writes=[b_abc])
        E("dve", lambda e: e.tensor_scalar(out=a_bc, in0=a_bc, scalar1=-1.0, scalar2=None, op0=ALU.mult),
          reads=[b_abc], writes=[b_abc])
        wada_v = wada_d.rearrange("(k p) c -> p k c", p=128)
        for ci in range(6):
            r = ci % 2
            E("sp", lambda e, ci=ci, r=r: e.dma_start(out=wa[r].rearrange("p (k c) -> p k c", k=8),
                                                     in_=wada_v[:, :, ci * 512:(ci + 1) * 512]),
              writes=[b_wa[r]], dma=True)
            if ci < 4:
                for mm in range(4):
                    m = ci * 4 + mm
                    for k in range(8):
                        E("pe", lambda e, r=r, mm=mm, m=m, k=k: e.matmul(
                            bank(0)[:, m:m + 1], lhsT=wa[r][:, k * 512 + mm * 128:k * 512 + (mm + 1) * 128],
                            rhs=cact[:, k:k + 1], start=(k == 0), stop=(k == 7)),
                          reads=[b_wa[r], b_cact], pwrites=[PB[0]])
            else:
                n = ci - 4
                for k in range(8):
                    E("pe", lambda e, r=r, n=n, k=k: e.matmul(
                        bank(1 + n), lhsT=cact_rep[:, k * 128:(k + 1) * 128],
                        rhs=wa[r][:, k * 512:(k + 1) * 512], start=(k == 0), stop=(k == 7)),
                      reads=[b_wa[r], b_crep], pwrites=[PB[1 + n]])
        E("dve", lambda e: e.tensor_tensor(out=modT[:, 0:16], in0=bank(0)[:, 0:16], in1=pcol[:, PC_BADA:PC_BADA + 16],
                                           op=ALU.add), reads=[PB[0], b_pcol], writes=[b_modT])
        E("dve", lambda e: e.scalar_tensor_tensor(out=g1, in0=modT[:, 8:16], scalar=1.0, in1=pcol[:, PC_GIN:PC_GIN + 8],
                                                  op0=ALU.add, op1=ALU.mult), reads=[b_modT, b_pcol], writes=[b_g1])
        E("dve", lambda e: e.tensor_tensor(out=gate_bc, in0=psum[:, 512:1536], in1=prow[:, PR_BG:PR_BG + 1024], op=ALU.add),
          reads=[PB[1], PB[2], b_prow], writes=[b_gate])
        shiftc = modT
        S_.barrier()
        A.top = p0_mark

        xt = [A.f32(4096) for _ in range(2)]; b_xt = [Buf("xt%d" % i) for i in range(2)]
        hT = [A.bf16(8 * 512) for _ in range(2)]; b_hT = [Buf("hT%d" % i) for i in range(2)]
        junk = A.bf16(1024); b_junk = Buf("junk")
        ssq1 = A.f32(4 * NT); b_ssq1 = Buf("ssq1")
        rs1 = A.f32(4 * NT); b_rs1 = Buf("rs1")
        stq = A.bf16(8 * 512); b_stq = Buf("stq")
        stk = A.bf16(8 * 512); b_stk = Buf("stk")
        stz = A.bf16(8 * 512); b_stz = Buf("stz")
        stv = A.bf16(4 * 1024); b_stv = Buf("stv")
        stages = [(stq, b_stq, QT_d), (stk, b_stk, KT_d), (stz, b_stz, ZA_d)]
        pcnt = [0]

        def next_bank(lo, n):
            i = lo + (pcnt[0] % n)
            pcnt[0] += 1
            return i

        evq = [0]
        def p1a_front_a(tt):
            r = tt % 2
            E("sp", lambda e, tt=tt, r=r: e.dma_start(
                out=xt[r].rearrange("p (j d) -> p j d", j=4),
                in_=x_d[tt * 512:(tt + 1) * 512, :].rearrange("(j p) d -> p j d", p=128)),
              writes=[b_xt[r]], dma=True)
            for j in range(4):
                E("act", lambda e, r=r, j=j, tt=tt: e.activation(
                    out=junk, in_=xt[r][:, j * 1024:(j + 1) * 1024], func=AF.Square,
                    accum_out=ssq1[:, tt * 4 + j:tt * 4 + j + 1]),
                  reads=[b_xt[r]], writes=[b_junk], pwrites=[b_ssq1])
            E("act", lambda e, tt=tt: e.activation(out=rs1[:, tt * 4:tt * 4 + 4], in_=ssq1[:, tt * 4:tt * 4 + 4],
                                                   func=AF.Ln, scale=1.0 / 1024, bias=EPS),
              reads=[b_ssq1], pwrites=[b_rs1])
            E("act", lambda e, tt=tt: e.activation(out=rs1[:, tt * 4:tt * 4 + 4], in_=rs1[:, tt * 4:tt * 4 + 4],
                                                   func=AF.Exp, scale=-0.5),
              reads=[b_rs1], pwrites=[b_rs1])
            for j in range(4):
                E("dve", lambda e, r=r, j=j, tt=tt: e.tensor_scalar(
                    out=xt[r][:, j * 1024:(j + 1) * 1024], in0=xt[r][:, j * 1024:(j + 1) * 1024],
                    scalar1=rs1[:, tt * 4 + j:tt * 4 + j + 1], scalar2=None, op0=ALU.mult),
                  reads=[b_rs1, b_xt[r]], writes=[b_xt[r]])

        def p1a_front_b(tt):
            r = tt % 2
            for k in range(8):
                bi = next_bank(0, 4)
                for j in range(4):
                    E("pe", lambda e, r=r, j=j, k=k, bi=bi: e.transpose(
                        bank(bi)[:, j * 128:(j + 1) * 128], xt[r][:, j * 1024 + k * 128:j * 1024 + (k + 1) * 128], idf),
                      reads=[b_xt[r], b_cf], writes=[PB[bi]] if j == 0 else (), pwrites=() if j == 0 else [PB[bi]])
                if k % 2 == 0:
                    E("dve", lambda e, r=r, k=k, bi=bi: e.tensor_scalar(
                        out=hT[r][:, k * 512:(k + 1) * 512], in0=bank(bi), scalar1=g1[:, k:k + 1],
                        scalar2=shiftc[:, k:k + 1], op0=ALU.mult, op1=ALU.add),
                      reads=[PB[bi], b_g1, b_modT], pwrites=[b_hT[r]] if k else (), writes=() if k else [b_hT[r]])
                else:
                    E("act", lambda e, r=r, k=k, bi=bi: e.activation(
                        out=hT[r][:, k * 512:(k + 1) * 512], in_=bank(bi), func=AF.Identity,
                        scale=g1[:, k:k + 1], bias=shiftc[:, k:k + 1]),
                      reads=[PB[bi], b_g1, b_modT], pwrites=[b_hT[r]])
            E("pool", lambda e, r=r, tt=tt: e.dma_start(
                out=HT_d[:, :, tt * 512:(tt + 1) * 512].rearrange("k p t -> p k t"),
                in_=hT[r].rearrange("p (k t) -> p k t", k=8)), reads=[b_hT[r]], dma=True, key="hTo%d" % r)

        def p1a_back1(tt):
            r = tt % 2
            for grp in range(3):
                stg, b_stg, dst = stages[grp]
                c_base = [0, 1024, 3072][grp]
                for m in range(8):
                    bi = next_bank(4, 4)
                    for k in range(8):
                        E("pe", lambda e, r=r, k=k, bi=bi, c0=c_base + m * 128: e.matmul(
                            bank(bi), lhsT=Wa[:, k * 4096 + c0:k * 4096 + c0 + 128],
                            rhs=hT[r][:, k * 512:(k + 1) * 512], start=(k == 0), stop=(k == 7)),
                          reads=[b_Wa, b_hT[r]], writes=[PB[bi]] if k == 0 else (), pwrites=() if k == 0 else [PB[bi]])
                    wkw = dict(writes=[b_stg]) if m == 0 else dict(pwrites=[b_stg])
                    osl = stg[:, m * 512:(m + 1) * 512]
                    if grp == 0:
                        if evq[0] % 2 == 0:
                            E("act", lambda e, osl=osl, bi=bi: e.mul(out=osl, in_=bank(bi), mul=0.125),
                              reads=[PB[bi]], **wkw)
                        else:
                            E("dve", lambda e, osl=osl, bi=bi: e.tensor_scalar(out=osl, in0=bank(bi), scalar1=0.125,
                                                                               scalar2=None, op0=ALU.mult),
                              reads=[PB[bi]], **wkw)
                        evq[0] += 1
                    elif grp == 1:
                        if evq[0] % 2 == 0:
                            E("act", lambda e, osl=osl, bi=bi: e.copy(out=osl, in_=bank(bi)), reads=[PB[bi]], **wkw)
                        else:
                            E("dve", lambda e, osl=osl, bi=bi: e.tensor_copy(out=osl, in_=bank(bi)), reads=[PB[bi]], **wkw)
                        evq[0] += 1
                    else:
                        E("act", lambda e, osl=osl, bi=bi: e.activation(out=osl, in_=bank(bi), func=AF.Silu),
                          reads=[PB[bi]], **wkw)
                E("pool", lambda e, stg=stg, dst=dst, tt=tt: e.dma_start(
                    out=dst[:, tt * 512:(tt + 1) * 512].rearrange("(m p) t -> p m t", p=128),
                    in_=stg.rearrange("p (m t) -> p m t", m=8)), reads=[b_stg], dma=True)

        def p1a_back2(tt):
            r = tt % 2
            for j in range(4):
                for n in range(2):
                    bi = next_bank(4, 4)
                    for k in range(8):
                        E("pe", lambda e, r=r, k=k, bi=bi, j=j, n=n: e.matmul(
                            bank(bi), lhsT=hT[r][:, k * 512 + j * 128:k * 512 + (j + 1) * 128],
                            rhs=Wa[:, k * 4096 + 2048 + n * 512:k * 4096 + 2048 + (n + 1) * 512],
                            start=(k == 0), stop=(k == 7)),
                          reads=[b_Wa, b_hT[r]], writes=[PB[bi]] if k == 0 else (), pwrites=() if k == 0 else [PB[bi]])
                    wkw = dict(writes=[b_stv]) if (j == 0 and n == 0) else dict(pwrites=[b_stv])
                    osl = stv[:, j * 1024 + n * 512:j * 1024 + (n + 1) * 512]
                    if evq[0] % 2 == 0:
                        E("act", lambda e, osl=osl, bi=bi: e.copy(out=osl, in_=bank(bi)), reads=[PB[bi]], **wkw)
                    else:
                        E("dve", lambda e, osl=osl, bi=bi: e.tensor_copy(out=osl, in_=bank(bi)), reads=[PB[bi]], **wkw)
                    evq[0] += 1
            E("pool", lambda e, tt=tt: e.dma_start(
                out=V_d[tt * 512:(tt + 1) * 512, :].rearrange("(j p) c -> p j c", p=128),
                in_=stv.rearrange("p (j c) -> p j c", j=4)), reads=[b_stv], dma=True)

        p1a_front_a(0)
        p1a_front_b(0)
        for tt in range(NT):
            if tt + 1 < NT:
                p1a_front_a(tt + 1)
            p1a_back1(tt)
            if tt + 1 < NT:
                p1a_front_b(tt + 1)
            p1a_back2(tt)
        S_.barrier()
        A.top = base_mark

        NWS = 2576
        Ws = A.bf16(8 * NWS); b_Ws = Buf("Ws")
        Ws3 = Ws.rearrange("p (k c) -> p k c", k=8)
        for ci in range(6):
            c0 = ci * 512
            c1 = min(NWS, c0 + 512)
            E("pool", lambda e, c0=c0, c1=c1: e.dma_start(out=Ws3[:, :, c0:c1], in_=win_v[:, :, 4096 + c0:4096 + c1]),
              pwrites=[b_Ws], dma=True, key="Ws")
        hSr = [A.bf16(8 * 512) for _ in range(2)]; b_hSr = [Buf("hS%d" % i) for i in range(2)]
        XW_ = 515
        xin = A.f32(12 * XW_); b_xin = [Buf("xin%d" % m) for m in range(12)]
        cv = A.f32(12 * 512); b_cv = [Buf("cv%d" % m) for m in range(12)]
        BCs = A.bf16(4 * 512); b_BCs = Buf("BCs")
        ZS = A.bf16(8 * 512); b_ZS = Buf("ZSst")
        dtv = A.f32(64); b_dtv = Buf("dtv")
        dts = A.f32(64); b_dts = Buf("dts")
        lds = A.f32(64); b_lds = Buf("lds")
        cw = pcol[:, PC_CW:PC_CW + 48]
        cbias = pcol[:, PC_CB:PC_CB + 12]
        E("dve", lambda e: e.memset(xin, 0.0), writes=b_xin)
        for tt in range(NT):
            hS = hSr[tt % 2]; b_hS = b_hSr[tt % 2]
            E("sp", lambda e, tt=tt, hS=hS: e.dma_start(
                out=hS.rearrange("p (k t) -> p k t", k=8),
                in_=HT_d[:, :, tt * 512:(tt + 1) * 512].rearrange("k p t -> p k t")), writes=[b_hS], dma=True)
            for m in range(12):
                if tt > 0:
                    E("dve", lambda e, m=m: e.tensor_copy(out=xin[:, m * XW_:m * XW_ + 3], in_=xin[:, m * XW_ + 512:m * XW_ + 515]),
                      reads=[b_xin[m]], writes=[b_xin[m]])
                bi = next_bank(0, 4)
                for k in range(8):
                    E("pe", lambda e, k=k, bi=bi, c0=m * 128, hS=hS: e.matmul(
                        bank(bi), lhsT=Ws[:, k * NWS + c0:k * NWS + c0 + 128], rhs=hS[:, k * 512:(k + 1) * 512],
                        start=(k == 0), stop=(k == 7)),
                      reads=[b_Ws, b_hS], writes=[PB[bi]] if k == 0 else (), pwrites=() if k == 0 else [PB[bi]])
                E("act", lambda e, m=m, bi=bi: e.copy(out=xin[:, m * XW_ + 3:m * XW_ + 515], in_=bank(bi)),
                  reads=[PB[bi]], writes=[b_xin[m]])
                acc = cv[:, m * 512:(m + 1) * 512]
                E("dve", lambda e, m=m, acc=acc: e.tensor_scalar(
                    out=acc, in0=xin[:, m * XW_ + 3:m * XW_ + 515], scalar1=cw[:, m * 4 + 3:m * 4 + 4],
                    scalar2=cbias[:, m:m + 1], op0=ALU.mult, op1=ALU.add),
                  reads=[b_xin[m], b_pcol], writes=[b_cv[m]])
                for kk in range(3):
                    E("dve", lambda e, m=m, acc=acc, kk=kk: e.scalar_tensor_tensor(
                        out=acc, in0=xin[:, m * XW_ + kk:m * XW_ + kk + 512], scalar=cw[:, m * 4 + kk:m * 4 + kk + 1],
                        in1=acc, op0=ALU.mult, op1=ALU.add),
                      reads=[b_xin[m], b_pcol, b_cv[m]], writes=[b_cv[m]])
                if m < 8:
                    E("act", lambda e, acc=acc: e.activation(out=acc, in_=acc, func=AF.Silu),
                      reads=[b_cv[m]], writes=[b_cv[m]])
                else:
                    dst = BCs[:, (m - 8) * 512:(m - 7) * 512]
                    E("act", lambda e, acc=acc, dst=dst: e.activation(out=dst, in_=acc, func=AF.Silu),
                      reads=[b_cv[m]], writes=[b_BCs] if m == 8 else (), pwrites=() if m == 8 else [b_BCs])
            E("pool", lambda e, tt=tt: e.dma_start(
                out=XS_d[:, tt * 512:(tt + 1) * 512].rearrange("(m p) t -> p m t", p=128),
                in_=cv[:, 0:8 * 512].rearrange("p (m t) -> p m t", m=8)), reads=b_cv[0:8], dma=True, key="cvo")
            E("pool", lambda e, tt=tt: e.dma_start(
                out=BC_d[:, tt * 512:(tt + 1) * 512].rearrange("(m p) t -> p m t", p=128),
                in_=BCs.rearrange("p (m t) -> p m t", m=4)), reads=[b_BCs], dma=True, key="bco")
            for m in range(8):
                bi = next_bank(0, 4)
                for k in range(8):
                    E("pe", lambda e, k=k, bi=bi, c0=1552 + m * 128, hS=hS: e.matmul(
                        bank(bi), lhsT=Ws[:, k * NWS + c0:k * NWS + c0 + 128], rhs=hS[:, k * 512:(k + 1) * 512],
                        start=(k == 0), stop=(k == 7)),
                      reads=[b_Ws, b_hS], writes=[PB[bi]] if k == 0 else (), pwrites=() if k == 0 else [PB[bi]])
                E("act", lambda e, m=m, bi=bi: e.activation(out=ZS[:, m * 512:(m + 1) * 512], in_=bank(bi), func=AF.Silu),
                  reads=[PB[bi]], writes=[b_ZS] if m == 0 else (), pwrites=() if m == 0 else [b_ZS])
            E("pool", lambda e, tt=tt: e.dma_start(
                out=ZS_d[:, tt * 512:(tt + 1) * 512].rearrange("(m p) t -> p m t", p=128),
                in_=ZS.rearrange("p (m t) -> p m t", m=8)), reads=[b_ZS], dma=True, key="zso")
            for j in range(4):
                for k in range(8):
                    E("pe", lambda e, j=j, k=k, hS=hS: e.matmul(
                        bank(7)[:, j * 16:(j + 1) * 16], lhsT=hS[:, k * 512 + j * 128:k * 512 + (j + 1) * 128],
                        rhs=Ws[:, k * NWS + 1536:k * NWS + 1552], start=(k == 0), stop=(k == 7)),
                      reads=[b_Ws, b_hS], writes=[PB[7]] if (j == 0 and k == 0) else (),
                      pwrites=() if (j == 0 and k == 0) else [PB[7]])
            E("dve", lambda e: e.tensor_tensor(
                out=dtv.rearrange("p (j h) -> p j h", j=4), in0=bank(7)[:, 0:64].rearrange("p (j h) -> p j h", j=4),
                in1=prow[:, PR_DTB:PR_DTB + 16].unsqueeze(1).to_broadcast([128, 4, 16]), op=ALU.add),
              reads=[PB[7], b_prow], writes=[b_dtv])
            E("act", lambda e: e.activation(out=dtv, in_=dtv, func=AF.Exp), reads=[b_dtv], writes=[b_dtv])
            E("act", lambda e: e.activation(out=dts, in_=dtv, func=AF.Ln, bias=1.0), reads=[b_dtv], writes=[b_dts])
            E("dve", lambda e: e.tensor_tensor(
                out=lds.rearrange("p (j h) -> p j h", j=4), in0=dts.rearrange("p (j h) -> p j h", j=4),
                in1=a_bc.unsqueeze(1).to_broadcast([128, 4, 16]), op=ALU.mult),
              reads=[b_dts, b_abc], writes=[b_lds])
            E("pool", lambda e, tt=tt: e.dma_start(
                out=DT_d[tt * 512:(tt + 1) * 512, 0:16].rearrange("(j p) h -> p j h", p=128),
                in_=dts.rearrange("p (j h) -> p j h", j=4)), reads=[b_dts], dma=True, key="dto")
            E("pool", lambda e, tt=tt: e.dma_start(
                out=DT_d[tt * 512:(tt + 1) * 512, 16:32].rearrange("(j p) h -> p j h", p=128),
                in_=lds.rearrange("p (j h) -> p j h", j=4)), reads=[b_lds], dma=True, key="ldo")
        S_.barrier()
        A.top = base_mark

        NB = NCH
        QT = [A.bf16(S) for _ in range(1)] * 2; KT = [A.bf16(S) for _ in range(1)] * 2
        Vall = A.bf16(NB * 1024); b_Vall = Buf("Vall")
        ZAq = [A.bf16(512) for _ in range(2)]; b_ZAq = [Buf("ZAq%d" % i) for i in range(2)]
        b_QT = [Buf("QT0")] * 2; b_KT = [Buf("KT0")] * 2
        NE, NL, NX, NWB = 3, 4, 2, 3
        eb = [A.f32(1024) for _ in range(NE)]; b_eb = [Buf("e%d" % i) for i in range(NE)]
        lb = [A.bf16(1024) for _ in range(NL)]; b_lb = [Buf("l%d" % i) for i in range(NL)]
        xb = [A.f32(1024) for _ in range(NX)]; b_xb = [Buf("xx%d" % i) for i in range(NX)]
        wb = [A.bf16(1024) for _ in range(NWB)]; b_wb = [Buf("w%d" % i) for i in range(NWB)]
        sqa = A.f32(512); b_sqa = Buf("sqa")
        ats = [A.bf16(512) for _ in range(2)]; b_ats = [Buf("ats%d" % i) for i in range(2)]
        atq = [0]

        SB7 = 7
        b7 = bank(SB7)
        xsI = [A.f32(1024) for _ in range(2)]; b_xsI = [Buf("xsI%d" % i) for i in range(2)]
        bcI = [A.bf16(512) for _ in range(2)]; b_bcI = [Buf("bcI%d" % i) for i in range(2)]
        zsI = [A.bf16(1024) for _ in range(2)]; b_zsI = [Buf("zsI%d" % i) for i in range(2)]
        dtI = [A.f32(32) for _ in range(2)]; b_dtI = [Buf("dtI%d" % i) for i in range(2)]
        Rr = A.f32(16 * 128); b_R = Buf("R")
        decay = A.f32(16 * 128); b_decay = Buf("decay")
        Mm = A.bf16(16 * 128); b_M = Buf("M")
        CE = A.bf16(16 * 128); b_CE = Buf("CE")
        xtf = A.f32(1024); b_xtf = Buf("xtf")
        xtb = A.bf16(1024); b_xtb = Buf("xtb")
        Btk = A.bf16(256); b_Btk = Buf("Btk")
        XWt = A.bf16(1024); b_XW = Buf("XW")
        cbs = A.f32(256); b_cbs = Buf("cbs")
        dcd = A.f32(32); b_dcd = Buf("dcd")
        wgt = A.f32(16); b_wgt = Buf("wgt")
        state = A.f32(1024); b_state = Buf("state")
        stateb = A.bf16(1024); b_stateb = Buf("stateb")
        stsT = A.bf16(8 * 512); b_stsT = Buf("stsT")
        ub = Rr[:, 0:1024]
        sqs = Rr[:, 1024:2048]
        E("dve", lambda e: e.memset(state, 0.0), writes=[b_state])
        E("dve", lambda e: e.memset(stateb, 0.0), writes=[b_stateb])

        def ssd_load(c):
            sl = c % 2
            E("sp", lambda e: e.dma_start(out=xsI[sl].rearrange("p (m t) -> p m t", m=8),
                                          in_=XS_d[:, c * 128:(c + 1) * 128].rearrange("(m p) t -> p m t", p=128)),
              writes=[b_xsI[sl]], dma=True)
            E("sp", lambda e: e.dma_start(out=bcI[sl].rearrange("p (m t) -> p m t", m=4),
                                          in_=BC_d[:, c * 128:(c + 1) * 128].rearrange("(m p) t -> p m t", p=128)),
              writes=[b_bcI[sl]], dma=True)
            E("sp", lambda e: e.dma_start(out=zsI[sl].rearrange("p (m t) -> p m t", m=8),
                                          in_=ZS_d[:, c * 128:(c + 1) * 128].rearrange("(m p) t -> p m t", p=128)),
              writes=[b_zsI[sl]], dma=True)
            E("sp", lambda e: e.dma_start(out=dtI[sl], in_=DT_d[c * 128:(c + 1) * 128, :]), writes=[b_dtI[sl]], dma=True)

        def ssd_chunk(c):
            sl = c % 2
            xsT, bc_, zs_, dt_ = xsI[sl], bcI[sl], zsI[sl], dtI[sl]
            bx, bb, bz, bd = b_xsI[sl], b_bcI[sl], b_zsI[sl], b_dtI[sl]
            dtj = dt_[:, 0:16]
            ldj = dt_[:, 16:32]
            jq = c % 4
            if c + 1 < NCH:
                ssd_load(c + 1)
            E("dve", lambda e: e.tensor_tensor(
                out=Rr.rearrange("p (h l) -> p h l", h=16), in0=LEf.unsqueeze(1).to_broadcast([128, 16, 128]),
                in1=ldj.unsqueeze(2).to_broadcast([128, 16, 128]), op=ALU.mult),
              reads=[b_cf, bd], writes=[b_R])
            for hf in range(2):
                for m4 in range(4):
                    m = hf * 4 + m4
                    E("pe", lambda e, m=m, m4=m4: e.transpose(b7[:, m4 * 128:(m4 + 1) * 128], xsT[:, m * 128:(m + 1) * 128], idf),
                      reads=[bx, b_cf], writes=[PB[SB7]] if m4 == 0 else (), pwrites=() if m4 == 0 else [PB[SB7]])
                E("dve", lambda e, hf=hf: e.tensor_copy(out=xtf[:, hf * 512:(hf + 1) * 512], in_=b7), reads=[PB[SB7]],
                  writes=[b_xtf] if hf == 0 else (), pwrites=() if hf == 0 else [b_xtf])
                yield
            E("dve", lambda e: e.tensor_copy(out=xtb, in_=xtf), reads=[b_xtf], writes=[b_xtb])
            b7b = b7[:, 0:128].bitcast(BF16)
            for g in range(2):
                E("pe", lambda e, g=g: e.transpose(b7b[:, g * 128:(g + 1) * 128], bc_[:, g * 128:(g + 1) * 128], idb),
                  reads=[bb, b_cb], writes=[PB[SB7]] if g == 0 else (), pwrites=() if g == 0 else [PB[SB7]])
            for g in range(2):
                E("pe", lambda e, g=g: e.matmul(b7[:, 256 + g * 128:256 + (g + 1) * 128], lhsT=bc_[:, g * 128:(g + 1) * 128],
                                                rhs=bc_[:, 256 + g * 128:256 + (g + 1) * 128], start=True, stop=True),
                  reads=[bb], pwrites=[PB[SB7]])
            E("dve", lambda e: e.tensor_copy(out=Btk, in_=b7b), reads=[PB[SB7]], writes=[b_Btk])
            E("dve", lambda e: e.tensor_copy(out=cbs, in_=b7[:, 256:512]), reads=[PB[SB7]], writes=[b_cbs])
            yield
            for q in range(4):
                E("pe", lambda e, q=q: e.matmul(b7, lhsT=GTf, rhs=Rr[:, q * 512:(q + 1) * 512], start=True, stop=False,
                                                skip_group_check=True), reads=[b_R, b_cf], writes=[PB[SB7]])
                E("pe", lambda e: e.matmul(b7, lhsT=idb, rhs=neg4, start=False, stop=True, skip_group_check=True),
                  reads=[b_cb, b_neg4], pwrites=[PB[SB7]])
                E("act", lambda e, q=q: e.activation(out=decay[:, q * 512:(q + 1) * 512], in_=b7, func=AF.Exp),
                  reads=[PB[SB7]], writes=[b_decay] if q == 0 else (), pwrites=() if q == 0 else [b_decay])
                yield
            for hq in range(2):
                for h in range(hq * 8, hq * 8 + 8):
                    g = h // 8
                    E("dve", lambda e, h=h, g=g: e.scalar_tensor_tensor(
                        out=Mm[:, h * 128:(h + 1) * 128], in0=decay[:, h * 128:(h + 1) * 128], scalar=dtj[:, h:h + 1],
                        in1=cbs[:, g * 128:(g + 1) * 128], op0=ALU.mult, op1=ALU.mult),
                      reads=[b_decay, bd, b_cbs], writes=[b_M] if h == 0 else (), pwrites=() if h == 0 else [b_M])
                if hq == 0:
                    E("pe", lambda e: e.matmul(b7[:, 0:16], lhsT=GTf, rhs=ldj, start=True, stop=True),
                      reads=[bd, b_cf], writes=[PB[SB7]])
                    E("pe", lambda e: e.matmul(b7[:, 16:32], lhsT=onesf, rhs=ldj, start=True, stop=True),
                      reads=[bd, b_cf], pwrites=[PB[SB7]])
                    E("act", lambda e: e.activation(out=dcd, in_=b7[:, 0:32], func=AF.Exp), reads=[PB[SB7]], writes=[b_dcd])
                yield
            E("dve", lambda e: e.tensor_tensor(out=wgt, in0=dtj, in1=dcd[:, 0:16], op=ALU.mult),
              reads=[bd, b_dcd], writes=[b_wgt])
            E("dve", lambda e: e.tensor_tensor(
                out=XWt.rearrange("p (h q) -> p h q", h=16), in0=xtf.rearrange("p (h q) -> p h q", h=16),
                in1=wgt.unsqueeze(2).to_broadcast([128, 16, 64]), op=ALU.mult),
              reads=[b_xtf, b_wgt], writes=[b_XW])
            for q in range(4):
                E("pe", lambda e, q=q: e.matmul(b7, lhsT=onesf, rhs=Rr[:, q * 512:(q + 1) * 512], start=True, stop=True),
                  reads=[b_R, b_cf], writes=[PB[SB7]])
                E("act", lambda e, q=q: e.activation(out=decay[:, q * 512:(q + 1) * 512], in_=b7, func=AF.Exp),
                  reads=[PB[SB7], b_M], writes=[b_decay] if q == 0 else (), pwrites=() if q == 0 else [b_decay])
                yield
            for g in range(2):
                E("dve", lambda e, g=g: e.tensor_tensor(
                    out=CE[:, g * 1024:(g + 1) * 1024].rearrange("p (h l) -> p h l", h=8),
                    in0=decay[:, g * 1024:(g + 1) * 1024].rearrange("p (h l) -> p h l", h=8),
                    in1=bc_[:, 256 + g * 128:256 + (g + 1) * 128].unsqueeze(1).to_broadcast([128, 8, 128]), op=ALU.mult),
                  reads=[b_decay, bb], writes=[b_CE] if g == 0 else (), pwrites=() if g == 0 else [b_CE])
            yield
            for hb in range(2):
                for p4 in range(4):
                    pr = hb * 4 + p4
                    csl = slice(p4 * 128, (p4 + 1) * 128)
                    for half in range(2):
                        h = 2 * pr + half
                        first = (p4 == 0 and half == 0)
                        E("pe", lambda e, csl=csl, half=half, h=h: e.matmul(
                            b7[64 * half:64 * half + 64, csl], lhsT=xtb[:, h * 64:(h + 1) * 64],
                            rhs=Mm[:, h * 128:(h + 1) * 128], start=True, stop=False, skip_group_check=True),
                          reads=[b_xtb, b_M], writes=[PB[SB7]] if first else (), pwrites=() if first else [PB[SB7]])
                        E("pe", lambda e, csl=csl, half=half, h=h: e.matmul(
                            b7[64 * half:64 * half + 64, csl], lhsT=stateb[:, h * 64:(h + 1) * 64],
                            rhs=CE[:, h * 128:(h + 1) * 128], start=False, stop=True, skip_group_check=True),
                          reads=[b_stateb, b_CE], pwrites=[PB[SB7]])
                usl = ub[:, hb * 512:(hb + 1) * 512]
                u3 = usl.rearrange("p (m t) -> p m t", m=4)
                E("dve", lambda e, hb=hb, u3=u3: e.tensor_tensor(
                    out=u3, in0=xsT[:, hb * 512:(hb + 1) * 512].rearrange("p (m t) -> p m t", m=4),
                    in1=pcol[:, PC_D + hb * 4:PC_D + hb * 4 + 4].unsqueeze(2).to_broadcast([128, 4, 128]), op=ALU.mult),
                  reads=[bx, b_pcol], writes=[b_R] if hb == 0 else (), pwrites=() if hb == 0 else [b_R])
                E("dve", lambda e, usl=usl: e.tensor_tensor(out=usl, in0=usl, in1=b7, op=ALU.add),
                  reads=[b_R, PB[SB7]], pwrites=[b_R])
                E("dve", lambda e, hb=hb, usl=usl: e.tensor_tensor(out=usl, in0=usl, in1=zs_[:, hb * 512:(hb + 1) * 512],
                                                                   op=ALU.mult), reads=[b_R, bz], pwrites=[b_R])
                o3 = stsT.rearrange("p (m t) -> p m t", m=8)[:, hb * 4:hb * 4 + 4, jq * 128:(jq + 1) * 128]
                E("dve", lambda e, hb=hb, u3=u3, o3=o3: e.tensor_tensor(
                    out=o3, in0=u3,
                    in1=pcol[:, PC_GS + hb * 4:PC_GS + hb * 4 + 4].unsqueeze(2).to_broadcast([128, 4, 128]), op=ALU.mult),
                  reads=[b_R, b_pcol], pwrites=[b_stsT])
                yield
            E("dve", lambda e: e.tensor_tensor(out=sqs, in0=ub, in1=ub, op=ALU.mult), reads=[b_R], pwrites=[b_R])
            yield
            for pr in range(8):
                E("pe", lambda e, pr=pr: e.matmul(b7[:, 0:1], lhsT=sqs[:, pr * 128:(pr + 1) * 128], rhs=onesf[:, 0:1],
                                                 start=(pr == 0), stop=(pr == 7)),
                  reads=[b_R, b_cf], writes=[PB[SB7]] if pr == 0 else (), pwrites=() if pr == 0 else [PB[SB7]])
            E("dve", lambda e: e.tensor_copy(out=ssq_s[:, c:c + 1], in_=b7[:, 0:1]), reads=[PB[SB7]], pwrites=[b_ssqs])
            yield
            for g in range(2):
                E("pe", lambda e, g=g: e.matmul(b7, lhsT=Btk[:, g * 128:(g + 1) * 128], rhs=XWt[:, g * 512:(g + 1) * 512],
                                                start=True, stop=True), reads=[b_Btk, b_XW], writes=[PB[SB7]])
                ssl = state[:, g * 512:(g + 1) * 512]
                E("dve", lambda e, g=g, ssl=ssl: e.tensor_tensor(
                    out=ssl.rearrange("p (h q) -> p h q", h=8), in0=ssl.rearrange("p (h q) -> p h q", h=8),
                    in1=dcd[:, 16 + g * 8:16 + g * 8 + 8].unsqueeze(2).to_broadcast([128, 8, 64]), op=ALU.mult),
                  reads=[b_state, b_dcd], writes=[b_state] if g == 0 else (), pwrites=() if g == 0 else [b_state])
                E("dve", lambda e, ssl=ssl: e.tensor_tensor(out=ssl, in0=ssl, in1=b7, op=ALU.add),
                  reads=[b_state, PB[SB7]], pwrites=[b_state])
                yield
            E("dve", lambda e: e.tensor_copy(out=stateb, in_=state), reads=[b_state], writes=[b_stateb])
            if jq == 3:
                tt_ = c // 4
                E("sp", lambda e: e.dma_start(
                    out=AT_d[1024:2048, tt_ * 512:(tt_ + 1) * 512].rearrange("(m p) t -> p m t", p=128),
                    in_=stsT.rearrange("p (m t) -> p m t", m=8)), reads=[b_stsT], dma=True)
            yield

        def ssd_all():
            for c in range(NCH):
                for _ in ssd_chunk(c):
                    yield

        ssd_load(0)
        ssd_gen = ssd_all()

        def ssd_step():
            try:
                next(ssd_gen)
                return True
            except StopIteration:
                return False

        def v2(ap, c0):
            return ap.rearrange("p (b c) -> p b c", b=2)[:, :, c0:512]

        def load_hp(hp):
            r = hp % 2
            E("sp", lambda e: e.dma_start(out=QT[r], in_=QT_d[hp * 128:(hp + 1) * 128, :]), writes=[b_QT[r]], dma=True)
            E("sp", lambda e: e.dma_start(out=KT[r], in_=KT_d[hp * 128:(hp + 1) * 128, :]), writes=[b_KT[r]], dma=True)

        Vall3 = Vall.rearrange("p (n c) -> p n c", c=1024)
        V_v = V_d.rearrange("(n p) c -> p n c", p=128)
        nvq = max(1, NB // 8)
        for vq in range(0, NB, nvq):
            E("sp", lambda e, vq=vq: e.dma_start(out=Vall3[:, vq:vq + nvq, :], in_=V_v[:, vq:vq + nvq, :]),
              pwrites=[b_Vall], dma=True, key="Vall")
        load_hp(0)
        ucount = [0]

        def do_hp(hp, r):
            if hp > 0:
                load_hp(hp)
            steps = []
            for qt in range(NT):
                for jj in range(4 * qt + 3, -1, -1):
                    steps.append((qt, jj))
            nu = len(steps)
            info = {}

            def st_qk(n):
                qt, jj = steps[n]
                u = ucount[0] + n
                kk = jj - 4 * qt
                c0 = 128 * kk if kk > 0 else 0
                ap_ = 2 * (u % 2)
                info[n] = (qt, jj, u, kk, c0, ap_)
                if jj == 4 * qt + 3:
                    zi = (hp * NT + qt) % 2
                    E("sp", lambda e, zi=zi: e.dma_start(out=ZAq[zi], in_=ZA_d[hp * 128:(hp + 1) * 128, qt * 512:(qt + 1) * 512]),
                      writes=[b_ZAq[zi]], dma=True)
                for half in range(2):
                    pb = 64 * half
                    E("pe", lambda e, pb=pb, half=half: e.matmul(
                        bank(ap_ + half)[:, c0:512], lhsT=KT[r][pb:pb + 64, jj * 128:(jj + 1) * 128],
                        rhs=QT[r][pb:pb + 64, qt * 512 + c0:(qt + 1) * 512], start=True, stop=True),
                      reads=[b_KT[r], b_QT[r]], writes=[PB[ap_ + half]])

            def st_act1(n):
                qt, jj, u, kk, c0, ap_ = info[n]
                ei, li = u % NE, u % NL
                E("act", lambda e: e.activation(out=v2(eb[ei], c0), in_=v2(psum[:, ap_ * 512:(ap_ + 2) * 512], c0), func=AF.Exp),
                  reads=[PB[ap_], PB[ap_ + 1]], writes=[b_eb[ei]])
                if kk >= 0:
                    ev = eb[ei].rearrange("p (b c) -> p b c", b=2)[:, :, c0:c0 + 128]
                    E("pool", lambda e: e.tensor_tensor(out=ev, in0=ev, in1=LTf.unsqueeze(1).to_broadcast([128, 2, 128]),
                                                        op=ALU.mult), reads=[b_eb[ei], b_cf], writes=[b_eb[ei]])

            def st_act2(n):
                qt, jj, u, kk, c0, ap_ = info[n]
                ei, li = u % NE, u % NL
                E("act", lambda e: e.activation(out=v2(lb[li], c0), in_=v2(eb[ei], c0), func=AF.Ln, bias=1.0),
                  reads=[b_eb[ei]], writes=[b_lb[li]])

            def st_ge(n):
                qt, jj, u, kk, c0, ap_ = info[n]
                li, xi = u % NL, u % NX
                for half in range(2):
                    cbk = 4 + half
                    if jj == 4 * qt + 3:
                        E("pe", lambda e, cbk=cbk: e.matmul(bank(cbk), lhsT=zb[:, 0:128], rhs=zb, start=True, stop=False,
                                                            skip_group_check=True), reads=[b_zb], writes=[PB[cbk]])
                    E("pe", lambda e, cbk=cbk, half=half: e.matmul(
                        bank(cbk)[:, c0:512], lhsT=GEb, rhs=lb[li][:, half * 512 + c0:(half + 1) * 512],
                        start=False, stop=False, skip_group_check=True), reads=[b_lb[li], b_cb], writes=[PB[cbk]])
                E("act", lambda e: e.activation(out=v2(xb[xi], c0), in_=v2(psum[:, 2048:3072], c0), func=AF.Exp, scale=-1.0),
                  reads=[PB[4], PB[5]], writes=[b_xb[xi]])

            def st_lt(n):
                qt, jj, u, kk, c0, ap_ = info[n]
                if jj == 0:
                    return
                li = u % NL
                for half in range(2):
                    cbk = 4 + half
                    E("pe", lambda e, cbk=cbk, half=half: e.matmul(
                        bank(cbk)[:, c0:512], lhsT=LTb, rhs=lb[li][:, half * 512 + c0:(half + 1) * 512],
                        start=False, stop=False, skip_group_check=True), reads=[b_lb[li], b_cb], writes=[PB[cbk]])

            def st_w(n):
                qt, jj, u, kk, c0, ap_ = info[n]
                ei, xi, wi = u % NE, u % NX, u % NWB
                E("dve", lambda e: e.tensor_tensor(out=v2(wb[wi], c0), in0=v2(eb[ei], c0), in1=v2(xb[xi], c0), op=ALU.mult),
                  reads=[b_eb[ei], b_xb[xi]], writes=[b_wb[wi]])

            def st_pv(n):
                qt, jj, u, kk, c0, ap_ = info[n]
                wi = u % NWB
                ob = 6
                for half in range(2):
                    pb = 64 * half
                    hh = 2 * hp + half
                    if jj == 4 * qt + 3:
                        E("pe", lambda e, pb=pb: e.matmul(bank(ob)[pb:pb + 64, :], lhsT=zb[:, 0:64], rhs=zb, start=True,
                                                          stop=False, skip_group_check=True), reads=[b_zb],
                          writes=[PB[ob]] if half == 0 else (), pwrites=() if half == 0 else [PB[ob]])
                    E("pe", lambda e, pb=pb, half=half, hh=hh: e.matmul(
                        bank(ob)[pb:pb + 64, c0:512], lhsT=Vall[:, jj * 1024 + hh * 64:jj * 1024 + hh * 64 + 64],
                        rhs=wb[wi][:, half * 512 + c0:(half + 1) * 512], start=False, stop=(jj == 0), skip_group_check=True),
                      reads=[b_Vall, b_wb[wi]], pwrites=[PB[ob]])

            def st_epi(n):
                qt, jj, u, kk, c0, ap_ = info[n]
                ob = 6
                if jj == 0:
                    E("act", lambda e: e.activation(out=sqa, in_=bank(ob), func=AF.Square), reads=[PB[ob]], writes=[b_sqa])
                    for sub in range(4):
                        E("pe", lambda e, sub=sub: e.matmul(bank(0)[:, sub:sub + 1], lhsT=sqa[:, sub * 128:(sub + 1) * 128],
                                                            rhs=onesf[:, 0:1], start=True, stop=True),
                          reads=[b_sqa, b_cf], writes=[PB[0]] if sub == 0 else (), pwrites=() if sub == 0 else [PB[0]])
                    if hp == 0:
                        E("dve", lambda e: e.tensor_copy(out=ssq_a[:, 4 * qt:4 * qt + 4], in_=bank(0)[:, 0:4]),
                          reads=[PB[0]], pwrites=[b_ssqa])
                    else:
                        E("dve", lambda e: e.tensor_tensor(out=ssq_a[:, 4 * qt:4 * qt + 4], in0=ssq_a[:, 4 * qt:4 * qt + 4],
                                                           in1=bank(0)[:, 0:4], op=ALU.add),
                          reads=[PB[0], b_ssqa], writes=[b_ssqa])
                    ai = atq[0] % 2
                    atq[0] += 1
                    E("dve", lambda e: e.scalar_tensor_tensor(
                        out=ats[ai], in0=bank(ob), scalar=pcol[:, PC_GA + hp:PC_GA + hp + 1],
                        in1=ZAq[(hp * NT + qt) % 2], op0=ALU.mult, op1=ALU.mult),
                      reads=[PB[ob], b_pcol, b_ZAq[(hp * NT + qt) % 2]], writes=[b_ats[ai]])
                    E("sp", lambda e: e.dma_start(out=AT_d[hp * 128:(hp + 1) * 128, qt * 512:(qt + 1) * 512], in_=ats[ai]),
                      reads=[b_ats[ai]], dma=True)

            for n in range(nu + 3):
                ssd_step()
                if n < nu:
                    st_qk(n)
                if 0 <= n - 2 < nu:
                    st_lt(n - 2)
                if n < nu:
                    st_act1(n)
                if 0 <= n - 1 < nu:
                    st_ge(n - 1)
                if n < nu:
                    st_act2(n)
                if 0 <= n - 1 < nu:
                    st_w(n - 1)
                if 0 <= n - 2 < nu:
                    st_pv(n - 2)
                if 0 <= n - 2 < nu:
                    st_epi(n - 2)
            ucount[0] += nu

        for hp_ in range(8):
            do_hp(hp_, hp_ % 2)
        while ssd_step():
            pass
        S_.barrier()
        A.top = base_mark

        Wg = A.bf16(16 * 1024); b_Wg = Buf("Wg")
        wst = [A.f32(4096) for _ in range(2)]; b_wst = [Buf("wst%d" % i) for i in range(2)]
        wout_v = wout_d.rearrange("(m p) c -> p m c", p=128)
        for ch in range(4):
            rr = ch % 2
            E("sp", lambda e, ch=ch, rr=rr: e.dma_start(out=wst[rr].rearrange("p (m c) -> p m c", m=4),
                                                      in_=wout_v[:, ch * 4:(ch + 1) * 4, :]), writes=[b_wst[rr]], dma=True)
            E("dve" if ch % 2 == 0 else "pool", lambda e, ch=ch, rr=rr: e.tensor_tensor(
                out=Wg[:, ch * 4096:(ch + 1) * 4096].rearrange("p (m c) -> p m c", m=4),
                in0=wst[rr].rearrange("p (m c) -> p m c", m=4),
                in1=gate_bc.unsqueeze(1).to_broadcast([128, 4, 1024]), op=ALU.mult),
              reads=[b_wst[rr], b_gate], pwrites=[b_Wg])
        for (src, dst) in ((ssq_a, rstd_a), (ssq_s, rstd_s)):
            E("act", lambda e, src=src, dst=dst: e.activation(out=dst, in_=src, func=AF.Ln, scale=1.0 / 1024, bias=EPS),
              reads=[b_ssqa, b_ssqs], pwrites=[b_rstd])
            E("act", lambda e, dst=dst: e.activation(out=dst, in_=dst, func=AF.Exp, scale=-0.5),
              reads=[b_rstd], pwrites=[b_rstd])
        ATt = [A.bf16(16 * 512) for _ in range(2)]; b_ATt = [Buf("ATt%d" % i) for i in range(2)]
        xo = [A.f32(4096) for _ in range(2)]; b_xo = [Buf("xo%d" % i) for i in range(2)]
        yb = [A.f32(1024) for _ in range(2)]; b_yb = [Buf("yb%d" % i) for i in range(2)]
        junk3 = A.bf16(1024); b_junk3 = Buf("junk3")
        b_out = Buf("outd")
        yq = [0]
        gf_bc = prow[:, PR_GF:PR_GF + 1024]
        for tt in range(NT):
            r = tt % 2
            E("sp", lambda e, tt=tt, r=r: e.dma_start(
                out=ATt[r].rearrange("p (m t) -> p m t", m=16),
                in_=AT_d[:, tt * 512:(tt + 1) * 512].rearrange("(m p) t -> p m t", p=128)), writes=[b_ATt[r]], dma=True)
            E("sp", lambda e, tt=tt, r=r: e.dma_start(
                out=xo[r].rearrange("p (j d) -> p j d", j=4),
                in_=x_d[tt * 512:(tt + 1) * 512, :].rearrange("(j p) d -> p j d", p=128)), writes=[b_xo[r]], dma=True)
            for j in range(4):
                c = tt * 4 + j
                yi = yq[0] % 2
                yq[0] += 1
                for n in range(2):
                    ba = 4 * (c % 2) + 2 * n
                    bs = ba + 1
                    for m in range(8):
                        E("pe", lambda e, r=r, m=m, j=j, n=n, ba=ba: e.matmul(
                            bank(ba), lhsT=ATt[r][:, m * 512 + j * 128:m * 512 + (j + 1) * 128],
                            rhs=Wg[:, m * 1024 + n * 512:m * 1024 + (n + 1) * 512], start=(m == 0), stop=(m == 7)),
                          reads=[b_ATt[r], b_Wg], writes=[PB[ba]] if m == 0 else (), pwrites=() if m == 0 else [PB[ba]])
                    for m in range(8, 16):
                        E("pe", lambda e, r=r, m=m, j=j, n=n, bs=bs: e.matmul(
                            bank(bs), lhsT=ATt[r][:, m * 512 + j * 128:m * 512 + (j + 1) * 128],
                            rhs=Wg[:, m * 1024 + n * 512:m * 1024 + (n + 1) * 512], start=(m == 8), stop=(m == 15)),
                          reads=[b_ATt[r], b_Wg], writes=[PB[bs]] if m == 8 else (), pwrites=() if m == 8 else [PB[bs]])
                    ysl = yb[yi][:, n * 512:(n + 1) * 512]
                    E("dve", lambda e, r=r, j=j, n=n, ba=ba, ysl=ysl, c=c: e.scalar_tensor_tensor(
                        out=ysl, in0=bank(ba), scalar=rstd_a[:, c:c + 1],
                        in1=xo[r][:, j * 1024 + n * 512:j * 1024 + (n + 1) * 512], op0=ALU.mult, op1=ALU.add),
                      reads=[PB[ba], b_rstd, b_xo[r]], writes=[b_yb[yi]] if n == 0 else (), pwrites=() if n == 0 else [b_yb[yi]])
                    E("dve", lambda e, bs=bs, ysl=ysl, c=c: e.scalar_tensor_tensor(
                        out=ysl, in0=bank(bs), scalar=rstd_s[:, c:c + 1], in1=ysl, op0=ALU.mult, op1=ALU.add),
                      reads=[PB[bs], b_rstd, b_yb[yi]], pwrites=[b_yb[yi]])
                E("act", lambda e, yi=yi, c=c: e.activation(out=junk3, in_=yb[yi], func=AF.Square,
                                                            accum_out=ssqf[:, c:c + 1]),
                  reads=[b_yb[yi]], writes=[b_junk3], pwrites=[b_ssqf])
                E("act", lambda e, c=c: e.activation(out=rstdf[:, c:c + 1], in_=ssqf[:, c:c + 1], func=AF.Ln,
                                                     scale=1.0 / 1024, bias=EPS), reads=[b_ssqf], pwrites=[b_rstdf])
                E("act", lambda e, c=c: e.activation(out=rstdf[:, c:c + 1], in_=rstdf[:, c:c + 1], func=AF.Exp, scale=-0.5),
                  reads=[b_rstdf], pwrites=[b_rstdf])
                E("dve", lambda e, yi=yi, c=c: e.scalar_tensor_tensor(
                    out=yb[yi], in0=yb[yi], scalar=rstdf[:, c:c + 1], in1=gf_bc, op0=ALU.mult, op1=ALU.mult),
                  reads=[b_yb[yi], b_rstdf, b_prow], writes=[b_yb[yi]])
                E("pool", lambda e, yi=yi, c=c: e.dma_start(out=out_d[c * 128:(c + 1) * 128, :], in_=yb[yi]),
                  reads=[b_yb[yi]], pwrites=[b_out], dma=True, key="yo%d" % yi)
        E("sp", lambda e: e.nop(), reads=[b_out], real=False)
        S_.run()
    return nc


def _consts():
    a = np.arange(128)[:, None]
    b = np.arange(128)[None, :]
    c = np.zeros((128, NCONST), np.float32)
    c[:, C_ID:C_ID + 128] = (a == b)
    c[:, C_GE:C_GE + 128] = (a >= b)
    c[:, C_LT:C_LT + 128] = (a < b)
    c[:, C_LE:C_LE + 128] = (a <= b)
    c[:, C_GT:C_GT + 128] = (a > b)
    c[:, C_NEG:C_NEG + 128] = NEGV * (a > b)
    c[:, C_ONE:C_ONE + 128] = 1.0
    return c


def _col(v, n):
    return np.ascontiguousarray(np.asarray(v, np.float32).reshape(n, 128).T)


def make_in_maps(x, c, w_ada, b_ada, norm_in_gain, w_in, conv_w, conv_b, dt_bias, a_log, d_skip,
                 sb_norm_gain, ssm_norm_gain, w_out, norm_f_gain, cores=None):
    B = x.shape[0]
    consts = _consts()
    maps = []
    w_ada0 = np.ascontiguousarray(w_ada[0], np.float32)
    w_in0 = np.ascontiguousarray(w_in[0], np.float32)
    w_out0 = np.ascontiguousarray(w_out[0], np.float32)
    prow = np.zeros((1, NPROW), np.float32)
    prow[0, PR_BG:PR_BG + 1024] = b_ada[0, 2048:3072]
    prow[0, PR_GF:PR_GF + 1024] = norm_f_gain
    prow[0, PR_DTB:PR_DTB + 16] = dt_bias[0]
    prow[0, PR_AL:PR_AL + 16] = a_log[0]
    for b in (range(B) if cores is None else cores):
        pcol = np.zeros((128, NPCOL), np.float32)
        pcol[:, PC_C:PC_C + 8] = _col(c[b], 8)
        pcol[:, PC_BADA:PC_BADA + 24] = _col(b_ada[0], 24)
        pcol[:, PC_GIN:PC_GIN + 8] = _col(norm_in_gain[0], 8)
        pcol[:, PC_GA:PC_GA + 8] = _col(sb_norm_gain[0], 8)
        pcol[:, PC_GS:PC_GS + 8] = _col(ssm_norm_gain[0], 8)
        pcol[:, PC_D:PC_D + 8] = _col(np.repeat(np.asarray(d_skip[0], np.float32), 64), 8)
        cw = np.asarray(conv_w[0], np.float32)
        pcol[:, PC_CW:PC_CW + 48] = cw.T.reshape(12, 128, 4).transpose(1, 0, 2).reshape(128, 48)
        pcol[:, PC_CB:PC_CB + 12] = _col(conv_b[0], 12)
        maps.append({"x": np.ascontiguousarray(x[b], np.float32), "w_ada": w_ada0, "w_in": w_in0, "w_out": w_out0,
                     "consts": consts, "pcol": pcol, "prow": prow})
    return maps


_NC_CACHE = {}


def kernel(x, c, w_ada, b_ada, norm_in_gain, w_in, conv_w, conv_b, dt_bias, a_log, d_skip,
           sb_norm_gain, ssm_norm_gain, w_out, norm_f_gain):
    x = np.asarray(x)
    B, S, _ = x.shape
    if S not in _NC_CACHE:
        _NC_CACHE[S] = build(S)
    nc = _NC_CACHE[S]
    maps = make_in_maps(x, np.asarray(c), np.asarray(w_ada), np.asarray(b_ada), np.asarray(norm_in_gain),
                        np.asarray(w_in), np.asarray(conv_w), np.asarray(conv_b), np.asarray(dt_bias),
                        np.asarray(a_log), np.asarray(d_skip), np.asarray(sb_norm_gain),
                        np.asarray(ssm_norm_gain), np.asarray(w_out), np.asarray(norm_f_gain))
    res = run_bass_kernel_spmd(nc, maps, core_ids=list(range(B)))
    return np.stack([np.asarray(r["out"], np.float32) for r in res.results], axis=0)
```

```python
import numpy as np
from contextlib import ExitStack
import concourse.bass as bass
import concourse.mybir as mybir
from concourse.bass_utils import run_bass_kernel_spmd

F32 = mybir.dt.float32
BF16 = mybir.dt.bfloat16
AF = mybir.ActivationFunctionType
ALU = mybir.AluOpType
AX = mybir.AxisListType

ENGS = ("pe", "act", "dve", "pool", "sp")
EPS = 1e-6
NEGV = -30000.0


class Buf:
    __slots__ = ("name", "writers", "readers", "const")

    def __init__(self, name, const=False):
        self.name = name
        self.writers = []
        self.readers = []
        self.const = const


class Op:
    __slots__ = ("eng", "fn", "deps", "dma", "idx", "need_sig", "val", "key", "real")

    def __init__(self, eng, fn, dma, key, real):
        self.eng = eng
        self.fn = fn
        self.dma = dma
        self.key = key
        self.real = real
        self.deps = ()
        self.idx = -1
        self.need_sig = False
        self.val = 0


class Sched:
    def __init__(self, nc):
        self.nc = nc
        self.ops = {e: [] for e in ENGS}
        self.last_dma = {}
        self.groups = {e: {} for e in ENGS}
        self._gstart = {}

    def group_begin(self, eng):
        self._gstart[eng] = len(self.ops[eng])

    def group_end(self, eng):
        g0 = self._gstart.pop(eng)
        g1 = len(self.ops[eng])
        if g1 > g0 + 1:
            self.groups[eng][g0] = g1

    def emit(self, eng, fn, reads=(), writes=(), pwrites=(), dma=False, key=None, extra=(), real=True):
        if dma and key is None:
            key = writes[0].name if writes else (pwrites[0].name if pwrites else reads[0].name)
        op = Op(eng, fn, dma, key, real)
        deps = set(extra)
        for b in reads:
            deps.update(b.writers)
        for b in writes:
            deps.update(b.writers)
            deps.update(b.readers)
        for b in pwrites:
            deps.update(b.readers)
            if b.writers:
                deps.add(b.writers[0])
        op.deps = deps
        for b in reads:
            if not b.const:
                b.readers.append(op)
        for b in writes:
            b.writers = [op]
            b.readers = []
        for b in pwrites:
            b.writers.append(op)
        op.idx = len(self.ops[eng])
        self.ops[eng].append(op)
        if dma:
            self.last_dma[key] = op
        return op

    def barrier(self):
        lasts = []
        for e in ENGS:
            for op in reversed(self.ops[e]):
                if op.real and not op.dma:
                    lasts.append(op)
                    break
        deps = set(lasts) | set(self.last_dma.values())
        for e in ENGS:
            self.emit(e, lambda g: g.nop(), extra=deps, real=False)

    def _needed(self, op):
        comp = {}
        dmas = []
        for d in op.deps:
            if d.dma:
                dmas.append(d)
                continue
            if d.eng == op.eng and not op.dma:
                if d.eng == "pe":
                    continue
                if d.eng in ("act", "dve") and d.idx < op.idx - 2:
                    continue
            cur = comp.get(d.eng)
            if cur is None or d.idx > cur.idx:
                comp[d.eng] = d
        return comp, dmas

    def run(self):
        nc = self.nc
        for e in ENGS:
            for op in self.ops[e]:
                comp, _ = self._needed(op)
                for d in comp.values():
                    d.need_sig = True
        for e in ENGS:
            cnt = 0
            for op in self.ops[e]:
                if op.dma:
                    continue
                if op.need_sig:
                    cnt += 1
                    op.val = cnt
        keycnt = {}
        for e in ENGS:
            for op in self.ops[e]:
                if op.dma:
                    keycnt[op.key] = keycnt.get(op.key, 0) + 1
                    op.val = 16 * keycnt[op.key]
        keys = sorted(keycnt.keys())
        with ExitStack() as st:
            esem = {e: st.enter_context(nc.semaphore("s_" + e)) for e in ENGS}
            ksem = {k: st.enter_context(nc.semaphore("d%d" % i)) for i, k in enumerate(keys)}
            block = st.enter_context(nc.Block())
            sched = self

            def replay(ename, eng):
                seen = {}

                def do_waits(op):
                    comp, dmas = sched._needed(op)
                    for d in comp.values():
                        if seen.get(("e", d.eng), 0) < d.val:
                            eng.wait_ge(esem[d.eng], d.val)
                            seen[("e", d.eng)] = d.val
                    kmax = {}
                    for d in dmas:
                        if kmax.get(d.key, 0) < d.val:
                            kmax[d.key] = d.val
                    for kk_, vv_ in kmax.items():
                        if seen.get(("k", kk_), 0) < vv_:
                            eng.wait_ge(ksem[kk_], vv_)
                            seen[("k", kk_)] = vv_

                oplist = sched.ops[ename]
                for op in oplist:
                    g1 = sched.groups[ename].get(op.idx)
                    if g1 is not None:
                        for op2 in oplist[op.idx:g1]:
                            do_waits(op2)
                    comp, dmas = sched._needed(op)
                    for d in comp.values():
                        if seen.get(("e", d.eng), 0) < d.val:
                            eng.wait_ge(esem[d.eng], d.val)
                            seen[("e", d.eng)] = d.val
                    kmax = {}
                    for d in dmas:
                        if kmax.get(d.key, 0) < d.val:
                            kmax[d.key] = d.val
                    for kk_, vv_ in kmax.items():
                        if seen.get(("k", kk_), 0) < vv_:
                            eng.wait_ge(ksem[kk_], vv_)
                            seen[("k", kk_)] = vv_
                    ins = op.fn(eng)
                    if op.dma:
                        ins.then_inc(ksem[op.key], 16)
                    elif op.need_sig:
                        ins.then_inc(esem[ename], 1)

            @block.tensor
            def _(e):
                replay("pe", e)

            @block.scalar
            def _(e):
                replay("act", e)

            @block.vector
            def _(e):
                replay("dve", e)

            @block.gpsimd
            def _(e):
                replay("pool", e)

            @block.sync
            def _(e):
                replay("sp", e)


class Arena:
    def __init__(self, big, nw):
        self.big = big
        self.nw = nw
        self.top = 0

    def f32(self, n):
        off = self.top
        self.top += n
        assert self.top <= self.nw, ("sbuf arena overflow", self.top, self.nw)
        return self.big[:, off:off + n]

    def bf16(self, n):
        w = (n + 1) // 2
        off = self.top
        self.top += w
        assert self.top <= self.nw, ("sbuf arena overflow", self.top, self.nw)
        return self.big[:, off:off + w].bitcast(BF16)


C_ID, C_GE, C_LT, C_LE, C_GT, C_NEG, C_ONE = 0, 128, 256, 384, 512, 640, 768
NCONST = 896
PC_C, PC_BADA, PC_GIN, PC_GA, PC_GS, PC_D, PC_CW, PC_CB = 0, 8, 32, 40, 48, 56, 64, 112
NPCOL = 124
PR_BG, PR_GF, PR_DTB, PR_AL = 0, 1024, 2048, 2064
NPROW = 2080
NW_SBUF = 52224
NWARM = 0
NFILL = 0


def build(S, debug=False):
    NT = S // 512
    NCH = S // 128
    nc = bass.Bass("TRN2", target_bir_lowering=False)
    dk = "ExternalOutput" if debug else "Internal"
    x_d = nc.dram_tensor("x", [S, 1024], F32, kind="ExternalInput").ap()
    wada_d = nc.dram_tensor("w_ada", [1024, 3072], F32, kind="ExternalInput").ap()
    win_d = nc.dram_tensor("w_in", [1024, 6672], F32, kind="ExternalInput").ap()
    wout_d = nc.dram_tensor("w_out", [2048, 1024], F32, kind="ExternalInput").ap()
    const_d = nc.dram_tensor("consts", [128, NCONST], F32, kind="ExternalInput").ap()
    pcol_d = nc.dram_tensor("pcol", [128, NPCOL], F32, kind="ExternalInput").ap()
    prow_d = nc.dram_tensor("prow", [1, NPROW], F32, kind="ExternalInput").ap()
    out_d = nc.dram_tensor("out", [S, 1024], F32, kind="ExternalOutput").ap()
    HT_d = nc.dram_tensor("s_ht", [8, 128, S], BF16, kind=dk).ap()
    QT_d = nc.dram_tensor("s_qt", [1024, S], BF16, kind=dk).ap()
    KT_d = nc.dram_tensor("s_kt", [1024, S], BF16, kind=dk).ap()
    ZA_d = nc.dram_tensor("s_za", [1024, S], BF16, kind=dk).ap()
    V_d = nc.dram_tensor("s_v", [S, 1024], BF16, kind=dk).ap()
    AT_d = nc.dram_tensor("s_at", [2048, S], BF16, kind=dk).ap()
    XS_d = nc.dram_tensor("s_xs", [1024, S], F32, kind=dk).ap()
    BC_d = nc.dram_tensor("s_bc", [512, S], BF16, kind=dk).ap()
    ZS_d = nc.dram_tensor("s_zs", [1024, S], BF16, kind=dk).ap()
    DT_d = nc.dram_tensor("s_dt", [S, 32], F32, kind=dk).ap()

    S_ = Sched(nc)
    E = S_.emit

    with ExitStack() as st:
        big = st.enter_context(nc.sbuf_tensor("big", [128, NW_SBUF], F32))
        psum = st.enter_context(nc.psum_tensor("psum", [128, 4096], F32))
        A = Arena(big, NW_SBUF)

        def bank(i):
            return psum[:, 512 * i:512 * (i + 1)]

        PB = [Buf("ps%d" % i) for i in range(8)]

        cf = A.f32(NCONST); b_cf = Buf("cf", const=True)
        cb = A.bf16(NCONST); b_cb = Buf("cb", const=True)
        zb = A.bf16(512); b_zb = Buf("zb", const=True)
        neg4 = A.bf16(512); b_neg4 = Buf("neg4", const=True)
        pcol = A.f32(NPCOL); b_pcol = Buf("pcol", const=True)
        prow = A.f32(NPROW); b_prow = Buf("prow", const=True)
        modT = A.f32(24); b_modT = Buf("modT")
        g1 = A.f32(8); b_g1 = Buf("g1")
        gate_bc = A.f32(1024); b_gate = Buf("gate_bc")
        a_bc = A.f32(16); b_abc = Buf("a_bc")
        ssq_a = A.f32(NCH); b_ssqa = Buf("ssq_a")
        ssq_s = A.f32(NCH); b_ssqs = Buf("ssq_s")
        rstd_a = A.f32(NCH); rstd_s = A.f32(NCH); b_rstd = Buf("rstd_as")
        ssqf = A.f32(NCH); b_ssqf = Buf("ssqf")
        rstdf = A.f32(NCH); b_rstdf = Buf("rstdf")
        tmp_small = A.f32(64); b_tmps = Buf("tmps")

        idf = cf[:, C_ID:C_ID + 128]
        GEb = cb[:, C_GE:C_GE + 128]
        LTb = cb[:, C_LT:C_LT + 128]
        LTf = cf[:, C_LT:C_LT + 128]
        LEf = cf[:, C_LE:C_LE + 128]
        GTf = cf[:, C_GT:C_GT + 128]
        idb = cb[:, C_ID:C_ID + 128]
        onesf = cf[:, C_ONE:C_ONE + 128]

        E("sp", lambda e: e.dma_start(out=cf, in_=const_d), writes=[b_cf], dma=True)
        E("pool", lambda e: e.dma_start(out=cb, in_=const_d), writes=[b_cb], dma=True)
        E("sp", lambda e: e.dma_start(out=pcol, in_=pcol_d), writes=[b_pcol], dma=True)
        E("sp", lambda e: e.dma_start(out=prow, in_=prow_d.partition_broadcast(128)), writes=[b_prow], dma=True)
        E("dve", lambda e: e.memset(zb, 0.0), writes=[b_zb])
        for q in range(4):
            E("dve", lambda e, q=q: e.tensor_copy(out=neg4[:, q * 128:(q + 1) * 128], in_=cf[:, C_NEG:C_NEG + 128]),
              reads=[b_cf], pwrites=[b_neg4])

        base_mark = A.top
        Wa = A.bf16(8 * 4096); b_Wa = Buf("Wa")
        win_v = win_d.rearrange("(k p) c -> p k c", p=128)
        Wa3 = Wa.rearrange("p (k c) -> p k c", k=8)
        for ci in range(8):
            E("pool", lambda e, ci=ci: e.dma_start(out=Wa3[:, :, ci * 512:(ci + 1) * 512],
                                                  in_=win_v[:, :, ci * 512:(ci + 1) * 512]),
              pwrites=[b_Wa], dma=True, key="Wa")
        p0_mark = A.top

        cact = A.f32(8); b_cact = Buf("cact")
        cact_rep = A.f32(8 * 128); b_crep = Buf("crep")
        wa = [A.f32(8 * 512) for _ in range(2)]
        b_wa = [Buf("wa%d" % i) for i in range(2)]
        E("act", lambda e: e.activation(out=cact, in_=pcol[:, PC_C:PC_C + 8], func=AF.Silu),
          reads=[b_pcol], writes=[b_cact])
        E("dve", lambda e: e.tensor_copy(out=cact_rep.rearrange("p (k m) -> p k m", k=8),
                                         in_=cact.unsqueeze(2).to_broadcast([128, 8, 128])),
          reads=[b_cact], writes=[b_crep])
        E("act", lambda e: e.activation(out=a_bc, in_=prow[:, PR_AL:PR_AL + 16], func=AF.Exp),
          reads=[b_prow], writes=[b_abc])
        E("dve", lambda e: e.tensor_scalar(out=a_bc, in0=a_bc, scalar1=-1.0, scalar2=None, op0=ALU.mult),
          reads=[b_abc], writes=[b_abc])
        wada_v = wada_d.rearrange("(k p) c -> p k c", p=128)
        for ci in range(6):
            r = ci % 2
            E("sp", lambda e, ci=ci, r=r: e.dma_start(out=wa[r].rearrange("p (k c) -> p k c", k=8),
                                                     in_=wada_v[:, :, ci * 512:(ci + 1) * 512]),
              writes=[b_wa[r]], dma=True)
            if ci < 4:
                for mm in range(4):
                    m = ci * 4 + mm
                    for k in range(8):
                        E("pe", lambda e, r=r, mm=mm, m=m, k=k: e.matmul(
                            bank(0)[:, m:m + 1], lhsT=wa[r][:, k * 512 + mm * 128:k * 512 + (mm + 1) * 128],
                            rhs=cact[:, k:k + 1], start=(k == 0), stop=(k == 7)),
                          reads=[b_wa[r], b_cact], pwrites=[PB[0]])
            else:
                n = ci - 4
                for k in range(8):
                    E("pe", lambda e, r=r, n=n, k=k: e.matmul(
                        bank(1 + n), lhsT=cact_rep[:, k * 128:(k + 1) * 128],
                        rhs=wa[r][:, k * 512:(k + 1) * 512], start=(k == 0), stop=(k == 7)),
                      reads=[b_wa[r], b_crep], pwrites=[PB[1 + n]])
        E("dve", lambda e: e.tensor_tensor(out=modT[:, 0:16], in0=bank(0)[:, 0:16], in1=pcol[:, PC_BADA:PC_BADA + 16],
                                           op=ALU.add), reads=[PB[0], b_pcol], writes=[b_modT])
        E("dve", lambda e: e.scalar_tensor_tensor(out=g1, in0=modT[:, 8:16], scalar=1.0, in1=pcol[:, PC_GIN:PC_GIN + 8],
                                                  op0=ALU.add, op1=ALU.mult), reads=[b_modT, b_pcol], writes=[b_g1])
        E("dve", lambda e: e.tensor_tensor(out=gate_bc, in0=psum[:, 512:1536], in1=prow[:, PR_BG:PR_BG + 1024], op=ALU.add),
          reads=[PB[1], PB[2], b_prow], writes=[b_gate])
        shiftc = modT
        S_.barrier()
        A.top = p0_mark

        xt = [A.f32(4096) for _ in range(2)]; b_xt = [Buf("xt%d" % i) for i in range(2)]
        hT = [A.bf16(8 * 512) for _ in range(2)]; b_hT = [Buf("hT%d" % i) for i in range(2)]
        junk = A.bf16(1024); b_junk = Buf("junk")
        ssq1 = A.f32(4 * NT); b_ssq1 = Buf("ssq1")
        rs1 = A.f32(4 * NT); b_rs1 = Buf("rs1")
        stq = A.bf16(8 * 512); b_stq = Buf("stq")
        stk = A.bf16(8 * 512); b_stk = Buf("stk")
        stz = A.bf16(8 * 512); b_stz = Buf("stz")
        stv = A.bf16(4 * 1024); b_stv = Buf("stv")
        stages = [(stq, b_stq, QT_d), (stk, b_stk, KT_d), (stz, b_stz, ZA_d)]
        pcnt = [0]

        def next_bank(lo, n):
            i = lo + (pcnt[0] % n)
            pcnt[0] += 1
            return i

        evq = [0]
        def p1a_front_a(tt):
            r = tt % 2
            E("sp", lambda e, tt=tt, r=r: e.dma_start(
                out=xt[r].rearrange("p (j d) -> p j d", j=4),
                in_=x_d[tt * 512:(tt + 1) * 512, :].rearrange("(j p) d -> p j d", p=128)),
              writes=[b_xt[r]], dma=True)
            for j in range(4):
                E("act", lambda e, r=r, j=j, tt=tt: e.activation(
                    out=junk, in_=xt[r][:, j * 1024:(j + 1) * 1024], func=AF.Square,
                    accum_out=ssq1[:, tt * 4 + j:tt * 4 + j + 1]),
                  reads=[b_xt[r]], writes=[b_junk], pwrites=[b_ssq1])
            E("act", lambda e, tt=tt: e.activation(out=rs1[:, tt * 4:tt * 4 + 4], in_=ssq1[:, tt * 4:tt * 4 + 4],
                                                   func=AF.Ln, scale=1.0 / 1024, bias=EPS),
              reads=[b_ssq1], pwrites=[b_rs1])
            E("act", lambda e, tt=tt: e.activation(out=rs1[:, tt * 4:tt * 4 + 4], in_=rs1[:, tt * 4:tt * 4 + 4],
                                                   func=AF.Exp, scale=-0.5),
              reads=[b_rs1], pwrites=[b_rs1])
            for j in range(4):
                E("dve", lambda e, r=r, j=j, tt=tt: e.tensor_scalar(
                    out=xt[r][:, j * 1024:(j + 1) * 1024], in0=xt[r][:, j * 1024:(j + 1) * 1024],
                    scalar1=rs1[:, tt * 4 + j:tt * 4 + j + 1], scalar2=None, op0=ALU.mult),
                  reads=[b_rs1, b_xt[r]], writes=[b_xt[r]])

        def p1a_front_b(tt):
            r = tt % 2
            for k in range(8):
                bi = next_bank(0, 4)
                for j in range(4):
                    E("pe", lambda e, r=r, j=j, k=k, bi=bi: e.transpose(
                        bank(bi)[:, j * 128:(j + 1) * 128], xt[r][:, j * 1024 + k * 128:j * 1024 + (k + 1) * 128], idf),
                      reads=[b_xt[r], b_cf], writes=[PB[bi]] if j == 0 else (), pwrites=() if j == 0 else [PB[bi]])
                if k % 2 == 0:
                    E("dve", lambda e, r=r, k=k, bi=bi: e.tensor_scalar(
                        out=hT[r][:, k * 512:(k + 1) * 512], in0=bank(bi), scalar1=g1[:, k:k + 1],
                        scalar2=shiftc[:, k:k + 1], op0=ALU.mult, op1=ALU.add),
                      reads=[PB[bi], b_g1, b_modT], pwrites=[b_hT[r]] if k else (), writes=() if k else [b_hT[r]])
                else:
                    E("act", lambda e, r=r, k=k, bi=bi: e.activation(
                        out=hT[r][:, k * 512:(k + 1) * 512], in_=bank(bi), func=AF.Identity,
                        scale=g1[:, k:k + 1], bias=shiftc[:, k:k + 1]),
                      reads=[PB[bi], b_g1, b_modT], pwrites=[b_hT[r]])
            E("pool", lambda e, r=r, tt=tt: e.dma_start(
                out=HT_d[:, :, tt * 512:(tt + 1) * 512].rearrange("k p t -> p k t"),
                in_=hT[r].rearrange("p (k t) -> p k t", k=8)), reads=[b_hT[r]], dma=True, key="hTo%d" % r)

        def p1a_back1(tt):
            r = tt % 2
            for grp in range(3):
                stg, b_stg, dst = stages[grp]
                c_base = [0, 1024, 3072][grp]
                for m in range(8):
                    bi = next_bank(4, 4)
                    for k in range(8):
                        E("pe", lambda e, r=r, k=k, bi=bi, c0=c_base + m * 128: e.matmul(
                            bank(bi), lhsT=Wa[:, k * 4096 + c0:k * 4096 + c0 + 128],
                            rhs=hT[r][:, k * 512:(k + 1) * 512], start=(k == 0), stop=(k == 7)),
                          reads=[b_Wa, b_hT[r]], writes=[PB[bi]] if k == 0 else (), pwrites=() if k == 0 else [PB[bi]])
                    wkw = dict(writes=[b_stg]) if m == 0 else dict(pwrites=[b_stg])
                    osl = stg[:, m * 512:(m + 1) * 512]
                    if grp == 0:
                        if evq[0] % 2 == 0:
                            E("act", lambda e, osl=osl, bi=bi: e.mul(out=osl, in_=bank(bi), mul=0.125),
                              reads=[PB[bi]], **wkw)
                        else:
                            E("dve", lambda e, osl=osl, bi=bi: e.tensor_scalar(out=osl, in0=bank(bi), scalar1=0.125,
                                                                               scalar2=None, op0=ALU.mult),
                              reads=[PB[bi]], **wkw)
                        evq[0] += 1
                    elif grp == 1:
                        if evq[0] % 2 == 0:
                            E("act", lambda e, osl=osl, bi=bi: e.copy(out=osl, in_=bank(bi)), reads=[PB[bi]], **wkw)
                        else:
                            E("dve", lambda e, osl=osl, bi=bi: e.tensor_copy(out=osl, in_=bank(bi)), reads=[PB[bi]], **wkw)
                        evq[0] += 1
                    else:
                        E("act", lambda e, osl=osl, bi=bi: e.activation(out=osl, in_=bank(bi), func=AF.Silu),
                          reads=[PB[bi]], **wkw)
                E("pool", lambda e, stg=stg, dst=dst, tt=tt: e.dma_start(
                    out=dst[:, tt * 512:(tt + 1) * 512].rearrange("(m p) t -> p m t", p=128),
                    in_=stg.rearrange("p (m t) -> p m t", m=8)), reads=[b_stg], dma=True)

        def p1a_back2(tt):
            r = tt % 2
            for j in range(4):
                for n in range(2):
                    bi = next_bank(4, 4)
                    for k in range(8):
                        E("pe", lambda e, r=r, k=k, bi=bi, j=j, n=n: e.matmul(
                            bank(bi), lhsT=hT[r][:, k * 512 + j * 128:k * 512 + (j + 1) * 128],
                            rhs=Wa[:, k * 4096 + 2048 + n * 512:k * 4096 + 2048 + (n + 1) * 512],
                            start=(k == 0), stop=(k == 7)),
                          reads=[b_Wa, b_hT[r]], writes=[PB[bi]] if k == 0 else (), pwrites=() if k == 0 else [PB[bi]])
                    wkw = dict(writes=[b_stv]) if (j == 0 and n == 0) else dict(pwrites=[b_stv])
                    osl = stv[:, j * 1024 + n * 512:j * 1024 + (n + 1) * 512]
                    if evq[0] % 2 == 0:
                        E("act", lambda e, osl=osl, bi=bi: e.copy(out=osl, in_=bank(bi)), reads=[PB[bi]], **wkw)
                    else:
                        E("dve", lambda e, osl=osl, bi=bi: e.tensor_copy(out=osl, in_=bank(bi)), reads=[PB[bi]], **wkw)
                    evq[0] += 1
            E("pool", lambda e, tt=tt: e.dma_start(
                out=V_d[tt * 512:(tt + 1) * 512, :].rearrange("(j p) c -> p j c", p=128),
                in_=stv.rearrange("p (j c) -> p j c", j=4)), reads=[b_stv], dma=True)

        p1a_front_a(0)
        p1a_front_b(0)
        for tt in range(NT):
            if tt + 1 < NT:
                p1a_front_a(tt + 1)
            p1a_back1(tt)
            if tt + 1 < NT:
                p1a_front_b(tt + 1)
            p1a_back2(tt)
        S_.barrier()
        A.top = base_mark

        NWS = 2576
        Ws = A.bf16(8 * NWS); b_Ws = Buf("Ws")
        Ws3 = Ws.rearrange("p (k c) -> p k c", k=8)
        for ci in range(6):
            c0 = ci * 512
            c1 = min(NWS, c0 + 512)
            E("pool", lambda e, c0=c0, c1=c1: e.dma_start(out=Ws3[:, :, c0:c1], in_=win_v[:, :, 4096 + c0:4096 + c1]),
              pwrites=[b_Ws], dma=True, key="Ws")
        hSr = [A.bf16(8 * 512) for _ in range(2)]; b_hSr = [Buf("hS%d" % i) for i in range(2)]
        XW_ = 515
        xin = A.f32(12 * XW_); b_xin = [Buf("xin%d" % m) for m in range(12)]
        cv = A.f32(12 * 512); b_cv = [Buf("cv%d" % m) for m in range(12)]
        BCs = A.bf16(4 * 512); b_BCs = Buf("BCs")
        ZS = A.bf16(8 * 512); b_ZS = Buf("ZSst")
        dtv = A.f32(64); b_dtv = Buf("dtv")
        dts = A.f32(64); b_dts = Buf("dts")
        lds = A.f32(64); b_lds = Buf("lds")
        cw = pcol[:, PC_CW:PC_CW + 48]
        cbias = pcol[:, PC_CB:PC_CB + 12]
        E("dve", lambda e: e.memset(xin, 0.0), writes=b_xin)
        for tt in range(NT):
            hS = hSr[tt % 2]; b_hS = b_hSr[tt % 2]
            E("sp", lambda e, tt=tt, hS=hS: e.dma_start(
                out=hS.rearrange("p (k t) -> p k t", k=8),
                in_=HT_d[:, :, tt * 512:(tt + 1) * 512].rearrange("k p t -> p k t")), writes=[b_hS], dma=True)
            for m in range(12):
                if tt > 0:
                    E("dve", lambda e, m=m: e.tensor_copy(out=xin[:, m * XW_:m * XW_ + 3], in_=xin[:, m * XW_ + 512:m * XW_ + 515]),
                      reads=[b_xin[m]], writes=[b_xin[m]])
                bi = next_bank(0, 4)
                for k in range(8):
                    E("pe", lambda e, k=k, bi=bi, c0=m * 128, hS=hS: e.matmul(
                        bank(bi), lhsT=Ws[:, k * NWS + c0:k * NWS + c0 + 128], rhs=hS[:, k * 512:(k + 1) * 512],
                        start=(k == 0), stop=(k == 7)),
                      reads=[b_Ws, b_hS], writes=[PB[bi]] if k == 0 else (), pwrites=() if k == 0 else [PB[bi]])
                E("act", lambda e, m=m, bi=bi: e.copy(out=xin[:, m * XW_ + 3:m * XW_ + 515], in_=bank(bi)),
                  reads=[PB[bi]], writes=[b_xin[m]])
                acc = cv[:, m * 512:(m + 1) * 512]
                E("dve", lambda e, m=m, acc=acc: e.tensor_scalar(
                    out=acc, in0=xin[:, m * XW_ + 3:m * XW_ + 515], scalar1=cw[:, m * 4 + 3:m * 4 + 4],
                    scalar2=cbias[:, m:m + 1], op0=ALU.mult, op1=ALU.add),
                  reads=[b_xin[m], b_pcol], writes=[b_cv[m]])
                for kk in range(3):
                    E("dve", lambda e, m=m, acc=acc, kk=kk: e.scalar_tensor_tensor(
                        out=acc, in0=xin[:, m * XW_ + kk:m * XW_ + kk + 512], scalar=cw[:, m * 4 + kk:m * 4 + kk + 1],
                        in1=acc, op0=ALU.mult, op1=ALU.add),
                      reads=[b_xin[m], b_pcol, b_cv[m]], writes=[b_cv[m]])
                if m < 8:
                    E("act", lambda e, acc=acc: e.activation(out=acc, in_=acc, func=AF.Silu),
                      reads=[b_cv[m]], writes=[b_cv[m]])
                else:
                    dst = BCs[:, (m - 8) * 512:(m - 7) * 512]
                    E("act", lambda e, acc=acc, dst=dst: e.activation(out=dst, in_=acc, func=AF.Silu),
                      reads=[b_cv[m]], writes=[b_BCs] if m == 8 else (), pwrites=() if m == 8 else [b_BCs])
            E("pool", lambda e, tt=tt: e.dma_start(
                out=XS_d[:, tt * 512:(tt + 1) * 512].rearrange("(m p) t -> p m t", p=128),
                in_=cv[:, 0:8 * 512].rearrange("p (m t) -> p m t", m=8)), reads=b_cv[0:8], dma=True, key="cvo")
            E("pool", lambda e, tt=tt: e.dma_start(
                out=BC_d[:, tt * 512:(tt + 1) * 512].rearrange("(m p) t -> p m t", p=128),
                in_=BCs.rearrange("p (m t) -> p m t", m=4)), reads=[b_BCs], dma=True, key="bco")
            for m in range(8):
                bi = next_bank(0, 4)
                for k in range(8):
                    E("pe", lambda e, k=k, bi=bi, c0=1552 + m * 128, hS=hS: e.matmul(
                        bank(bi), lhsT=Ws[:, k * NWS + c0:k * NWS + c0 + 128], rhs=hS[:, k * 512:(k + 1) * 512],
                        start=(k == 0), stop=(k == 7)),
                      reads=[b_Ws, b_hS], writes=[PB[bi]] if k == 0 else (), pwrites=() if k == 0 else [PB[bi]])
                E("act", lambda e, m=m, bi=bi: e.activation(out=ZS[:, m * 512:(m + 1) * 512], in_=bank(bi), func=AF.Silu),
                  reads=[PB[bi]], writes=[b_ZS] if m == 0 else (), pwrites=() if m == 0 else [b_ZS])
            E("pool", lambda e, tt=tt: e.dma_start(
                out=ZS_d[:, tt * 512:(tt + 1) * 512].rearrange("(m p) t -> p m t", p=128),
                in_=ZS.rearrange("p (m t) -> p m t", m=8)), reads=[b_ZS], dma=True, key="zso")
            for j in range(4):
                for k in range(8):
                    E("pe", lambda e, j=j, k=k, hS=hS: e.matmul(
                        bank(7)[:, j * 16:(j + 1) * 16], lhsT=hS[:, k * 512 + j * 128:k * 512 + (j + 1) * 128],
                        rhs=Ws[:, k * NWS + 1536:k * NWS + 1552], start=(k == 0), stop=(k == 7)),
                      reads=[b_Ws, b_hS], writes=[PB[7]] if (j == 0 and k == 0) else (),
                      pwrites=() if (j == 0 and k == 0) else [PB[7]])
            E("dve", lambda e: e.tensor_tensor(
                out=dtv.rearrange("p (j h) -> p j h", j=4), in0=bank(7)[:, 0:64].rearrange("p (j h) -> p j h", j=4),
                in1=prow[:, PR_DTB:PR_DTB + 16].unsqueeze(1).to_broadcast([128, 4, 16]), op=ALU.add),
              reads=[PB[7], b_prow], writes=[b_dtv])
            E("act", lambda e: e.activation(out=dtv, in_=dtv, func=AF.Exp), reads=[b_dtv], writes=[b_dtv])
            E("act", lambda e: e.activation(out=dts, in_=dtv, func=AF.Ln, bias=1.0), reads=[b_dtv], writes=[b_dts])
            E("dve", lambda e: e.tensor_tensor(
                out=lds.rearrange("p (j h) -> p j h", j=4), in0=dts.rearrange("p (j h) -> p j h", j=4),
                in1=a_bc.unsqueeze(1).to_broadcast([128, 4, 16]), op=ALU.mult),
              reads=[b_dts, b_abc], writes=[b_lds])
            E("pool", lambda e, tt=tt: e.dma_start(
                out=DT_d[tt * 512:(tt + 1) * 512, 0:16].rearrange("(j p) h -> p j h", p=128),
                in_=dts.rearrange("p (j h) -> p j h", j=4)), reads=[b_dts], dma=True, key="dto")
            E("pool", lambda e, tt=tt: e.dma_start(
                out=DT_d[tt * 512:(tt + 1) * 512, 16:32].rearrange("(j p) h -> p j h", p=128),
                in_=lds.rearrange("p (j h) -> p j h", j=4)), reads=[b_lds], dma=True, key="ldo")
        S_.barrier()
        A.top = base_mark

        NB = NCH
        QT = [A.bf16(S) for _ in range(1)] * 2; KT = [A.bf16(S) for _ in range(1)] * 2
        Vall = A.bf16(NB * 1024); b_Vall = Buf("Vall")
        ZAq = [A.bf16(512) for _ in range(2)]; b_ZAq = [Buf("ZAq%d" % i) for i in range(2)]
        b_QT = [Buf("QT0")] * 2; b_KT = [Buf("KT0")] * 2
        NE, NL, NX, NWB = 3, 4, 2, 3
        eb = [A.f32(1024) for _ in range(NE)]; b_eb = [Buf("e%d" % i) for i in range(NE)]
        lb = [A.bf16(1024) for _ in range(NL)]; b_lb = [Buf("l%d" % i) for i in range(NL)]
        xb = [A.f32(1024) for _ in range(NX)]; b_xb = [Buf("xx%d" % i) for i in range(NX)]
        wb = [A.bf16(1024) for _ in range(NWB)]; b_wb = [Buf("w%d" % i) for i in range(NWB)]
        sqa = A.f32(512); b_sqa = Buf("sqa")
        ats = [A.bf16(512) for _ in range(2)]; b_ats = [Buf("ats%d" % i) for i in range(2)]
        atq = [0]

        SB7 = 7
        b7 = bank(SB7)
        xsI = [A.f32(1024) for _ in range(2)]; b_xsI = [Buf("xsI%d" % i) for i in range(2)]
        bcI = [A.bf16(512) for _ in range(2)]; b_bcI = [Buf("bcI%d" % i) for i in range(2)]
        zsI = [A.bf16(1024) for _ in range(2)]; b_zsI = [Buf("zsI%d" % i) for i in range(2)]
        dtI = [A.f32(32) for _ in range(2)]; b_dtI = [Buf("dtI%d" % i) for i in range(2)]
        Rr = A.f32(16 * 128); b_R = Buf("R")
        decay = A.f32(16 * 128); b_decay = Buf("decay")
        Mm = A.bf16(16 * 128); b_M = Buf("M")
        CE = A.bf16(16 * 128); b_CE = Buf("CE")
        xtf = A.f32(1024); b_xtf = Buf("xtf")
        xtb = A.bf16(1024); b_xtb = Buf("xtb")
        Btk = A.bf16(256); b_Btk = Buf("Btk")
        XWt = A.bf16(1024); b_XW = Buf("XW")
        cbs = A.f32(256); b_cbs = Buf("cbs")
        dcd = A.f32(32); b_dcd = Buf("dcd")
        wgt = A.f32(16); b_wgt = Buf("wgt")
        state = A.f32(1024); b_state = Buf("state")
        stateb = A.bf16(1024); b_stateb = Buf("stateb")
        stsT = A.bf16(8 * 512); b_stsT = Buf("stsT")
        ub = Rr[:, 0:1024]
        sqs = Rr[:, 1024:2048]
        E("dve", lambda e: e.memset(state, 0.0), writes=[b_state])
        E("dve", lambda e: e.memset(stateb, 0.0), writes=[b_stateb])

        def ssd_load(c):
            sl = c % 2
            E("sp", lambda e: e.dma_start(out=xsI[sl].rearrange("p (m t) -> p m t", m=8),
                                          in_=XS_d[:, c * 128:(c + 1) * 128].rearrange("(m p) t -> p m t", p=128)),
              writes=[b_xsI[sl]], dma=True)
            E("sp", lambda e: e.dma_start(out=bcI[sl].rearrange("p (m t) -> p m t", m=4),
                                          in_=BC_d[:, c * 128:(c + 1) * 128].rearrange("(m p) t -> p m t", p=128)),
              writes=[b_bcI[sl]], dma=True)
            E("sp", lambda e: e.dma_start(out=zsI[sl].rearrange("p (m t) -> p m t", m=8),
                                          in_=ZS_d[:, c * 128:(c + 1) * 128].rearrange("(m p) t -> p m t", p=128)),
              writes=[b_zsI[sl]], dma=True)
            E("sp", lambda e: e.dma_start(out=dtI[sl], in_=DT_d[c * 128:(c + 1) * 128, :]), writes=[b_dtI[sl]], dma=True)

        def ssd_chunk(c):
            sl = c % 2
            xsT, bc_, zs_, dt_ = xsI[sl], bcI[sl], zsI[sl], dtI[sl]
            bx, bb, bz, bd = b_xsI[sl], b_bcI[sl], b_zsI[sl], b_dtI[sl]
            dtj = dt_[:, 0:16]
            ldj = dt_[:, 16:32]
            jq = c % 4
            if c + 1 < NCH:
                ssd_load(c + 1)
            R3 = Rr.rearrange("p (h l) -> p h l", h=16)

            def m_ops(q):
                for h in range(q * 4, q * 4 + 4):
                    g = h // 8
                    E("dve", lambda e, h=h, g=g: e.scalar_tensor_tensor(
                        out=Mm[:, h * 128:(h + 1) * 128], in0=decay[:, h * 128:(h + 1) * 128], scalar=dtj[:, h:h + 1],
                        in1=cbs[:, g * 128:(g + 1) * 128], op0=ALU.mult, op1=ALU.mult),
                      reads=[b_decay, bd, b_cbs], writes=[b_M] if h == 0 else (), pwrites=() if h == 0 else [b_M])

            def ce_op(g):
                E("dve", lambda e: e.tensor_tensor(
                    out=CE[:, g * 1024:(g + 1) * 1024].rearrange("p (h l) -> p h l", h=8),
                    in0=decay[:, g * 1024:(g + 1) * 1024].rearrange("p (h l) -> p h l", h=8),
                    in1=bc_[:, 256 + g * 128:256 + (g + 1) * 128].unsqueeze(1).to_broadcast([128, 8, 128]), op=ALU.mult),
                  reads=[b_decay, bb], writes=[b_CE] if g == 0 else (), pwrites=() if g == 0 else [b_CE])

            for hf in range(2):
                E("dve", lambda e, hf=hf: e.tensor_tensor(
                    out=R3[:, hf * 8:(hf + 1) * 8, :], in0=LEf.unsqueeze(1).to_broadcast([128, 8, 128]),
                    in1=ldj[:, hf * 8:(hf + 1) * 8].unsqueeze(2).to_broadcast([128, 8, 128]), op=ALU.mult),
                  reads=[b_cf, bd], writes=[b_R] if hf == 0 else (), pwrites=() if hf == 0 else [b_R])
                for m4 in range(4):
                    m = hf * 4 + m4
                    E("pe", lambda e, m=m, m4=m4: e.transpose(b7[:, m4 * 128:(m4 + 1) * 128], xsT[:, m * 128:(m + 1) * 128], idf),
                      reads=[bx, b_cf], writes=[PB[SB7]] if m4 == 0 else (), pwrites=() if m4 == 0 else [PB[SB7]])
                E("dve", lambda e, hf=hf: e.tensor_copy(out=xtf[:, hf * 512:(hf + 1) * 512], in_=b7), reads=[PB[SB7]],
                  writes=[b_xtf] if hf == 0 else (), pwrites=() if hf == 0 else [b_xtf])
                yield
            E("dve", lambda e: e.tensor_copy(out=xtb, in_=xtf), reads=[b_xtf], writes=[b_xtb])
            b7b = b7[:, 0:128].bitcast(BF16)
            for g in range(2):
                E("pe", lambda e, g=g: e.transpose(b7b[:, g * 128:(g + 1) * 128], bc_[:, g * 128:(g + 1) * 128], idb),
                  reads=[bb, b_cb], writes=[PB[SB7]] if g == 0 else (), pwrites=() if g == 0 else [PB[SB7]])
            for g in range(2):
                E("pe", lambda e, g=g: e.matmul(b7[:, 256 + g * 128:256 + (g + 1) * 128], lhsT=bc_[:, g * 128:(g + 1) * 128],
                                                rhs=bc_[:, 256 + g * 128:256 + (g + 1) * 128], start=True, stop=True),
                  reads=[bb], pwrites=[PB[SB7]])
            E("dve", lambda e: e.tensor_copy(out=Btk, in_=b7b), reads=[PB[SB7]], writes=[b_Btk])
            E("dve", lambda e: e.tensor_copy(out=cbs, in_=b7[:, 256:512]), reads=[PB[SB7]], writes=[b_cbs])
            yield
            for q in range(4):
                E("pe", lambda e, q=q: e.matmul(b7, lhsT=GTf, rhs=Rr[:, q * 512:(q + 1) * 512], start=True, stop=False,
                                                skip_group_check=True), reads=[b_R, b_cf], writes=[PB[SB7]])
                E("pe", lambda e: e.matmul(b7, lhsT=idb, rhs=neg4, start=False, stop=True, skip_group_check=True),
                  reads=[b_cb, b_neg4], pwrites=[PB[SB7]])
                E("act", lambda e, q=q: e.activation(out=decay[:, q * 512:(q + 1) * 512], in_=b7, func=AF.Exp),
                  reads=[PB[SB7]], writes=[b_decay] if q == 0 else (), pwrites=() if q == 0 else [b_decay])
                if q >= 1:
                    m_ops(q - 1)
                yield
            E("pe", lambda e: e.matmul(b7[:, 0:16], lhsT=GTf, rhs=ldj, start=True, stop=True),
              reads=[bd, b_cf], writes=[PB[SB7]])
            E("pe", lambda e: e.matmul(b7[:, 16:32], lhsT=onesf, rhs=ldj, start=True, stop=True),
              reads=[bd, b_cf], pwrites=[PB[SB7]])
            E("act", lambda e: e.activation(out=dcd, in_=b7[:, 0:32], func=AF.Exp), reads=[PB[SB7]], writes=[b_dcd])
            m_ops(3)
            yield
            for q in range(4):
                E("pe", lambda e, q=q: e.matmul(b7, lhsT=onesf, rhs=Rr[:, q * 512:(q + 1) * 512], start=True, stop=True),
                  reads=[b_R, b_cf], writes=[PB[SB7]])
                E("act", lambda e, q=q: e.activation(out=decay[:, q * 512:(q + 1) * 512], in_=b7, func=AF.Exp),
                  reads=[PB[SB7]], writes=[b_decay] if q == 0 else (), pwrites=() if q == 0 else [b_decay])
                if q == 0:
                    E("dve", lambda e: e.tensor_tensor(out=wgt, in0=dtj, in1=dcd[:, 0:16], op=ALU.mult),
                      reads=[bd, b_dcd], writes=[b_wgt])
                    E("dve", lambda e: e.tensor_tensor(
                        out=XWt.rearrange("p (h q) -> p h q", h=16), in0=xtf.rearrange("p (h q) -> p h q", h=16),
                        in1=wgt.unsqueeze(2).to_broadcast([128, 16, 64]), op=ALU.mult),
                      reads=[b_xtf, b_wgt], writes=[b_XW])
                if q == 2:
                    ce_op(0)
                yield
            ce_op(1)
            yield
            for hb in range(2):
                for p4 in range(4):
                    pr = hb * 4 + p4
                    csl = slice(p4 * 128, (p4 + 1) * 128)
                    for half in range(2):
                        h = 2 * pr + half
                        first = (p4 == 0 and half == 0)
                        E("pe", lambda e, csl=csl, half=half, h=h: e.matmul(
                            b7[64 * half:64 * half + 64, csl], lhsT=xtb[:, h * 64:(h + 1) * 64],
                            rhs=Mm[:, h * 128:(h + 1) * 128], start=True, stop=False, skip_group_check=True),
                          reads=[b_xtb, b_M], writes=[PB[SB7]] if first else (), pwrites=() if first else [PB[SB7]])
                        E("pe", lambda e, csl=csl, half=half, h=h: e.matmul(
                            b7[64 * half:64 * half + 64, csl], lhsT=stateb[:, h * 64:(h + 1) * 64],
                            rhs=CE[:, h * 128:(h + 1) * 128], start=False, stop=True, skip_group_check=True),
                          reads=[b_stateb, b_CE], pwrites=[PB[SB7]])
                usl = ub[:, hb * 512:(hb + 1) * 512]
                u3 = usl.rearrange("p (m t) -> p m t", m=4)
                E("dve", lambda e, hb=hb, u3=u3: e.tensor_tensor(
                    out=u3, in0=xsT[:, hb * 512:(hb + 1) * 512].rearrange("p (m t) -> p m t", m=4),
                    in1=pcol[:, PC_D + hb * 4:PC_D + hb * 4 + 4].unsqueeze(2).to_broadcast([128, 4, 128]), op=ALU.mult),
                  reads=[bx, b_pcol], writes=[b_R] if hb == 0 else (), pwrites=() if hb == 0 else [b_R])
                E("dve", lambda e, usl=usl: e.tensor_tensor(out=usl, in0=usl, in1=b7, op=ALU.add),
                  reads=[b_R, PB[SB7]], pwrites=[b_R])
                yield
                E("dve", lambda e, hb=hb, usl=usl: e.tensor_tensor(out=usl, in0=usl, in1=zs_[:, hb * 512:(hb + 1) * 512],
                                                                  op=ALU.mult), reads=[b_R, bz], pwrites=[b_R])
                o3 = stsT.rearrange("p (m t) -> p m t", m=8)[:, hb * 4:hb * 4 + 4, jq * 128:(jq + 1) * 128]
                E("dve", lambda e, hb=hb, u3=u3, o3=o3: e.tensor_tensor(
                    out=o3, in0=u3,
                    in1=pcol[:, PC_GS + hb * 4:PC_GS + hb * 4 + 4].unsqueeze(2).to_broadcast([128, 4, 128]), op=ALU.mult),
                  reads=[b_R, b_pcol], pwrites=[b_stsT])
                yield
            E("dve", lambda e: e.tensor_tensor(out=sqs, in0=ub, in1=ub, op=ALU.mult), reads=[b_R], pwrites=[b_R])
            yield
            for pr in range(8):
                E("pe", lambda e, pr=pr: e.matmul(b7[:, 0:1], lhsT=sqs[:, pr * 128:(pr + 1) * 128], rhs=onesf[:, 0:1],
                                                 start=(pr == 0), stop=(pr == 7)),
                  reads=[b_R, b_cf], writes=[PB[SB7]] if pr == 0 else (), pwrites=() if pr == 0 else [PB[SB7]])
            E("dve", lambda e: e.tensor_copy(out=ssq_s[:, c:c + 1], in_=b7[:, 0:1]), reads=[PB[SB7]], pwrites=[b_ssqs])
            yield
            for g in range(2):
                E("pe", lambda e, g=g: e.matmul(b7, lhsT=Btk[:, g * 128:(g + 1) * 128], rhs=XWt[:, g * 512:(g + 1) * 512],
                                                start=True, stop=True), reads=[b_Btk, b_XW], writes=[PB[SB7]])
                ssl = state[:, g * 512:(g + 1) * 512]
                E("dve", lambda e, g=g, ssl=ssl: e.tensor_tensor(
                    out=ssl.rearrange("p (h q) -> p h q", h=8), in0=ssl.rearrange("p (h q) -> p h q", h=8),
                    in1=dcd[:, 16 + g * 8:16 + g * 8 + 8].unsqueeze(2).to_broadcast([128, 8, 64]), op=ALU.mult),
                  reads=[b_state, b_dcd], writes=[b_state] if g == 0 else (), pwrites=() if g == 0 else [b_state])
                E("dve", lambda e, ssl=ssl: e.tensor_tensor(out=ssl, in0=ssl, in1=b7, op=ALU.add),
                  reads=[b_state, PB[SB7]], pwrites=[b_state])
                yield
            E("dve", lambda e: e.tensor_copy(out=stateb, in_=state), reads=[b_state], writes=[b_stateb])
            if jq == 3:
                tt_ = c // 4
                E("sp", lambda e: e.dma_start(
                    out=AT_d[1024:2048, tt_ * 512:(tt_ + 1) * 512].rearrange("(m p) t -> p m t", p=128),
                    in_=stsT.rearrange("p (m t) -> p m t", m=8)), reads=[b_stsT], dma=True)
            yield

        def ssd_all():
            for c in range(NCH):
                for _ in ssd_chunk(c):
                    yield

        ssd_load(0)
        ssd_gen = ssd_all()

        def ssd_step():
            try:
                next(ssd_gen)
                return True
            except StopIteration:
                return False

        def v2(ap, c0):
            return ap.rearrange("p (b c) -> p b c", b=2)[:, :, c0:512]

        def load_hp(hp):
            r = hp % 2
            E("sp", lambda e: e.dma_start(out=QT[r], in_=QT_d[hp * 128:(hp + 1) * 128, :]), writes=[b_QT[r]], dma=True)
            E("sp", lambda e: e.dma_start(out=KT[r], in_=KT_d[hp * 128:(hp + 1) * 128, :]), writes=[b_KT[r]], dma=True)

        Vall3 = Vall.rearrange("p (n c) -> p n c", c=1024)
        V_v = V_d.rearrange("(n p) c -> p n c", p=128)
        nvq = max(1, NB // 8)
        for vq in range(0, NB, nvq):
            E("sp", lambda e, vq=vq: e.dma_start(out=Vall3[:, vq:vq + nvq, :], in_=V_v[:, vq:vq + nvq, :]),
              pwrites=[b_Vall], dma=True, key="Vall")
        load_hp(0)
        ucount = [0]

        def do_hp(hp, r):
            if hp > 0:
                load_hp(hp)
            steps = []
            for qt in range(NT):
                for jj in range(4 * qt + 3, -1, -1):
                    steps.append((qt, jj))
            nu = len(steps)
            info = {}

            def st_qk(n):
                qt, jj = steps[n]
                u = ucount[0] + n
                kk = jj - 4 * qt
                c0 = 128 * kk if kk > 0 else 0
                ap_ = 2 * (u % 2)
                info[n] = (qt, jj, u, kk, c0, ap_)
                if jj == 4 * qt + 3:
                    zi = (hp * NT + qt) % 2
                    E("sp", lambda e, zi=zi: e.dma_start(out=ZAq[zi], in_=ZA_d[hp * 128:(hp + 1) * 128, qt * 512:(qt + 1) * 512]),
                      writes=[b_ZAq[zi]], dma=True)
                for half in range(2):
                    pb = 64 * half
                    E("pe", lambda e, pb=pb, half=half: e.matmul(
                        bank(ap_ + half)[:, c0:512], lhsT=KT[r][pb:pb + 64, jj * 128:(jj + 1) * 128],
                        rhs=QT[r][pb:pb + 64, qt * 512 + c0:(qt + 1) * 512], start=True, stop=True),
                      reads=[b_KT[r], b_QT[r]], writes=[PB[ap_ + half]])

            def st_act1(n):
                qt, jj, u, kk, c0, ap_ = info[n]
                ei, li = u % NE, u % NL
                E("act", lambda e: e.activation(out=v2(eb[ei], c0), in_=v2(psum[:, ap_ * 512:(ap_ + 2) * 512], c0), func=AF.Exp),
                  reads=[PB[ap_], PB[ap_ + 1]], writes=[b_eb[ei]])
                if kk >= 0:
                    ev = eb[ei].rearrange("p (b c) -> p b c", b=2)[:, :, c0:c0 + 128]
                    E("pool", lambda e: e.tensor_tensor(out=ev, in0=ev, in1=LTf.unsqueeze(1).to_broadcast([128, 2, 128]),
                                                        op=ALU.mult), reads=[b_eb[ei], b_cf], writes=[b_eb[ei]])

            def st_act2(n):
                qt, jj, u, kk, c0, ap_ = info[n]
                ei, li = u % NE, u % NL
                E("act", lambda e: e.activation(out=v2(lb[li], c0), in_=v2(eb[ei], c0), func=AF.Ln, bias=1.0),
                  reads=[b_eb[ei]], writes=[b_lb[li]])

            def st_ge(n):
                qt, jj, u, kk, c0, ap_ = info[n]
                li, xi = u % NL, u % NX
                for half in range(2):
                    cbk = 4 + half
                    if jj == 4 * qt + 3:
                        E("pe", lambda e, cbk=cbk: e.matmul(bank(cbk), lhsT=zb[:, 0:128], rhs=zb, start=True, stop=False,
                                                            skip_group_check=True), reads=[b_zb], writes=[PB[cbk]])
                    E("pe", lambda e, cbk=cbk, half=half: e.matmul(
                        bank(cbk)[:, c0:512], lhsT=GEb, rhs=lb[li][:, half * 512 + c0:(half + 1) * 512],
                        start=False, stop=False, skip_group_check=True), reads=[b_lb[li], b_cb], writes=[PB[cbk]])
                E("act", lambda e: e.activation(out=v2(xb[xi], c0), in_=v2(psum[:, 2048:3072], c0), func=AF.Exp, scale=-1.0),
                  reads=[PB[4], PB[5]], writes=[b_xb[xi]])

            def st_lt(n):
                qt, jj, u, kk, c0, ap_ = info[n]
                if jj == 0:
                    return
                li = u % NL
                for half in range(2):
                    cbk = 4 + half
                    E("pe", lambda e, cbk=cbk, half=half: e.matmul(
                        bank(cbk)[:, c0:512], lhsT=LTb, rhs=lb[li][:, half * 512 + c0:(half + 1) * 512],
                        start=False, stop=False, skip_group_check=True), reads=[b_lb[li], b_cb], writes=[PB[cbk]])

            def st_w(n):
                qt, jj, u, kk, c0, ap_ = info[n]
                ei, xi, wi = u % NE, u % NX, u % NWB
                E("dve", lambda e: e.tensor_tensor(out=v2(wb[wi], c0), in0=v2(eb[ei], c0), in1=v2(xb[xi], c0), op=ALU.mult),
                  reads=[b_eb[ei], b_xb[xi]], writes=[b_wb[wi]])

            def st_pv(n):
                qt, jj, u, kk, c0, ap_ = info[n]
                wi = u % NWB
                ob = 6
                for half in range(2):
                    pb = 64 * half
                    hh = 2 * hp + half
                    if jj == 4 * qt + 3:
                        E("pe", lambda e, pb=pb: e.matmul(bank(ob)[pb:pb + 64, :], lhsT=zb[:, 0:64], rhs=zb, start=True,
                                                          stop=False, skip_group_check=True), reads=[b_zb],
                          writes=[PB[ob]] if half == 0 else (), pwrites=() if half == 0 else [PB[ob]])
                    E("pe", lambda e, pb=pb, half=half, hh=hh: e.matmul(
                        bank(ob)[pb:pb + 64, c0:512], lhsT=Vall[:, jj * 1024 + hh * 64:jj * 1024 + hh * 64 + 64],
                        rhs=wb[wi][:, half * 512 + c0:(half + 1) * 512], start=False, stop=(jj == 0), skip_group_check=True),
                      reads=[b_Vall, b_wb[wi]], pwrites=[PB[ob]])

            def st_epi(n):
                qt, jj, u, kk, c0, ap_ = info[n]
                ob = 6
                if jj == 0:
                    E("act", lambda e: e.activation(out=sqa, in_=bank(ob), func=AF.Square), reads=[PB[ob]], writes=[b_sqa])
                    for sub in range(4):
                        E("pe", lambda e, sub=sub: e.matmul(bank(0)[:, sub:sub + 1], lhsT=sqa[:, sub * 128:(sub + 1) * 128],
                                                            rhs=onesf[:, 0:1], start=True, stop=True),
                          reads=[b_sqa, b_cf], writes=[PB[0]] if sub == 0 else (), pwrites=() if sub == 0 else [PB[0]])
                    if hp == 0:
                        E("dve", lambda e: e.tensor_copy(out=ssq_a[:, 4 * qt:4 * qt + 4], in_=bank(0)[:, 0:4]),
                          reads=[PB[0]], pwrites=[b_ssqa])
                    else:
                        E("dve", lambda e: e.tensor_tensor(out=ssq_a[:, 4 * qt:4 * qt + 4], in0=ssq_a[:, 4 * qt:4 * qt + 4],
                                                           in1=bank(0)[:, 0:4], op=ALU.add),
                          reads=[PB[0], b_ssqa], writes=[b_ssqa])
                    ai = atq[0] % 2
                    atq[0] += 1
                    E("dve", lambda e: e.scalar_tensor_tensor(
                        out=ats[ai], in0=bank(ob), scalar=pcol[:, PC_GA + hp:PC_GA + hp + 1],
                        in1=ZAq[(hp * NT + qt) % 2], op0=ALU.mult, op1=ALU.mult),
                      reads=[PB[ob], b_pcol, b_ZAq[(hp * NT + qt) % 2]], writes=[b_ats[ai]])
                    E("sp", lambda e: e.dma_start(out=AT_d[hp * 128:(hp + 1) * 128, qt * 512:(qt + 1) * 512], in_=ats[ai]),
                      reads=[b_ats[ai]], dma=True)

            for n in range(nu + 3):
                if n < nu:
                    st_qk(n)
                if 0 <= n - 2 < nu:
                    st_lt(n - 2)
                if n < nu:
                    st_act1(n)
                if 0 <= n - 1 < nu:
                    st_ge(n - 1)
                if n < nu:
                    st_act2(n)
                if 0 <= n - 1 < nu:
                    st_w(n - 1)
                if 0 <= n - 2 < nu:
                    st_pv(n - 2)
                if 0 <= n - 2 < nu:
                    st_epi(n - 2)
                ssd_step()
            ucount[0] += nu

        for hp_ in range(8):
            do_hp(hp_, hp_ % 2)
        while ssd_step():
            pass
        S_.barrier()
        A.top = base_mark

        Wg = A.bf16(16 * 1024); b_Wg = Buf("Wg")
        wst = [A.f32(4096) for _ in range(2)]; b_wst = [Buf("wst%d" % i) for i in range(2)]
        wout_v = wout_d.rearrange("(m p) c -> p m c", p=128)
        for ch in range(4):
            rr = ch % 2
            E("sp", lambda e, ch=ch, rr=rr: e.dma_start(out=wst[rr].rearrange("p (m c) -> p m c", m=4),
                                                      in_=wout_v[:, ch * 4:(ch + 1) * 4, :]), writes=[b_wst[rr]], dma=True)
            E("dve" if ch % 2 == 0 else "pool", lambda e, ch=ch, rr=rr: e.tensor_tensor(
                out=Wg[:, ch * 4096:(ch + 1) * 4096].rearrange("p (m c) -> p m c", m=4),
                in0=wst[rr].rearrange("p (m c) -> p m c", m=4),
                in1=gate_bc.unsqueeze(1).to_broadcast([128, 4, 1024]), op=ALU.mult),
              reads=[b_wst[rr], b_gate], pwrites=[b_Wg])
        for (src, dst) in ((ssq_a, rstd_a), (ssq_s, rstd_s)):
            E("act", lambda e, src=src, dst=dst: e.activation(out=dst, in_=src, func=AF.Ln, scale=1.0 / 1024, bias=EPS),
              reads=[b_ssqa, b_ssqs], pwrites=[b_rstd])
            E("act", lambda e, dst=dst: e.activation(out=dst, in_=dst, func=AF.Exp, scale=-0.5),
              reads=[b_rstd], pwrites=[b_rstd])
        ATt = [A.bf16(16 * 512) for _ in range(2)]; b_ATt = [Buf("ATt%d" % i) for i in range(2)]
        xo = [A.f32(4096) for _ in range(2)]; b_xo = [Buf("xo%d" % i) for i in range(2)]
        yb = [A.f32(1024) for _ in range(2)]; b_yb = [Buf("yb%d" % i) for i in range(2)]
        junk3 = A.bf16(1024); b_junk3 = Buf("junk3")
        b_out = Buf("outd")
        yq = [0]
        gf_bc = prow[:, PR_GF:PR_GF + 1024]
        for tt in range(NT):
            r = tt % 2
            E("sp", lambda e, tt=tt, r=r: e.dma_start(
                out=ATt[r].rearrange("p (m t) -> p m t", m=16),
                in_=AT_d[:, tt * 512:(tt + 1) * 512].rearrange("(m p) t -> p m t", p=128)), writes=[b_ATt[r]], dma=True)
            E("sp", lambda e, tt=tt, r=r: e.dma_start(
                out=xo[r].rearrange("p (j d) -> p j d", j=4),
                in_=x_d[tt * 512:(tt + 1) * 512, :].rearrange("(j p) d -> p j d", p=128)), writes=[b_xo[r]], dma=True)
            for j in range(4):
                c = tt * 4 + j
                yi = yq[0] % 2
                yq[0] += 1
                for n in range(2):
                    ba = 4 * (c % 2) + 2 * n
                    bs = ba + 1
                    for m in range(8):
                        E("pe", lambda e, r=r, m=m, j=j, n=n, ba=ba: e.matmul(
                            bank(ba), lhsT=ATt[r][:, m * 512 + j * 128:m * 512 + (j + 1) * 128],
                            rhs=Wg[:, m * 1024 + n * 512:m * 1024 + (n + 1) * 512], start=(m == 0), stop=(m == 7)),
                          reads=[b_ATt[r], b_Wg], writes=[PB[ba]] if m == 0 else (), pwrites=() if m == 0 else [PB[ba]])
                    for m in range(8, 16):
                        E("pe", lambda e, r=r, m=m, j=j, n=n, bs=bs: e.matmul(
                            bank(bs), lhsT=ATt[r][:, m * 512 + j * 128:m * 512 + (j + 1) * 128],
                            rhs=Wg[:, m * 1024 + n * 512:m * 1024 + (n + 1) * 512], start=(m == 8), stop=(m == 15)),
                          reads=[b_ATt[r], b_Wg], writes=[PB[bs]] if m == 8 else (), pwrites=() if m == 8 else [PB[bs]])
                    ysl = yb[yi][:, n * 512:(n + 1) * 512]
                    E("dve", lambda e, r=r, j=j, n=n, ba=ba, ysl=ysl, c=c: e.scalar_tensor_tensor(
                        out=ysl, in0=bank(ba), scalar=rstd_a[:, c:c + 1],
                        in1=xo[r][:, j * 1024 + n * 512:j * 1024 + (n + 1) * 512], op0=ALU.mult, op1=ALU.add),
                      reads=[PB[ba], b_rstd, b_xo[r]], writes=[b_yb[yi]] if n == 0 else (), pwrites=() if n == 0 else [b_yb[yi]])
                    E("dve", lambda e, bs=bs, ysl=ysl, c=c: e.scalar_tensor_tensor(
                        out=ysl, in0=bank(bs), scalar=rstd_s[:, c:c + 1], in1=ysl, op0=ALU.mult, op1=ALU.add),
                      reads=[PB[bs], b_rstd, b_yb[yi]], pwrites=[b_yb[yi]])
                E("act", lambda e, yi=yi, c=c: e.activation(out=junk3, in_=yb[yi], func=AF.Square,
                                                            accum_out=ssqf[:, c:c + 1]),
                  reads=[b_yb[yi]], writes=[b_junk3], pwrites=[b_ssqf])
                E("act", lambda e, c=c: e.activation(out=rstdf[:, c:c + 1], in_=ssqf[:, c:c + 1], func=AF.Ln,
                                                     scale=1.0 / 1024, bias=EPS), reads=[b_ssqf], pwrites=[b_rstdf])
                E("act", lambda e, c=c: e.activation(out=rstdf[:, c:c + 1], in_=rstdf[:, c:c + 1], func=AF.Exp, scale=-0.5),
                  reads=[b_rstdf], pwrites=[b_rstdf])
                E("dve", lambda e, yi=yi, c=c: e.scalar_tensor_tensor(
                    out=yb[yi], in0=yb[yi], scalar=rstdf[:, c:c + 1], in1=gf_bc, op0=ALU.mult, op1=ALU.mult),
                  reads=[b_yb[yi], b_rstdf, b_prow], writes=[b_yb[yi]])
                E("pool", lambda e, yi=yi, c=c: e.dma_start(out=out_d[c * 128:(c + 1) * 128, :], in_=yb[yi]),
                  reads=[b_yb[yi]], pwrites=[b_out], dma=True, key="yo%d" % yi)
        E("sp", lambda e: e.nop(), reads=[b_out], real=False)
        S_.run()
    return nc


def _consts():
    a = np.arange(128)[:, None]
    b = np.arange(128)[None, :]
    c = np.zeros((128, NCONST), np.float32)
    c[:, C_ID:C_ID + 128] = (a == b)
    c[:, C_GE:C_GE + 128] = (a >= b)
    c[:, C_LT:C_LT + 128] = (a < b)
    c[:, C_LE:C_LE + 128] = (a <= b)
    c[:, C_GT:C_GT + 128] = (a > b)
    c[:, C_NEG:C_NEG + 128] = NEGV * (a > b)
    c[:, C_ONE:C_ONE + 128] = 1.0
    return c


def _col(v, n):
    return np.ascontiguousarray(np.asarray(v, np.float32).reshape(n, 128).T)


def make_in_maps(x, c, w_ada, b_ada, norm_in_gain, w_in, conv_w, conv_b, dt_bias, a_log, d_skip,
                 sb_norm_gain, ssm_norm_gain, w_out, norm_f_gain, cores=None):
    B = x.shape[0]
    consts = _consts()
    maps = []
    w_ada0 = np.ascontiguousarray(w_ada[0], np.float32)
    w_in0 = np.ascontiguousarray(w_in[0], np.float32)
    w_out0 = np.ascontiguousarray(w_out[0], np.float32)
    prow = np.zeros((1, NPROW), np.float32)
    prow[0, PR_BG:PR_BG + 1024] = b_ada[0, 2048:3072]
    prow[0, PR_GF:PR_GF + 1024] = norm_f_gain
    prow[0, PR_DTB:PR_DTB + 16] = dt_bias[0]
    prow[0, PR_AL:PR_AL + 16] = a_log[0]
    for b in (range(B) if cores is None else cores):
        pcol = np.zeros((128, NPCOL), np.float32)
        pcol[:, PC_C:PC_C + 8] = _col(c[b], 8)
        pcol[:, PC_BADA:PC_BADA + 24] = _col(b_ada[0], 24)
        pcol[:, PC_GIN:PC_GIN + 8] = _col(norm_in_gain[0], 8)
        pcol[:, PC_GA:PC_GA + 8] = _col(sb_norm_gain[0], 8)
        pcol[:, PC_GS:PC_GS + 8] = _col(ssm_norm_gain[0], 8)
        pcol[:, PC_D:PC_D + 8] = _col(np.repeat(np.asarray(d_skip[0], np.float32), 64), 8)
        cw = np.asarray(conv_w[0], np.float32)
        pcol[:, PC_CW:PC_CW + 48] = cw.T.reshape(12, 128, 4).transpose(1, 0, 2).reshape(128, 48)
        pcol[:, PC_CB:PC_CB + 12] = _col(conv_b[0], 12)
        maps.append({"x": np.ascontiguousarray(x[b], np.float32), "w_ada": w_ada0, "w_in": w_in0, "w_out": w_out0,
                     "consts": consts, "pcol": pcol, "prow": prow})
    return maps


_NC_CACHE = {}


def kernel(x, c, w_ada, b_ada, norm_in_gain, w_in, conv_w, conv_b, dt_bias, a_log, d_skip,
           sb_norm_gain, ssm_norm_gain, w_out, norm_f_gain):
    x = np.asarray(x)
    B, S, _ = x.shape
    if S not in _NC_CACHE:
        _NC_CACHE[S] = build(S)
    nc = _NC_CACHE[S]
    maps = make_in_maps(x, np.asarray(c), np.asarray(w_ada), np.asarray(b_ada), np.asarray(norm_in_gain),
                        np.asarray(w_in), np.asarray(conv_w), np.asarray(conv_b), np.asarray(dt_bias),
                        np.asarray(a_log), np.asarray(d_skip), np.asarray(sb_norm_gain),
                        np.asarray(ssm_norm_gain), np.asarray(w_out), np.asarray(norm_f_gain))
    res = run_bass_kernel_spmd(nc, maps, core_ids=list(range(B)))
    return np.stack([np.asarray(r["out"], np.float32) for r in res.results], axis=0)
```

```python
import numpy as np
from contextlib import ExitStack
import concourse.bass as bass
import concourse.mybir as mybir
from concourse.bass_utils import run_bass_kernel_spmd

F32 = mybir.dt.float32
BF16 = mybir.dt.bfloat16
AF = mybir.ActivationFunctionType
ALU = mybir.AluOpType
AX = mybir.AxisListType

ENGS = ("pe", "act", "dve", "pool", "sp")
EPS = 1e-6
NEGV = -30000.0


class Buf:
    __slots__ = ("name", "writers", "readers", "const")

    def __init__(self, name, const=False):
        self.name = name
        self.writers = []
        self.readers = []
        self.const = const


class Op:
    __slots__ = ("eng", "fn", "deps", "dma", "idx", "need_sig", "val", "key", "real")

    def __init__(self, eng, fn, dma, key, real):
        self.eng = eng
        self.fn = fn
        self.dma = dma
        self.key = key
        self.real = real
        self.deps = ()
        self.idx = -1
        self.need_sig = False
        self.val = 0


class Sched:
    def __init__(self, nc):
        self.nc = nc
        self.ops = {e: [] for e in ENGS}
        self.last_dma = {}
        self.groups = {e: {} for e in ENGS}
        self._gstart = {}

    def group_begin(self, eng):
        self._gstart[eng] = len(self.ops[eng])

    def group_end(self, eng):
        g0 = self._gstart.pop(eng)
        g1 = len(self.ops[eng])
        if g1 > g0 + 1:
            self.groups[eng][g0] = g1

    def emit(self, eng, fn, reads=(), writes=(), pwrites=(), dma=False, key=None, extra=(), real=True):
        if dma and key is None:
            key = writes[0].name if writes else (pwrites[0].name if pwrites else reads[0].name)
        op = Op(eng, fn, dma, key, real)
        deps = set(extra)
        for b in reads:
            deps.update(b.writers)
        for b in writes:
            deps.update(b.writers)
            deps.update(b.readers)
        for b in pwrites:
            deps.update(b.readers)
            if b.writers:
                deps.add(b.writers[0])
        op.deps = deps
        for b in reads:
            if not b.const:
                b.readers.append(op)
        for b in writes:
            b.writers = [op]
            b.readers = []
        for b in pwrites:
            b.writers.append(op)
        op.idx = len(self.ops[eng])
        self.ops[eng].append(op)
        if dma:
            self.last_dma[key] = op
        return op

    def barrier(self):
        lasts = []
        for e in ENGS:
            for op in reversed(self.ops[e]):
                if op.real and not op.dma:
                    lasts.append(op)
                    break
        deps = set(lasts) | set(self.last_dma.values())
        for e in ENGS:
            self.emit(e, lambda g: g.nop(), extra=deps, real=False)

    def _needed(self, op):
        comp = {}
        dmas = []
        for d in op.deps:
            if d.dma:
                dmas.append(d)
                continue
            if d.eng == op.eng and not op.dma:
                if d.eng == "pe":
                    continue
                if d.eng in ("act", "dve") and d.idx < op.idx - 2:
                    continue
            cur = comp.get(d.eng)
            if cur is None or d.idx > cur.idx:
                comp[d.eng] = d
        return comp, dmas

    def run(self):
        nc = self.nc
        for e in ENGS:
            for op in self.ops[e]:
                comp, _ = self._needed(op)
                for d in comp.values():
                    d.need_sig = True
        for e in ENGS:
            cnt = 0
            for op in self.ops[e]:
                if op.dma:
                    continue
                if op.need_sig:
                    cnt += 1
                    op.val = cnt
        keycnt = {}
        for e in ENGS:
            for op in self.ops[e]:
                if op.dma:
                    keycnt[op.key] = keycnt.get(op.key, 0) + 1
                    op.val = 16 * keycnt[op.key]
        keys = sorted(keycnt.keys())
        with ExitStack() as st:
            esem = {e: st.enter_context(nc.semaphore("s_" + e)) for e in ENGS}
            ksem = {k: st.enter_context(nc.semaphore("d%d" % i)) for i, k in enumerate(keys)}
            block = st.enter_context(nc.Block())
            sched = self

            def replay(ename, eng):
                seen = {}

                def do_waits(op):
                    comp, dmas = sched._needed(op)
                    for d in comp.values():
                        if seen.get(("e", d.eng), 0) < d.val:
                            eng.wait_ge(esem[d.eng], d.val)
                            seen[("e", d.eng)] = d.val
                    kmax = {}
                    for d in dmas:
                        if kmax.get(d.key, 0) < d.val:
                            kmax[d.key] = d.val
                    for kk_, vv_ in kmax.items():
                        if seen.get(("k", kk_), 0) < vv_:
                            eng.wait_ge(ksem[kk_], vv_)
                            seen[("k", kk_)] = vv_

                oplist = sched.ops[ename]
                for op in oplist:
                    g1 = sched.groups[ename].get(op.idx)
                    if g1 is not None:
                        for op2 in oplist[op.idx:g1]:
                            do_waits(op2)
                    comp, dmas = sched._needed(op)
                    for d in comp.values():
                        if seen.get(("e", d.eng), 0) < d.val:
                            eng.wait_ge(esem[d.eng], d.val)
                            seen[("e", d.eng)] = d.val
                    kmax = {}
                    for d in dmas:
                        if kmax.get(d.key, 0) < d.val:
                            kmax[d.key] = d.val
                    for kk_, vv_ in kmax.items():
                        if seen.get(("k", kk_), 0) < vv_:
                            eng.wait_ge(ksem[kk_], vv_)
                            seen[("k", kk_)] = vv_
                    ins = op.fn(eng)
                    if op.dma:
                        ins.then_inc(ksem[op.key], 16)
                    elif op.need_sig:
                        ins.then_inc(esem[ename], 1)

            @block.tensor
            def _(e):
                replay("pe", e)

            @block.scalar
            def _(e):
                replay("act", e)

            @block.vector
            def _(e):
                replay("dve", e)

            @block.gpsimd
            def _(e):
                replay("pool", e)

            @block.sync
            def _(e):
                replay("sp", e)


class Arena:
    def __init__(self, big, nw):
        self.big = big
        self.nw = nw
        self.top = 0

    def f32(self, n):
        off = self.top
        self.top += n
        assert self.top <= self.nw, ("sbuf arena overflow", self.top, self.nw)
        return self.big[:, off:off + n]

    def bf16(self, n):
        w = (n + 1) // 2
        off = self.top
        self.top += w
        assert self.top <= self.nw, ("sbuf arena overflow", self.top, self.nw)
        return self.big[:, off:off + w].bitcast(BF16)


C_ID, C_GE, C_LT, C_LE, C_GT, C_NEG, C_ONE = 0, 128, 256, 384, 512, 640, 768
NCONST = 896
PC_C, PC_BADA, PC_GIN, PC_GA, PC_GS, PC_D, PC_CW, PC_CB = 0, 8, 32, 40, 48, 56, 64, 112
NPCOL = 124
PR_BG, PR_GF, PR_DTB, PR_AL = 0, 1024, 2048, 2064
NPROW = 2080
NW_SBUF = 52224
NWARM = 0
NFILL = 0


def build(S, debug=False):
    NT = S // 512
    NCH = S // 128
    nc = bass.Bass("TRN2", target_bir_lowering=False)
    dk = "ExternalOutput" if debug else "Internal"
    x_d = nc.dram_tensor("x", [S, 1024], F32, kind="ExternalInput").ap()
    wada_d = nc.dram_tensor("w_ada", [1024, 3072], F32, kind="ExternalInput").ap()
    win_d = nc.dram_tensor("w_in", [1024, 6672], F32, kind="ExternalInput").ap()
    wout_d = nc.dram_tensor("w_out", [2048, 1024], F32, kind="ExternalInput").ap()
    const_d = nc.dram_tensor("consts", [128, NCONST], F32, kind="ExternalInput").ap()
    pcol_d = nc.dram_tensor("pcol", [128, NPCOL], F32, kind="ExternalInput").ap()
    prow_d = nc.dram_tensor("prow", [1, NPROW], F32, kind="ExternalInput").ap()
    out_d = nc.dram_tensor("out", [S, 1024], F32, kind="ExternalOutput").ap()
    HT_d = nc.dram_tensor("s_ht", [8, 128, S], BF16, kind=dk).ap()
    QT_d = nc.dram_tensor("s_qt", [1024, S], BF16, kind=dk).ap()
    KT_d = nc.dram_tensor("s_kt", [1024, S], BF16, kind=dk).ap()
    ZA_d = nc.dram_tensor("s_za", [1024, S], BF16, kind=dk).ap()
    V_d = nc.dram_tensor("s_v", [S, 1024], BF16, kind=dk).ap()
    AT_d = nc.dram_tensor("s_at", [2048, S], BF16, kind=dk).ap()
    XS_d = nc.dram_tensor("s_xs", [1024, S], F32, kind=dk).ap()
    BC_d = nc.dram_tensor("s_bc", [512, S], BF16, kind=dk).ap()
    ZS_d = nc.dram_tensor("s_zs", [1024, S], BF16, kind=dk).ap()
    DT_d = nc.dram_tensor("s_dt", [S, 32], F32, kind=dk).ap()

    S_ = Sched(nc)
    E = S_.emit

    with ExitStack() as st:
        big = st.enter_context(nc.sbuf_tensor("big", [128, NW_SBUF], F32))
        psum = st.enter_context(nc.psum_tensor("psum", [128, 4096], F32))
        A = Arena(big, NW_SBUF)

        def bank(i):
            return psum[:, 512 * i:512 * (i + 1)]

        PB = [Buf("ps%d" % i) for i in range(8)]

        cf = A.f32(NCONST); b_cf = Buf("cf", const=True)
        cb = A.bf16(NCONST); b_cb = Buf("cb", const=True)
        zb = A.bf16(512); b_zb = Buf("zb", const=True)
        neg4 = A.bf16(512); b_neg4 = Buf("neg4", const=True)
        pcol = A.f32(NPCOL); b_pcol = Buf("pcol", const=True)
        prow = A.f32(NPROW); b_prow = Buf("prow", const=True)
        modT = A.f32(24); b_modT = Buf("modT")
        g1 = A.f32(8); b_g1 = Buf("g1")
        gate_bc = A.f32(1024); b_gate = Buf("gate_bc")
        a_bc = A.f32(16); b_abc = Buf("a_bc")
        ssq_a = A.f32(NCH); b_ssqa = Buf("ssq_a")
        ssq_s = A.f32(NCH); b_ssqs = Buf("ssq_s")
        rstd_a = A.f32(NCH); rstd_s = A.f32(NCH); b_rstd = Buf("rstd_as")
        ssqf = A.f32(NCH); b_ssqf = Buf("ssqf")
        rstdf = A.f32(NCH); b_rstdf = Buf("rstdf")
        tmp_small = A.f32(64); b_tmps = Buf("tmps")

        idf = cf[:, C_ID:C_ID + 128]
        GEb = cb[:, C_GE:C_GE + 128]
        LTb = cb[:, C_LT:C_LT + 128]
        LTf = cf[:, C_LT:C_LT + 128]
        LEf = cf[:, C_LE:C_LE + 128]
        GTf = cf[:, C_GT:C_GT + 128]
        idb = cb[:, C_ID:C_ID + 128]
        onesf = cf[:, C_ONE:C_ONE + 128]

        E("sp", lambda e: e.dma_start(out=cf, in_=const_d), writes=[b_cf], dma=True)
        E("pool", lambda e: e.dma_start(out=cb, in_=const_d), writes=[b_cb], dma=True)
        E("sp", lambda e: e.dma_start(out=pcol, in_=pcol_d), writes=[b_pcol], dma=True)
        E("sp", lambda e: e.dma_start(out=prow, in_=prow_d.partition_broadcast(128)), writes=[b_prow], dma=True)
        E("dve", lambda e: e.memset(zb, 0.0), writes=[b_zb])
        for q in range(4):
            E("dve", lambda e, q=q: e.tensor_copy(out=neg4[:, q * 128:(q + 1) * 128], in_=cf[:, C_NEG:C_NEG + 128]),
              reads=[b_cf], pwrites=[b_neg4])

        base_mark = A.top
        NWS = 2576
        Wa = A.bf16(8 * 4096); b_Wa = Buf("Wa")
        win_v = win_d.rearrange("(k p) c -> p k c", p=128)
        Wa3 = Wa.rearrange("p (k c) -> p k c", k=8)
        for ci in range(8):
            E("pool", lambda e, ci=ci: e.dma_start(out=Wa3[:, :, ci * 512:(ci + 1) * 512],
                                                  in_=win_v[:, :, ci * 512:(ci + 1) * 512]),
              pwrites=[b_Wa], dma=True, key="Wa")
        p0_mark = A.top
        NVW = (S // 128) * 512
        Vall = big[:, NW_SBUF - NVW:NW_SBUF].bitcast(BF16); b_Vall = Buf("Vall")

        cact = A.f32(8); b_cact = Buf("cact")
        cact_rep = A.f32(8 * 128); b_crep = Buf("crep")
        wa = [A.f32(8 * 512) for _ in range(2)]
        b_wa = [Buf("wa%d" % i) for i in range(2)]
        E("act", lambda e: e.activation(out=cact, in_=pcol[:, PC_C:PC_C + 8], func=AF.Silu),
          reads=[b_pcol], writes=[b_cact])
        E("dve", lambda e: e.tensor_copy(out=cact_rep.rearrange("p (k m) -> p k m", k=8),
                                         in_=cact.unsqueeze(2).to_broadcast([128, 8, 128])),
          reads=[b_cact], writes=[b_crep])
        E("act", lambda e: e.activation(out=a_bc, in_=prow[:, PR_AL:PR_AL + 16], func=AF.Exp),
          reads=[b_prow], writes=[b_abc])
        E("dve", lambda e: e.tensor_scalar(out=a_bc, in0=a_bc, scalar1=-1.0, scalar2=None, op0=ALU.mult),
          reads=[b_abc], writes=[b_abc])
        wada_v = wada_d.rearrange("(k p) c -> p k c", p=128)
        for ci in range(6):
            r = ci % 2
            E("sp", lambda e, ci=ci, r=r: e.dma_start(out=wa[r].rearrange("p (k c) -> p k c", k=8),
                                                     in_=wada_v[:, :, ci * 512:(ci + 1) * 512]),
              writes=[b_wa[r]], dma=True)
            if ci < 4:
                for mm in range(4):
                    m = ci * 4 + mm
                    for k in range(8):
                        E("pe", lambda e, r=r, mm=mm, m=m, k=k: e.matmul(
                            bank(0)[:, m:m + 1], lhsT=wa[r][:, k * 512 + mm * 128:k * 512 + (mm + 1) * 128],
                            rhs=cact[:, k:k + 1], start=(k == 0), stop=(k == 7)),
                          reads=[b_wa[r], b_cact], pwrites=[PB[0]])
            else:
                n = ci - 4
                for k in range(8):
                    E("pe", lambda e, r=r, n=n, k=k: e.matmul(
                        bank(1 + n), lhsT=cact_rep[:, k * 128:(k + 1) * 128],
                        rhs=wa[r][:, k * 512:(k + 1) * 512], start=(k == 0), stop=(k == 7)),
                      reads=[b_wa[r], b_crep], pwrites=[PB[1 + n]])
        E("dve", lambda e: e.tensor_tensor(out=modT[:, 0:16], in0=bank(0)[:, 0:16], in1=pcol[:, PC_BADA:PC_BADA + 16],
                                           op=ALU.add), reads=[PB[0], b_pcol], writes=[b_modT])
        E("dve", lambda e: e.scalar_tensor_tensor(out=g1, in0=modT[:, 8:16], scalar=1.0, in1=pcol[:, PC_GIN:PC_GIN + 8],
                                                  op0=ALU.add, op1=ALU.mult), reads=[b_modT, b_pcol], writes=[b_g1])
        E("dve", lambda e: e.tensor_tensor(out=gate_bc, in0=psum[:, 512:1536], in1=prow[:, PR_BG:PR_BG + 1024], op=ALU.add),
          reads=[PB[1], PB[2], b_prow], writes=[b_gate])
        shiftc = modT
        S_.barrier()
        A.top = p0_mark

        xt = [A.f32(4096) for _ in range(2)]; b_xt = [Buf("xt%d" % i) for i in range(2)]
        hT = [A.bf16(8 * 512) for _ in range(2)]; b_hT = [Buf("hT%d" % i) for i in range(2)]
        junk = A.bf16(1024); b_junk = Buf("junk")
        ssq1 = A.f32(4 * NT); b_ssq1 = Buf("ssq1")
        rs1 = A.f32(4 * NT); b_rs1 = Buf("rs1")
        stq = A.bf16(8 * 512); b_stq = Buf("stq")
        stk = A.bf16(8 * 512); b_stk = Buf("stk")
        stz = A.bf16(8 * 512); b_stz = Buf("stz")
        stv = A.bf16(4 * 1024); b_stv = Buf("stv")
        stages = [(stq, b_stq, QT_d), (stk, b_stk, KT_d), (stz, b_stz, ZA_d)]
        pcnt = [0]

        def next_bank(lo, n):
            i = lo + (pcnt[0] % n)
            pcnt[0] += 1
            return i

        evq = [0]
        def p1a_front_a(tt):
            r = tt % 2
            E("sp", lambda e, tt=tt, r=r: e.dma_start(
                out=xt[r].rearrange("p (j d) -> p j d", j=4),
                in_=x_d[tt * 512:(tt + 1) * 512, :].rearrange("(j p) d -> p j d", p=128)),
              writes=[b_xt[r]], dma=True)
            for j in range(4):
                E("act", lambda e, r=r, j=j, tt=tt: e.activation(
                    out=junk, in_=xt[r][:, j * 1024:(j + 1) * 1024], func=AF.Square,
                    accum_out=ssq1[:, tt * 4 + j:tt * 4 + j + 1]),
                  reads=[b_xt[r]], writes=[b_junk], pwrites=[b_ssq1])
            E("act", lambda e, tt=tt: e.activation(out=rs1[:, tt * 4:tt * 4 + 4], in_=ssq1[:, tt * 4:tt * 4 + 4],
                                                   func=AF.Ln, scale=1.0 / 1024, bias=EPS),
              reads=[b_ssq1], pwrites=[b_rs1])
            E("act", lambda e, tt=tt: e.activation(out=rs1[:, tt * 4:tt * 4 + 4], in_=rs1[:, tt * 4:tt * 4 + 4],
                                                   func=AF.Exp, scale=-0.5),
              reads=[b_rs1], pwrites=[b_rs1])
            for j in range(4):
                E("dve", lambda e, r=r, j=j, tt=tt: e.tensor_scalar(
                    out=xt[r][:, j * 1024:(j + 1) * 1024], in0=xt[r][:, j * 1024:(j + 1) * 1024],
                    scalar1=rs1[:, tt * 4 + j:tt * 4 + j + 1], scalar2=None, op0=ALU.mult),
                  reads=[b_rs1, b_xt[r]], writes=[b_xt[r]])

        def p1a_front_b(tt):
            r = tt % 2
            for k in range(8):
                bi = next_bank(0, 4)
                for j in range(4):
                    E("pe", lambda e, r=r, j=j, k=k, bi=bi: e.transpose(
                        bank(bi)[:, j * 128:(j + 1) * 128], xt[r][:, j * 1024 + k * 128:j * 1024 + (k + 1) * 128], idf),
                      reads=[b_xt[r], b_cf], writes=[PB[bi]] if j == 0 else (), pwrites=() if j == 0 else [PB[bi]])
                if k % 2 == 0:
                    E("dve", lambda e, r=r, k=k, bi=bi: e.tensor_scalar(
                        out=hT[r][:, k * 512:(k + 1) * 512], in0=bank(bi), scalar1=g1[:, k:k + 1],
                        scalar2=shiftc[:, k:k + 1], op0=ALU.mult, op1=ALU.add),
                      reads=[PB[bi], b_g1, b_modT], pwrites=[b_hT[r]] if k else (), writes=() if k else [b_hT[r]])
                else:
                    E("act", lambda e, r=r, k=k, bi=bi: e.activation(
                        out=hT[r][:, k * 512:(k + 1) * 512], in_=bank(bi), func=AF.Identity,
                        scale=g1[:, k:k + 1], bias=shiftc[:, k:k + 1]),
                      reads=[PB[bi], b_g1, b_modT], pwrites=[b_hT[r]])
            E("pool", lambda e, r=r, tt=tt: e.dma_start(
                out=HT_d[:, :, tt * 512:(tt + 1) * 512].rearrange("k p t -> p k t"),
                in_=hT[r].rearrange("p (k t) -> p k t", k=8)), reads=[b_hT[r]], dma=True, key="hTo%d" % r)

        def p1a_back1(tt):
            r = tt % 2
            for grp in range(3):
                stg, b_stg, dst = stages[grp]
                c_base = [0, 1024, 3072][grp]
                for m in range(8):
                    bi = next_bank(4, 4)
                    for k in range(8):
                        E("pe", lambda e, r=r, k=k, bi=bi, c0=c_base + m * 128: e.matmul(
                            bank(bi), lhsT=Wa[:, k * 4096 + c0:k * 4096 + c0 + 128],
                            rhs=hT[r][:, k * 512:(k + 1) * 512], start=(k == 0), stop=(k == 7)),
                          reads=[b_Wa, b_hT[r]], writes=[PB[bi]] if k == 0 else (), pwrites=() if k == 0 else [PB[bi]])
                    wkw = dict(writes=[b_stg]) if m == 0 else dict(pwrites=[b_stg])
                    osl = stg[:, m * 512:(m + 1) * 512]
                    if grp == 0:
                        if evq[0] % 2 == 0:
                            E("act", lambda e, osl=osl, bi=bi: e.mul(out=osl, in_=bank(bi), mul=0.125),
                              reads=[PB[bi]], **wkw)
                        else:
                            E("dve", lambda e, osl=osl, bi=bi: e.tensor_scalar(out=osl, in0=bank(bi), scalar1=0.125,
                                                                               scalar2=None, op0=ALU.mult),
                              reads=[PB[bi]], **wkw)
                        evq[0] += 1
                    elif grp == 1:
                        if evq[0] % 2 == 0:
                            E("act", lambda e, osl=osl, bi=bi: e.copy(out=osl, in_=bank(bi)), reads=[PB[bi]], **wkw)
                        else:
                            E("dve", lambda e, osl=osl, bi=bi: e.tensor_copy(out=osl, in_=bank(bi)), reads=[PB[bi]], **wkw)
                        evq[0] += 1
                    else:
                        E("act", lambda e, osl=osl, bi=bi: e.activation(out=osl, in_=bank(bi), func=AF.Silu),
                          reads=[PB[bi]], **wkw)
                E("pool", lambda e, stg=stg, dst=dst, tt=tt: e.dma_start(
                    out=dst[:, tt * 512:(tt + 1) * 512].rearrange("(m p) t -> p m t", p=128),
                    in_=stg.rearrange("p (m t) -> p m t", m=8)), reads=[b_stg], dma=True)

        def p1a_back2(tt):
            r = tt % 2
            for j in range(4):
                for n in range(2):
                    bi = next_bank(4, 4)
                    for k in range(8):
                        E("pe", lambda e, r=r, k=k, bi=bi, j=j, n=n: e.matmul(
                            bank(bi), lhsT=hT[r][:, k * 512 + j * 128:k * 512 + (j + 1) * 128],
                            rhs=Wa[:, k * 4096 + 2048 + n * 512:k * 4096 + 2048 + (n + 1) * 512],
                            start=(k == 0), stop=(k == 7)),
                          reads=[b_Wa, b_hT[r]], writes=[PB[bi]] if k == 0 else (), pwrites=() if k == 0 else [PB[bi]])
                    wkw = dict(writes=[b_stv]) if (j == 0 and n == 0) else dict(pwrites=[b_stv])
                    osl = stv[:, j * 1024 + n * 512:j * 1024 + (n + 1) * 512]
                    if evq[0] % 2 == 0:
                        E("act", lambda e, osl=osl, bi=bi: e.copy(out=osl, in_=bank(bi)), reads=[PB[bi]], **wkw)
                    else:
                        E("dve", lambda e, osl=osl, bi=bi: e.tensor_copy(out=osl, in_=bank(bi)), reads=[PB[bi]], **wkw)
                    evq[0] += 1
            E("pool", lambda e, tt=tt: e.dma_start(
                out=V_d[tt * 512:(tt + 1) * 512, :].rearrange("(j p) c -> p j c", p=128),
                in_=stv.rearrange("p (j c) -> p j c", j=4)), reads=[b_stv], dma=True)

        p1a_front_a(0)
        p1a_front_b(0)
        for tt in range(NT):
            if tt + 1 < NT:
                p1a_front_a(tt + 1)
            p1a_back1(tt)
            if tt + 1 < NT:
                p1a_front_b(tt + 1)
            p1a_back2(tt)
        S_.barrier()
        A.top = base_mark

        A.nw = NW_SBUF - NVW
        Ws = A.bf16(8 * NWS); b_Ws = Buf("Ws")
        Ws3 = Ws.rearrange("p (k c) -> p k c", k=8)
        for ci in range(6):
            c0 = ci * 512
            c1 = min(NWS, c0 + 512)
            E("pool", lambda e, c0=c0, c1=c1: e.dma_start(out=Ws3[:, :, c0:c1], in_=win_v[:, :, 4096 + c0:4096 + c1]),
              pwrites=[b_Ws], dma=True, key="Ws")
        Vall3 = Vall.rearrange("p (n c) -> p n c", c=1024)
        V_v = V_d.rearrange("(n p) c -> p n c", p=128)
        nvq = max(1, (S // 128) // 8)
        for vq in range(0, S // 128, nvq):
            E("sp", lambda e, vq=vq: e.dma_start(out=Vall3[:, vq:vq + nvq, :], in_=V_v[:, vq:vq + nvq, :]),
              pwrites=[b_Vall], dma=True, key="Vall")
        hSr = [A.bf16(8 * 512) for _ in range(2)]; b_hSr = [Buf("hS%d" % i) for i in range(2)]
        XW_ = 515
        xin = A.f32(12 * XW_); b_xin = [Buf("xin%d" % m) for m in range(12)]
        cv = A.f32(12 * 512); b_cv = [Buf("cv%d" % m) for m in range(12)]
        BCs = A.bf16(4 * 512); b_BCs = Buf("BCs")
        ZS = A.bf16(8 * 512); b_ZS = Buf("ZSst")
        dtv = A.f32(64); b_dtv = Buf("dtv")
        dts = A.f32(64); b_dts = Buf("dts")
        lds = A.f32(64); b_lds = Buf("lds")
        cw = pcol[:, PC_CW:PC_CW + 48]
        cbias = pcol[:, PC_CB:PC_CB + 12]
        E("dve", lambda e: e.memset(xin, 0.0), writes=b_xin)
        for tt in range(NT):
            hS = hSr[tt % 2]; b_hS = b_hSr[tt % 2]
            E("sp", lambda e, tt=tt, hS=hS: e.dma_start(
                out=hS.rearrange("p (k t) -> p k t", k=8),
                in_=HT_d[:, :, tt * 512:(tt + 1) * 512].rearrange("k p t -> p k t")), writes=[b_hS], dma=True)
            for m in range(12):
                if tt > 0:
                    E("dve", lambda e, m=m: e.tensor_copy(out=xin[:, m * XW_:m * XW_ + 3], in_=xin[:, m * XW_ + 512:m * XW_ + 515]),
                      reads=[b_xin[m]], writes=[b_xin[m]])
                bi = next_bank(0, 4)
                for k in range(8):
                    E("pe", lambda e, k=k, bi=bi, c0=m * 128, hS=hS: e.matmul(
                        bank(bi), lhsT=Ws[:, k * NWS + c0:k * NWS + c0 + 128], rhs=hS[:, k * 512:(k + 1) * 512],
                        start=(k == 0), stop=(k == 7)),
                      reads=[b_Ws, b_hS], writes=[PB[bi]] if k == 0 else (), pwrites=() if k == 0 else [PB[bi]])
                E("act", lambda e, m=m, bi=bi: e.copy(out=xin[:, m * XW_ + 3:m * XW_ + 515], in_=bank(bi)),
                  reads=[PB[bi]], writes=[b_xin[m]])
                acc = cv[:, m * 512:(m + 1) * 512]
                E("dve", lambda e, m=m, acc=acc: e.tensor_scalar(
                    out=acc, in0=xin[:, m * XW_ + 3:m * XW_ + 515], scalar1=cw[:, m * 4 + 3:m * 4 + 4],
                    scalar2=cbias[:, m:m + 1], op0=ALU.mult, op1=ALU.add),
                  reads=[b_xin[m], b_pcol], writes=[b_cv[m]])
                for kk in range(3):
                    E("dve", lambda e, m=m, acc=acc, kk=kk: e.scalar_tensor_tensor(
                        out=acc, in0=xin[:, m * XW_ + kk:m * XW_ + kk + 512], scalar=cw[:, m * 4 + kk:m * 4 + kk + 1],
                        in1=acc, op0=ALU.mult, op1=ALU.add),
                      reads=[b_xin[m], b_pcol, b_cv[m]], writes=[b_cv[m]])
                if m < 8:
                    E("act", lambda e, acc=acc: e.activation(out=acc, in_=acc, func=AF.Silu),
                      reads=[b_cv[m]], writes=[b_cv[m]])
                else:
                    dst = BCs[:, (m - 8) * 512:(m - 7) * 512]
                    E("act", lambda e, acc=acc, dst=dst: e.activation(out=dst, in_=acc, func=AF.Silu),
                      reads=[b_cv[m]], writes=[b_BCs] if m == 8 else (), pwrites=() if m == 8 else [b_BCs])
            E("pool", lambda e, tt=tt: e.dma_start(
                out=XS_d[:, tt * 512:(tt + 1) * 512].rearrange("(m p) t -> p m t", p=128),
                in_=cv[:, 0:8 * 512].rearrange("p (m t) -> p m t", m=8)), reads=b_cv[0:8], dma=True, key="cvo")
            E("pool", lambda e, tt=tt: e.dma_start(
                out=BC_d[:, tt * 512:(tt + 1) * 512].rearrange("(m p) t -> p m t", p=128),
                in_=BCs.rearrange("p (m t) -> p m t", m=4)), reads=[b_BCs], dma=True, key="bco")
            for m in range(8):
                bi = next_bank(0, 4)
                for k in range(8):
                    E("pe", lambda e, k=k, bi=bi, c0=1552 + m * 128, hS=hS: e.matmul(
                        bank(bi), lhsT=Ws[:, k * NWS + c0:k * NWS + c0 + 128], rhs=hS[:, k * 512:(k + 1) * 512],
                        start=(k == 0), stop=(k == 7)),
                      reads=[b_Ws, b_hS], writes=[PB[bi]] if k == 0 else (), pwrites=() if k == 0 else [PB[bi]])
                E("act", lambda e, m=m, bi=bi: e.activation(out=ZS[:, m * 512:(m + 1) * 512], in_=bank(bi), func=AF.Silu),
                  reads=[PB[bi]], writes=[b_ZS] if m == 0 else (), pwrites=() if m == 0 else [b_ZS])
            E("pool", lambda e, tt=tt: e.dma_start(
                out=ZS_d[:, tt * 512:(tt + 1) * 512].rearrange("(m p) t -> p m t", p=128),
                in_=ZS.rearrange("p (m t) -> p m t", m=8)), reads=[b_ZS], dma=True, key="zso")
            for j in range(4):
                for k in range(8):
                    E("pe", lambda e, j=j, k=k, hS=hS: e.matmul(
                        bank(7)[:, j * 16:(j + 1) * 16], lhsT=hS[:, k * 512 + j * 128:k * 512 + (j + 1) * 128],
                        rhs=Ws[:, k * NWS + 1536:k * NWS + 1552], start=(k == 0), stop=(k == 7)),
                      reads=[b_Ws, b_hS], writes=[PB[7]] if (j == 0 and k == 0) else (),
                      pwrites=() if (j == 0 and k == 0) else [PB[7]])
            E("dve", lambda e: e.tensor_tensor(
                out=dtv.rearrange("p (j h) -> p j h", j=4), in0=bank(7)[:, 0:64].rearrange("p (j h) -> p j h", j=4),
                in1=prow[:, PR_DTB:PR_DTB + 16].unsqueeze(1).to_broadcast([128, 4, 16]), op=ALU.add),
              reads=[PB[7], b_prow], writes=[b_dtv])
            E("act", lambda e: e.activation(out=dtv, in_=dtv, func=AF.Exp), reads=[b_dtv], writes=[b_dtv])
            E("act", lambda e: e.activation(out=dts, in_=dtv, func=AF.Ln, bias=1.0), reads=[b_dtv], writes=[b_dts])
            E("dve", lambda e: e.tensor_tensor(
                out=lds.rearrange("p (j h) -> p j h", j=4), in0=dts.rearrange("p (j h) -> p j h", j=4),
                in1=a_bc.unsqueeze(1).to_broadcast([128, 4, 16]), op=ALU.mult),
              reads=[b_dts, b_abc], writes=[b_lds])
            E("pool", lambda e, tt=tt: e.dma_start(
                out=DT_d[tt * 512:(tt + 1) * 512, 0:16].rearrange("(j p) h -> p j h", p=128),
                in_=dts.rearrange("p (j h) -> p j h", j=4)), reads=[b_dts], dma=True, key="dto")
            E("pool", lambda e, tt=tt: e.dma_start(
                out=DT_d[tt * 512:(tt + 1) * 512, 16:32].rearrange("(j p) h -> p j h", p=128),
                in_=lds.rearrange("p (j h) -> p j h", j=4)), reads=[b_lds], dma=True, key="ldo")
        S_.barrier()
        A.top = base_mark

        NB = NCH
        QT = [A.bf16(S) for _ in range(1)] * 2; KT = [A.bf16(S) for _ in range(1)] * 2
        ZAq = [A.bf16(512) for _ in range(2)]; b_ZAq = [Buf("ZAq%d" % i) for i in range(2)]
        b_QT = [Buf("QT0")] * 2; b_KT = [Buf("KT0")] * 2
        NE, NL, NX, NWB = 3, 4, 2, 3
        eb = [A.f32(1024) for _ in range(NE)]; b_eb = [Buf("e%d" % i) for i in range(NE)]
        lb = [A.bf16(1024) for _ in range(NL)]; b_lb = [Buf("l%d" % i) for i in range(NL)]
        xb = [A.f32(1024) for _ in range(NX)]; b_xb = [Buf("xx%d" % i) for i in range(NX)]
        wb = [A.bf16(1024) for _ in range(NWB)]; b_wb = [Buf("w%d" % i) for i in range(NWB)]
        sqa = A.f32(512); b_sqa = Buf("sqa")
        ats = [A.bf16(512) for _ in range(2)]; b_ats = [Buf("ats%d" % i) for i in range(2)]
        atq = [0]

        SB7 = 7
        b7 = bank(SB7)
        xsI = [A.f32(1024) for _ in range(2)]; b_xsI = [Buf("xsI%d" % i) for i in range(2)]
        bcI = [A.bf16(512) for _ in range(2)]; b_bcI = [Buf("bcI%d" % i) for i in range(2)]
        zsI = [A.bf16(1024) for _ in range(2)]; b_zsI = [Buf("zsI%d" % i) for i in range(2)]
        dtI = [A.f32(32) for _ in range(2)]; b_dtI = [Buf("dtI%d" % i) for i in range(2)]
        Rr = A.f32(16 * 128); b_R = Buf("R")
        decay = A.f32(16 * 128); b_decay = Buf("decay")
        Mm = A.bf16(16 * 128); b_M = Buf("M")
        CE = A.bf16(16 * 128); b_CE = Buf("CE")
        xtf = A.f32(1024); b_xtf = Buf("xtf")
        xtb = A.bf16(1024); b_xtb = Buf("xtb")
        Btk = A.bf16(256); b_Btk = Buf("Btk")
        XWt = A.bf16(1024); b_XW = Buf("XW")
        cbs = A.f32(256); b_cbs = Buf("cbs")
        dcd = A.f32(32); b_dcd = Buf("dcd")
        wgt = A.f32(16); b_wgt = Buf("wgt")
        state = A.f32(1024); b_state = Buf("state")
        stateb = A.bf16(1024); b_stateb = Buf("stateb")
        stsT = A.bf16(8 * 512); b_stsT = Buf("stsT")
        ub = Rr[:, 0:1024]
        sqs = Rr[:, 1024:2048]
        E("dve", lambda e: e.memset(state, 0.0), writes=[b_state])
        E("dve", lambda e: e.memset(stateb, 0.0), writes=[b_stateb])

        def ssd_load(c):
            sl = c % 2
            E("sp", lambda e: e.dma_start(out=xsI[sl].rearrange("p (m t) -> p m t", m=8),
                                          in_=XS_d[:, c * 128:(c + 1) * 128].rearrange("(m p) t -> p m t", p=128)),
              writes=[b_xsI[sl]], dma=True)
            E("sp", lambda e: e.dma_start(out=bcI[sl].rearrange("p (m t) -> p m t", m=4),
                                          in_=BC_d[:, c * 128:(c + 1) * 128].rearrange("(m p) t -> p m t", p=128)),
              writes=[b_bcI[sl]], dma=True)
            E("sp", lambda e: e.dma_start(out=zsI[sl].rearrange("p (m t) -> p m t", m=8),
                                          in_=ZS_d[:, c * 128:(c + 1) * 128].rearrange("(m p) t -> p m t", p=128)),
              writes=[b_zsI[sl]], dma=True)
            E("sp", lambda e: e.dma_start(out=dtI[sl], in_=DT_d[c * 128:(c + 1) * 128, :]), writes=[b_dtI[sl]], dma=True)

        def ssd_chunk(c):
            sl = c % 2
            xsT, bc_, zs_, dt_ = xsI[sl], bcI[sl], zsI[sl], dtI[sl]
            bx, bb, bz, bd = b_xsI[sl], b_bcI[sl], b_zsI[sl], b_dtI[sl]
            dtj = dt_[:, 0:16]
            ldj = dt_[:, 16:32]
            jq = c % 4
            if c + 1 < NCH:
                ssd_load(c + 1)
            R3 = Rr.rearrange("p (h l) -> p h l", h=16)

            def m_ops(q):
                for h in range(q * 4, q * 4 + 4):
                    g = h // 8
                    E("dve", lambda e, h=h, g=g: e.scalar_tensor_tensor(
                        out=Mm[:, h * 128:(h + 1) * 128], in0=decay[:, h * 128:(h + 1) * 128], scalar=dtj[:, h:h + 1],
                        in1=cbs[:, g * 128:(g + 1) * 128], op0=ALU.mult, op1=ALU.mult),
                      reads=[b_decay, bd, b_cbs], writes=[b_M] if h == 0 else (), pwrites=() if h == 0 else [b_M])

            def ce_op(g):
                E("dve", lambda e: e.tensor_tensor(
                    out=CE[:, g * 1024:(g + 1) * 1024].rearrange("p (h l) -> p h l", h=8),
                    in0=decay[:, g * 1024:(g + 1) * 1024].rearrange("p (h l) -> p h l", h=8),
                    in1=bc_[:, 256 + g * 128:256 + (g + 1) * 128].unsqueeze(1).to_broadcast([128, 8, 128]), op=ALU.mult),
                  reads=[b_decay, bb], writes=[b_CE] if g == 0 else (), pwrites=() if g == 0 else [b_CE])

            for hf in range(2):
                E("dve", lambda e, hf=hf: e.tensor_tensor(
                    out=R3[:, hf * 8:(hf + 1) * 8, :], in0=LEf.unsqueeze(1).to_broadcast([128, 8, 128]),
                    in1=ldj[:, hf * 8:(hf + 1) * 8].unsqueeze(2).to_broadcast([128, 8, 128]), op=ALU.mult),
                  reads=[b_cf, bd], writes=[b_R] if hf == 0 else (), pwrites=() if hf == 0 else [b_R])
                for m4 in range(4):
                    m = hf * 4 + m4
                    E("pe", lambda e, m=m, m4=m4: e.transpose(b7[:, m4 * 128:(m4 + 1) * 128], xsT[:, m * 128:(m + 1) * 128], idf),
                      reads=[bx, b_cf], writes=[PB[SB7]] if m4 == 0 else (), pwrites=() if m4 == 0 else [PB[SB7]])
                E("dve", lambda e, hf=hf: e.tensor_copy(out=xtf[:, hf * 512:(hf + 1) * 512], in_=b7), reads=[PB[SB7]],
                  writes=[b_xtf] if hf == 0 else (), pwrites=() if hf == 0 else [b_xtf])
                yield
            E("dve", lambda e: e.tensor_copy(out=xtb, in_=xtf), reads=[b_xtf], writes=[b_xtb])
            b7b = b7[:, 0:128].bitcast(BF16)
            for g in range(2):
                E("pe", lambda e, g=g: e.transpose(b7b[:, g * 128:(g + 1) * 128], bc_[:, g * 128:(g + 1) * 128], idb),
                  reads=[bb, b_cb], writes=[PB[SB7]] if g == 0 else (), pwrites=() if g == 0 else [PB[SB7]])
            for g in range(2):
                E("pe", lambda e, g=g: e.matmul(b7[:, 256 + g * 128:256 + (g + 1) * 128], lhsT=bc_[:, g * 128:(g + 1) * 128],
                                                rhs=bc_[:, 256 + g * 128:256 + (g + 1) * 128], start=True, stop=True),
                  reads=[bb], pwrites=[PB[SB7]])
            E("dve", lambda e: e.tensor_copy(out=Btk, in_=b7b), reads=[PB[SB7]], writes=[b_Btk])
            E("dve", lambda e: e.tensor_copy(out=cbs, in_=b7[:, 256:512]), reads=[PB[SB7]], writes=[b_cbs])
            yield
            for q in range(4):
                E("pe", lambda e, q=q: e.matmul(b7, lhsT=GTf, rhs=Rr[:, q * 512:(q + 1) * 512], start=True, stop=False,
                                                skip_group_check=True), reads=[b_R, b_cf], writes=[PB[SB7]])
                E("pe", lambda e: e.matmul(b7, lhsT=idb, rhs=neg4, start=False, stop=True, skip_group_check=True),
                  reads=[b_cb, b_neg4], pwrites=[PB[SB7]])
                E("act", lambda e, q=q: e.activation(out=decay[:, q * 512:(q + 1) * 512], in_=b7, func=AF.Exp),
                  reads=[PB[SB7]], writes=[b_decay] if q == 0 else (), pwrites=() if q == 0 else [b_decay])
                if q >= 1:
                    m_ops(q - 1)
                yield
            E("pe", lambda e: e.matmul(b7[:, 0:16], lhsT=GTf, rhs=ldj, start=True, stop=True),
              reads=[bd, b_cf], writes=[PB[SB7]])
            E("pe", lambda e: e.matmul(b7[:, 16:32], lhsT=onesf, rhs=ldj, start=True, stop=True),
              reads=[bd, b_cf], pwrites=[PB[SB7]])
            E("act", lambda e: e.activation(out=dcd, in_=b7[:, 0:32], func=AF.Exp), reads=[PB[SB7]], writes=[b_dcd])
            m_ops(3)
            yield
            for q in range(4):
                E("pe", lambda e, q=q: e.matmul(b7, lhsT=onesf, rhs=Rr[:, q * 512:(q + 1) * 512], start=True, stop=True),
                  reads=[b_R, b_cf], writes=[PB[SB7]])
                E("act", lambda e, q=q: e.activation(out=decay[:, q * 512:(q + 1) * 512], in_=b7, func=AF.Exp),
                  reads=[PB[SB7]], writes=[b_decay] if q == 0 else (), pwrites=() if q == 0 else [b_decay])
                if q == 0:
                    E("dve", lambda e: e.tensor_tensor(out=wgt, in0=dtj, in1=dcd[:, 0:16], op=ALU.mult),
                      reads=[bd, b_dcd], writes=[b_wgt])
                    E("dve", lambda e: e.tensor_tensor(
                        out=XWt.rearrange("p (h q) -> p h q", h=16), in0=xtf.rearrange("p (h q) -> p h q", h=16),
                        in1=wgt.unsqueeze(2).to_broadcast([128, 16, 64]), op=ALU.mult),
                      reads=[b_xtf, b_wgt], writes=[b_XW])
                if q == 2:
                    ce_op(0)
                yield
            ce_op(1)
            yield
            for hb in range(2):
                for p4 in range(4):
                    pr = hb * 4 + p4
                    csl = slice(p4 * 128, (p4 + 1) * 128)
                    for half in range(2):
                        h = 2 * pr + half
                        first = (p4 == 0 and half == 0)
                        E("pe", lambda e, csl=csl, half=half, h=h: e.matmul(
                            b7[64 * half:64 * half + 64, csl], lhsT=xtb[:, h * 64:(h + 1) * 64],
                            rhs=Mm[:, h * 128:(h + 1) * 128], start=True, stop=False, skip_group_check=True),
                          reads=[b_xtb, b_M], writes=[PB[SB7]] if first else (), pwrites=() if first else [PB[SB7]])
                        E("pe", lambda e, csl=csl, half=half, h=h: e.matmul(
                            b7[64 * half:64 * half + 64, csl], lhsT=stateb[:, h * 64:(h + 1) * 64],
                            rhs=CE[:, h * 128:(h + 1) * 128], start=False, stop=True, skip_group_check=True),
                          reads=[b_stateb, b_CE], pwrites=[PB[SB7]])
                usl = ub[:, hb * 512:(hb + 1) * 512]
                u3 = usl.rearrange("p (m t) -> p m t", m=4)
                E("dve", lambda e, hb=hb, u3=u3: e.tensor_tensor(
                    out=u3, in0=xsT[:, hb * 512:(hb + 1) * 512].rearrange("p (m t) -> p m t", m=4),
                    in1=pcol[:, PC_D + hb * 4:PC_D + hb * 4 + 4].unsqueeze(2).to_broadcast([128, 4, 128]), op=ALU.mult),
                  reads=[bx, b_pcol], writes=[b_R] if hb == 0 else (), pwrites=() if hb == 0 else [b_R])
                E("dve", lambda e, usl=usl: e.tensor_tensor(out=usl, in0=usl, in1=b7, op=ALU.add),
                  reads=[b_R, PB[SB7]], pwrites=[b_R])
                yield
                E("dve", lambda e, hb=hb, usl=usl: e.tensor_tensor(out=usl, in0=usl, in1=zs_[:, hb * 512:(hb + 1) * 512],
                                                                  op=ALU.mult), reads=[b_R, bz], pwrites=[b_R])
                o3 = stsT.rearrange("p (m t) -> p m t", m=8)[:, hb * 4:hb * 4 + 4, jq * 128:(jq + 1) * 128]
                E("dve", lambda e, hb=hb, u3=u3, o3=o3: e.tensor_tensor(
                    out=o3, in0=u3,
                    in1=pcol[:, PC_GS + hb * 4:PC_GS + hb * 4 + 4].unsqueeze(2).to_broadcast([128, 4, 128]), op=ALU.mult),
                  reads=[b_R, b_pcol], pwrites=[b_stsT])
                yield
            E("dve", lambda e: e.tensor_tensor(out=sqs, in0=ub, in1=ub, op=ALU.mult), reads=[b_R], pwrites=[b_R])
            yield
            for pr in range(8):
                E("pe", lambda e, pr=pr: e.matmul(b7[:, 0:1], lhsT=sqs[:, pr * 128:(pr + 1) * 128], rhs=onesf[:, 0:1],
                                                 start=(pr == 0), stop=(pr == 7)),
                  reads=[b_R, b_cf], writes=[PB[SB7]] if pr == 0 else (), pwrites=() if pr == 0 else [PB[SB7]])
            E("dve", lambda e: e.tensor_copy(out=ssq_s[:, c:c + 1], in_=b7[:, 0:1]), reads=[PB[SB7]], pwrites=[b_ssqs])
            yield
            for g in range(2):
                E("pe", lambda e, g=g: e.matmul(b7, lhsT=Btk[:, g * 128:(g + 1) * 128], rhs=XWt[:, g * 512:(g + 1) * 512],
                                                start=True, stop=True), reads=[b_Btk, b_XW], writes=[PB[SB7]])
                ssl = state[:, g * 512:(g + 1) * 512]
                E("dve", lambda e, g=g, ssl=ssl: e.tensor_tensor(
                    out=ssl.rearrange("p (h q) -> p h q", h=8), in0=ssl.rearrange("p (h q) -> p h q", h=8),
                    in1=dcd[:, 16 + g * 8:16 + g * 8 + 8].unsqueeze(2).to_broadcast([128, 8, 64]), op=ALU.mult),
                  reads=[b_state, b_dcd], writes=[b_state] if g == 0 else (), pwrites=() if g == 0 else [b_state])
                E("dve", lambda e, ssl=ssl: e.tensor_tensor(out=ssl, in0=ssl, in1=b7, op=ALU.add),
                  reads=[b_state, PB[SB7]], pwrites=[b_state])
                yield
            E("dve", lambda e: e.tensor_copy(out=stateb, in_=state), reads=[b_state], writes=[b_stateb])
            if jq == 3:
                tt_ = c // 4
                E("sp", lambda e: e.dma_start(
                    out=AT_d[1024:2048, tt_ * 512:(tt_ + 1) * 512].rearrange("(m p) t -> p m t", p=128),
                    in_=stsT.rearrange("p (m t) -> p m t", m=8)), reads=[b_stsT], dma=True)
            yield

        def ssd_all():
            for c in range(NCH):
                for _ in ssd_chunk(c):
                    yield

        ssd_load(0)
        ssd_gen = ssd_all()

        def ssd_step():
            try:
                next(ssd_gen)
                return True
            except StopIteration:
                return False

        def v2(ap, c0):
            return ap.rearrange("p (b c) -> p b c", b=2)[:, :, c0:512]

        def load_hp(hp):
            r = hp % 2
            E("sp", lambda e: e.dma_start(out=QT[r], in_=QT_d[hp * 128:(hp + 1) * 128, :]), writes=[b_QT[r]], dma=True)
            E("sp", lambda e: e.dma_start(out=KT[r], in_=KT_d[hp * 128:(hp + 1) * 128, :]), writes=[b_KT[r]], dma=True)

        load_hp(0)
        ucount = [0]

        def do_hp(hp, r):
            if hp > 0:
                load_hp(hp)
            steps = []
            for qt in range(NT):
                for jj in range(4 * qt + 3, -1, -1):
                    steps.append((qt, jj))
            nu = len(steps)
            info = {}

            def st_qk(n):
                qt, jj = steps[n]
                u = ucount[0] + n
                kk = jj - 4 * qt
                c0 = 128 * kk if kk > 0 else 0
                ap_ = 2 * (u % 2)
                info[n] = (qt, jj, u, kk, c0, ap_)
                if jj == 4 * qt + 3:
                    zi = (hp * NT + qt) % 2
                    E("sp", lambda e, zi=zi: e.dma_start(out=ZAq[zi], in_=ZA_d[hp * 128:(hp + 1) * 128, qt * 512:(qt + 1) * 512]),
                      writes=[b_ZAq[zi]], dma=True)
                for half in range(2):
                    pb = 64 * half
                    E("pe", lambda e, pb=pb, half=half: e.matmul(
                        bank(ap_ + half)[:, c0:512], lhsT=KT[r][pb:pb + 64, jj * 128:(jj + 1) * 128],
                        rhs=QT[r][pb:pb + 64, qt * 512 + c0:(qt + 1) * 512], start=True, stop=True),
                      reads=[b_KT[r], b_QT[r]], writes=[PB[ap_ + half]])

            def st_act1(n):
                qt, jj, u, kk, c0, ap_ = info[n]
                ei, li = u % NE, u % NL
                E("act", lambda e: e.activation(out=v2(eb[ei], c0), in_=v2(psum[:, ap_ * 512:(ap_ + 2) * 512], c0), func=AF.Exp),
                  reads=[PB[ap_], PB[ap_ + 1]], writes=[b_eb[ei]])
                if kk >= 0:
                    ev = eb[ei].rearrange("p (b c) -> p b c", b=2)[:, :, c0:c0 + 128]
                    E("pool", lambda e: e.tensor_tensor(out=ev, in0=ev, in1=LTf.unsqueeze(1).to_broadcast([128, 2, 128]),
                                                        op=ALU.mult), reads=[b_eb[ei], b_cf], writes=[b_eb[ei]])

            def st_act2(n):
                qt, jj, u, kk, c0, ap_ = info[n]
                ei, li = u % NE, u % NL
                E("act", lambda e: e.activation(out=v2(lb[li], c0), in_=v2(eb[ei], c0), func=AF.Ln, bias=1.0),
                  reads=[b_eb[ei]], writes=[b_lb[li]])

            def st_ge(n):
                qt, jj, u, kk, c0, ap_ = info[n]
                li, xi = u % NL, u % NX
                for half in range(2):
                    cbk = 4 + half
                    if jj == 4 * qt + 3:
                        E("pe", lambda e, cbk=cbk: e.matmul(bank(cbk), lhsT=zb[:, 0:128], rhs=zb, start=True, stop=False,
                                                            skip_group_check=True), reads=[b_zb], writes=[PB[cbk]])
                    E("pe", lambda e, cbk=cbk, half=half: e.matmul(
                        bank(cbk)[:, c0:512], lhsT=GEb, rhs=lb[li][:, half * 512 + c0:(half + 1) * 512],
                        start=False, stop=False, skip_group_check=True), reads=[b_lb[li], b_cb], writes=[PB[cbk]])
                E("act", lambda e: e.activation(out=v2(xb[xi], c0), in_=v2(psum[:, 2048:3072], c0), func=AF.Exp, scale=-1.0),
                  reads=[PB[4], PB[5]], writes=[b_xb[xi]])

            def st_lt(n):
                qt, jj, u, kk, c0, ap_ = info[n]
                if jj == 0:
                    return
                li = u % NL
                for half in range(2):
                    cbk = 4 + half
                    E("pe", lambda e, cbk=cbk, half=half: e.matmul(
                        bank(cbk)[:, c0:512], lhsT=LTb, rhs=lb[li][:, half * 512 + c0:(half + 1) * 512],
                        start=False, stop=False, skip_group_check=True), reads=[b_lb[li], b_cb], writes=[PB[cbk]])

            def st_w(n):
                qt, jj, u, kk, c0, ap_ = info[n]
                ei, xi, wi = u % NE, u % NX, u % NWB
                E("dve", lambda e: e.tensor_tensor(out=v2(wb[wi], c0), in0=v2(eb[ei], c0), in1=v2(xb[xi], c0), op=ALU.mult),
                  reads=[b_eb[ei], b_xb[xi]], writes=[b_wb[wi]])

            def st_pv(n):
                qt, jj, u, kk, c0, ap_ = info[n]
                wi = u % NWB
                ob = 6
                for half in range(2):
                    pb = 64 * half
                    hh = 2 * hp + half
                    if jj == 4 * qt + 3:
                        E("pe", lambda e, pb=pb: e.matmul(bank(ob)[pb:pb + 64, :], lhsT=zb[:, 0:64], rhs=zb, start=True,
                                                          stop=False, skip_group_check=True), reads=[b_zb],
                          writes=[PB[ob]] if half == 0 else (), pwrites=() if half == 0 else [PB[ob]])
                    E("pe", lambda e, pb=pb, half=half, hh=hh: e.matmul(
                        bank(ob)[pb:pb + 64, c0:512], lhsT=Vall[:, jj * 1024 + hh * 64:jj * 1024 + hh * 64 + 64],
                        rhs=wb[wi][:, half * 512 + c0:(half + 1) * 512], start=False, stop=(jj == 0), skip_group_check=True),
                      reads=[b_Vall, b_wb[wi]], pwrites=[PB[ob]])

            def st_epi(n):
                qt, jj, u, kk, c0, ap_ = info[n]
                ob = 6
                if jj == 0:
                    E("act", lambda e: e.activation(out=sqa, in_=bank(ob), func=AF.Square), reads=[PB[ob]], writes=[b_sqa])
                    for sub in range(4):
                        E("pe", lambda e, sub=sub: e.matmul(bank(0)[:, sub:sub + 1], lhsT=sqa[:, sub * 128:(sub + 1) * 128],
                                                            rhs=onesf[:, 0:1], start=True, stop=True),
                          reads=[b_sqa, b_cf], writes=[PB[0]] if sub == 0 else (), pwrites=() if sub == 0 else [PB[0]])
                    if hp == 0:
                        E("dve", lambda e: e.tensor_copy(out=ssq_a[:, 4 * qt:4 * qt + 4], in_=bank(0)[:, 0:4]),
                          reads=[PB[0]], pwrites=[b_ssqa])
                    else:
                        E("dve", lambda e: e.tensor_tensor(out=ssq_a[:, 4 * qt:4 * qt + 4], in0=ssq_a[:, 4 * qt:4 * qt + 4],
                                                           in1=bank(0)[:, 0:4], op=ALU.add),
                          reads=[PB[0], b_ssqa], writes=[b_ssqa])
                    ai = atq[0] % 2
                    atq[0] += 1
                    E("dve", lambda e: e.scalar_tensor_tensor(
                        out=ats[ai], in0=bank(ob), scalar=pcol[:, PC_GA + hp:PC_GA + hp + 1],
                        in1=ZAq[(hp * NT + qt) % 2], op0=ALU.mult, op1=ALU.mult),
                      reads=[PB[ob], b_pcol, b_ZAq[(hp * NT + qt) % 2]], writes=[b_ats[ai]])
                    E("sp", lambda e: e.dma_start(out=AT_d[hp * 128:(hp + 1) * 128, qt * 512:(qt + 1) * 512], in_=ats[ai]),
                      reads=[b_ats[ai]], dma=True)

            for n in range(nu + 3):
                if n < nu:
                    st_qk(n)
                if 0 <= n - 2 < nu:
                    st_lt(n - 2)
                if n < nu:
                    st_act1(n)
                if 0 <= n - 1 < nu:
                    st_ge(n - 1)
                if n < nu:
                    st_act2(n)
                if 0 <= n - 1 < nu:
                    st_w(n - 1)
                if 0 <= n - 2 < nu:
                    st_pv(n - 2)
                if 0 <= n - 2 < nu:
                    st_epi(n - 2)
                ssd_step()
            ucount[0] += nu

        for hp_ in range(8):
            do_hp(hp_, hp_ % 2)
        while ssd_step():
            pass
        S_.barrier()
        A.top = base_mark

        A.nw = NW_SBUF
        Wg = A.bf16(16 * 1024); b_Wg = Buf("Wg")
        wst = [A.f32(4096) for _ in range(2)]; b_wst = [Buf("wst%d" % i) for i in range(2)]
        wout_v = wout_d.rearrange("(m p) c -> p m c", p=128)
        for ch in range(4):
            rr = ch % 2
            E("sp", lambda e, ch=ch, rr=rr: e.dma_start(out=wst[rr].rearrange("p (m c) -> p m c", m=4),
                                                      in_=wout_v[:, ch * 4:(ch + 1) * 4, :]), writes=[b_wst[rr]], dma=True)
            E("dve" if ch % 2 == 0 else "pool", lambda e, ch=ch, rr=rr: e.tensor_tensor(
                out=Wg[:, ch * 4096:(ch + 1) * 4096].rearrange("p (m c) -> p m c", m=4),
                in0=wst[rr].rearrange("p (m c) -> p m c", m=4),
                in1=gate_bc.unsqueeze(1).to_broadcast([128, 4, 1024]), op=ALU.mult),
              reads=[b_wst[rr], b_gate], pwrites=[b_Wg])
        for (src, dst) in ((ssq_a, rstd_a), (ssq_s, rstd_s)):
            E("act", lambda e, src=src, dst=dst: e.activation(out=dst, in_=src, func=AF.Ln, scale=1.0 / 1024, bias=EPS),
              reads=[b_ssqa, b_ssqs], pwrites=[b_rstd])
            E("act", lambda e, dst=dst: e.activation(out=dst, in_=dst, func=AF.Exp, scale=-0.5),
              reads=[b_rstd], pwrites=[b_rstd])
        ATt = [A.bf16(16 * 512) for _ in range(2)]; b_ATt = [Buf("ATt%d" % i) for i in range(2)]
        xo = [A.f32(4096) for _ in range(2)]; b_xo = [Buf("xo%d" % i) for i in range(2)]
        yb = [A.f32(1024) for _ in range(2)]; b_yb = [Buf("yb%d" % i) for i in range(2)]
        junk3 = A.bf16(1024); b_junk3 = Buf("junk3")
        b_out = Buf("outd")
        yq = [0]
        gf_bc = prow[:, PR_GF:PR_GF + 1024]
        for tt in range(NT):
            r = tt % 2
            E("sp", lambda e, tt=tt, r=r: e.dma_start(
                out=ATt[r].rearrange("p (m t) -> p m t", m=16),
                in_=AT_d[:, tt * 512:(tt + 1) * 512].rearrange("(m p) t -> p m t", p=128)), writes=[b_ATt[r]], dma=True)
            E("sp", lambda e, tt=tt, r=r: e.dma_start(
                out=xo[r].rearrange("p (j d) -> p j d", j=4),
                in_=x_d[tt * 512:(tt + 1) * 512, :].rearrange("(j p) d -> p j d", p=128)), writes=[b_xo[r]], dma=True)
            for j in range(4):
                c = tt * 4 + j
                yi = yq[0] % 2
                yq[0] += 1
                for n in range(2):
                    ba = 4 * (c % 2) + 2 * n
                    bs = ba + 1
                    for m in range(8):
                        E("pe", lambda e, r=r, m=m, j=j, n=n, ba=ba: e.matmul(
                            bank(ba), lhsT=ATt[r][:, m * 512 + j * 128:m * 512 + (j + 1) * 128],
                            rhs=Wg[:, m * 1024 + n * 512:m * 1024 + (n + 1) * 512], start=(m == 0), stop=(m == 7)),
                          reads=[b_ATt[r], b_Wg], writes=[PB[ba]] if m == 0 else (), pwrites=() if m == 0 else [PB[ba]])
                    for m in range(8, 16):
                        E("pe", lambda e, r=r, m=m, j=j, n=n, bs=bs: e.matmul(
                            bank(bs), lhsT=ATt[r][:, m * 512 + j * 128:m * 512 + (j + 1) * 128],
                            rhs=Wg[:, m * 1024 + n * 512:m * 1024 + (n + 1) * 512], start=(m == 8), stop=(m == 15)),
                          reads=[b_ATt[r], b_Wg], writes=[PB[bs]] if m == 8 else (), pwrites=() if m == 8 else [PB[bs]])
                    ysl = yb[yi][:, n * 512:(n + 1) * 512]
                    E("dve", lambda e, r=r, j=j, n=n, ba=ba, ysl=ysl, c=c: e.scalar_tensor_tensor(
                        out=ysl, in0=bank(ba), scalar=rstd_a[:, c:c + 1],
                        in1=xo[r][:, j * 1024 + n * 512:j * 1024 + (n + 1) * 512], op0=ALU.mult, op1=ALU.add),
                      reads=[PB[ba], b_rstd, b_xo[r]], writes=[b_yb[yi]] if n == 0 else (), pwrites=() if n == 0 else [b_yb[yi]])
                    E("dve", lambda e, bs=bs, ysl=ysl, c=c: e.scalar_tensor_tensor(
                        out=ysl, in0=bank(bs), scalar=rstd_s[:, c:c + 1], in1=ysl, op0=ALU.mult, op1=ALU.add),
                      reads=[PB[bs], b_rstd, b_yb[yi]], pwrites=[b_yb[yi]])
                E("act", lambda e, yi=yi, c=c: e.activation(out=junk3, in_=yb[yi], func=AF.Square,
                                                            accum_out=ssqf[:, c:c + 1]),
                  reads=[b_yb[yi]], writes=[b_junk3], pwrites=[b_ssqf])
                E("act", lambda e, c=c: e.activation(out=rstdf[:, c:c + 1], in_=ssqf[:, c:c + 1], func=AF.Ln,
                                                     scale=1.0 / 1024, bias=EPS), reads=[b_ssqf], pwrites=[b_rstdf])
                E("act", lambda e, c=c: e.activation(out=rstdf[:, c:c + 1], in_=rstdf[:, c:c + 1], func=AF.Exp, scale=-0.5),
                  reads=[b_rstdf], pwrites=[b_rstdf])
                E("dve", lambda e, yi=yi, c=c: e.scalar_tensor_tensor(
                    out=yb[yi], in0=yb[yi], scalar=rstdf[:, c:c + 1], in1=gf_bc, op0=ALU.mult, op1=ALU.mult),
                  reads=[b_yb[yi], b_rstdf, b_prow], writes=[b_yb[yi]])
                E("pool", lambda e, yi=yi, c=c: e.dma_start(out=out_d[c * 128:(c + 1) * 128, :], in_=yb[yi]),
                  reads=[b_yb[yi]], pwrites=[b_out], dma=True, key="yo%d" % yi)
        E("sp", lambda e: e.nop(), reads=[b_out], real=False)
        S_.run()
    return nc


def _consts():
    a = np.arange(128)[:, None]
    b = np.arange(128)[None, :]
    c = np.zeros((128, NCONST), np.float32)
    c[:, C_ID:C_ID + 128] = (a == b)
    c[:, C_GE:C_GE + 128] = (a >= b)
    c[:, C_LT:C_LT + 128] = (a < b)
    c[:, C_LE:C_LE + 128] = (a <= b)
    c[:, C_GT:C_GT + 128] = (a > b)
    c[:, C_NEG:C_NEG + 128] = NEGV * (a > b)
    c[:, C_ONE:C_ONE + 128] = 1.0
    return c


def _col(v, n):
    return np.ascontiguousarray(np.asarray(v, np.float32).reshape(n, 128).T)


def make_in_maps(x, c, w_ada, b_ada, norm_in_gain, w_in, conv_w, conv_b, dt_bias, a_log, d_skip,
                 sb_norm_gain, ssm_norm_gain, w_out, norm_f_gain, cores=None):
    B = x.shape[0]
    consts = _consts()
    maps = []
    w_ada0 = np.ascontiguousarray(w_ada[0], np.float32)
    w_in0 = np.ascontiguousarray(w_in[0], np.float32)
    w_out0 = np.ascontiguousarray(w_out[0], np.float32)
    prow = np.zeros((1, NPROW), np.float32)
    prow[0, PR_BG:PR_BG + 1024] = b_ada[0, 2048:3072]
    prow[0, PR_GF:PR_GF + 1024] = norm_f_gain
    prow[0, PR_DTB:PR_DTB + 16] = dt_bias[0]
    prow[0, PR_AL:PR_AL + 16] = a_log[0]
    for b in (range(B) if cores is None else cores):
        pcol = np.zeros((128, NPCOL), np.float32)
        pcol[:, PC_C:PC_C + 8] = _col(c[b], 8)
        pcol[:, PC_BADA:PC_BADA + 24] = _col(b_ada[0], 24)
        pcol[:, PC_GIN:PC_GIN + 8] = _col(norm_in_gain[0], 8)
        pcol[:, PC_GA:PC_GA + 8] = _col(sb_norm_gain[0], 8)
        pcol[:, PC_GS:PC_GS + 8] = _col(ssm_norm_gain[0], 8)
        pcol[:, PC_D:PC_D + 8] = _col(np.repeat(np.asarray(d_skip[0], np.float32), 64), 8)
        cw = np.asarray(conv_w[0], np.float32)
        pcol[:, PC_CW:PC_CW + 48] = cw.T.reshape(12, 128, 4).transpose(1, 0, 2).reshape(128, 48)
        pcol[:, PC_CB:PC_CB + 12] = _col(conv_b[0], 12)
        maps.append({"x": np.ascontiguousarray(x[b], np.float32), "w_ada": w_ada0, "w_in": w_in0, "w_out": w_out0,
                     "consts": consts, "pcol": pcol, "prow": prow})
    return maps


_NC_CACHE = {}


def kernel(x, c, w_ada, b_ada, norm_in_gain, w_in, conv_w, conv_b, dt_bias, a_log, d_skip,
           sb_norm_gain, ssm_norm_gain, w_out, norm_f_gain):
    x = np.asarray(x)
    B, S, _ = x.shape
    if S not in _NC_CACHE:
        _NC_CACHE[S] = build(S)
    nc = _NC_CACHE[S]
    maps = make_in_maps(x, np.asarray(c), np.asarray(w_ada), np.asarray(b_ada), np.asarray(norm_in_gain),
                        np.asarray(w_in), np.asarray(conv_w), np.asarray(conv_b), np.asarray(dt_bias),
                        np.asarray(a_log), np.asarray(d_skip), np.asarray(sb_norm_gain),
                        np.asarray(ssm_norm_gain), np.asarray(w_out), np.asarray(norm_f_gain))
    res = run_bass_kernel_spmd(nc, maps, core_ids=list(range(B)))
    return np.stack([np.asarray(r["out"], np.float32) for r in res.results], axis=0)
```

```python
import numpy as np
from contextlib import ExitStack
import concourse.bass as bass
import concourse.mybir as mybir
from concourse.bass_utils import run_bass_kernel_spmd

F32 = mybir.dt.float32
BF16 = mybir.dt.bfloat16
AF = mybir.ActivationFunctionType
ALU = mybir.AluOpType
AX = mybir.AxisListType

ENGS = ("pe", "act", "dve", "pool", "sp")
EPS = 1e-6
NEGV = -30000.0


class Buf:
    __slots__ = ("name", "writers", "readers", "const")

    def __init__(self, name, const=False):
        self.name = name
        self.writers = []
        self.readers = []
        self.const = const


class Op:
    __slots__ = ("eng", "fn", "deps", "dma", "idx", "need_sig", "val", "key", "real")

    def __init__(self, eng, fn, dma, key, real):
        self.eng = eng
        self.fn = fn
        self.dma = dma
        self.key = key
        self.real = real
        self.deps = ()
        self.idx = -1
        self.need_sig = False
        self.val = 0


class Sched:
    def __init__(self, nc):
        self.nc = nc
        self.ops = {e: [] for e in ENGS}
        self.last_dma = {}
        self.groups = {e: {} for e in ENGS}
        self._gstart = {}

    def group_begin(self, eng):
        self._gstart[eng] = len(self.ops[eng])

    def group_end(self, eng):
        g0 = self._gstart.pop(eng)
        g1 = len(self.ops[eng])
        if g1 > g0 + 1:
            self.groups[eng][g0] = g1

    def emit(self, eng, fn, reads=(), writes=(), pwrites=(), dma=False, key=None, extra=(), real=True):
        if dma and key is None:
            key = writes[0].name if writes else (pwrites[0].name if pwrites else reads[0].name)
        op = Op(eng, fn, dma, key, real)
        deps = set(extra)
        for b in reads:
            deps.update(b.writers)
        for b in writes:
            deps.update(b.writers)
            deps.update(b.readers)
        for b in pwrites:
            deps.update(b.readers)
            if b.writers:
                deps.add(b.writers[0])
        op.deps = deps
        for b in reads:
            if not b.const:
                b.readers.append(op)
        for b in writes:
            b.writers = [op]
            b.readers = []
        for b in pwrites:
            b.writers.append(op)
        op.idx = len(self.ops[eng])
        self.ops[eng].append(op)
        if dma:
            self.last_dma[key] = op
        return op

    def barrier(self):
        lasts = []
        for e in ENGS:
            for op in reversed(self.ops[e]):
                if op.real and not op.dma:
                    lasts.append(op)
                    break
        deps = set(lasts) | set(self.last_dma.values())
        for e in ENGS:
            self.emit(e, lambda g: g.nop(), extra=deps, real=False)

    def _needed(self, op):
        comp = {}
        dmas = []
        for d in op.deps:
            if d.dma:
                dmas.append(d)
                continue
            if d.eng == op.eng and not op.dma:
                if d.eng == "pe":
                    continue
                if d.eng in ("act", "dve") and d.idx < op.idx - 2:
                    continue
            cur = comp.get(d.eng)
            if cur is None or d.idx > cur.idx:
                comp[d.eng] = d
        return comp, dmas

    def run(self):
        nc = self.nc
        for e in ENGS:
            for op in self.ops[e]:
                comp, _ = self._needed(op)
                for d in comp.values():
                    d.need_sig = True
        for e in ENGS:
            cnt = 0
            for op in self.ops[e]:
                if op.dma:
                    continue
                if op.need_sig:
                    cnt += 1
                    op.val = cnt
        keycnt = {}
        for e in ENGS:
            for op in self.ops[e]:
                if op.dma:
                    keycnt[op.key] = keycnt.get(op.key, 0) + 1
                    op.val = 16 * keycnt[op.key]
        keys = sorted(keycnt.keys())
        with ExitStack() as st:
            esem = {e: st.enter_context(nc.semaphore("s_" + e)) for e in ENGS}
            ksem = {k: st.enter_context(nc.semaphore("d%d" % i)) for i, k in enumerate(keys)}
            block = st.enter_context(nc.Block())
            sched = self

            def replay(ename, eng):
                seen = {}

                def do_waits(op):
                    comp, dmas = sched._needed(op)
                    for d in comp.values():
                        if seen.get(("e", d.eng), 0) < d.val:
                            eng.wait_ge(esem[d.eng], d.val)
                            seen[("e", d.eng)] = d.val
                    kmax = {}
                    for d in dmas:
                        if kmax.get(d.key, 0) < d.val:
                            kmax[d.key] = d.val
                    for kk_, vv_ in kmax.items():
                        if seen.get(("k", kk_), 0) < vv_:
                            eng.wait_ge(ksem[kk_], vv_)
                            seen[("k", kk_)] = vv_

                oplist = sched.ops[ename]
                for op in oplist:
                    g1 = sched.groups[ename].get(op.idx)
                    if g1 is not None:
                        for op2 in oplist[op.idx:g1]:
                            do_waits(op2)
                    comp, dmas = sched._needed(op)
                    for d in comp.values():
                        if seen.get(("e", d.eng), 0) < d.val:
                            eng.wait_ge(esem[d.eng], d.val)
                            seen[("e", d.eng)] = d.val
                    kmax = {}
                    for d in dmas:
                        if kmax.get(d.key, 0) < d.val:
                            kmax[d.key] = d.val
                    for kk_, vv_ in kmax.items():
                        if seen.get(("k", kk_), 0) < vv_:
                            eng.wait_ge(ksem[kk_], vv_)
                            seen[("k", kk_)] = vv_
                    ins = op.fn(eng)
                    if op.dma:
                        ins.then_inc(ksem[op.key], 16)
                    elif op.need_sig:
                        ins.then_inc(esem[ename], 1)

            @block.tensor
            def _(e):
                replay("pe", e)

            @block.scalar
            def _(e):
                replay("act", e)

            @block.vector
            def _(e):
                replay("dve", e)

            @block.gpsimd
            def _(e):
                replay("pool", e)

            @block.sync
            def _(e):
                replay("sp", e)


class Arena:
    def __init__(self, big, nw):
        self.big = big
        self.nw = nw
        self.top = 0

    def f32(self, n):
        off = self.top
        self.top += n
        assert self.top <= self.nw, ("sbuf arena overflow", self.top, self.nw)
        return self.big[:, off:off + n]

    def bf16(self, n):
        w = (n + 1) // 2
        off = self.top
        self.top += w
        assert self.top <= self.nw, ("sbuf arena overflow", self.top, self.nw)
        return self.big[:, off:off + w].bitcast(BF16)


C_ID, C_GE, C_LT, C_LE, C_GT, C_NEG, C_ONE = 0, 128, 256, 384, 512, 640, 768
NCONST = 896
PC_C, PC_BADA, PC_GIN, PC_GA, PC_GS, PC_D, PC_CW, PC_CB = 0, 8, 32, 40, 48, 56, 64, 112
NPCOL = 124
PR_BG, PR_GF, PR_DTB, PR_AL = 0, 1024, 2048, 2064
NPROW = 2080
NW_SBUF = 52224
NWARM = 0
NFILL = 0


def build(S, debug=False):
    NT = S // 512
    NCH = S // 128
    nc = bass.Bass("TRN2", target_bir_lowering=False)
    dk = "ExternalOutput" if debug else "Internal"
    x_d = nc.dram_tensor("x", [S, 1024], F32, kind="ExternalInput").ap()
    wada_d = nc.dram_tensor("w_ada", [1024, 3072], F32, kind="ExternalInput").ap()
    win_d = nc.dram_tensor("w_in", [1024, 6672], F32, kind="ExternalInput").ap()
    wout_d = nc.dram_tensor("w_out", [2048, 1024], F32, kind="ExternalInput").ap()
    const_d = nc.dram_tensor("consts", [128, NCONST], F32, kind="ExternalInput").ap()
    pcol_d = nc.dram_tensor("pcol", [128, NPCOL], F32, kind="ExternalInput").ap()
    prow_d = nc.dram_tensor("prow", [1, NPROW], F32, kind="ExternalInput").ap()
    out_d = nc.dram_tensor("out", [S, 1024], F32, kind="ExternalOutput").ap()
    HT_d = nc.dram_tensor("s_ht", [8, 128, S], BF16, kind=dk).ap()
    QT_d = nc.dram_tensor("s_qt", [1024, S], BF16, kind=dk).ap()
    KT_d = nc.dram_tensor("s_kt", [1024, S], BF16, kind=dk).ap()
    ZA_d = nc.dram_tensor("s_za", [1024, S], BF16, kind=dk).ap()
    V_d = nc.dram_tensor("s_v", [S, 1024], BF16, kind=dk).ap()
    AT_d = nc.dram_tensor("s_at", [2048, S], BF16, kind=dk).ap()
    XS_d = nc.dram_tensor("s_xs", [1024, S], F32, kind=dk).ap()
    BC_d = nc.dram_tensor("s_bc", [512, S], BF16, kind=dk).ap()
    ZS_d = nc.dram_tensor("s_zs", [1024, S], BF16, kind=dk).ap()
    DT_d = nc.dram_tensor("s_dt", [S, 32], F32, kind=dk).ap()

    S_ = Sched(nc)
    E = S_.emit

    with ExitStack() as st:
        big = st.enter_context(nc.sbuf_tensor("big", [128, NW_SBUF], F32))
        psum = st.enter_context(nc.psum_tensor("psum", [128, 4096], F32))
        A = Arena(big, NW_SBUF)

        def bank(i):
            return psum[:, 512 * i:512 * (i + 1)]

        PB = [Buf("ps%d" % i) for i in range(8)]

        cf = A.f32(NCONST); b_cf = Buf("cf", const=True)
        cb = A.bf16(NCONST); b_cb = Buf("cb", const=True)
        zb = A.bf16(512); b_zb = Buf("zb", const=True)
        neg4 = A.bf16(512); b_neg4 = Buf("neg4", const=True)
        pcol = A.f32(NPCOL); b_pcol = Buf("pcol", const=True)
        prow = A.f32(NPROW); b_prow = Buf("prow", const=True)
        modT = A.f32(24); b_modT = Buf("modT")
        g1 = A.f32(8); b_g1 = Buf("g1")
        gate_bc = A.f32(1024); b_gate = Buf("gate_bc")
        a_bc = A.f32(16); b_abc = Buf("a_bc")
        ssq_a = A.f32(NCH); b_ssqa = Buf("ssq_a")
        ssq_s = A.f32(NCH); b_ssqs = Buf("ssq_s")
        rstd_a = A.f32(NCH); rstd_s = A.f32(NCH); b_rstd = Buf("rstd_as")
        ssqf = A.f32(NCH); b_ssqf = Buf("ssqf")
        rstdf = A.f32(NCH); b_rstdf = Buf("rstdf")
        tmp_small = A.f32(64); b_tmps = Buf("tmps")

        idf = cf[:, C_ID:C_ID + 128]
        GEb = cb[:, C_GE:C_GE + 128]
        LTb = cb[:, C_LT:C_LT + 128]
        LTf = cf[:, C_LT:C_LT + 128]
        LEf = cf[:, C_LE:C_LE + 128]
        GTf = cf[:, C_GT:C_GT + 128]
        idb = cb[:, C_ID:C_ID + 128]
        onesf = cf[:, C_ONE:C_ONE + 128]

        E("sp", lambda e: e.dma_start(out=cf, in_=const_d), writes=[b_cf], dma=True)
        E("pool", lambda e: e.dma_start(out=cb, in_=const_d), writes=[b_cb], dma=True)
        E("sp", lambda e: e.dma_start(out=pcol, in_=pcol_d), writes=[b_pcol], dma=True)
        E("sp", lambda e: e.dma_start(out=prow, in_=prow_d.partition_broadcast(128)), writes=[b_prow], dma=True)
        E("dve", lambda e: e.memset(zb, 0.0), writes=[b_zb])
        for q in range(4):
            E("dve", lambda e, q=q: e.tensor_copy(out=neg4[:, q * 128:(q + 1) * 128], in_=cf[:, C_NEG:C_NEG + 128]),
              reads=[b_cf], pwrites=[b_neg4])

        base_mark = A.top
        NWA, NWB2 = 1536, 1040
        Wsa = A.bf16(8 * NWA); b_Wsa = Buf("Wsa")
        wsa_mark = A.top
        Wa = A.bf16(8 * 4096); b_Wa = Buf("Wa")
        win_v = win_d.rearrange("(k p) c -> p k c", p=128)
        Wa3 = Wa.rearrange("p (k c) -> p k c", k=8)
        for ci in range(8):
            E("pool", lambda e, ci=ci: e.dma_start(out=Wa3[:, :, ci * 512:(ci + 1) * 512],
                                                  in_=win_v[:, :, ci * 512:(ci + 1) * 512]),
              pwrites=[b_Wa], dma=True, key="Wa")
        Wsa3 = Wsa.rearrange("p (k c) -> p k c", k=8)
        for ci in range(3):
            E("pool", lambda e, ci=ci: e.dma_start(out=Wsa3[:, :, ci * 512:(ci + 1) * 512],
                                                  in_=win_v[:, :, 4096 + ci * 512:4096 + (ci + 1) * 512]),
              pwrites=[b_Wsa], dma=True, key="Wsa")
        p0_mark = A.top
        NVW = (S // 128) * 512
        Vall = big[:, NW_SBUF - NVW:NW_SBUF].bitcast(BF16); b_Vall = Buf("Vall")

        cact = A.f32(8); b_cact = Buf("cact")
        cact_rep = A.f32(8 * 128); b_crep = Buf("crep")
        wa = [A.f32(8 * 512) for _ in range(2)]
        b_wa = [Buf("wa%d" % i) for i in range(2)]
        E("act", lambda e: e.activation(out=cact, in_=pcol[:, PC_C:PC_C + 8], func=AF.Silu),
          reads=[b_pcol], writes=[b_cact])
        E("dve", lambda e: e.tensor_copy(out=cact_rep.rearrange("p (k m) -> p k m", k=8),
                                         in_=cact.unsqueeze(2).to_broadcast([128, 8, 128])),
          reads=[b_cact], writes=[b_crep])
        E("act", lambda e: e.activation(out=a_bc, in_=prow[:, PR_AL:PR_AL + 16], func=AF.Exp),
          reads=[b_prow], writes=[b_abc])
        E("dve", lambda e: e.tensor_scalar(out=a_bc, in0=a_bc, scalar1=-1.0, scalar2=None, op0=ALU.mult),
          reads=[b_abc], writes=[b_abc])
        wada_v = wada_d.rearrange("(k p) c -> p k c", p=128)
        for ci in range(6):
            r = ci % 2
            E("sp", lambda e, ci=ci, r=r: e.dma_start(out=wa[r].rearrange("p (k c) -> p k c", k=8),
                                                     in_=wada_v[:, :, ci * 512:(ci + 1) * 512]),
              writes=[b_wa[r]], dma=True)
            if ci < 4:
                for mm in range(4):
                    m = ci * 4 + mm
                    for k in range(8):
                        E("pe", lambda e, r=r, mm=mm, m=m, k=k: e.matmul(
                            bank(0)[:, m:m + 1], lhsT=wa[r][:, k * 512 + mm * 128:k * 512 + (mm + 1) * 128],
                            rhs=cact[:, k:k + 1], start=(k == 0), stop=(k == 7)),
                          reads=[b_wa[r], b_cact], pwrites=[PB[0]])
            else:
                n = ci - 4
                for k in range(8):
                    E("pe", lambda e, r=r, n=n, k=k: e.matmul(
                        bank(1 + n), lhsT=cact_rep[:, k * 128:(k + 1) * 128],
                        rhs=wa[r][:, k * 512:(k + 1) * 512], start=(k == 0), stop=(k == 7)),
                      reads=[b_wa[r], b_crep], pwrites=[PB[1 + n]])
        E("dve", lambda e: e.tensor_tensor(out=modT[:, 0:16], in0=bank(0)[:, 0:16], in1=pcol[:, PC_BADA:PC_BADA + 16],
                                           op=ALU.add), reads=[PB[0], b_pcol], writes=[b_modT])
        E("dve", lambda e: e.scalar_tensor_tensor(out=g1, in0=modT[:, 8:16], scalar=1.0, in1=pcol[:, PC_GIN:PC_GIN + 8],
                                                  op0=ALU.add, op1=ALU.mult), reads=[b_modT, b_pcol], writes=[b_g1])
        E("dve", lambda e: e.tensor_tensor(out=gate_bc, in0=psum[:, 512:1536], in1=prow[:, PR_BG:PR_BG + 1024], op=ALU.add),
          reads=[PB[1], PB[2], b_prow], writes=[b_gate])
        shiftc = modT
        S_.barrier()
        A.top = p0_mark

        xt = [A.f32(4096) for _ in range(2)]; b_xt = [Buf("xt%d" % i) for i in range(2)]
        hT = [A.bf16(8 * 512) for _ in range(2)]; b_hT = [Buf("hT%d" % i) for i in range(2)]
        junk = A.bf16(1024); b_junk = Buf("junk")
        ssq1 = A.f32(4 * NT); b_ssq1 = Buf("ssq1")
        rs1 = A.f32(4 * NT); b_rs1 = Buf("rs1")
        stq = A.bf16(8 * 512); b_stq = Buf("stq")
        stk = A.bf16(8 * 512); b_stk = Buf("stk")
        stz = A.bf16(8 * 512); b_stz = Buf("stz")
        stv = A.bf16(4 * 1024); b_stv = Buf("stv")
        stages = [(stq, b_stq, QT_d), (stk, b_stk, KT_d), (stz, b_stz, ZA_d)]
        pcnt = [0]

        def next_bank(lo, n):
            i = lo + (pcnt[0] % n)
            pcnt[0] += 1
            return i

        evq = [0]
        def p1a_front_a(tt):
            r = tt % 2
            E("sp", lambda e, tt=tt, r=r: e.dma_start(
                out=xt[r].rearrange("p (j d) -> p j d", j=4),
                in_=x_d[tt * 512:(tt + 1) * 512, :].rearrange("(j p) d -> p j d", p=128)),
              writes=[b_xt[r]], dma=True)
            for j in range(4):
                E("act", lambda e, r=r, j=j, tt=tt: e.activation(
                    out=junk, in_=xt[r][:, j * 1024:(j + 1) * 1024], func=AF.Square,
                    accum_out=ssq1[:, tt * 4 + j:tt * 4 + j + 1]),
                  reads=[b_xt[r]], writes=[b_junk], pwrites=[b_ssq1])
            E("act", lambda e, tt=tt: e.activation(out=rs1[:, tt * 4:tt * 4 + 4], in_=ssq1[:, tt * 4:tt * 4 + 4],
                                                   func=AF.Ln, scale=1.0 / 1024, bias=EPS),
              reads=[b_ssq1], pwrites=[b_rs1])
            E("act", lambda e, tt=tt: e.activation(out=rs1[:, tt * 4:tt * 4 + 4], in_=rs1[:, tt * 4:tt * 4 + 4],
                                                   func=AF.Exp, scale=-0.5),
              reads=[b_rs1], pwrites=[b_rs1])
            for j in range(4):
                E("dve", lambda e, r=r, j=j, tt=tt: e.tensor_scalar(
                    out=xt[r][:, j * 1024:(j + 1) * 1024], in0=xt[r][:, j * 1024:(j + 1) * 1024],
                    scalar1=rs1[:, tt * 4 + j:tt * 4 + j + 1], scalar2=None, op0=ALU.mult),
                  reads=[b_rs1, b_xt[r]], writes=[b_xt[r]])

        def p1a_front_b(tt):
            r = tt % 2
            for k in range(8):
                bi = next_bank(0, 4)
                for j in range(4):
                    E("pe", lambda e, r=r, j=j, k=k, bi=bi: e.transpose(
                        bank(bi)[:, j * 128:(j + 1) * 128], xt[r][:, j * 1024 + k * 128:j * 1024 + (k + 1) * 128], idf),
                      reads=[b_xt[r], b_cf], writes=[PB[bi]] if j == 0 else (), pwrites=() if j == 0 else [PB[bi]])
                if k % 2 == 0:
                    E("dve", lambda e, r=r, k=k, bi=bi: e.tensor_scalar(
                        out=hT[r][:, k * 512:(k + 1) * 512], in0=bank(bi), scalar1=g1[:, k:k + 1],
                        scalar2=shiftc[:, k:k + 1], op0=ALU.mult, op1=ALU.add),
                      reads=[PB[bi], b_g1, b_modT], pwrites=[b_hT[r]] if k else (), writes=() if k else [b_hT[r]])
                else:
                    E("act", lambda e, r=r, k=k, bi=bi: e.activation(
                        out=hT[r][:, k * 512:(k + 1) * 512], in_=bank(bi), func=AF.Identity,
                        scale=g1[:, k:k + 1], bias=shiftc[:, k:k + 1]),
                      reads=[PB[bi], b_g1, b_modT], pwrites=[b_hT[r]])
            E("pool", lambda e, r=r, tt=tt: e.dma_start(
                out=HT_d[:, :, tt * 512:(tt + 1) * 512].rearrange("k p t -> p k t"),
                in_=hT[r].rearrange("p (k t) -> p k t", k=8)), reads=[b_hT[r]], dma=True, key="hTo%d" % r)

        def p1a_back1(tt):
            r = tt % 2
            for grp in range(3):
                stg, b_stg, dst = stages[grp]
                c_base = [0, 1024, 3072][grp]
                for m in range(8):
                    bi = next_bank(4, 4)
                    for k in range(8):
                        E("pe", lambda e, r=r, k=k, bi=bi, c0=c_base + m * 128: e.matmul(
                            bank(bi), lhsT=Wa[:, k * 4096 + c0:k * 4096 + c0 + 128],
                            rhs=hT[r][:, k * 512:(k + 1) * 512], start=(k == 0), stop=(k == 7)),
                          reads=[b_Wa, b_hT[r]], writes=[PB[bi]] if k == 0 else (), pwrites=() if k == 0 else [PB[bi]])
                    wkw = dict(writes=[b_stg]) if m == 0 else dict(pwrites=[b_stg])
                    osl = stg[:, m * 512:(m + 1) * 512]
                    if grp == 0:
                        if evq[0] % 2 == 0:
                            E("act", lambda e, osl=osl, bi=bi: e.mul(out=osl, in_=bank(bi), mul=0.125),
                              reads=[PB[bi]], **wkw)
                        else:
                            E("dve", lambda e, osl=osl, bi=bi: e.tensor_scalar(out=osl, in0=bank(bi), scalar1=0.125,
                                                                               scalar2=None, op0=ALU.mult),
                              reads=[PB[bi]], **wkw)
                        evq[0] += 1
                    elif grp == 1:
                        if evq[0] % 2 == 0:
                            E("act", lambda e, osl=osl, bi=bi: e.copy(out=osl, in_=bank(bi)), reads=[PB[bi]], **wkw)
                        else:
                            E("dve", lambda e, osl=osl, bi=bi: e.tensor_copy(out=osl, in_=bank(bi)), reads=[PB[bi]], **wkw)
                        evq[0] += 1
                    else:
                        E("act", lambda e, osl=osl, bi=bi: e.activation(out=osl, in_=bank(bi), func=AF.Silu),
                          reads=[PB[bi]], **wkw)
                E("pool", lambda e, stg=stg, dst=dst, tt=tt: e.dma_start(
                    out=dst[:, tt * 512:(tt + 1) * 512].rearrange("(m p) t -> p m t", p=128),
                    in_=stg.rearrange("p (m t) -> p m t", m=8)), reads=[b_stg], dma=True)

        def p1a_back2(tt):
            r = tt % 2
            for j in range(4):
                for n in range(2):
                    bi = next_bank(4, 4)
                    for k in range(8):
                        E("pe", lambda e, r=r, k=k, bi=bi, j=j, n=n: e.matmul(
                            bank(bi), lhsT=hT[r][:, k * 512 + j * 128:k * 512 + (j + 1) * 128],
                            rhs=Wa[:, k * 4096 + 2048 + n * 512:k * 4096 + 2048 + (n + 1) * 512],
                            start=(k == 0), stop=(k == 7)),
                          reads=[b_Wa, b_hT[r]], writes=[PB[bi]] if k == 0 else (), pwrites=() if k == 0 else [PB[bi]])
                    wkw = dict(writes=[b_stv]) if (j == 0 and n == 0) else dict(pwrites=[b_stv])
                    osl = stv[:, j * 1024 + n * 512:j * 1024 + (n + 1) * 512]
                    if evq[0] % 2 == 0:
                        E("act", lambda e, osl=osl, bi=bi: e.copy(out=osl, in_=bank(bi)), reads=[PB[bi]], **wkw)
                    else:
                        E("dve", lambda e, osl=osl, bi=bi: e.tensor_copy(out=osl, in_=bank(bi)), reads=[PB[bi]], **wkw)
                    evq[0] += 1
            E("pool", lambda e, tt=tt: e.dma_start(
                out=V_d[tt * 512:(tt + 1) * 512, :].rearrange("(j p) c -> p j c", p=128),
                in_=stv.rearrange("p (j c) -> p j c", j=4)), reads=[b_stv], dma=True)

        p1a_front_a(0)
        p1a_front_b(0)
        for tt in range(NT):
            if tt + 1 < NT:
                p1a_front_a(tt + 1)
            p1a_back1(tt)
            if tt + 1 < NT:
                p1a_front_b(tt + 1)
            p1a_back2(tt)
        S_.barrier()
        A.top = base_mark

        A.top = wsa_mark
        A.nw = NW_SBUF - NVW
        Wsb = A.bf16(8 * NWB2); b_Wsb = Buf("Wsb")
        Wsb3 = Wsb.rearrange("p (k c) -> p k c", k=8)
        for (c0, c1) in ((0, 528), (528, 1040)):
            E("pool", lambda e, c0=c0, c1=c1: e.dma_start(out=Wsb3[:, :, c0:c1], in_=win_v[:, :, 5632 + c0:5632 + c1]),
              pwrites=[b_Wsb], dma=True, key="Wsb")
        Vall3 = Vall.rearrange("p (n c) -> p n c", c=1024)
        V_v = V_d.rearrange("(n p) c -> p n c", p=128)
        nvq = max(1, (S // 128) // 8)
        for vq in range(0, S // 128, nvq):
            E("sp", lambda e, vq=vq: e.dma_start(out=Vall3[:, vq:vq + nvq, :], in_=V_v[:, vq:vq + nvq, :]),
              pwrites=[b_Vall], dma=True, key="Vall")
        hSr = [A.bf16(8 * 512) for _ in range(2)]; b_hSr = [Buf("hS%d" % i) for i in range(2)]
        XW_ = 515
        xin = A.f32(12 * XW_); b_xin = [Buf("xin%d" % m) for m in range(12)]
        cv = A.f32(12 * 512); b_cv = [Buf("cv%d" % m) for m in range(12)]
        BCs = A.bf16(4 * 512); b_BCs = Buf("BCs")
        ZS = A.bf16(8 * 512); b_ZS = Buf("ZSst")
        dtv = A.f32(64); b_dtv = Buf("dtv")
        dts = A.f32(64); b_dts = Buf("dts")
        lds = A.f32(64); b_lds = Buf("lds")
        cw = pcol[:, PC_CW:PC_CW + 48]
        cbias = pcol[:, PC_CB:PC_CB + 12]
        E("dve", lambda e: e.memset(xin, 0.0), writes=b_xin)
        for tt in range(NT):
            hS = hSr[tt % 2]; b_hS = b_hSr[tt % 2]
            E("sp", lambda e, tt=tt, hS=hS: e.dma_start(
                out=hS.rearrange("p (k t) -> p k t", k=8),
                in_=HT_d[:, :, tt * 512:(tt + 1) * 512].rearrange("k p t -> p k t")), writes=[b_hS], dma=True)
            for m in range(12):
                if tt > 0:
                    E("dve", lambda e, m=m: e.tensor_copy(out=xin[:, m * XW_:m * XW_ + 3], in_=xin[:, m * XW_ + 512:m * XW_ + 515]),
                      reads=[b_xin[m]], writes=[b_xin[m]])
                bi = next_bank(0, 4)
                for k in range(8):
                    E("pe", lambda e, k=k, bi=bi, c0=m * 128, hS=hS: e.matmul(
                        bank(bi), lhsT=Wsa[:, k * NWA + c0:k * NWA + c0 + 128], rhs=hS[:, k * 512:(k + 1) * 512],
                        start=(k == 0), stop=(k == 7)),
                      reads=[b_Wsa, b_hS], writes=[PB[bi]] if k == 0 else (), pwrites=() if k == 0 else [PB[bi]])
                E("act", lambda e, m=m, bi=bi: e.copy(out=xin[:, m * XW_ + 3:m * XW_ + 515], in_=bank(bi)),
                  reads=[PB[bi]], writes=[b_xin[m]])
                acc = cv[:, m * 512:(m + 1) * 512]
                E("dve", lambda e, m=m, acc=acc: e.tensor_scalar(
                    out=acc, in0=xin[:, m * XW_ + 3:m * XW_ + 515], scalar1=cw[:, m * 4 + 3:m * 4 + 4],
                    scalar2=cbias[:, m:m + 1], op0=ALU.mult, op1=ALU.add),
                  reads=[b_xin[m], b_pcol], writes=[b_cv[m]])
                for kk in range(3):
                    E("dve", lambda e, m=m, acc=acc, kk=kk: e.scalar_tensor_tensor(
                        out=acc, in0=xin[:, m * XW_ + kk:m * XW_ + kk + 512], scalar=cw[:, m * 4 + kk:m * 4 + kk + 1],
                        in1=acc, op0=ALU.mult, op1=ALU.add),
                      reads=[b_xin[m], b_pcol, b_cv[m]], writes=[b_cv[m]])
                if m < 8:
                    E("act", lambda e, acc=acc: e.activation(out=acc, in_=acc, func=AF.Silu),
                      reads=[b_cv[m]], writes=[b_cv[m]])
                else:
                    dst = BCs[:, (m - 8) * 512:(m - 7) * 512]
                    E("act", lambda e, acc=acc, dst=dst: e.activation(out=dst, in_=acc, func=AF.Silu),
                      reads=[b_cv[m]], writes=[b_BCs] if m == 8 else (), pwrites=() if m == 8 else [b_BCs])
            E("pool", lambda e, tt=tt: e.dma_start(
                out=XS_d[:, tt * 512:(tt + 1) * 512].rearrange("(m p) t -> p m t", p=128),
                in_=cv[:, 0:8 * 512].rearrange("p (m t) -> p m t", m=8)), reads=b_cv[0:8], dma=True, key="cvo")
            E("pool", lambda e, tt=tt: e.dma_start(
                out=BC_d[:, tt * 512:(tt + 1) * 512].rearrange("(m p) t -> p m t", p=128),
                in_=BCs.rearrange("p (m t) -> p m t", m=4)), reads=[b_BCs], dma=True, key="bco")
            for m in range(8):
                bi = next_bank(0, 4)
                for k in range(8):
                    E("pe", lambda e, k=k, bi=bi, c0=16 + m * 128, hS=hS: e.matmul(
                        bank(bi), lhsT=Wsb[:, k * NWB2 + c0:k * NWB2 + c0 + 128], rhs=hS[:, k * 512:(k + 1) * 512],
                        start=(k == 0), stop=(k == 7)),
                      reads=[b_Wsb, b_hS], writes=[PB[bi]] if k == 0 else (), pwrites=() if k == 0 else [PB[bi]])
                E("act", lambda e, m=m, bi=bi: e.activation(out=ZS[:, m * 512:(m + 1) * 512], in_=bank(bi), func=AF.Silu),
                  reads=[PB[bi]], writes=[b_ZS] if m == 0 else (), pwrites=() if m == 0 else [b_ZS])
            E("pool", lambda e, tt=tt: e.dma_start(
                out=ZS_d[:, tt * 512:(tt + 1) * 512].rearrange("(m p) t -> p m t", p=128),
                in_=ZS.rearrange("p (m t) -> p m t", m=8)), reads=[b_ZS], dma=True, key="zso")
            for j in range(4):
                for k in range(8):
                    E("pe", lambda e, j=j, k=k, hS=hS: e.matmul(
                        bank(7)[:, j * 16:(j + 1) * 16], lhsT=hS[:, k * 512 + j * 128:k * 512 + (j + 1) * 128],
                        rhs=Wsb[:, k * NWB2:k * NWB2 + 16], start=(k == 0), stop=(k == 7)),
                      reads=[b_Wsb, b_hS], writes=[PB[7]] if (j == 0 and k == 0) else (),
                      pwrites=() if (j == 0 and k == 0) else [PB[7]])
            E("dve", lambda e: e.tensor_tensor(
                out=dtv.rearrange("p (j h) -> p j h", j=4), in0=bank(7)[:, 0:64].rearrange("p (j h) -> p j h", j=4),
                in1=prow[:, PR_DTB:PR_DTB + 16].unsqueeze(1).to_broadcast([128, 4, 16]), op=ALU.add),
              reads=[PB[7], b_prow], writes=[b_dtv])
            E("act", lambda e: e.activation(out=dtv, in_=dtv, func=AF.Exp), reads=[b_dtv], writes=[b_dtv])
            E("act", lambda e: e.activation(out=dts, in_=dtv, func=AF.Ln, bias=1.0), reads=[b_dtv], writes=[b_dts])
            E("dve", lambda e: e.tensor_tensor(
                out=lds.rearrange("p (j h) -> p j h", j=4), in0=dts.rearrange("p (j h) -> p j h", j=4),
                in1=a_bc.unsqueeze(1).to_broadcast([128, 4, 16]), op=ALU.mult),
              reads=[b_dts, b_abc], writes=[b_lds])
            E("pool", lambda e, tt=tt: e.dma_start(
                out=DT_d[tt * 512:(tt + 1) * 512, 0:16].rearrange("(j p) h -> p j h", p=128),
                in_=dts.rearrange("p (j h) -> p j h", j=4)), reads=[b_dts], dma=True, key="dto")
            E("pool", lambda e, tt=tt: e.dma_start(
                out=DT_d[tt * 512:(tt + 1) * 512, 16:32].rearrange("(j p) h -> p j h", p=128),
                in_=lds.rearrange("p (j h) -> p j h", j=4)), reads=[b_lds], dma=True, key="ldo")
        S_.barrier()
        A.top = base_mark

        NB = NCH
        QT = [A.bf16(S) for _ in range(1)] * 2; KT = [A.bf16(S) for _ in range(1)] * 2
        ZAq = [A.bf16(512) for _ in range(2)]; b_ZAq = [Buf("ZAq%d" % i) for i in range(2)]
        b_QT = [Buf("QT0")] * 2; b_KT = [Buf("KT0")] * 2
        NE, NL, NX, NWB = 3, 4, 2, 3
        eb = [A.f32(1024) for _ in range(NE)]; b_eb = [Buf("e%d" % i) for i in range(NE)]
        lb = [A.bf16(1024) for _ in range(NL)]; b_lb = [Buf("l%d" % i) for i in range(NL)]
        xb = [A.f32(1024) for _ in range(NX)]; b_xb = [Buf("xx%d" % i) for i in range(NX)]
        wb = [A.bf16(1024) for _ in range(NWB)]; b_wb = [Buf("w%d" % i) for i in range(NWB)]
        sqa = A.f32(512); b_sqa = Buf("sqa")
        ats = [A.bf16(512) for _ in range(2)]; b_ats = [Buf("ats%d" % i) for i in range(2)]
        atq = [0]

        SB7 = 7
        b7 = bank(SB7)
        xsI = [A.f32(1024) for _ in range(2)]; b_xsI = [Buf("xsI%d" % i) for i in range(2)]
        bcI = [A.bf16(512) for _ in range(2)]; b_bcI = [Buf("bcI%d" % i) for i in range(2)]
        zsI = [A.bf16(1024) for _ in range(2)]; b_zsI = [Buf("zsI%d" % i) for i in range(2)]
        dtI = [A.f32(32) for _ in range(2)]; b_dtI = [Buf("dtI%d" % i) for i in range(2)]
        Rr = A.f32(16 * 128); b_R = Buf("R")
        decay = A.f32(16 * 128); b_decay = Buf("decay")
        Mm = A.bf16(16 * 128); b_M = Buf("M")
        CE = A.bf16(16 * 128); b_CE = Buf("CE")
        xtf = A.f32(1024); b_xtf = Buf("xtf")
        xtb = A.bf16(1024); b_xtb = Buf("xtb")
        Btk = A.bf16(256); b_Btk = Buf("Btk")
        XWt = A.bf16(1024); b_XW = Buf("XW")
        cbs = A.f32(256); b_cbs = Buf("cbs")
        dcd = A.f32(32); b_dcd = Buf("dcd")
        wgt = A.f32(16); b_wgt = Buf("wgt")
        state = A.f32(1024); b_state = Buf("state")
        stateb = A.bf16(1024); b_stateb = Buf("stateb")
        stsT = A.bf16(8 * 512); b_stsT = Buf("stsT")
        ub = Rr[:, 0:1024]
        sqs = Rr[:, 1024:2048]
        E("dve", lambda e: e.memset(state, 0.0), writes=[b_state])
        E("dve", lambda e: e.memset(stateb, 0.0), writes=[b_stateb])

        def ssd_load(c):
            sl = c % 2
            E("sp", lambda e: e.dma_start(out=xsI[sl].rearrange("p (m t) -> p m t", m=8),
                                          in_=XS_d[:, c * 128:(c + 1) * 128].rearrange("(m p) t -> p m t", p=128)),
              writes=[b_xsI[sl]], dma=True)
            E("sp", lambda e: e.dma_start(out=bcI[sl].rearrange("p (m t) -> p m t", m=4),
                                          in_=BC_d[:, c * 128:(c + 1) * 128].rearrange("(m p) t -> p m t", p=128)),
              writes=[b_bcI[sl]], dma=True)
            E("sp", lambda e: e.dma_start(out=zsI[sl].rearrange("p (m t) -> p m t", m=8),
                                          in_=ZS_d[:, c * 128:(c + 1) * 128].rearrange("(m p) t -> p m t", p=128)),
              writes=[b_zsI[sl]], dma=True)
            E("sp", lambda e: e.dma_start(out=dtI[sl], in_=DT_d[c * 128:(c + 1) * 128, :]), writes=[b_dtI[sl]], dma=True)

        def ssd_chunk(c):
            sl = c % 2
            xsT, bc_, zs_, dt_ = xsI[sl], bcI[sl], zsI[sl], dtI[sl]
            bx, bb, bz, bd = b_xsI[sl], b_bcI[sl], b_zsI[sl], b_dtI[sl]
            dtj = dt_[:, 0:16]
            ldj = dt_[:, 16:32]
            jq = c % 4
            if c + 1 < NCH:
                ssd_load(c + 1)
            R3 = Rr.rearrange("p (h l) -> p h l", h=16)

            def m_ops(q):
                for h in range(q * 4, q * 4 + 4):
                    g = h // 8
                    E("dve", lambda e, h=h, g=g: e.scalar_tensor_tensor(
                        out=Mm[:, h * 128:(h + 1) * 128], in0=decay[:, h * 128:(h + 1) * 128], scalar=dtj[:, h:h + 1],
                        in1=cbs[:, g * 128:(g + 1) * 128], op0=ALU.mult, op1=ALU.mult),
                      reads=[b_decay, bd, b_cbs], writes=[b_M] if h == 0 else (), pwrites=() if h == 0 else [b_M])

            def ce_op(g):
                E("dve", lambda e: e.tensor_tensor(
                    out=CE[:, g * 1024:(g + 1) * 1024].rearrange("p (h l) -> p h l", h=8),
                    in0=decay[:, g * 1024:(g + 1) * 1024].rearrange("p (h l) -> p h l", h=8),
                    in1=bc_[:, 256 + g * 128:256 + (g + 1) * 128].unsqueeze(1).to_broadcast([128, 8, 128]), op=ALU.mult),
                  reads=[b_decay, bb], writes=[b_CE] if g == 0 else (), pwrites=() if g == 0 else [b_CE])

            for hf in range(2):
                E("dve", lambda e, hf=hf: e.tensor_tensor(
                    out=R3[:, hf * 8:(hf + 1) * 8, :], in0=LEf.unsqueeze(1).to_broadcast([128, 8, 128]),
                    in1=ldj[:, hf * 8:(hf + 1) * 8].unsqueeze(2).to_broadcast([128, 8, 128]), op=ALU.mult),
                  reads=[b_cf, bd], writes=[b_R] if hf == 0 else (), pwrites=() if hf == 0 else [b_R])
                for m4 in range(4):
                    m = hf * 4 + m4
                    E("pe", lambda e, m=m, m4=m4: e.transpose(b7[:, m4 * 128:(m4 + 1) * 128], xsT[:, m * 128:(m + 1) * 128], idf),
                      reads=[bx, b_cf], writes=[PB[SB7]] if m4 == 0 else (), pwrites=() if m4 == 0 else [PB[SB7]])
                E("dve", lambda e, hf=hf: e.tensor_copy(out=xtf[:, hf * 512:(hf + 1) * 512], in_=b7), reads=[PB[SB7]],
                  writes=[b_xtf] if hf == 0 else (), pwrites=() if hf == 0 else [b_xtf])
                yield
            E("dve", lambda e: e.tensor_copy(out=xtb, in_=xtf), reads=[b_xtf], writes=[b_xtb])
            b7b = b7[:, 0:128].bitcast(BF16)
            for g in range(2):
                E("pe", lambda e, g=g: e.transpose(b7b[:, g * 128:(g + 1) * 128], bc_[:, g * 128:(g + 1) * 128], idb),
                  reads=[bb, b_cb], writes=[PB[SB7]] if g == 0 else (), pwrites=() if g == 0 else [PB[SB7]])
            for g in range(2):
                E("pe", lambda e, g=g: e.matmul(b7[:, 256 + g * 128:256 + (g + 1) * 128], lhsT=bc_[:, g * 128:(g + 1) * 128],
                                                rhs=bc_[:, 256 + g * 128:256 + (g + 1) * 128], start=True, stop=True),
                  reads=[bb], pwrites=[PB[SB7]])
            E("dve", lambda e: e.tensor_copy(out=Btk, in_=b7b), reads=[PB[SB7]], writes=[b_Btk])
            E("dve", lambda e: e.tensor_copy(out=cbs, in_=b7[:, 256:512]), reads=[PB[SB7]], writes=[b_cbs])
            yield
            for q in range(4):
                E("pe", lambda e, q=q: e.matmul(b7, lhsT=GTf, rhs=Rr[:, q * 512:(q + 1) * 512], start=True, stop=False,
                                                skip_group_check=True), reads=[b_R, b_cf], writes=[PB[SB7]])
                E("pe", lambda e: e.matmul(b7, lhsT=idb, rhs=neg4, start=False, stop=True, skip_group_check=True),
                  reads=[b_cb, b_neg4], pwrites=[PB[SB7]])
                E("act", lambda e, q=q: e.activation(out=decay[:, q * 512:(q + 1) * 512], in_=b7, func=AF.Exp),
                  reads=[PB[SB7]], writes=[b_decay] if q == 0 else (), pwrites=() if q == 0 else [b_decay])
                if q >= 1:
                    m_ops(q - 1)
                yield
            E("pe", lambda e: e.matmul(b7[:, 0:16], lhsT=GTf, rhs=ldj, start=True, stop=True),
              reads=[bd, b_cf], writes=[PB[SB7]])
            E("pe", lambda e: e.matmul(b7[:, 16:32], lhsT=onesf, rhs=ldj, start=True, stop=True),
              reads=[bd, b_cf], pwrites=[PB[SB7]])
            E("act", lambda e: e.activation(out=dcd, in_=b7[:, 0:32], func=AF.Exp), reads=[PB[SB7]], writes=[b_dcd])
            m_ops(3)
            yield
            for q in range(4):
                E("pe", lambda e, q=q: e.matmul(b7, lhsT=onesf, rhs=Rr[:, q * 512:(q + 1) * 512], start=True, stop=True),
                  reads=[b_R, b_cf], writes=[PB[SB7]])
                E("act", lambda e, q=q: e.activation(out=decay[:, q * 512:(q + 1) * 512], in_=b7, func=AF.Exp),
                  reads=[PB[SB7]], writes=[b_decay] if q == 0 else (), pwrites=() if q == 0 else [b_decay])
                if q == 0:
                    E("dve", lambda e: e.tensor_tensor(out=wgt, in0=dtj, in1=dcd[:, 0:16], op=ALU.mult),
                      reads=[bd, b_dcd], writes=[b_wgt])
                    E("dve", lambda e: e.tensor_tensor(
                        out=XWt.rearrange("p (h q) -> p h q", h=16), in0=xtf.rearrange("p (h q) -> p h q", h=16),
                        in1=wgt.unsqueeze(2).to_broadcast([128, 16, 64]), op=ALU.mult),
                      reads=[b_xtf, b_wgt], writes=[b_XW])
                if q == 2:
                    ce_op(0)
                yield
            ce_op(1)
            yield
            for hb in range(2):
                for p4 in range(4):
                    pr = hb * 4 + p4
                    csl = slice(p4 * 128, (p4 + 1) * 128)
                    for half in range(2):
                        h = 2 * pr + half
                        first = (p4 == 0 and half == 0)
                        E("pe", lambda e, csl=csl, half=half, h=h: e.matmul(
                            b7[64 * half:64 * half + 64, csl], lhsT=xtb[:, h * 64:(h + 1) * 64],
                            rhs=Mm[:, h * 128:(h + 1) * 128], start=True, stop=False, skip_group_check=True),
                          reads=[b_xtb, b_M], writes=[PB[SB7]] if first else (), pwrites=() if first else [PB[SB7]])
                        E("pe", lambda e, csl=csl, half=half, h=h: e.matmul(
                            b7[64 * half:64 * half + 64, csl], lhsT=stateb[:, h * 64:(h + 1) * 64],
                            rhs=CE[:, h * 128:(h + 1) * 128], start=False, stop=True, skip_group_check=True),
                          reads=[b_stateb, b_CE], pwrites=[PB[SB7]])
                usl = ub[:, hb * 512:(hb + 1) * 512]
                u3 = usl.rearrange("p (m t) -> p m t", m=4)
                E("dve", lambda e, hb=hb, u3=u3: e.tensor_tensor(
                    out=u3, in0=xsT[:, hb * 512:(hb + 1) * 512].rearrange("p (m t) -> p m t", m=4),
                    in1=pcol[:, PC_D + hb * 4:PC_D + hb * 4 + 4].unsqueeze(2).to_broadcast([128, 4, 128]), op=ALU.mult),
                  reads=[bx, b_pcol], writes=[b_R] if hb == 0 else (), pwrites=() if hb == 0 else [b_R])
                E("dve", lambda e, usl=usl: e.tensor_tensor(out=usl, in0=usl, in1=b7, op=ALU.add),
                  reads=[b_R, PB[SB7]], pwrites=[b_R])
                yield
                E("dve", lambda e, hb=hb, usl=usl: e.tensor_tensor(out=usl, in0=usl, in1=zs_[:, hb * 512:(hb + 1) * 512],
                                                                  op=ALU.mult), reads=[b_R, bz], pwrites=[b_R])
                o3 = stsT.rearrange("p (m t) -> p m t", m=8)[:, hb * 4:hb * 4 + 4, jq * 128:(jq + 1) * 128]
                E("dve", lambda e, hb=hb, u3=u3, o3=o3: e.tensor_tensor(
                    out=o3, in0=u3,
                    in1=pcol[:, PC_GS + hb * 4:PC_GS + hb * 4 + 4].unsqueeze(2).to_broadcast([128, 4, 128]), op=ALU.mult),
                  reads=[b_R, b_pcol], pwrites=[b_stsT])
                yield
            E("dve", lambda e: e.tensor_tensor(out=sqs, in0=ub, in1=ub, op=ALU.mult), reads=[b_R], pwrites=[b_R])
            yield
            for pr in range(8):
                E("pe", lambda e, pr=pr: e.matmul(b7[:, 0:1], lhsT=sqs[:, pr * 128:(pr + 1) * 128], rhs=onesf[:, 0:1],
                                                 start=(pr == 0), stop=(pr == 7)),
                  reads=[b_R, b_cf], writes=[PB[SB7]] if pr == 0 else (), pwrites=() if pr == 0 else [PB[SB7]])
            E("dve", lambda e: e.tensor_copy(out=ssq_s[:, c:c + 1], in_=b7[:, 0:1]), reads=[PB[SB7]], pwrites=[b_ssqs])
            yield
            for g in range(2):
                E("pe", lambda e, g=g: e.matmul(b7, lhsT=Btk[:, g * 128:(g + 1) * 128], rhs=XWt[:, g * 512:(g + 1) * 512],
                                                start=True, stop=True), reads=[b_Btk, b_XW], writes=[PB[SB7]])
                ssl = state[:, g * 512:(g + 1) * 512]
                E("dve", lambda e, g=g, ssl=ssl: e.tensor_tensor(
                    out=ssl.rearrange("p (h q) -> p h q", h=8), in0=ssl.rearrange("p (h q) -> p h q", h=8),
                    in1=dcd[:, 16 + g * 8:16 + g * 8 + 8].unsqueeze(2).to_broadcast([128, 8, 64]), op=ALU.mult),
                  reads=[b_state, b_dcd], writes=[b_state] if g == 0 else (), pwrites=() if g == 0 else [b_state])
                E("dve", lambda e, ssl=ssl: e.tensor_tensor(out=ssl, in0=ssl, in1=b7, op=ALU.add),
                  reads=[b_state, PB[SB7]], pwrites=[b_state])
                yield
            E("dve", lambda e: e.tensor_copy(out=stateb, in_=state), reads=[b_state], writes=[b_stateb])
            if jq == 3:
                tt_ = c // 4
                E("sp", lambda e: e.dma_start(
                    out=AT_d[1024:2048, tt_ * 512:(tt_ + 1) * 512].rearrange("(m p) t -> p m t", p=128),
                    in_=stsT.rearrange("p (m t) -> p m t", m=8)), reads=[b_stsT], dma=True)
            yield

        def ssd_all():
            for c in range(NCH):
                for _ in ssd_chunk(c):
                    yield

        ssd_load(0)
        ssd_gen = ssd_all()

        def ssd_step():
            try:
                next(ssd_gen)
                return True
            except StopIteration:
                return False

        def v2(ap, c0):
            return ap.rearrange("p (b c) -> p b c", b=2)[:, :, c0:512]

        def load_hp(hp):
            r = hp % 2
            E("sp", lambda e: e.dma_start(out=QT[r], in_=QT_d[hp * 128:(hp + 1) * 128, :]), writes=[b_QT[r]], dma=True)
            E("sp", lambda e: e.dma_start(out=KT[r], in_=KT_d[hp * 128:(hp + 1) * 128, :]), writes=[b_KT[r]], dma=True)

        load_hp(0)
        ucount = [0]

        def do_hp(hp, r):
            if hp > 0:
                load_hp(hp)
            steps = []
            for qt in range(NT):
                for jj in range(4 * qt + 3, -1, -1):
                    steps.append((qt, jj))
            nu = len(steps)
            info = {}

            def st_qk(n):
                qt, jj = steps[n]
                u = ucount[0] + n
                kk = jj - 4 * qt
                c0 = 128 * kk if kk > 0 else 0
                ap_ = 2 * (u % 2)
                info[n] = (qt, jj, u, kk, c0, ap_)
                if jj == 4 * qt + 3:
                    zi = (hp * NT + qt) % 2
                    E("sp", lambda e, zi=zi: e.dma_start(out=ZAq[zi], in_=ZA_d[hp * 128:(hp + 1) * 128, qt * 512:(qt + 1) * 512]),
                      writes=[b_ZAq[zi]], dma=True)
                for half in range(2):
                    pb = 64 * half
                    E("pe", lambda e, pb=pb, half=half: e.matmul(
                        bank(ap_ + half)[:, c0:512], lhsT=KT[r][pb:pb + 64, jj * 128:(jj + 1) * 128],
                        rhs=QT[r][pb:pb + 64, qt * 512 + c0:(qt + 1) * 512], start=True, stop=True),
                      reads=[b_KT[r], b_QT[r]], writes=[PB[ap_ + half]])

            def st_act1(n):
                qt, jj, u, kk, c0, ap_ = info[n]
                ei, li = u % NE, u % NL
                E("act", lambda e: e.activation(out=v2(eb[ei], c0), in_=v2(psum[:, ap_ * 512:(ap_ + 2) * 512], c0), func=AF.Exp),
                  reads=[PB[ap_], PB[ap_ + 1]], writes=[b_eb[ei]])
                if kk >= 0:
                    ev = eb[ei].rearrange("p (b c) -> p b c", b=2)[:, :, c0:c0 + 128]
                    E("pool", lambda e: e.tensor_tensor(out=ev, in0=ev, in1=LTf.unsqueeze(1).to_broadcast([128, 2, 128]),
                                                        op=ALU.mult), reads=[b_eb[ei], b_cf], writes=[b_eb[ei]])

            def st_act2(n):
                qt, jj, u, kk, c0, ap_ = info[n]
                ei, li = u % NE, u % NL
                E("act", lambda e: e.activation(out=v2(lb[li], c0), in_=v2(eb[ei], c0), func=AF.Ln, bias=1.0),
                  reads=[b_eb[ei]], writes=[b_lb[li]])

            def st_ge(n):
                qt, jj, u, kk, c0, ap_ = info[n]
                li, xi = u % NL, u % NX
                for half in range(2):
                    cbk = 4 + half
                    if jj == 4 * qt + 3:
                        E("pe", lambda e, cbk=cbk: e.matmul(bank(cbk), lhsT=zb[:, 0:128], rhs=zb, start=True, stop=False,
                                                            skip_group_check=True), reads=[b_zb], writes=[PB[cbk]])
                    E("pe", lambda e, cbk=cbk, half=half: e.matmul(
                        bank(cbk)[:, c0:512], lhsT=GEb, rhs=lb[li][:, half * 512 + c0:(half + 1) * 512],
                        start=False, stop=False, skip_group_check=True), reads=[b_lb[li], b_cb], writes=[PB[cbk]])
                E("act", lambda e: e.activation(out=v2(xb[xi], c0), in_=v2(psum[:, 2048:3072], c0), func=AF.Exp, scale=-1.0),
                  reads=[PB[4], PB[5]], writes=[b_xb[xi]])

            def st_lt(n):
                qt, jj, u, kk, c0, ap_ = info[n]
                if jj == 0:
                    return
                li = u % NL
                for half in range(2):
                    cbk = 4 + half
                    E("pe", lambda e, cbk=cbk, half=half: e.matmul(
                        bank(cbk)[:, c0:512], lhsT=LTb, rhs=lb[li][:, half * 512 + c0:(half + 1) * 512],
                        start=False, stop=False, skip_group_check=True), reads=[b_lb[li], b_cb], writes=[PB[cbk]])

            def st_w(n):
                qt, jj, u, kk, c0, ap_ = info[n]
                ei, xi, wi = u % NE, u % NX, u % NWB
                E("dve", lambda e: e.tensor_tensor(out=v2(wb[wi], c0), in0=v2(eb[ei], c0), in1=v2(xb[xi], c0), op=ALU.mult),
                  reads=[b_eb[ei], b_xb[xi]], writes=[b_wb[wi]])

            def st_pv(n):
                qt, jj, u, kk, c0, ap_ = info[n]
                wi = u % NWB
                ob = 6
                for half in range(2):
                    pb = 64 * half
                    hh = 2 * hp + half
                    if jj == 4 * qt + 3:
                        E("pe", lambda e, pb=pb: e.matmul(bank(ob)[pb:pb + 64, :], lhsT=zb[:, 0:64], rhs=zb, start=True,
                                                          stop=False, skip_group_check=True), reads=[b_zb],
                          writes=[PB[ob]] if half == 0 else (), pwrites=() if half == 0 else [PB[ob]])
                    E("pe", lambda e, pb=pb, half=half, hh=hh: e.matmul(
                        bank(ob)[pb:pb + 64, c0:512], lhsT=Vall[:, jj * 1024 + hh * 64:jj * 1024 + hh * 64 + 64],
                        rhs=wb[wi][:, half * 512 + c0:(half + 1) * 512], start=False, stop=(jj == 0), skip_group_check=True),
                      reads=[b_Vall, b_wb[wi]], pwrites=[PB[ob]])

            def st_epi(n):
                qt, jj, u, kk, c0, ap_ = info[n]
                ob = 6
                if jj == 0:
                    E("act", lambda e: e.activation(out=sqa, in_=bank(ob), func=AF.Square), reads=[PB[ob]], writes=[b_sqa])
                    for sub in range(4):
                        E("pe", lambda e, sub=sub: e.matmul(bank(0)[:, sub:sub + 1], lhsT=sqa[:, sub * 128:(sub + 1) * 128],
                                                            rhs=onesf[:, 0:1], start=True, stop=True),
                          reads=[b_sqa, b_cf], writes=[PB[0]] if sub == 0 else (), pwrites=() if sub == 0 else [PB[0]])
                    if hp == 0:
                        E("dve", lambda e: e.tensor_copy(out=ssq_a[:, 4 * qt:4 * qt + 4], in_=bank(0)[:, 0:4]),
                          reads=[PB[0]], pwrites=[b_ssqa])
                    else:
                        E("dve", lambda e: e.tensor_tensor(out=ssq_a[:, 4 * qt:4 * qt + 4], in0=ssq_a[:, 4 * qt:4 * qt + 4],
                                                           in1=bank(0)[:, 0:4], op=ALU.add),
                          reads=[PB[0], b_ssqa], writes=[b_ssqa])
                    ai = atq[0] % 2
                    atq[0] += 1
                    E("dve", lambda e: e.scalar_tensor_tensor(
                        out=ats[ai], in0=bank(ob), scalar=pcol[:, PC_GA + hp:PC_GA + hp + 1],
                        in1=ZAq[(hp * NT + qt) % 2], op0=ALU.mult, op1=ALU.mult),
                      reads=[PB[ob], b_pcol, b_ZAq[(hp * NT + qt) % 2]], writes=[b_ats[ai]])
                    E("sp", lambda e: e.dma_start(out=AT_d[hp * 128:(hp + 1) * 128, qt * 512:(qt + 1) * 512], in_=ats[ai]),
                      reads=[b_ats[ai]], dma=True)

            for n in range(nu + 3):
                if n < nu:
                    st_qk(n)
                if 0 <= n - 2 < nu:
                    st_lt(n - 2)
                if n < nu:
                    st_act1(n)
                if 0 <= n - 1 < nu:
                    st_ge(n - 1)
                if n < nu:
                    st_act2(n)
                if 0 <= n - 1 < nu:
                    st_w(n - 1)
                if 0 <= n - 2 < nu:
                    st_pv(n - 2)
                if 0 <= n - 2 < nu:
                    st_epi(n - 2)
                ssd_step()
            ucount[0] += nu

        for hp_ in range(8):
            do_hp(hp_, hp_ % 2)
        while ssd_step():
            pass
        S_.barrier()
        A.top = base_mark

        A.nw = NW_SBUF
        Wg = A.bf16(16 * 1024); b_Wg = Buf("Wg")
        wst = [A.f32(4096) for _ in range(2)]; b_wst = [Buf("wst%d" % i) for i in range(2)]
        wout_v = wout_d.rearrange("(m p) c -> p m c", p=128)
        for ch in range(4):
            rr = ch % 2
            E("sp", lambda e, ch=ch, rr=rr: e.dma_start(out=wst[rr].rearrange("p (m c) -> p m c", m=4),
                                                      in_=wout_v[:, ch * 4:(ch + 1) * 4, :]), writes=[b_wst[rr]], dma=True)
            E("dve" if ch % 2 == 0 else "pool", lambda e, ch=ch, rr=rr: e.tensor_tensor(
                out=Wg[:, ch * 4096:(ch + 1) * 4096].rearrange("p (m c) -> p m c", m=4),
                in0=wst[rr].rearrange("p (m c) -> p m c", m=4),
                in1=gate_bc.unsqueeze(1).to_broadcast([128, 4, 1024]), op=ALU.mult),
              reads=[b_wst[rr], b_gate], pwrites=[b_Wg])
        for (src, dst) in ((ssq_a, rstd_a), (ssq_s, rstd_s)):
            E("act", lambda e, src=src, dst=dst: e.activation(out=dst, in_=src, func=AF.Ln, scale=1.0 / 1024, bias=EPS),
              reads=[b_ssqa, b_ssqs], pwrites=[b_rstd])
            E("act", lambda e, dst=dst: e.activation(out=dst, in_=dst, func=AF.Exp, scale=-0.5),
              reads=[b_rstd], pwrites=[b_rstd])
        ATt = [A.bf16(16 * 512) for _ in range(2)]; b_ATt = [Buf("ATt%d" % i) for i in range(2)]
        xo = [A.f32(4096) for _ in range(2)]; b_xo = [Buf("xo%d" % i) for i in range(2)]
        yb = [A.f32(1024) for _ in range(2)]; b_yb = [Buf("yb%d" % i) for i in range(2)]
        junk3 = A.bf16(1024); b_junk3 = Buf("junk3")
        b_out = Buf("outd")
        yq = [0]
        gf_bc = prow[:, PR_GF:PR_GF + 1024]
        for tt in range(NT):
            r = tt % 2
            E("sp", lambda e, tt=tt, r=r: e.dma_start(
                out=ATt[r].rearrange("p (m t) -> p m t", m=16),
                in_=AT_d[:, tt * 512:(tt + 1) * 512].rearrange("(m p) t -> p m t", p=128)), writes=[b_ATt[r]], dma=True)
            E("sp", lambda e, tt=tt, r=r: e.dma_start(
                out=xo[r].rearrange("p (j d) -> p j d", j=4),
                in_=x_d[tt * 512:(tt + 1) * 512, :].rearrange("(j p) d -> p j d", p=128)), writes=[b_xo[r]], dma=True)
            for j in range(4):
                c = tt * 4 + j
                yi = yq[0] % 2
                yq[0] += 1
                for n in range(2):
                    ba = 4 * (c % 2) + 2 * n
                    bs = ba + 1
                    for m in range(8):
                        E("pe", lambda e, r=r, m=m, j=j, n=n, ba=ba: e.matmul(
                            bank(ba), lhsT=ATt[r][:, m * 512 + j * 128:m * 512 + (j + 1) * 128],
                            rhs=Wg[:, m * 1024 + n * 512:m * 1024 + (n + 1) * 512], start=(m == 0), stop=(m == 7)),
                          reads=[b_ATt[r], b_Wg], writes=[PB[ba]] if m == 0 else (), pwrites=() if m == 0 else [PB[ba]])
                    for m in range(8, 16):
                        E("pe", lambda e, r=r, m=m, j=j, n=n, bs=bs: e.matmul(
                            bank(bs), lhsT=ATt[r][:, m * 512 + j * 128:m * 512 + (j + 1) * 128],
                            rhs=Wg[:, m * 1024 + n * 512:m * 1024 + (n + 1) * 512], start=(m == 8), stop=(m == 15)),
                          reads=[b_ATt[r], b_Wg], writes=[PB[bs]] if m == 8 else (), pwrites=() if m == 8 else [PB[bs]])
                    ysl = yb[yi][:, n * 512:(n + 1) * 512]
                    E("dve", lambda e, r=r, j=j, n=n, ba=ba, ysl=ysl, c=c: e.scalar_tensor_tensor(
                        out=ysl, in0=bank(ba), scalar=rstd_a[:, c:c + 1],
                        in1=xo[r][:, j * 1024 + n * 512:j * 1024 + (n + 1) * 512], op0=ALU.mult, op1=ALU.add),
                      reads=[PB[ba], b_rstd, b_xo[r]], writes=[b_yb[yi]] if n == 0 else (), pwrites=() if n == 0 else [b_yb[yi]])
                    E("dve", lambda e, bs=bs, ysl=ysl, c=c: e.scalar_tensor_tensor(
                        out=ysl, in0=bank(bs), scalar=rstd_s[:, c:c + 1], in1=ysl, op0=ALU.mult, op1=ALU.add),
                      reads=[PB[bs], b_rstd, b_yb[yi]], pwrites=[b_yb[yi]])
                E("act", lambda e, yi=yi, c=c: e.activation(out=junk3, in_=yb[yi], func=AF.Square,
                                                            accum_out=ssqf[:, c:c + 1]),
                  reads=[b_yb[yi]], writes=[b_junk3], pwrites=[b_ssqf])
                E("act", lambda e, c=c: e.activation(out=rstdf[:, c:c + 1], in_=ssqf[:, c:c + 1], func=AF.Ln,
                                                     scale=1.0 / 1024, bias=EPS), reads=[b_ssqf], pwrites=[b_rstdf])
                E("act", lambda e, c=c: e.activation(out=rstdf[:, c:c + 1], in_=rstdf[:, c:c + 1], func=AF.Exp, scale=-0.5),
                  reads=[b_rstdf], pwrites=[b_rstdf])
                E("dve", lambda e, yi=yi, c=c: e.scalar_tensor_tensor(
                    out=yb[yi], in0=yb[yi], scalar=rstdf[:, c:c + 1], in1=gf_bc, op0=ALU.mult, op1=ALU.mult),
                  reads=[b_yb[yi], b_rstdf, b_prow], writes=[b_yb[yi]])
                E("pool", lambda e, yi=yi, c=c: e.dma_start(out=out_d[c * 128:(c + 1) * 128, :], in_=yb[yi]),
                  reads=[b_yb[yi]], pwrites=[b_out], dma=True, key="yo%d" % yi)
        E("sp", lambda e: e.nop(), reads=[b_out], real=False)
        S_.run()
    return nc


def _consts():
    a = np.arange(128)[:, None]
    b = np.arange(128)[None, :]
    c = np.zeros((128, NCONST), np.float32)
    c[:, C_ID:C_ID + 128] = (a == b)
    c[:, C_GE:C_GE + 128] = (a >= b)
    c[:, C_LT:C_LT + 128] = (a < b)
    c[:, C_LE:C_LE + 128] = (a <= b)
    c[:, C_GT:C_GT + 128] = (a > b)
    c[:, C_NEG:C_NEG + 128] = NEGV * (a > b)
    c[:, C_ONE:C_ONE + 128] = 1.0
    return c


def _col(v, n):
    return np.ascontiguousarray(np.asarray(v, np.float32).reshape(n, 128).T)


def make_in_maps(x, c, w_ada, b_ada, norm_in_gain, w_in, conv_w, conv_b, dt_bias, a_log, d_skip,
                 sb_norm_gain, ssm_norm_gain, w_out, norm_f_gain, cores=None):
    B = x.shape[0]
    consts = _consts()
    maps = []
    w_ada0 = np.ascontiguousarray(w_ada[0], np.float32)
    w_in0 = np.ascontiguousarray(w_in[0], np.float32)
    w_out0 = np.ascontiguousarray(w_out[0], np.float32)
    prow = np.zeros((1, NPROW), np.float32)
    prow[0, PR_BG:PR_BG + 1024] = b_ada[0, 2048:3072]
    prow[0, PR_GF:PR_GF + 1024] = norm_f_gain
    prow[0, PR_DTB:PR_DTB + 16] = dt_bias[0]
    prow[0, PR_AL:PR_AL + 16] = a_log[0]
    for b in (range(B) if cores is None else cores):
        pcol = np.zeros((128, NPCOL), np.float32)
        pcol[:, PC_C:PC_C + 8] = _col(c[b], 8)
        pcol[:, PC_BADA:PC_BADA + 24] = _col(b_ada[0], 24)
        pcol[:, PC_GIN:PC_GIN + 8] = _col(norm_in_gain[0], 8)
        pcol[:, PC_GA:PC_GA + 8] = _col(sb_norm_gain[0], 8)
        pcol[:, PC_GS:PC_GS + 8] = _col(ssm_norm_gain[0], 8)
        pcol[:, PC_D:PC_D + 8] = _col(np.repeat(np.asarray(d_skip[0], np.float32), 64), 8)
        cw = np.asarray(conv_w[0], np.float32)
        pcol[:, PC_CW:PC_CW + 48] = cw.T.reshape(12, 128, 4).transpose(1, 0, 2).reshape(128, 48)
        pcol[:, PC_CB:PC_CB + 12] = _col(conv_b[0], 12)
        maps.append({"x": np.ascontiguousarray(x[b], np.float32), "w_ada": w_ada0, "w_in": w_in0, "w_out": w_out0,
                     "consts": consts, "pcol": pcol, "prow": prow})
    return maps


_NC_CACHE = {}


def kernel(x, c, w_ada, b_ada, norm_in_gain, w_in, conv_w, conv_b, dt_bias, a_log, d_skip,
           sb_norm_gain, ssm_norm_gain, w_out, norm_f_gain):
    x = np.asarray(x)
    B, S, _ = x.shape
    if S not in _NC_CACHE:
        _NC_CACHE[S] = build(S)
    nc = _NC_CACHE[S]
    maps = make_in_maps(x, np.asarray(c), np.asarray(w_ada), np.asarray(b_ada), np.asarray(norm_in_gain),
                        np.asarray(w_in), np.asarray(conv_w), np.asarray(conv_b), np.asarray(dt_bias),
                        np.asarray(a_log), np.asarray(d_skip), np.asarray(sb_norm_gain),
                        np.asarray(ssm_norm_gain), np.asarray(w_out), np.asarray(norm_f_gain))
    res = run_bass_kernel_spmd(nc, maps, core_ids=list(range(B)))
    return np.stack([np.asarray(r["out"], np.float32) for r in res.results], axis=0)
```

```python
import numpy as np
from contextlib import ExitStack
import concourse.bass as bass
import concourse.mybir as mybir
from concourse.bass_utils import run_bass_kernel_spmd

F32 = mybir.dt.float32
BF16 = mybir.dt.bfloat16
AF = mybir.ActivationFunctionType
ALU = mybir.AluOpType
AX = mybir.AxisListType

ENGS = ("pe", "act", "dve", "pool", "sp")
EPS = 1e-6
NEGV = -30000.0


class Buf:
    __slots__ = ("name", "writers", "readers", "const")

    def __init__(self, name, const=False):
        self.name = name
        self.writers = []
        self.readers = []
        self.const = const


class Op:
    __slots__ = ("eng", "fn", "deps", "dma", "idx", "need_sig", "val", "key", "real")

    def __init__(self, eng, fn, dma, key, real):
        self.eng = eng
        self.fn = fn
        self.dma = dma
        self.key = key
        self.real = real
        self.deps = ()
        self.idx = -1
        self.need_sig = False
        self.val = 0


class Sched:
    def __init__(self, nc):
        self.nc = nc
        self.ops = {e: [] for e in ENGS}
        self.last_dma = {}
        self.groups = {e: {} for e in ENGS}
        self._gstart = {}

    def group_begin(self, eng):
        self._gstart[eng] = len(self.ops[eng])

    def group_end(self, eng):
        g0 = self._gstart.pop(eng)
        g1 = len(self.ops[eng])
        if g1 > g0 + 1:
            self.groups[eng][g0] = g1

    def emit(self, eng, fn, reads=(), writes=(), pwrites=(), dma=False, key=None, extra=(), real=True):
        if dma and key is None:
            key = writes[0].name if writes else (pwrites[0].name if pwrites else reads[0].name)
        op = Op(eng, fn, dma, key, real)
        deps = set(extra)
        for b in reads:
            deps.update(b.writers)
        for b in writes:
            deps.update(b.writers)
            deps.update(b.readers)
        for b in pwrites:
            deps.update(b.readers)
            if b.writers:
                deps.add(b.writers[0])
        op.deps = deps
        for b in reads:
            if not b.const:
                b.readers.append(op)
        for b in writes:
            b.writers = [op]
            b.readers = []
        for b in pwrites:
            b.writers.append(op)
        op.idx = len(self.ops[eng])
        self.ops[eng].append(op)
        if dma:
            self.last_dma[key] = op
        return op

    def barrier(self):
        lasts = []
        for e in ENGS:
            for op in reversed(self.ops[e]):
                if op.real and not op.dma:
                    lasts.append(op)
                    break
        deps = set(lasts) | set(self.last_dma.values())
        for e in ENGS:
            self.emit(e, lambda g: g.nop(), extra=deps, real=False)

    def _needed(self, op):
        comp = {}
        dmas = []
        for d in op.deps:
            if d.dma:
                dmas.append(d)
                continue
            if d.eng == op.eng and not op.dma:
                if d.eng == "pe":
                    continue
                if d.eng in ("act", "dve") and d.idx < op.idx - 2:
                    continue
            cur = comp.get(d.eng)
            if cur is None or d.idx > cur.idx:
                comp[d.eng] = d
        return comp, dmas

    def run(self):
        nc = self.nc
        for e in ENGS:
            for op in self.ops[e]:
                comp, _ = self._needed(op)
                for d in comp.values():
                    d.need_sig = True
        for e in ENGS:
            cnt = 0
            for op in self.ops[e]:
                if op.dma:
                    continue
                if op.need_sig:
                    cnt += 1
                    op.val = cnt
        keycnt = {}
        for e in ENGS:
            for op in self.ops[e]:
                if op.dma:
                    keycnt[op.key] = keycnt.get(op.key, 0) + 1
                    op.val = 16 * keycnt[op.key]
        keys = sorted(keycnt.keys())
        with ExitStack() as st:
            esem = {e: st.enter_context(nc.semaphore("s_" + e)) for e in ENGS}
            ksem = {k: st.enter_context(nc.semaphore("d%d" % i)) for i, k in enumerate(keys)}
            block = st.enter_context(nc.Block())
            sched = self

            def replay(ename, eng):
                seen = {}

                def do_waits(op):
                    comp, dmas = sched._needed(op)
                    for d in comp.values():
                        if seen.get(("e", d.eng), 0) < d.val:
                            eng.wait_ge(esem[d.eng], d.val)
                            seen[("e", d.eng)] = d.val
                    kmax = {}
                    for d in dmas:
                        if kmax.get(d.key, 0) < d.val:
                            kmax[d.key] = d.val
                    for kk_, vv_ in kmax.items():
                        if seen.get(("k", kk_), 0) < vv_:
                            eng.wait_ge(ksem[kk_], vv_)
                            seen[("k", kk_)] = vv_

                oplist = sched.ops[ename]
                for op in oplist:
                    g1 = sched.groups[ename].get(op.idx)
                    if g1 is not None:
                        for op2 in oplist[op.idx:g1]:
                            do_waits(op2)
                    comp, dmas = sched._needed(op)
                    for d in comp.values():
                        if seen.get(("e", d.eng), 0) < d.val:
                            eng.wait_ge(esem[d.eng], d.val)
                            seen[("e", d.eng)] = d.val
                    kmax = {}
                    for d in dmas:
                        if kmax.get(d.key, 0) < d.val:
                            kmax[d.key] = d.val
                    for kk_, vv_ in kmax.items():
                        if seen.get(("k", kk_), 0) < vv_:
                            eng.wait_ge(ksem[kk_], vv_)
                            seen[("k", kk_)] = vv_
                    ins = op.fn(eng)
                    if op.dma:
                        ins.then_inc(ksem[op.key], 16)
                    elif op.need_sig:
                        ins.then_inc(esem[ename], 1)

            @block.tensor
            def _(e):
                replay("pe", e)

            @block.scalar
            def _(e):
                replay("act", e)

            @block.vector
            def _(e):
                replay("dve", e)

            @block.gpsimd
            def _(e):
                replay("pool", e)

            @block.sync
            def _(e):
                replay("sp", e)


class Arena:
    def __init__(self, big, nw):
        self.big = big
        self.nw = nw
        self.top = 0

    def f32(self, n):
        off = self.top
        self.top += n
        assert self.top <= self.nw, ("sbuf arena overflow", self.top, self.nw)
        return self.big[:, off:off + n]

    def bf16(self, n):
        w = (n + 1) // 2
        off = self.top
        self.top += w
        assert self.top <= self.nw, ("sbuf arena overflow", self.top, self.nw)
        return self.big[:, off:off + w].bitcast(BF16)


C_ID, C_GE, C_LT, C_LE, C_GT, C_NEG, C_ONE = 0, 128, 256, 384, 512, 640, 768
NCONST = 896
PC_C, PC_BADA, PC_GIN, PC_GA, PC_GS, PC_D, PC_CW, PC_CB = 0, 8, 32, 40, 48, 56, 64, 112
NPCOL = 124
PR_BG, PR_GF, PR_DTB, PR_AL = 0, 1024, 2048, 2064
NPROW = 2080
NW_SBUF = 52224
NWARM = 0
NFILL = 0


def build(S, debug=False):
    NT = S // 512
    NCH = S // 128
    nc = bass.Bass("TRN2", target_bir_lowering=False)
    dk = "ExternalOutput" if debug else "Internal"
    x_d = nc.dram_tensor("x", [S, 1024], F32, kind="ExternalInput").ap()
    wada_d = nc.dram_tensor("w_ada", [1024, 3072], F32, kind="ExternalInput").ap()
    win_d = nc.dram_tensor("w_in", [1024, 6672], F32, kind="ExternalInput").ap()
    wout_d = nc.dram_tensor("w_out", [2048, 1024], F32, kind="ExternalInput").ap()
    const_d = nc.dram_tensor("consts", [128, NCONST], F32, kind="ExternalInput").ap()
    pcol_d = nc.dram_tensor("pcol", [128, NPCOL], F32, kind="ExternalInput").ap()
    prow_d = nc.dram_tensor("prow", [1, NPROW], F32, kind="ExternalInput").ap()
    out_d = nc.dram_tensor("out", [S, 1024], F32, kind="ExternalOutput").ap()
    HT_d = nc.dram_tensor("s_ht", [8, 128, S], BF16, kind=dk).ap()
    QT_d = nc.dram_tensor("s_qt", [1024, S], BF16, kind=dk).ap()
    KT_d = nc.dram_tensor("s_kt", [1024, S], BF16, kind=dk).ap()
    ZA_d = nc.dram_tensor("s_za", [1024, S], BF16, kind=dk).ap()
    V_d = nc.dram_tensor("s_v", [S, 1024], BF16, kind=dk).ap()
    AT_d = nc.dram_tensor("s_at", [2048, S], BF16, kind=dk).ap()
    XS_d = nc.dram_tensor("s_xs", [1024, S], F32, kind=dk).ap()
    BC_d = nc.dram_tensor("s_bc", [512, S], BF16, kind=dk).ap()
    ZS_d = nc.dram_tensor("s_zs", [1024, S], BF16, kind=dk).ap()
    DT_d = nc.dram_tensor("s_dt", [S, 32], F32, kind=dk).ap()

    S_ = Sched(nc)
    E = S_.emit

    with ExitStack() as st:
        big = st.enter_context(nc.sbuf_tensor("big", [128, NW_SBUF], F32))
        psum = st.enter_context(nc.psum_tensor("psum", [128, 4096], F32))
        A = Arena(big, NW_SBUF)

        def bank(i):
            return psum[:, 512 * i:512 * (i + 1)]

        PB = [Buf("ps%d" % i) for i in range(8)]

        cf = A.f32(NCONST); b_cf = Buf("cf", const=True)
        cb = A.bf16(NCONST); b_cb = Buf("cb", const=True)
        zb = A.bf16(512); b_zb = Buf("zb", const=True)
        neg4 = A.bf16(512); b_neg4 = Buf("neg4", const=True)
        pcol = A.f32(NPCOL); b_pcol = Buf("pcol", const=True)
        prow = A.f32(NPROW); b_prow = Buf("prow", const=True)
        modT = A.f32(24); b_modT = Buf("modT")
        g1 = A.f32(8); b_g1 = Buf("g1")
        gate_bc = A.f32(1024); b_gate = Buf("gate_bc")
        a_bc = A.f32(16); b_abc = Buf("a_bc")
        ssq_a = A.f32(NCH); b_ssqa = Buf("ssq_a")
        ssq_s = A.f32(NCH); b_ssqs = Buf("ssq_s")
        rstd_a = A.f32(NCH); rstd_s = A.f32(NCH); b_rstd = Buf("rstd_as")
        ssqf = A.f32(NCH); b_ssqf = Buf("ssqf")
        rstdf = A.f32(NCH); b_rstdf = Buf("rstdf")
        tmp_small = A.f32(64); b_tmps = Buf("tmps")

        idf = cf[:, C_ID:C_ID + 128]
        GEb = cb[:, C_GE:C_GE + 128]
        LTb = cb[:, C_LT:C_LT + 128]
        LTf = cf[:, C_LT:C_LT + 128]
        LEf = cf[:, C_LE:C_LE + 128]
        GTf = cf[:, C_GT:C_GT + 128]
        idb = cb[:, C_ID:C_ID + 128]
        onesf = cf[:, C_ONE:C_ONE + 128]

        E("sp", lambda e: e.dma_start(out=cf, in_=const_d), writes=[b_cf], dma=True)
        E("pool", lambda e: e.dma_start(out=cb, in_=const_d), writes=[b_cb], dma=True)
        E("sp", lambda e: e.dma_start(out=pcol, in_=pcol_d), writes=[b_pcol], dma=True)
        E("sp", lambda e: e.dma_start(out=prow, in_=prow_d.partition_broadcast(128)), writes=[b_prow], dma=True)
        E("dve", lambda e: e.memset(zb, 0.0), writes=[b_zb])
        for q in range(4):
            E("dve", lambda e, q=q: e.tensor_copy(out=neg4[:, q * 128:(q + 1) * 128], in_=cf[:, C_NEG:C_NEG + 128]),
              reads=[b_cf], pwrites=[b_neg4])

        base_mark = A.top
        NWA, NWB2 = 1536, 1040
        Wsa = A.bf16(8 * NWA); b_Wsa = Buf("Wsa")
        wsa_mark = A.top
        Wa = A.bf16(8 * 4096); b_Wa = Buf("Wa")
        win_v = win_d.rearrange("(k p) c -> p k c", p=128)
        Wa3 = Wa.rearrange("p (k c) -> p k c", k=8)
        for ci in range(8):
            E("pool", lambda e, ci=ci: e.dma_start(out=Wa3[:, :, ci * 512:(ci + 1) * 512],
                                                  in_=win_v[:, :, ci * 512:(ci + 1) * 512]),
              pwrites=[b_Wa], dma=True, key="Wa")
        Wsa3 = Wsa.rearrange("p (k c) -> p k c", k=8)
        for ci in range(3):
            E("pool", lambda e, ci=ci: e.dma_start(out=Wsa3[:, :, ci * 512:(ci + 1) * 512],
                                                  in_=win_v[:, :, 4096 + ci * 512:4096 + (ci + 1) * 512]),
              pwrites=[b_Wsa], dma=True, key="Wsa")
        p0_mark = A.top
        NVW = (S // 128) * 512
        Vall = big[:, NW_SBUF - NVW:NW_SBUF].bitcast(BF16); b_Vall = Buf("Vall")

        cact = A.f32(8); b_cact = Buf("cact")
        cact_rep = A.f32(8 * 128); b_crep = Buf("crep")
        wa = [A.f32(8 * 512) for _ in range(2)]
        b_wa = [Buf("wa%d" % i) for i in range(2)]
        E("act", lambda e: e.activation(out=cact, in_=pcol[:, PC_C:PC_C + 8], func=AF.Silu),
          reads=[b_pcol], writes=[b_cact])
        E("dve", lambda e: e.tensor_copy(out=cact_rep.rearrange("p (k m) -> p k m", k=8),
                                         in_=cact.unsqueeze(2).to_broadcast([128, 8, 128])),
          reads=[b_cact], writes=[b_crep])
        E("act", lambda e: e.activation(out=a_bc, in_=prow[:, PR_AL:PR_AL + 16], func=AF.Exp),
          reads=[b_prow], writes=[b_abc])
        E("dve", lambda e: e.tensor_scalar(out=a_bc, in0=a_bc, scalar1=-1.0, scalar2=None, op0=ALU.mult),
          reads=[b_abc], writes=[b_abc])
        wada_v = wada_d.rearrange("(k p) c -> p k c", p=128)
        for ci in range(6):
            r = ci % 2
            E("sp", lambda e, ci=ci, r=r: e.dma_start(out=wa[r].rearrange("p (k c) -> p k c", k=8),
                                                     in_=wada_v[:, :, ci * 512:(ci + 1) * 512]),
              writes=[b_wa[r]], dma=True)
            if ci < 4:
                for mm in range(4):
                    m = ci * 4 + mm
                    for k in range(8):
                        E("pe", lambda e, r=r, mm=mm, m=m, k=k: e.matmul(
                            bank(0)[:, m:m + 1], lhsT=wa[r][:, k * 512 + mm * 128:k * 512 + (mm + 1) * 128],
                            rhs=cact[:, k:k + 1], start=(k == 0), stop=(k == 7)),
                          reads=[b_wa[r], b_cact], pwrites=[PB[0]])
            else:
                n = ci - 4
                for k in range(8):
                    E("pe", lambda e, r=r, n=n, k=k: e.matmul(
                        bank(1 + n), lhsT=cact_rep[:, k * 128:(k + 1) * 128],
                        rhs=wa[r][:, k * 512:(k + 1) * 512], start=(k == 0), stop=(k == 7)),
                      reads=[b_wa[r], b_crep], pwrites=[PB[1 + n]])
        E("dve", lambda e: e.tensor_tensor(out=modT[:, 0:16], in0=bank(0)[:, 0:16], in1=pcol[:, PC_BADA:PC_BADA + 16],
                                           op=ALU.add), reads=[PB[0], b_pcol], writes=[b_modT])
        E("dve", lambda e: e.scalar_tensor_tensor(out=g1, in0=modT[:, 8:16], scalar=1.0, in1=pcol[:, PC_GIN:PC_GIN + 8],
                                                  op0=ALU.add, op1=ALU.mult), reads=[b_modT, b_pcol], writes=[b_g1])
        E("dve", lambda e: e.tensor_tensor(out=gate_bc, in0=psum[:, 512:1536], in1=prow[:, PR_BG:PR_BG + 1024], op=ALU.add),
          reads=[PB[1], PB[2], b_prow], writes=[b_gate])
        shiftc = modT
        S_.barrier()
        A.top = p0_mark

        xt = [A.f32(4096) for _ in range(2)]; b_xt = [Buf("xt%d" % i) for i in range(2)]
        hT = [A.bf16(8 * 512) for _ in range(2)]; b_hT = [Buf("hT%d" % i) for i in range(2)]
        junk = A.bf16(1024); b_junk = Buf("junk")
        ssq1 = A.f32(4 * NT); b_ssq1 = Buf("ssq1")
        rs1 = A.f32(4 * NT); b_rs1 = Buf("rs1")
        stq = A.bf16(8 * 512); b_stq = Buf("stq")
        stk = A.bf16(8 * 512); b_stk = Buf("stk")
        stz = A.bf16(8 * 512); b_stz = Buf("stz")
        stv = A.bf16(4 * 1024); b_stv = Buf("stv")
        stages = [(stq, b_stq, QT_d), (stk, b_stk, KT_d), (stz, b_stz, ZA_d)]
        pcnt = [0]

        def next_bank(lo, n):
            i = lo + (pcnt[0] % n)
            pcnt[0] += 1
            return i

        evq = [0]
        def p1a_front_a(tt):
            r = tt % 2
            E("sp", lambda e, tt=tt, r=r: e.dma_start(
                out=xt[r].rearrange("p (j d) -> p j d", j=4),
                in_=x_d[tt * 512:(tt + 1) * 512, :].rearrange("(j p) d -> p j d", p=128)),
              writes=[b_xt[r]], dma=True)
            for j in range(4):
                E("act", lambda e, r=r, j=j, tt=tt: e.activation(
                    out=junk, in_=xt[r][:, j * 1024:(j + 1) * 1024], func=AF.Square,
                    accum_out=ssq1[:, tt * 4 + j:tt * 4 + j + 1]),
                  reads=[b_xt[r]], writes=[b_junk], pwrites=[b_ssq1])
            E("act", lambda e, tt=tt: e.activation(out=rs1[:, tt * 4:tt * 4 + 4], in_=ssq1[:, tt * 4:tt * 4 + 4],
                                                   func=AF.Ln, scale=1.0 / 1024, bias=EPS),
              reads=[b_ssq1], pwrites=[b_rs1])
            E("act", lambda e, tt=tt: e.activation(out=rs1[:, tt * 4:tt * 4 + 4], in_=rs1[:, tt * 4:tt * 4 + 4],
                                                   func=AF.Exp, scale=-0.5),
              reads=[b_rs1], pwrites=[b_rs1])
            for j in range(4):
                E("dve", lambda e, r=r, j=j, tt=tt: e.tensor_scalar(
                    out=xt[r][:, j * 1024:(j + 1) * 1024], in0=xt[r][:, j * 1024:(j + 1) * 1024],
                    scalar1=rs1[:, tt * 4 + j:tt * 4 + j + 1], scalar2=None, op0=ALU.mult),
                  reads=[b_rs1, b_xt[r]], writes=[b_xt[r]])

        def p1a_front_b(tt):
            r = tt % 2
            for k in range(8):
                bi = next_bank(0, 4)
                for j in range(4):
                    E("pe", lambda e, r=r, j=j, k=k, bi=bi: e.transpose(
                        bank(bi)[:, j * 128:(j + 1) * 128], xt[r][:, j * 1024 + k * 128:j * 1024 + (k + 1) * 128], idf),
                      reads=[b_xt[r], b_cf], writes=[PB[bi]] if j == 0 else (), pwrites=() if j == 0 else [PB[bi]])
                if k % 2 == 0:
                    E("dve", lambda e, r=r, k=k, bi=bi: e.tensor_scalar(
                        out=hT[r][:, k * 512:(k + 1) * 512], in0=bank(bi), scalar1=g1[:, k:k + 1],
                        scalar2=shiftc[:, k:k + 1], op0=ALU.mult, op1=ALU.add),
                      reads=[PB[bi], b_g1, b_modT], pwrites=[b_hT[r]] if k else (), writes=() if k else [b_hT[r]])
                else:
                    E("act", lambda e, r=r, k=k, bi=bi: e.activation(
                        out=hT[r][:, k * 512:(k + 1) * 512], in_=bank(bi), func=AF.Identity,
                        scale=g1[:, k:k + 1], bias=shiftc[:, k:k + 1]),
                      reads=[PB[bi], b_g1, b_modT], pwrites=[b_hT[r]])
            E("pool", lambda e, r=r, tt=tt: e.dma_start(
                out=HT_d[:, :, tt * 512:(tt + 1) * 512].rearrange("k p t -> p k t"),
                in_=hT[r].rearrange("p (k t) -> p k t", k=8)), reads=[b_hT[r]], dma=True, key="hTo%d" % r)

        def p1a_back1(tt):
            r = tt % 2
            for grp in range(3):
                stg, b_stg, dst = stages[grp]
                c_base = [0, 1024, 3072][grp]
                for m in range(8):
                    bi = next_bank(4, 4)
                    for k in range(8):
                        E("pe", lambda e, r=r, k=k, bi=bi, c0=c_base + m * 128: e.matmul(
                            bank(bi), lhsT=Wa[:, k * 4096 + c0:k * 4096 + c0 + 128],
                            rhs=hT[r][:, k * 512:(k + 1) * 512], start=(k == 0), stop=(k == 7)),
                          reads=[b_Wa, b_hT[r]], writes=[PB[bi]] if k == 0 else (), pwrites=() if k == 0 else [PB[bi]])
                    wkw = dict(writes=[b_stg]) if m == 0 else dict(pwrites=[b_stg])
                    osl = stg[:, m * 512:(m + 1) * 512]
                    if grp == 0:
                        if evq[0] % 2 == 0:
                            E("act", lambda e, osl=osl, bi=bi: e.mul(out=osl, in_=bank(bi), mul=0.125),
                              reads=[PB[bi]], **wkw)
                        else:
                            E("dve", lambda e, osl=osl, bi=bi: e.tensor_scalar(out=osl, in0=bank(bi), scalar1=0.125,
                                                                               scalar2=None, op0=ALU.mult),
                              reads=[PB[bi]], **wkw)
                        evq[0] += 1
                    elif grp == 1:
                        if evq[0] % 2 == 0:
                            E("act", lambda e, osl=osl, bi=bi: e.copy(out=osl, in_=bank(bi)), reads=[PB[bi]], **wkw)
                        else:
                            E("dve", lambda e, osl=osl, bi=bi: e.tensor_copy(out=osl, in_=bank(bi)), reads=[PB[bi]], **wkw)
                        evq[0] += 1
                    else:
                        E("act", lambda e, osl=osl, bi=bi: e.activation(out=osl, in_=bank(bi), func=AF.Silu),
                          reads=[PB[bi]], **wkw)
                E("pool", lambda e, stg=stg, dst=dst, tt=tt: e.dma_start(
                    out=dst[:, tt * 512:(tt + 1) * 512].rearrange("(m p) t -> p m t", p=128),
                    in_=stg.rearrange("p (m t) -> p m t", m=8)), reads=[b_stg], dma=True)

        def p1a_back2(tt):
            r = tt % 2
            for j in range(4):
                for n in range(2):
                    bi = next_bank(4, 4)
                    for k in range(8):
                        E("pe", lambda e, r=r, k=k, bi=bi, j=j, n=n: e.matmul(
                            bank(bi), lhsT=hT[r][:, k * 512 + j * 128:k * 512 + (j + 1) * 128],
                            rhs=Wa[:, k * 4096 + 2048 + n * 512:k * 4096 + 2048 + (n + 1) * 512],
                            start=(k == 0), stop=(k == 7)),
                          reads=[b_Wa, b_hT[r]], writes=[PB[bi]] if k == 0 else (), pwrites=() if k == 0 else [PB[bi]])
                    wkw = dict(writes=[b_stv]) if (j == 0 and n == 0) else dict(pwrites=[b_stv])
                    osl = stv[:, j * 1024 + n * 512:j * 1024 + (n + 1) * 512]
                    if evq[0] % 2 == 0:
                        E("act", lambda e, osl=osl, bi=bi: e.copy(out=osl, in_=bank(bi)), reads=[PB[bi]], **wkw)
                    else:
                        E("dve", lambda e, osl=osl, bi=bi: e.tensor_copy(out=osl, in_=bank(bi)), reads=[PB[bi]], **wkw)
                    evq[0] += 1
            E("pool", lambda e, tt=tt: e.dma_start(
                out=V_d[tt * 512:(tt + 1) * 512, :].rearrange("(j p) c -> p j c", p=128),
                in_=stv.rearrange("p (j c) -> p j c", j=4)), reads=[b_stv], dma=True)

        p1a_front_a(0)
        p1a_front_b(0)
        for tt in range(NT):
            if tt + 1 < NT:
                p1a_front_a(tt + 1)
            p1a_back1(tt)
            if tt + 1 < NT:
                p1a_front_b(tt + 1)
            p1a_back2(tt)
        S_.barrier()
        A.top = base_mark

        A.top = wsa_mark
        A.nw = NW_SBUF - NVW
        Wsb = A.bf16(8 * NWB2); b_Wsb = Buf("Wsb")
        Wsb3 = Wsb.rearrange("p (k c) -> p k c", k=8)
        for (c0, c1) in ((0, 528), (528, 1040)):
            E("pool", lambda e, c0=c0, c1=c1: e.dma_start(out=Wsb3[:, :, c0:c1], in_=win_v[:, :, 5632 + c0:5632 + c1]),
              pwrites=[b_Wsb], dma=True, key="Wsb")
        Vall3 = Vall.rearrange("p (n c) -> p n c", c=1024)
        V_v = V_d.rearrange("(n p) c -> p n c", p=128)
        nvq = max(1, (S // 128) // 8)
        for vq in range(0, S // 128, nvq):
            E("sp", lambda e, vq=vq: e.dma_start(out=Vall3[:, vq:vq + nvq, :], in_=V_v[:, vq:vq + nvq, :]),
              pwrites=[b_Vall], dma=True, key="Vall")
        hSr = [A.bf16(8 * 512) for _ in range(2)]; b_hSr = [Buf("hS%d" % i) for i in range(2)]
        XW_ = 515
        xin = A.f32(12 * XW_); b_xin = [Buf("xin%d" % m) for m in range(12)]
        cv = A.f32(12 * 512); b_cv = [Buf("cv%d" % m) for m in range(12)]
        BCs = A.bf16(4 * 512); b_BCs = Buf("BCs")
        ZS = A.bf16(8 * 512); b_ZS = Buf("ZSst")
        dtv = A.f32(64); b_dtv = Buf("dtv")
        dts = A.f32(64); b_dts = Buf("dts")
        lds = A.f32(64); b_lds = Buf("lds")
        cw = pcol[:, PC_CW:PC_CW + 48]
        cbias = pcol[:, PC_CB:PC_CB + 12]
        E("dve", lambda e: e.memset(xin, 0.0), writes=b_xin)
        for tt in range(NT):
            hS = hSr[tt % 2]; b_hS = b_hSr[tt % 2]
            E("sp", lambda e, tt=tt, hS=hS: e.dma_start(
                out=hS.rearrange("p (k t) -> p k t", k=8),
                in_=HT_d[:, :, tt * 512:(tt + 1) * 512].rearrange("k p t -> p k t")), writes=[b_hS], dma=True)
            pend_silu = []
            for m in range(12):
                if tt > 0:
                    E("dve", lambda e, m=m: e.tensor_copy(out=xin[:, m * XW_:m * XW_ + 3], in_=xin[:, m * XW_ + 512:m * XW_ + 515]),
                      reads=[b_xin[m]], writes=[b_xin[m]])
                bi = next_bank(0, 4)
                for k in range(8):
                    E("pe", lambda e, k=k, bi=bi, c0=m * 128, hS=hS: e.matmul(
                        bank(bi), lhsT=Wsa[:, k * NWA + c0:k * NWA + c0 + 128], rhs=hS[:, k * 512:(k + 1) * 512],
                        start=(k == 0), stop=(k == 7)),
                      reads=[b_Wsa, b_hS], writes=[PB[bi]] if k == 0 else (), pwrites=() if k == 0 else [PB[bi]])
                E("act", lambda e, m=m, bi=bi: e.copy(out=xin[:, m * XW_ + 3:m * XW_ + 515], in_=bank(bi)),
                  reads=[PB[bi]], writes=[b_xin[m]])
                acc = cv[:, m * 512:(m + 1) * 512]
                E("dve", lambda e, m=m, acc=acc: e.tensor_scalar(
                    out=acc, in0=xin[:, m * XW_ + 3:m * XW_ + 515], scalar1=cw[:, m * 4 + 3:m * 4 + 4],
                    scalar2=cbias[:, m:m + 1], op0=ALU.mult, op1=ALU.add),
                  reads=[b_xin[m], b_pcol], writes=[b_cv[m]])
                for kk in range(3):
                    E("dve", lambda e, m=m, acc=acc, kk=kk: e.scalar_tensor_tensor(
                        out=acc, in0=xin[:, m * XW_ + kk:m * XW_ + kk + 512], scalar=cw[:, m * 4 + kk:m * 4 + kk + 1],
                        in1=acc, op0=ALU.mult, op1=ALU.add),
                      reads=[b_xin[m], b_pcol, b_cv[m]], writes=[b_cv[m]])

                def silu_emit(m=m, acc=acc):
                    if m < 8:
                        E("act", lambda e: e.activation(out=acc, in_=acc, func=AF.Silu),
                          reads=[b_cv[m]], writes=[b_cv[m]])
                    else:
                        dst = BCs[:, (m - 8) * 512:(m - 7) * 512]
                        E("act", lambda e: e.activation(out=dst, in_=acc, func=AF.Silu),
                          reads=[b_cv[m]], writes=[b_BCs] if m == 8 else (), pwrites=() if m == 8 else [b_BCs])

                pend_silu.append(silu_emit)
                if len(pend_silu) > 2:
                    pend_silu.pop(0)()
            while pend_silu:
                pend_silu.pop(0)()
            E("pool", lambda e, tt=tt: e.dma_start(
                out=XS_d[:, tt * 512:(tt + 1) * 512].rearrange("(m p) t -> p m t", p=128),
                in_=cv[:, 0:8 * 512].rearrange("p (m t) -> p m t", m=8)), reads=b_cv[0:8], dma=True, key="cvo")
            E("pool", lambda e, tt=tt: e.dma_start(
                out=BC_d[:, tt * 512:(tt + 1) * 512].rearrange("(m p) t -> p m t", p=128),
                in_=BCs.rearrange("p (m t) -> p m t", m=4)), reads=[b_BCs], dma=True, key="bco")
            for m in range(8):
                bi = next_bank(0, 4)
                for k in range(8):
                    E("pe", lambda e, k=k, bi=bi, c0=16 + m * 128, hS=hS: e.matmul(
                        bank(bi), lhsT=Wsb[:, k * NWB2 + c0:k * NWB2 + c0 + 128], rhs=hS[:, k * 512:(k + 1) * 512],
                        start=(k == 0), stop=(k == 7)),
                      reads=[b_Wsb, b_hS], writes=[PB[bi]] if k == 0 else (), pwrites=() if k == 0 else [PB[bi]])
                E("act", lambda e, m=m, bi=bi: e.activation(out=ZS[:, m * 512:(m + 1) * 512], in_=bank(bi), func=AF.Silu),
                  reads=[PB[bi]], writes=[b_ZS] if m == 0 else (), pwrites=() if m == 0 else [b_ZS])
            E("pool", lambda e, tt=tt: e.dma_start(
                out=ZS_d[:, tt * 512:(tt + 1) * 512].rearrange("(m p) t -> p m t", p=128),
                in_=ZS.rearrange("p (m t) -> p m t", m=8)), reads=[b_ZS], dma=True, key="zso")
            for j in range(4):
                for k in range(8):
                    E("pe", lambda e, j=j, k=k, hS=hS: e.matmul(
                        bank(7)[:, j * 16:(j + 1) * 16], lhsT=hS[:, k * 512 + j * 128:k * 512 + (j + 1) * 128],
                        rhs=Wsb[:, k * NWB2:k * NWB2 + 16], start=(k == 0), stop=(k == 7)),
                      reads=[b_Wsb, b_hS], writes=[PB[7]] if (j == 0 and k == 0) else (),
                      pwrites=() if (j == 0 and k == 0) else [PB[7]])
            E("dve", lambda e: e.tensor_tensor(
                out=dtv.rearrange("p (j h) -> p j h", j=4), in0=bank(7)[:, 0:64].rearrange("p (j h) -> p j h", j=4),
                in1=prow[:, PR_DTB:PR_DTB + 16].unsqueeze(1).to_broadcast([128, 4, 16]), op=ALU.add),
              reads=[PB[7], b_prow], writes=[b_dtv])
            E("act", lambda e: e.activation(out=dtv, in_=dtv, func=AF.Exp), reads=[b_dtv], writes=[b_dtv])
            E("act", lambda e: e.activation(out=dts, in_=dtv, func=AF.Ln, bias=1.0), reads=[b_dtv], writes=[b_dts])
            E("dve", lambda e: e.tensor_tensor(
                out=lds.rearrange("p (j h) -> p j h", j=4), in0=dts.rearrange("p (j h) -> p j h", j=4),
                in1=a_bc.unsqueeze(1).to_broadcast([128, 4, 16]), op=ALU.mult),
              reads=[b_dts, b_abc], writes=[b_lds])
            E("pool", lambda e, tt=tt: e.dma_start(
                out=DT_d[tt * 512:(tt + 1) * 512, 0:16].rearrange("(j p) h -> p j h", p=128),
                in_=dts.rearrange("p (j h) -> p j h", j=4)), reads=[b_dts], dma=True, key="dto")
            E("pool", lambda e, tt=tt: e.dma_start(
                out=DT_d[tt * 512:(tt + 1) * 512, 16:32].rearrange("(j p) h -> p j h", p=128),
                in_=lds.rearrange("p (j h) -> p j h", j=4)), reads=[b_lds], dma=True, key="ldo")
        S_.barrier()
        A.top = base_mark

        NB = NCH
        QT = [A.bf16(S) for _ in range(1)] * 2; KT = [A.bf16(S) for _ in range(1)] * 2
        ZAq = [A.bf16(512) for _ in range(2)]; b_ZAq = [Buf("ZAq%d" % i) for i in range(2)]
        b_QT = [Buf("QT0")] * 2; b_KT = [Buf("KT0")] * 2
        NE, NL, NX, NWB = 3, 4, 2, 3
        eb = [A.f32(1024) for _ in range(NE)]; b_eb = [Buf("e%d" % i) for i in range(NE)]
        lb = [A.bf16(1024) for _ in range(NL)]; b_lb = [Buf("l%d" % i) for i in range(NL)]
        xb = [A.f32(1024) for _ in range(NX)]; b_xb = [Buf("xx%d" % i) for i in range(NX)]
        wb = [A.bf16(1024) for _ in range(NWB)]; b_wb = [Buf("w%d" % i) for i in range(NWB)]
        sqa = A.f32(512); b_sqa = Buf("sqa")
        ats = [A.bf16(512) for _ in range(2)]; b_ats = [Buf("ats%d" % i) for i in range(2)]
        atq = [0]

        SB7 = 7
        b7 = bank(SB7)
        xsI = [A.f32(1024) for _ in range(2)]; b_xsI = [Buf("xsI%d" % i) for i in range(2)]
        bcI = [A.bf16(512) for _ in range(2)]; b_bcI = [Buf("bcI%d" % i) for i in range(2)]
        zsI = [A.bf16(1024) for _ in range(2)]; b_zsI = [Buf("zsI%d" % i) for i in range(2)]
        dtI = [A.f32(32) for _ in range(2)]; b_dtI = [Buf("dtI%d" % i) for i in range(2)]
        Rr = A.f32(16 * 128); b_R = Buf("R")
        decay = A.f32(16 * 128); b_decay = Buf("decay")
        Mm = A.bf16(16 * 128); b_M = Buf("M")
        CE = A.bf16(16 * 128); b_CE = Buf("CE")
        xtf = A.f32(1024); b_xtf = Buf("xtf")
        xtb = A.bf16(1024); b_xtb = Buf("xtb")
        Btk = A.bf16(256); b_Btk = Buf("Btk")
        XWt = A.bf16(1024); b_XW = Buf("XW")
        cbs = A.f32(256); b_cbs = Buf("cbs")
        dcd = A.f32(32); b_dcd = Buf("dcd")
        wgt = A.f32(16); b_wgt = Buf("wgt")
        state = A.f32(1024); b_state = Buf("state")
        stateb = A.bf16(1024); b_stateb = Buf("stateb")
        stsT = A.bf16(8 * 512); b_stsT = Buf("stsT")
        ub = Rr[:, 0:1024]
        sqs = Rr[:, 1024:2048]
        E("dve", lambda e: e.memset(state, 0.0), writes=[b_state])
        E("dve", lambda e: e.memset(stateb, 0.0), writes=[b_stateb])

        def ssd_load(c):
            sl = c % 2
            E("sp", lambda e: e.dma_start(out=xsI[sl].rearrange("p (m t) -> p m t", m=8),
                                          in_=XS_d[:, c * 128:(c + 1) * 128].rearrange("(m p) t -> p m t", p=128)),
              writes=[b_xsI[sl]], dma=True)
            E("sp", lambda e: e.dma_start(out=bcI[sl].rearrange("p (m t) -> p m t", m=4),
                                          in_=BC_d[:, c * 128:(c + 1) * 128].rearrange("(m p) t -> p m t", p=128)),
              writes=[b_bcI[sl]], dma=True)
            E("sp", lambda e: e.dma_start(out=zsI[sl].rearrange("p (m t) -> p m t", m=8),
                                          in_=ZS_d[:, c * 128:(c + 1) * 128].rearrange("(m p) t -> p m t", p=128)),
              writes=[b_zsI[sl]], dma=True)
            E("sp", lambda e: e.dma_start(out=dtI[sl], in_=DT_d[c * 128:(c + 1) * 128, :]), writes=[b_dtI[sl]], dma=True)

        def ssd_chunk(c):
            sl = c % 2
            xsT, bc_, zs_, dt_ = xsI[sl], bcI[sl], zsI[sl], dtI[sl]
            bx, bb, bz, bd = b_xsI[sl], b_bcI[sl], b_zsI[sl], b_dtI[sl]
            dtj = dt_[:, 0:16]
            ldj = dt_[:, 16:32]
            jq = c % 4
            if c + 1 < NCH:
                ssd_load(c + 1)
            R3 = Rr.rearrange("p (h l) -> p h l", h=16)

            def m_ops(q):
                for h in range(q * 4, q * 4 + 4):
                    g = h // 8
                    E("dve", lambda e, h=h, g=g: e.scalar_tensor_tensor(
                        out=Mm[:, h * 128:(h + 1) * 128], in0=decay[:, h * 128:(h + 1) * 128], scalar=dtj[:, h:h + 1],
                        in1=cbs[:, g * 128:(g + 1) * 128], op0=ALU.mult, op1=ALU.mult),
                      reads=[b_decay, bd, b_cbs], writes=[b_M] if h == 0 else (), pwrites=() if h == 0 else [b_M])

            def ce_op(g):
                E("dve", lambda e: e.tensor_tensor(
                    out=CE[:, g * 1024:(g + 1) * 1024].rearrange("p (h l) -> p h l", h=8),
                    in0=decay[:, g * 1024:(g + 1) * 1024].rearrange("p (h l) -> p h l", h=8),
                    in1=bc_[:, 256 + g * 128:256 + (g + 1) * 128].unsqueeze(1).to_broadcast([128, 8, 128]), op=ALU.mult),
                  reads=[b_decay, bb], writes=[b_CE] if g == 0 else (), pwrites=() if g == 0 else [b_CE])

            for hf in range(2):
                E("dve", lambda e, hf=hf: e.tensor_tensor(
                    out=R3[:, hf * 8:(hf + 1) * 8, :], in0=LEf.unsqueeze(1).to_broadcast([128, 8, 128]),
                    in1=ldj[:, hf * 8:(hf + 1) * 8].unsqueeze(2).to_broadcast([128, 8, 128]), op=ALU.mult),
                  reads=[b_cf, bd], writes=[b_R] if hf == 0 else (), pwrites=() if hf == 0 else [b_R])
                for m4 in range(4):
                    m = hf * 4 + m4
                    E("pe", lambda e, m=m, m4=m4: e.transpose(b7[:, m4 * 128:(m4 + 1) * 128], xsT[:, m * 128:(m + 1) * 128], idf),
                      reads=[bx, b_cf], writes=[PB[SB7]] if m4 == 0 else (), pwrites=() if m4 == 0 else [PB[SB7]])
                E("dve", lambda e, hf=hf: e.tensor_copy(out=xtf[:, hf * 512:(hf + 1) * 512], in_=b7), reads=[PB[SB7]],
                  writes=[b_xtf] if hf == 0 else (), pwrites=() if hf == 0 else [b_xtf])
                yield
            E("dve", lambda e: e.tensor_copy(out=xtb, in_=xtf), reads=[b_xtf], writes=[b_xtb])
            b7b = b7[:, 0:128].bitcast(BF16)
            for g in range(2):
                E("pe", lambda e, g=g: e.transpose(b7b[:, g * 128:(g + 1) * 128], bc_[:, g * 128:(g + 1) * 128], idb),
                  reads=[bb, b_cb], writes=[PB[SB7]] if g == 0 else (), pwrites=() if g == 0 else [PB[SB7]])
            for g in range(2):
                E("pe", lambda e, g=g: e.matmul(b7[:, 256 + g * 128:256 + (g + 1) * 128], lhsT=bc_[:, g * 128:(g + 1) * 128],
                                                rhs=bc_[:, 256 + g * 128:256 + (g + 1) * 128], start=True, stop=True),
                  reads=[bb], pwrites=[PB[SB7]])
            E("dve", lambda e: e.tensor_copy(out=Btk, in_=b7b), reads=[PB[SB7]], writes=[b_Btk])
            E("dve", lambda e: e.tensor_copy(out=cbs, in_=b7[:, 256:512]), reads=[PB[SB7]], writes=[b_cbs])
            yield
            for q in range(4):
                E("pe", lambda e, q=q: e.matmul(b7, lhsT=GTf, rhs=Rr[:, q * 512:(q + 1) * 512], start=True, stop=False,
                                                skip_group_check=True), reads=[b_R, b_cf], writes=[PB[SB7]])
                E("pe", lambda e: e.matmul(b7, lhsT=idb, rhs=neg4, start=False, stop=True, skip_group_check=True),
                  reads=[b_cb, b_neg4], pwrites=[PB[SB7]])
                E("act", lambda e, q=q: e.activation(out=decay[:, q * 512:(q + 1) * 512], in_=b7, func=AF.Exp),
                  reads=[PB[SB7]], writes=[b_decay] if q == 0 else (), pwrites=() if q == 0 else [b_decay])
                if q >= 1:
                    m_ops(q - 1)
                yield
            E("pe", lambda e: e.matmul(b7[:, 0:16], lhsT=GTf, rhs=ldj, start=True, stop=True),
              reads=[bd, b_cf], writes=[PB[SB7]])
            E("pe", lambda e: e.matmul(b7[:, 16:32], lhsT=onesf, rhs=ldj, start=True, stop=True),
              reads=[bd, b_cf], pwrites=[PB[SB7]])
            E("act", lambda e: e.activation(out=dcd, in_=b7[:, 0:32], func=AF.Exp), reads=[PB[SB7]], writes=[b_dcd])
            m_ops(3)
            yield
            for q in range(4):
                E("pe", lambda e, q=q: e.matmul(b7, lhsT=onesf, rhs=Rr[:, q * 512:(q + 1) * 512], start=True, stop=True),
                  reads=[b_R, b_cf], writes=[PB[SB7]])
                E("act", lambda e, q=q: e.activation(out=decay[:, q * 512:(q + 1) * 512], in_=b7, func=AF.Exp),
                  reads=[PB[SB7]], writes=[b_decay] if q == 0 else (), pwrites=() if q == 0 else [b_decay])
                if q == 0:
                    E("dve", lambda e: e.tensor_tensor(out=wgt, in0=dtj, in1=dcd[:, 0:16], op=ALU.mult),
                      reads=[bd, b_dcd], writes=[b_wgt])
                    E("dve", lambda e: e.tensor_tensor(
                        out=XWt.rearrange("p (h q) -> p h q", h=16), in0=xtf.rearrange("p (h q) -> p h q", h=16),
                        in1=wgt.unsqueeze(2).to_broadcast([128, 16, 64]), op=ALU.mult),
                      reads=[b_xtf, b_wgt], writes=[b_XW])
                if q == 2:
                    ce_op(0)
                yield
            ce_op(1)
            yield
            for hb in range(2):
                for p4 in range(4):
                    pr = hb * 4 + p4
                    csl = slice(p4 * 128, (p4 + 1) * 128)
                    for half in range(2):
                        h = 2 * pr + half
                        first = (p4 == 0 and half == 0)
                        E("pe", lambda e, csl=csl, half=half, h=h: e.matmul(
                            b7[64 * half:64 * half + 64, csl], lhsT=xtb[:, h * 64:(h + 1) * 64],
                            rhs=Mm[:, h * 128:(h + 1) * 128], start=True, stop=False, skip_group_check=True),
                          reads=[b_xtb, b_M], writes=[PB[SB7]] if first else (), pwrites=() if first else [PB[SB7]])
                        E("pe", lambda e, csl=csl, half=half, h=h: e.matmul(
                            b7[64 * half:64 * half + 64, csl], lhsT=stateb[:, h * 64:(h + 1) * 64],
                            rhs=CE[:, h * 128:(h + 1) * 128], start=False, stop=True, skip_group_check=True),
                          reads=[b_stateb, b_CE], pwrites=[PB[SB7]])
                usl = ub[:, hb * 512:(hb + 1) * 512]
                u3 = usl.rearrange("p (m t) -> p m t", m=4)
                E("dve", lambda e, hb=hb, u3=u3: e.tensor_tensor(
                    out=u3, in0=xsT[:, hb * 512:(hb + 1) * 512].rearrange("p (m t) -> p m t", m=4),
                    in1=pcol[:, PC_D + hb * 4:PC_D + hb * 4 + 4].unsqueeze(2).to_broadcast([128, 4, 128]), op=ALU.mult),
                  reads=[bx, b_pcol], writes=[b_R] if hb == 0 else (), pwrites=() if hb == 0 else [b_R])
                E("dve", lambda e, usl=usl: e.tensor_tensor(out=usl, in0=usl, in1=b7, op=ALU.add),
                  reads=[b_R, PB[SB7]], pwrites=[b_R])
                yield
                E("dve", lambda e, hb=hb, usl=usl: e.tensor_tensor(out=usl, in0=usl, in1=zs_[:, hb * 512:(hb + 1) * 512],
                                                                  op=ALU.mult), reads=[b_R, bz], pwrites=[b_R])
                o3 = stsT.rearrange("p (m t) -> p m t", m=8)[:, hb * 4:hb * 4 + 4, jq * 128:(jq + 1) * 128]
                E("dve", lambda e, hb=hb, u3=u3, o3=o3: e.tensor_tensor(
                    out=o3, in0=u3,
                    in1=pcol[:, PC_GS + hb * 4:PC_GS + hb * 4 + 4].unsqueeze(2).to_broadcast([128, 4, 128]), op=ALU.mult),
                  reads=[b_R, b_pcol], pwrites=[b_stsT])
                yield
            E("dve", lambda e: e.tensor_tensor(out=sqs, in0=ub, in1=ub, op=ALU.mult), reads=[b_R], pwrites=[b_R])
            yield
            for pr in range(8):
                E("pe", lambda e, pr=pr: e.matmul(b7[:, 0:1], lhsT=sqs[:, pr * 128:(pr + 1) * 128], rhs=onesf[:, 0:1],
                                                 start=(pr == 0), stop=(pr == 7)),
                  reads=[b_R, b_cf], writes=[PB[SB7]] if pr == 0 else (), pwrites=() if pr == 0 else [PB[SB7]])
            E("dve", lambda e: e.tensor_copy(out=ssq_s[:, c:c + 1], in_=b7[:, 0:1]), reads=[PB[SB7]], pwrites=[b_ssqs])
            yield
            for g in range(2):
                E("pe", lambda e, g=g: e.matmul(b7, lhsT=Btk[:, g * 128:(g + 1) * 128], rhs=XWt[:, g * 512:(g + 1) * 512],
                                                start=True, stop=True), reads=[b_Btk, b_XW], writes=[PB[SB7]])
                ssl = state[:, g * 512:(g + 1) * 512]
                E("dve", lambda e, g=g, ssl=ssl: e.tensor_tensor(
                    out=ssl.rearrange("p (h q) -> p h q", h=8), in0=ssl.rearrange("p (h q) -> p h q", h=8),
                    in1=dcd[:, 16 + g * 8:16 + g * 8 + 8].unsqueeze(2).to_broadcast([128, 8, 64]), op=ALU.mult),
                  reads=[b_state, b_dcd], writes=[b_state] if g == 0 else (), pwrites=() if g == 0 else [b_state])
                E("dve", lambda e, ssl=ssl: e.tensor_tensor(out=ssl, in0=ssl, in1=b7, op=ALU.add),
                  reads=[b_state, PB[SB7]], pwrites=[b_state])
                yield
            E("dve", lambda e: e.tensor_copy(out=stateb, in_=state), reads=[b_state], writes=[b_stateb])
            if jq == 3:
                tt_ = c // 4
                E("sp", lambda e: e.dma_start(
                    out=AT_d[1024:2048, tt_ * 512:(tt_ + 1) * 512].rearrange("(m p) t -> p m t", p=128),
                    in_=stsT.rearrange("p (m t) -> p m t", m=8)), reads=[b_stsT], dma=True)
            yield

        def ssd_all():
            for c in range(NCH):
                for _ in ssd_chunk(c):
                    yield

        ssd_load(0)
        ssd_gen = ssd_all()

        def ssd_step():
            try:
                next(ssd_gen)
                return True
            except StopIteration:
                return False

        def v2(ap, c0):
            return ap.rearrange("p (b c) -> p b c", b=2)[:, :, c0:512]

        def load_hp(hp):
            r = hp % 2
            E("sp", lambda e: e.dma_start(out=QT[r], in_=QT_d[hp * 128:(hp + 1) * 128, :]), writes=[b_QT[r]], dma=True)
            E("sp", lambda e: e.dma_start(out=KT[r], in_=KT_d[hp * 128:(hp + 1) * 128, :]), writes=[b_KT[r]], dma=True)

        load_hp(0)
        ucount = [0]

        def do_hp(hp, r):
            if hp > 0:
                load_hp(hp)
            steps = []
            for qt in range(NT):
                for jj in range(4 * qt + 3, -1, -1):
                    steps.append((qt, jj))
            nu = len(steps)
            info = {}

            def st_qk(n):
                qt, jj = steps[n]
                u = ucount[0] + n
                kk = jj - 4 * qt
                c0 = 128 * kk if kk > 0 else 0
                ap_ = 2 * (u % 2)
                info[n] = (qt, jj, u, kk, c0, ap_)
                if jj == 4 * qt + 3:
                    zi = (hp * NT + qt) % 2
                    E("sp", lambda e, zi=zi: e.dma_start(out=ZAq[zi], in_=ZA_d[hp * 128:(hp + 1) * 128, qt * 512:(qt + 1) * 512]),
                      writes=[b_ZAq[zi]], dma=True)
                for half in range(2):
                    pb = 64 * half
                    E("pe", lambda e, pb=pb, half=half: e.matmul(
                        bank(ap_ + half)[:, c0:512], lhsT=KT[r][pb:pb + 64, jj * 128:(jj + 1) * 128],
                        rhs=QT[r][pb:pb + 64, qt * 512 + c0:(qt + 1) * 512], start=True, stop=True),
                      reads=[b_KT[r], b_QT[r]], writes=[PB[ap_ + half]])

            def st_act1(n):
                qt, jj, u, kk, c0, ap_ = info[n]
                ei, li = u % NE, u % NL
                E("act", lambda e: e.activation(out=v2(eb[ei], c0), in_=v2(psum[:, ap_ * 512:(ap_ + 2) * 512], c0), func=AF.Exp),
                  reads=[PB[ap_], PB[ap_ + 1]], writes=[b_eb[ei]])
                if kk >= 0:
                    ev = eb[ei].rearrange("p (b c) -> p b c", b=2)[:, :, c0:c0 + 128]
                    E("pool", lambda e: e.tensor_tensor(out=ev, in0=ev, in1=LTf.unsqueeze(1).to_broadcast([128, 2, 128]),
                                                        op=ALU.mult), reads=[b_eb[ei], b_cf], writes=[b_eb[ei]])

            def st_act2(n):
                qt, jj, u, kk, c0, ap_ = info[n]
                ei, li = u % NE, u % NL
                E("act", lambda e: e.activation(out=v2(lb[li], c0), in_=v2(eb[ei], c0), func=AF.Ln, bias=1.0),
                  reads=[b_eb[ei]], writes=[b_lb[li]])

            def st_ge(n):
                qt, jj, u, kk, c0, ap_ = info[n]
                li, xi = u % NL, u % NX
                for half in range(2):
                    cbk = 4 + half
                    if jj == 4 * qt + 3:
                        E("pe", lambda e, cbk=cbk: e.matmul(bank(cbk), lhsT=zb[:, 0:128], rhs=zb, start=True, stop=False,
                                                            skip_group_check=True), reads=[b_zb], writes=[PB[cbk]])
                    E("pe", lambda e, cbk=cbk, half=half: e.matmul(
                        bank(cbk)[:, c0:512], lhsT=GEb, rhs=lb[li][:, half * 512 + c0:(half + 1) * 512],
                        start=False, stop=False, skip_group_check=True), reads=[b_lb[li], b_cb], writes=[PB[cbk]])
                E("act", lambda e: e.activation(out=v2(xb[xi], c0), in_=v2(psum[:, 2048:3072], c0), func=AF.Exp, scale=-1.0),
                  reads=[PB[4], PB[5]], writes=[b_xb[xi]])

            def st_lt(n):
                qt, jj, u, kk, c0, ap_ = info[n]
                if jj == 0:
                    return
                li = u % NL
                for half in range(2):
                    cbk = 4 + half
                    E("pe", lambda e, cbk=cbk, half=half: e.matmul(
                        bank(cbk)[:, c0:512], lhsT=LTb, rhs=lb[li][:, half * 512 + c0:(half + 1) * 512],
                        start=False, stop=False, skip_group_check=True), reads=[b_lb[li], b_cb], writes=[PB[cbk]])

            def st_w(n):
                qt, jj, u, kk, c0, ap_ = info[n]
                ei, xi, wi = u % NE, u % NX, u % NWB
                E("dve", lambda e: e.tensor_tensor(out=v2(wb[wi], c0), in0=v2(eb[ei], c0), in1=v2(xb[xi], c0), op=ALU.mult),
                  reads=[b_eb[ei], b_xb[xi]], writes=[b_wb[wi]])

            def st_pv(n):
                qt, jj, u, kk, c0, ap_ = info[n]
                wi = u % NWB
                ob = 6
                for half in range(2):
                    pb = 64 * half
                    hh = 2 * hp + half
                    if jj == 4 * qt + 3:
                        E("pe", lambda e, pb=pb: e.matmul(bank(ob)[pb:pb + 64, :], lhsT=zb[:, 0:64], rhs=zb, start=True,
                                                          stop=False, skip_group_check=True), reads=[b_zb],
                          writes=[PB[ob]] if half == 0 else (), pwrites=() if half == 0 else [PB[ob]])
                    E("pe", lambda e, pb=pb, half=half, hh=hh: e.matmul(
                        bank(ob)[pb:pb + 64, c0:512], lhsT=Vall[:, jj * 1024 + hh * 64:jj * 1024 + hh * 64 + 64],
                        rhs=wb[wi][:, half * 512 + c0:(half + 1) * 512], start=False, stop=(jj == 0), skip_group_check=True),
                      reads=[b_Vall, b_wb[wi]], pwrites=[PB[ob]])

            def st_epi(n):
                qt, jj, u, kk, c0, ap_ = info[n]
                ob = 6
                if jj == 0:
                    E("act", lambda e: e.activation(out=sqa, in_=bank(ob), func=AF.Square), reads=[PB[ob]], writes=[b_sqa])
                    for sub in range(4):
                        E("pe", lambda e, sub=sub: e.matmul(bank(0)[:, sub:sub + 1], lhsT=sqa[:, sub * 128:(sub + 1) * 128],
                                                            rhs=onesf[:, 0:1], start=True, stop=True),
                          reads=[b_sqa, b_cf], writes=[PB[0]] if sub == 0 else (), pwrites=() if sub == 0 else [PB[0]])
                    if hp == 0:
                        E("dve", lambda e: e.tensor_copy(out=ssq_a[:, 4 * qt:4 * qt + 4], in_=bank(0)[:, 0:4]),
                          reads=[PB[0]], pwrites=[b_ssqa])
                    else:
                        E("dve", lambda e: e.tensor_tensor(out=ssq_a[:, 4 * qt:4 * qt + 4], in0=ssq_a[:, 4 * qt:4 * qt + 4],
                                                           in1=bank(0)[:, 0:4], op=ALU.add),
                          reads=[PB[0], b_ssqa], writes=[b_ssqa])
                    ai = atq[0] % 2
                    atq[0] += 1
                    E("dve", lambda e: e.scalar_tensor_tensor(
                        out=ats[ai], in0=bank(ob), scalar=pcol[:, PC_GA + hp:PC_GA + hp + 1],
                        in1=ZAq[(hp * NT + qt) % 2], op0=ALU.mult, op1=ALU.mult),
                      reads=[PB[ob], b_pcol, b_ZAq[(hp * NT + qt) % 2]], writes=[b_ats[ai]])
                    E("sp", lambda e: e.dma_start(out=AT_d[hp * 128:(hp + 1) * 128, qt * 512:(qt + 1) * 512], in_=ats[ai]),
                      reads=[b_ats[ai]], dma=True)

            for n in range(nu + 3):
                if n < nu:
                    st_qk(n)
                if 0 <= n - 2 < nu:
                    st_lt(n - 2)
                if n < nu:
                    st_act1(n)
                if 0 <= n - 1 < nu:
                    st_ge(n - 1)
                if n < nu:
                    st_act2(n)
                if 0 <= n - 1 < nu:
                    st_w(n - 1)
                if 0 <= n - 2 < nu:
                    st_pv(n - 2)
                if 0 <= n - 2 < nu:
                    st_epi(n - 2)
                ssd_step()
            ucount[0] += nu

        for hp_ in range(8):
            do_hp(hp_, hp_ % 2)
        while ssd_step():
            pass
        S_.barrier()
        A.top = base_mark

        A.nw = NW_SBUF
        Wg = A.bf16(16 * 1024); b_Wg = Buf("Wg")
        wst = [A.f32(4096) for _ in range(2)]; b_wst = [Buf("wst%d" % i) for i in range(2)]
        wout_v = wout_d.rearrange("(m p) c -> p m c", p=128)
        for ch in range(4):
            rr = ch % 2
            E("sp", lambda e, ch=ch, rr=rr: e.dma_start(out=wst[rr].rearrange("p (m c) -> p m c", m=4),
                                                      in_=wout_v[:, ch * 4:(ch + 1) * 4, :]), writes=[b_wst[rr]], dma=True)
            E("dve" if ch % 2 == 0 else "pool", lambda e, ch=ch, rr=rr: e.tensor_tensor(
                out=Wg[:, ch * 4096:(ch + 1) * 4096].rearrange("p (m c) -> p m c", m=4),
                in0=wst[rr].rearrange("p (m c) -> p m c", m=4),
                in1=gate_bc.unsqueeze(1).to_broadcast([128, 4, 1024]), op=ALU.mult),
              reads=[b_wst[rr], b_gate], pwrites=[b_Wg])
        for (src, dst) in ((ssq_a, rstd_a), (ssq_s, rstd_s)):
            E("act", lambda e, src=src, dst=dst: e.activation(out=dst, in_=src, func=AF.Ln, scale=1.0 / 1024, bias=EPS),
              reads=[b_ssqa, b_ssqs], pwrites=[b_rstd])
            E("act", lambda e, dst=dst: e.activation(out=dst, in_=dst, func=AF.Exp, scale=-0.5),
              reads=[b_rstd], pwrites=[b_rstd])
        ATt = [A.bf16(16 * 512) for _ in range(2)]; b_ATt = [Buf("ATt%d" % i) for i in range(2)]
        xo = [A.f32(4096) for _ in range(2)]; b_xo = [Buf("xo%d" % i) for i in range(2)]
        yb = [A.f32(1024) for _ in range(2)]; b_yb = [Buf("yb%d" % i) for i in range(2)]
        junk3 = A.bf16(1024); b_junk3 = Buf("junk3")
        b_out = Buf("outd")
        yq = [0]
        gf_bc = prow[:, PR_GF:PR_GF + 1024]
        for tt in range(NT):
            r = tt % 2
            E("sp", lambda e, tt=tt, r=r: e.dma_start(
                out=ATt[r].rearrange("p (m t) -> p m t", m=16),
                in_=AT_d[:, tt * 512:(tt + 1) * 512].rearrange("(m p) t -> p m t", p=128)), writes=[b_ATt[r]], dma=True)
            E("sp", lambda e, tt=tt, r=r: e.dma_start(
                out=xo[r].rearrange("p (j d) -> p j d", j=4),
                in_=x_d[tt * 512:(tt + 1) * 512, :].rearrange("(j p) d -> p j d", p=128)), writes=[b_xo[r]], dma=True)
            for j in range(4):
                c = tt * 4 + j
                yi = yq[0] % 2
                yq[0] += 1
                for n in range(2):
                    ba = 4 * (c % 2) + 2 * n
                    bs = ba + 1
                    for m in range(8):
                        E("pe", lambda e, r=r, m=m, j=j, n=n, ba=ba: e.matmul(
                            bank(ba), lhsT=ATt[r][:, m * 512 + j * 128:m * 512 + (j + 1) * 128],
                            rhs=Wg[:, m * 1024 + n * 512:m * 1024 + (n + 1) * 512], start=(m == 0), stop=(m == 7)),
                          reads=[b_ATt[r], b_Wg], writes=[PB[ba]] if m == 0 else (), pwrites=() if m == 0 else [PB[ba]])
                    for m in range(8, 16):
                        E("pe", lambda e, r=r, m=m, j=j, n=n, bs=bs: e.matmul(
                            bank(bs), lhsT=ATt[r][:, m * 512 + j * 128:m * 512 + (j + 1) * 128],
                            rhs=Wg[:, m * 1024 + n * 512:m * 1024 + (n + 1) * 512], start=(m == 8), stop=(m == 15)),
                          reads=[b_ATt[r], b_Wg], writes=[PB[bs]] if m == 8 else (), pwrites=() if m == 8 else [PB[bs]])
                    ysl = yb[yi][:, n * 512:(n + 1) * 512]
                    E("dve", lambda e, r=r, j=j, n=n, ba=ba, ysl=ysl, c=c: e.scalar_tensor_tensor(
                        out=ysl, in0=bank(ba), scalar=rstd_a[:, c:c + 1],
                        in1=xo[r][:, j * 1024 + n * 512:j * 1024 + (n + 1) * 512], op0=ALU.mult, op1=ALU.add),
                      reads=[PB[ba], b_rstd, b_xo[r]], writes=[b_yb[yi]] if n == 0 else (), pwrites=() if n == 0 else [b_yb[yi]])
                    E("dve", lambda e, bs=bs, ysl=ysl, c=c: e.scalar_tensor_tensor(
                        out=ysl, in0=bank(bs), scalar=rstd_s[:, c:c + 1], in1=ysl, op0=ALU.mult, op1=ALU.add),
                      reads=[PB[bs], b_rstd, b_yb[yi]], pwrites=[b_yb[yi]])
                E("act", lambda e, yi=yi, c=c: e.activation(out=junk3, in_=yb[yi], func=AF.Square,
                                                            accum_out=ssqf[:, c:c + 1]),
                  reads=[b_yb[yi]], writes=[b_junk3], pwrites=[b_ssqf])
                E("act", lambda e, c=c: e.activation(out=rstdf[:, c:c + 1], in_=ssqf[:, c:c + 1], func=AF.Ln,
                                                     scale=1.0 / 1024, bias=EPS), reads=[b_ssqf], pwrites=[b_rstdf])
                E("act", lambda e, c=c: e.activation(out=rstdf[:, c:c + 1], in_=rstdf[:, c:c + 1], func=AF.Exp, scale=-0.5),
                  reads=[b_rstdf], pwrites=[b_rstdf])
                E("dve", lambda e, yi=yi, c=c: e.scalar_tensor_tensor(
                    out=yb[yi], in0=yb[yi], scalar=rstdf[:, c:c + 1], in1=gf_bc, op0=ALU.mult, op1=ALU.mult),
                  reads=[b_yb[yi], b_rstdf, b_prow], writes=[b_yb[yi]])
                E("pool", lambda e, yi=yi, c=c: e.dma_start(out=out_d[c * 128:(c + 1) * 128, :], in_=yb[yi]),
                  reads=[b_yb[yi]], pwrites=[b_out], dma=True, key="yo%d" % yi)
        E("sp", lambda e: e.nop(), reads=[b_out], real=False)
        S_.run()
    return nc


def _consts():
    a = np.arange(128)[:, None]
    b = np.arange(128)[None, :]
    c = np.zeros((128, NCONST), np.float32)
    c[:, C_ID:C_ID + 128] = (a == b)
    c[:, C_GE:C_GE + 128] = (a >= b)
    c[:, C_LT:C_LT + 128] = (a < b)
    c[:, C_LE:C_LE + 128] = (a <= b)
    c[:, C_GT:C_GT + 128] = (a > b)
    c[:, C_NEG:C_NEG + 128] = NEGV * (a > b)
    c[:, C_ONE:C_ONE + 128] = 1.0
    return c


def _col(v, n):
    return np.ascontiguousarray(np.asarray(v, np.float32).reshape(n, 128).T)


def make_in_maps(x, c, w_ada, b_ada, norm_in_gain, w_in, conv_w, conv_b, dt_bias, a_log, d_skip,
                 sb_norm_gain, ssm_norm_gain, w_out, norm_f_gain, cores=None):
    B = x.shape[0]
    consts = _consts()
    maps = []
    w_ada0 = np.ascontiguousarray(w_ada[0], np.float32)
    w_in0 = np.ascontiguousarray(w_in[0], np.float32)
    w_out0 = np.ascontiguousarray(w_out[0], np.float32)
    prow = np.zeros((1, NPROW), np.float32)
    prow[0, PR_BG:PR_BG + 1024] = b_ada[0, 2048:3072]
    prow[0, PR_GF:PR_GF + 1024] = norm_f_gain
    prow[0, PR_DTB:PR_DTB + 16] = dt_bias[0]
    prow[0, PR_AL:PR_AL + 16] = a_log[0]
    for b in (range(B) if cores is None else cores):
        pcol = np.zeros((128, NPCOL), np.float32)
        pcol[:, PC_C:PC_C + 8] = _col(c[b], 8)
        pcol[:, PC_BADA:PC_BADA + 24] = _col(b_ada[0], 24)
        pcol[:, PC_GIN:PC_GIN + 8] = _col(norm_in_gain[0], 8)
        pcol[:, PC_GA:PC_GA + 8] = _col(sb_norm_gain[0], 8)
        pcol[:, PC_GS:PC_GS + 8] = _col(ssm_norm_gain[0], 8)
        pcol[:, PC_D:PC_D + 8] = _col(np.repeat(np.asarray(d_skip[0], np.float32), 64), 8)
        cw = np.asarray(conv_w[0], np.float32)
        pcol[:, PC_CW:PC_CW + 48] = cw.T.reshape(12, 128, 4).transpose(1, 0, 2).reshape(128, 48)
        pcol[:, PC_CB:PC_CB + 12] = _col(conv_b[0], 12)
        maps.append({"x": np.ascontiguousarray(x[b], np.float32), "w_ada": w_ada0, "w_in": w_in0, "w_out": w_out0,
                     "consts": consts, "pcol": pcol, "prow": prow})
    return maps


_NC_CACHE = {}


def kernel(x, c, w_ada, b_ada, norm_in_gain, w_in, conv_w, conv_b, dt_bias, a_log, d_skip,
           sb_norm_gain, ssm_norm_gain, w_out, norm_f_gain):
    x = np.asarray(x)
    B, S, _ = x.shape
    if S not in _NC_CACHE:
        _NC_CACHE[S] = build(S)
    nc = _NC_CACHE[S]
    maps = make_in_maps(x, np.asarray(c), np.asarray(w_ada), np.asarray(b_ada), np.asarray(norm_in_gain),
                        np.asarray(w_in), np.asarray(conv_w), np.asarray(conv_b), np.asarray(dt_bias),
                        np.asarray(a_log), np.asarray(d_skip), np.asarray(sb_norm_gain),
                        np.asarray(ssm_norm_gain), np.asarray(w_out), np.asarray(norm_f_gain))
    res = run_bass_kernel_spmd(nc, maps, core_ids=list(range(B)))
    return np.stack([np.asarray(r["out"], np.float32) for r in res.results], axis=0)
```

```python
import numpy as np
from contextlib import ExitStack
import concourse.bass as bass
import concourse.mybir as mybir
from concourse.bass_utils import run_bass_kernel_spmd

F32 = mybir.dt.float32
BF16 = mybir.dt.bfloat16
AF = mybir.ActivationFunctionType
ALU = mybir.AluOpType
AX = mybir.AxisListType

ENGS = ("pe", "act", "dve", "pool", "sp")
EPS = 1e-6
NEGV = -30000.0


class Buf:
    __slots__ = ("name", "writers", "readers", "const")

    def __init__(self, name, const=False):
        self.name = name
        self.writers = []
        self.readers = []
        self.const = const


class Op:
    __slots__ = ("eng", "fn", "deps", "dma", "idx", "need_sig", "val", "key", "real")

    def __init__(self, eng, fn, dma, key, real):
        self.eng = eng
        self.fn = fn
        self.dma = dma
        self.key = key
        self.real = real
        self.deps = ()
        self.idx = -1
        self.need_sig = False
        self.val = 0


class Sched:
    def __init__(self, nc):
        self.nc = nc
        self.ops = {e: [] for e in ENGS}
        self.last_dma = {}
        self.groups = {e: {} for e in ENGS}
        self._gstart = {}

    def group_begin(self, eng):
        self._gstart[eng] = len(self.ops[eng])

    def group_end(self, eng):
        g0 = self._gstart.pop(eng)
        g1 = len(self.ops[eng])
        if g1 > g0 + 1:
            self.groups[eng][g0] = g1

    def emit(self, eng, fn, reads=(), writes=(), pwrites=(), dma=False, key=None, extra=(), real=True):
        if dma and key is None:
            key = writes[0].name if writes else (pwrites[0].name if pwrites else reads[0].name)
        op = Op(eng, fn, dma, key, real)
        deps = set(extra)
        for b in reads:
            deps.update(b.writers)
        for b in writes:
            deps.update(b.writers)
            deps.update(b.readers)
        for b in pwrites:
            deps.update(b.readers)
            if b.writers:
                deps.add(b.writers[0])
        op.deps = deps
        for b in reads:
            if not b.const:
                b.readers.append(op)
        for b in writes:
            b.writers = [op]
            b.readers = []
        for b in pwrites:
            b.writers.append(op)
        op.idx = len(self.ops[eng])
        self.ops[eng].append(op)
        if dma:
            self.last_dma[key] = op
        return op

    def barrier(self):
        lasts = []
        for e in ENGS:
            for op in reversed(self.ops[e]):
                if op.real and not op.dma:
                    lasts.append(op)
                    break
        deps = set(lasts) | set(self.last_dma.values())
        for e in ENGS:
            self.emit(e, lambda g: g.nop(), extra=deps, real=False)

    def _needed(self, op):
        comp = {}
        dmas = []
        for d in op.deps:
            if d.dma:
                dmas.append(d)
                continue
            if d.eng == op.eng and not op.dma:
                if d.eng == "pe":
                    continue
                if d.eng in ("act", "dve") and d.idx < op.idx - 2:
                    continue
            cur = comp.get(d.eng)
            if cur is None or d.idx > cur.idx:
                comp[d.eng] = d
        return comp, dmas

    def run(self):
        nc = self.nc
        for e in ENGS:
            for op in self.ops[e]:
                comp, _ = self._needed(op)
                for d in comp.values():
                    d.need_sig = True
        for e in ENGS:
            cnt = 0
            for op in self.ops[e]:
                if op.dma:
                    continue
                if op.need_sig:
                    cnt += 1
                    op.val = cnt
        keycnt = {}
        for e in ENGS:
            for op in self.ops[e]:
                if op.dma:
                    keycnt[op.key] = keycnt.get(op.key, 0) + 1
                    op.val = 16 * keycnt[op.key]
        keys = sorted(keycnt.keys())
        with ExitStack() as st:
            esem = {e: st.enter_context(nc.semaphore("s_" + e)) for e in ENGS}
            ksem = {k: st.enter_context(nc.semaphore("d%d" % i)) for i, k in enumerate(keys)}
            block = st.enter_context(nc.Block())
            sched = self

            def replay(ename, eng):
                seen = {}

                def do_waits(op):
                    comp, dmas = sched._needed(op)
                    for d in comp.values():
                        if seen.get(("e", d.eng), 0) < d.val:
                            eng.wait_ge(esem[d.eng], d.val)
                            seen[("e", d.eng)] = d.val
                    kmax = {}
                    for d in dmas:
                        if kmax.get(d.key, 0) < d.val:
                            kmax[d.key] = d.val
                    for kk_, vv_ in kmax.items():
                        if seen.get(("k", kk_), 0) < vv_:
                            eng.wait_ge(ksem[kk_], vv_)
                            seen[("k", kk_)] = vv_

                oplist = sched.ops[ename]
                for op in oplist:
                    g1 = sched.groups[ename].get(op.idx)
                    if g1 is not None:
                        for op2 in oplist[op.idx:g1]:
                            do_waits(op2)
                    comp, dmas = sched._needed(op)
                    for d in comp.values():
                        if seen.get(("e", d.eng), 0) < d.val:
                            eng.wait_ge(esem[d.eng], d.val)
                            seen[("e", d.eng)] = d.val
                    kmax = {}
                    for d in dmas:
                        if kmax.get(d.key, 0) < d.val:
                            kmax[d.key] = d.val
                    for kk_, vv_ in kmax.items():
                        if seen.get(("k", kk_), 0) < vv_:
                            eng.wait_ge(ksem[kk_], vv_)
                            seen[("k", kk_)] = vv_
                    ins = op.fn(eng)
                    if op.dma:
                        ins.then_inc(ksem[op.key], 16)
                    elif op.need_sig:
                        ins.then_inc(esem[ename], 1)

            @block.tensor
            def _(e):
                replay("pe", e)

            @block.scalar
            def _(e):
                replay("act", e)

            @block.vector
            def _(e):
                replay("dve", e)

            @block.gpsimd
            def _(e):
                replay("pool", e)

            @block.sync
            def _(e):
                replay("sp", e)


class Arena:
    def __init__(self, big, nw):
        self.big = big
        self.nw = nw
        self.top = 0

    def f32(self, n):
        off = self.top
        self.top += n
        assert self.top <= self.nw, ("sbuf arena overflow", self.top, self.nw)
        return self.big[:, off:off + n]

    def bf16(self, n):
        w = (n + 1) // 2
        off = self.top
        self.top += w
        assert self.top <= self.nw, ("sbuf arena overflow", self.top, self.nw)
        return self.big[:, off:off + w].bitcast(BF16)


C_ID, C_GE, C_LT, C_LE, C_GT, C_NEG, C_ONE = 0, 128, 256, 384, 512, 640, 768
NCONST = 896
PC_C, PC_BADA, PC_GIN, PC_GA, PC_GS, PC_D, PC_CW, PC_CB = 0, 8, 32, 40, 48, 56, 64, 112
NPCOL = 124
PR_BG, PR_GF, PR_DTB, PR_AL = 0, 1024, 2048, 2064
NPROW = 2080
NW_SBUF = 52224
NWARM = 0
NFILL = 0


def build(S, debug=False):
    NT = S // 512
    NCH = S // 128
    nc = bass.Bass("TRN2", target_bir_lowering=False)
    dk = "ExternalOutput" if debug else "Internal"
    x_d = nc.dram_tensor("x", [S, 1024], F32, kind="ExternalInput").ap()
    wada_d = nc.dram_tensor("w_ada", [1024, 3072], F32, kind="ExternalInput").ap()
    win_d = nc.dram_tensor("w_in", [1024, 6672], F32, kind="ExternalInput").ap()
    wout_d = nc.dram_tensor("w_out", [2048, 1024], F32, kind="ExternalInput").ap()
    const_d = nc.dram_tensor("consts", [128, NCONST], F32, kind="ExternalInput").ap()
    pcol_d = nc.dram_tensor("pcol", [128, NPCOL], F32, kind="ExternalInput").ap()
    prow_d = nc.dram_tensor("prow", [1, NPROW], F32, kind="ExternalInput").ap()
    out_d = nc.dram_tensor("out", [S, 1024], F32, kind="ExternalOutput").ap()
    HT_d = nc.dram_tensor("s_ht", [8, 128, S], BF16, kind=dk).ap()
    QT_d = nc.dram_tensor("s_qt", [1024, S], BF16, kind=dk).ap()
    KT_d = nc.dram_tensor("s_kt", [1024, S], BF16, kind=dk).ap()
    ZA_d = nc.dram_tensor("s_za", [1024, S], BF16, kind=dk).ap()
    V_d = nc.dram_tensor("s_v", [S, 1024], BF16, kind=dk).ap()
    AT_d = nc.dram_tensor("s_at", [2048, S], BF16, kind=dk).ap()
    XS_d = nc.dram_tensor("s_xs", [1024, S], F32, kind=dk).ap()
    BC_d = nc.dram_tensor("s_bc", [512, S], BF16, kind=dk).ap()
    ZS_d = nc.dram_tensor("s_zs", [1024, S], BF16, kind=dk).ap()
    DT_d = nc.dram_tensor("s_dt", [S, 32], F32, kind=dk).ap()

    S_ = Sched(nc)
    E = S_.emit

    with ExitStack() as st:
        big = st.enter_context(nc.sbuf_tensor("big", [128, NW_SBUF], F32))
        psum = st.enter_context(nc.psum_tensor("psum", [128, 4096], F32))
        A = Arena(big, NW_SBUF)

        def bank(i):
            return psum[:, 512 * i:512 * (i + 1)]

        PB = [Buf("ps%d" % i) for i in range(8)]

        cf = A.f32(NCONST); b_cf = Buf("cf", const=True)
        cb = A.bf16(NCONST); b_cb = Buf("cb", const=True)
        zb = A.bf16(512); b_zb = Buf("zb", const=True)
        neg4 = A.bf16(512); b_neg4 = Buf("neg4", const=True)
        pcol = A.f32(NPCOL); b_pcol = Buf("pcol", const=True)
        prow = A.f32(NPROW); b_prow = Buf("prow", const=True)
        modT = A.f32(24); b_modT = Buf("modT")
        g1 = A.f32(8); b_g1 = Buf("g1")
        gate_bc = A.f32(1024); b_gate = Buf("gate_bc")
        a_bc = A.f32(16); b_abc = Buf("a_bc")
        ssq_a = A.f32(NCH); b_ssqa = Buf("ssq_a")
        ssq_s = A.f32(NCH); b_ssqs = Buf("ssq_s")
        rstd_a = A.f32(NCH); rstd_s = A.f32(NCH); b_rstd = Buf("rstd_as")
        ssqf = A.f32(NCH); b_ssqf = Buf("ssqf")
        rstdf = A.f32(NCH); b_rstdf = Buf("rstdf")
        tmp_small = A.f32(64); b_tmps = Buf("tmps")

        idf = cf[:, C_ID:C_ID + 128]
        GEb = cb[:, C_GE:C_GE + 128]
        LTb = cb[:, C_LT:C_LT + 128]
        LTf = cf[:, C_LT:C_LT + 128]
        LEf = cf[:, C_LE:C_LE + 128]
        GTf = cf[:, C_GT:C_GT + 128]
        idb = cb[:, C_ID:C_ID + 128]
        onesf = cf[:, C_ONE:C_ONE + 128]

        E("sp", lambda e: e.dma_start(out=cf, in_=const_d), writes=[b_cf], dma=True)
        E("pool", lambda e: e.dma_start(out=cb, in_=const_d), writes=[b_cb], dma=True)
        E("sp", lambda e: e.dma_start(out=pcol, in_=pcol_d), writes=[b_pcol], dma=True)
        E("sp", lambda e: e.dma_start(out=prow, in_=prow_d.partition_broadcast(128)), writes=[b_prow], dma=True)
        E("dve", lambda e: e.memset(zb, 0.0), writes=[b_zb])
        for q in range(4):
            E("dve", lambda e, q=q: e.tensor_copy(out=neg4[:, q * 128:(q + 1) * 128], in_=cf[:, C_NEG:C_NEG + 128]),
              reads=[b_cf], pwrites=[b_neg4])

        base_mark = A.top
        NWA, NWB2 = 1536, 1040
        Wsa = A.bf16(8 * NWA); b_Wsa = Buf("Wsa")
        wsa_mark = A.top
        Wa = A.bf16(8 * 4096); b_Wa = Buf("Wa")
        win_v = win_d.rearrange("(k p) c -> p k c", p=128)
        Wa3 = Wa.rearrange("p (k c) -> p k c", k=8)
        for ci in range(8):
            E("pool", lambda e, ci=ci: e.dma_start(out=Wa3[:, :, ci * 512:(ci + 1) * 512],
                                                  in_=win_v[:, :, ci * 512:(ci + 1) * 512]),
              pwrites=[b_Wa], dma=True, key="Wa")
        Wsa3 = Wsa.rearrange("p (k c) -> p k c", k=8)
        for ci in range(3):
            E("pool", lambda e, ci=ci: e.dma_start(out=Wsa3[:, :, ci * 512:(ci + 1) * 512],
                                                  in_=win_v[:, :, 4096 + ci * 512:4096 + (ci + 1) * 512]),
              pwrites=[b_Wsa], dma=True, key="Wsa")
        p0_mark = A.top
        NVW = (S // 128) * 512
        Vall = big[:, NW_SBUF - NVW:NW_SBUF].bitcast(BF16); b_Vall = Buf("Vall")

        cact = A.f32(8); b_cact = Buf("cact")
        cact_rep = A.f32(8 * 128); b_crep = Buf("crep")
        wa = [A.f32(8 * 512) for _ in range(2)]
        b_wa = [Buf("wa%d" % i) for i in range(2)]
        E("act", lambda e: e.activation(out=cact, in_=pcol[:, PC_C:PC_C + 8], func=AF.Silu),
          reads=[b_pcol], writes=[b_cact])
        E("dve", lambda e: e.tensor_copy(out=cact_rep.rearrange("p (k m) -> p k m", k=8),
                                         in_=cact.unsqueeze(2).to_broadcast([128, 8, 128])),
          reads=[b_cact], writes=[b_crep])
        E("act", lambda e: e.activation(out=a_bc, in_=prow[:, PR_AL:PR_AL + 16], func=AF.Exp),
          reads=[b_prow], writes=[b_abc])
        E("dve", lambda e: e.tensor_scalar(out=a_bc, in0=a_bc, scalar1=-1.0, scalar2=None, op0=ALU.mult),
          reads=[b_abc], writes=[b_abc])
        wada_v = wada_d.rearrange("(k p) c -> p k c", p=128)
        for ci in range(6):
            r = ci % 2
            E("sp", lambda e, ci=ci, r=r: e.dma_start(out=wa[r].rearrange("p (k c) -> p k c", k=8),
                                                     in_=wada_v[:, :, ci * 512:(ci + 1) * 512]),
              writes=[b_wa[r]], dma=True)
            if ci < 4:
                for mm in range(4):
                    m = ci * 4 + mm
                    for k in range(8):
                        E("pe", lambda e, r=r, mm=mm, m=m, k=k: e.matmul(
                            bank(0)[:, m:m + 1], lhsT=wa[r][:, k * 512 + mm * 128:k * 512 + (mm + 1) * 128],
                            rhs=cact[:, k:k + 1], start=(k == 0), stop=(k == 7)),
                          reads=[b_wa[r], b_cact], pwrites=[PB[0]])
            else:
                n = ci - 4
                for k in range(8):
                    E("pe", lambda e, r=r, n=n, k=k: e.matmul(
                        bank(1 + n), lhsT=cact_rep[:, k * 128:(k + 1) * 128],
                        rhs=wa[r][:, k * 512:(k + 1) * 512], start=(k == 0), stop=(k == 7)),
                      reads=[b_wa[r], b_crep], pwrites=[PB[1 + n]])
        E("dve", lambda e: e.tensor_tensor(out=modT[:, 0:16], in0=bank(0)[:, 0:16], in1=pcol[:, PC_BADA:PC_BADA + 16],
                                           op=ALU.add), reads=[PB[0], b_pcol], writes=[b_modT])
        E("dve", lambda e: e.scalar_tensor_tensor(out=g1, in0=modT[:, 8:16], scalar=1.0, in1=pcol[:, PC_GIN:PC_GIN + 8],
                                                  op0=ALU.add, op1=ALU.mult), reads=[b_modT, b_pcol], writes=[b_g1])
        E("dve", lambda e: e.tensor_tensor(out=gate_bc, in0=psum[:, 512:1536], in1=prow[:, PR_BG:PR_BG + 1024], op=ALU.add),
          reads=[PB[1], PB[2], b_prow], writes=[b_gate])
        shiftc = modT
        S_.barrier()
        A.top = p0_mark

        xt = [A.f32(4096) for _ in range(2)]; b_xt = [Buf("xt%d" % i) for i in range(2)]
        hT = [A.bf16(8 * 512) for _ in range(2)]; b_hT = [Buf("hT%d" % i) for i in range(2)]
        junk = A.bf16(1024); b_junk = Buf("junk")
        ssq1 = A.f32(4 * NT); b_ssq1 = Buf("ssq1")
        rs1 = A.f32(4 * NT); b_rs1 = Buf("rs1")
        stq = A.bf16(8 * 512); b_stq = Buf("stq")
        stk = A.bf16(8 * 512); b_stk = Buf("stk")
        stz = A.bf16(8 * 512); b_stz = Buf("stz")
        stv = A.bf16(4 * 1024); b_stv = Buf("stv")
        stages = [(stq, b_stq, QT_d), (stk, b_stk, KT_d), (stz, b_stz, ZA_d)]
        pcnt = [0]

        def next_bank(lo, n):
            i = lo + (pcnt[0] % n)
            pcnt[0] += 1
            return i

        evq = [0]
        def p1a_front_a(tt):
            r = tt % 2
            E("sp", lambda e, tt=tt, r=r: e.dma_start(
                out=xt[r].rearrange("p (j d) -> p j d", j=4),
                in_=x_d[tt * 512:(tt + 1) * 512, :].rearrange("(j p) d -> p j d", p=128)),
              writes=[b_xt[r]], dma=True)
            for j in range(4):
                E("act", lambda e, r=r, j=j, tt=tt: e.activation(
                    out=junk, in_=xt[r][:, j * 1024:(j + 1) * 1024], func=AF.Square,
                    accum_out=ssq1[:, tt * 4 + j:tt * 4 + j + 1]),
                  reads=[b_xt[r]], writes=[b_junk], pwrites=[b_ssq1])
            E("act", lambda e, tt=tt: e.activation(out=rs1[:, tt * 4:tt * 4 + 4], in_=ssq1[:, tt * 4:tt * 4 + 4],
                                                   func=AF.Ln, scale=1.0 / 1024, bias=EPS),
              reads=[b_ssq1], pwrites=[b_rs1])
            E("act", lambda e, tt=tt: e.activation(out=rs1[:, tt * 4:tt * 4 + 4], in_=rs1[:, tt * 4:tt * 4 + 4],
                                                   func=AF.Exp, scale=-0.5),
              reads=[b_rs1], pwrites=[b_rs1])
            for j in range(4):
                E("dve", lambda e, r=r, j=j, tt=tt: e.tensor_scalar(
                    out=xt[r][:, j * 1024:(j + 1) * 1024], in0=xt[r][:, j * 1024:(j + 1) * 1024],
                    scalar1=rs1[:, tt * 4 + j:tt * 4 + j + 1], scalar2=None, op0=ALU.mult),
                  reads=[b_rs1, b_xt[r]], writes=[b_xt[r]])

        def p1a_front_b(tt):
            r = tt % 2
            for k in range(8):
                bi = next_bank(0, 4)
                for j in range(4):
                    E("pe", lambda e, r=r, j=j, k=k, bi=bi: e.transpose(
                        bank(bi)[:, j * 128:(j + 1) * 128], xt[r][:, j * 1024 + k * 128:j * 1024 + (k + 1) * 128], idf),
                      reads=[b_xt[r], b_cf], writes=[PB[bi]] if j == 0 else (), pwrites=() if j == 0 else [PB[bi]])
                if k % 2 == 0:
                    E("dve", lambda e, r=r, k=k, bi=bi: e.tensor_scalar(
                        out=hT[r][:, k * 512:(k + 1) * 512], in0=bank(bi), scalar1=g1[:, k:k + 1],
                        scalar2=shiftc[:, k:k + 1], op0=ALU.mult, op1=ALU.add),
                      reads=[PB[bi], b_g1, b_modT], pwrites=[b_hT[r]] if k else (), writes=() if k else [b_hT[r]])
                else:
                    E("act", lambda e, r=r, k=k, bi=bi: e.activation(
                        out=hT[r][:, k * 512:(k + 1) * 512], in_=bank(bi), func=AF.Identity,
                        scale=g1[:, k:k + 1], bias=shiftc[:, k:k + 1]),
                      reads=[PB[bi], b_g1, b_modT], pwrites=[b_hT[r]])
            E("pool", lambda e, r=r, tt=tt: e.dma_start(
                out=HT_d[:, :, tt * 512:(tt + 1) * 512].rearrange("k p t -> p k t"),
                in_=hT[r].rearrange("p (k t) -> p k t", k=8)), reads=[b_hT[r]], dma=True, key="hTo%d" % r)

        def p1a_back1(tt):
            r = tt % 2
            for grp in range(3):
                stg, b_stg, dst = stages[grp]
                c_base = [0, 1024, 3072][grp]
                for m in range(8):
                    bi = next_bank(4, 4)
                    for k in range(8):
                        E("pe", lambda e, r=r, k=k, bi=bi, c0=c_base + m * 128: e.matmul(
                            bank(bi), lhsT=Wa[:, k * 4096 + c0:k * 4096 + c0 + 128],
                            rhs=hT[r][:, k * 512:(k + 1) * 512], start=(k == 0), stop=(k == 7)),
                          reads=[b_Wa, b_hT[r]], writes=[PB[bi]] if k == 0 else (), pwrites=() if k == 0 else [PB[bi]])
                    wkw = dict(writes=[b_stg]) if m == 0 else dict(pwrites=[b_stg])
                    osl = stg[:, m * 512:(m + 1) * 512]
                    if grp == 0:
                        if evq[0] % 2 == 0:
                            E("act", lambda e, osl=osl, bi=bi: e.mul(out=osl, in_=bank(bi), mul=0.125),
                              reads=[PB[bi]], **wkw)
                        else:
                            E("dve", lambda e, osl=osl, bi=bi: e.tensor_scalar(out=osl, in0=bank(bi), scalar1=0.125,
                                                                               scalar2=None, op0=ALU.mult),
                              reads=[PB[bi]], **wkw)
                        evq[0] += 1
                    elif grp == 1:
                        if evq[0] % 2 == 0:
                            E("act", lambda e, osl=osl, bi=bi: e.copy(out=osl, in_=bank(bi)), reads=[PB[bi]], **wkw)
                        else:
                            E("dve", lambda e, osl=osl, bi=bi: e.tensor_copy(out=osl, in_=bank(bi)), reads=[PB[bi]], **wkw)
                        evq[0] += 1
                    else:
                        E("act", lambda e, osl=osl, bi=bi: e.activation(out=osl, in_=bank(bi), func=AF.Silu),
                          reads=[PB[bi]], **wkw)
                E("pool", lambda e, stg=stg, dst=dst, tt=tt: e.dma_start(
                    out=dst[:, tt * 512:(tt + 1) * 512].rearrange("(m p) t -> p m t", p=128),
                    in_=stg.rearrange("p (m t) -> p m t", m=8)), reads=[b_stg], dma=True)

        def p1a_back2(tt):
            r = tt % 2
            for j in range(4):
                for n in range(2):
                    bi = next_bank(4, 4)
                    for k in range(8):
                        E("pe", lambda e, r=r, k=k, bi=bi, j=j, n=n: e.matmul(
                            bank(bi), lhsT=hT[r][:, k * 512 + j * 128:k * 512 + (j + 1) * 128],
                            rhs=Wa[:, k * 4096 + 2048 + n * 512:k * 4096 + 2048 + (n + 1) * 512],
                            start=(k == 0), stop=(k == 7)),
                          reads=[b_Wa, b_hT[r]], writes=[PB[bi]] if k == 0 else (), pwrites=() if k == 0 else [PB[bi]])
                    wkw = dict(writes=[b_stv]) if (j == 0 and n == 0) else dict(pwrites=[b_stv])
                    osl = stv[:, j * 1024 + n * 512:j * 1024 + (n + 1) * 512]
                    if evq[0] % 2 == 0:
                        E("act", lambda e, osl=osl, bi=bi: e.copy(out=osl, in_=bank(bi)), reads=[PB[bi]], **wkw)
                    else:
                        E("dve", lambda e, osl=osl, bi=bi: e.tensor_copy(out=osl, in_=bank(bi)), reads=[PB[bi]], **wkw)
                    evq[0] += 1
            E("pool", lambda e, tt=tt: e.dma_start(
                out=V_d[tt * 512:(tt + 1) * 512, :].rearrange("(j p) c -> p j c", p=128),
                in_=stv.rearrange("p (j c) -> p j c", j=4)), reads=[b_stv], dma=True)

        p1a_front_a(0)
        p1a_front_b(0)
        for tt in range(NT):
            if tt + 1 < NT:
                p1a_front_a(tt + 1)
            p1a_back1(tt)
            if tt + 1 < NT:
                p1a_front_b(tt + 1)
            p1a_back2(tt)
        S_.barrier()
        A.top = base_mark

        A.top = wsa_mark
        A.nw = NW_SBUF - NVW
        Wsb = A.bf16(8 * NWB2); b_Wsb = Buf("Wsb")
        Wsb3 = Wsb.rearrange("p (k c) -> p k c", k=8)
        for (c0, c1) in ((0, 528), (528, 1040)):
            E("pool", lambda e, c0=c0, c1=c1: e.dma_start(out=Wsb3[:, :, c0:c1], in_=win_v[:, :, 5632 + c0:5632 + c1]),
              pwrites=[b_Wsb], dma=True, key="Wsb")
        Vall3 = Vall.rearrange("p (n c) -> p n c", c=1024)
        V_v = V_d.rearrange("(n p) c -> p n c", p=128)
        nvq = max(1, (S // 128) // 8)
        for vq in range(0, S // 128, nvq):
            E("sp", lambda e, vq=vq: e.dma_start(out=Vall3[:, vq:vq + nvq, :], in_=V_v[:, vq:vq + nvq, :]),
              pwrites=[b_Vall], dma=True, key="Vall")
        hSr = [A.bf16(8 * 512) for _ in range(2)]; b_hSr = [Buf("hS%d" % i) for i in range(2)]
        XW_ = 515
        xin = A.f32(12 * XW_); b_xin = [Buf("xin%d" % m) for m in range(12)]
        cv = A.f32(12 * 512); b_cv = [Buf("cv%d" % m) for m in range(12)]
        BCs = A.bf16(4 * 512); b_BCs = Buf("BCs")
        ZS = A.bf16(8 * 512); b_ZS = Buf("ZSst")
        dtv = A.f32(64); b_dtv = Buf("dtv")
        dts = A.f32(64); b_dts = Buf("dts")
        lds = A.f32(64); b_lds = Buf("lds")
        cw = pcol[:, PC_CW:PC_CW + 48]
        cbias = pcol[:, PC_CB:PC_CB + 12]
        E("dve", lambda e: e.memset(xin, 0.0), writes=b_xin)
        for tt in range(NT):
            hS = hSr[tt % 2]; b_hS = b_hSr[tt % 2]
            E("sp", lambda e, tt=tt, hS=hS: e.dma_start(
                out=hS.rearrange("p (k t) -> p k t", k=8),
                in_=HT_d[:, :, tt * 512:(tt + 1) * 512].rearrange("k p t -> p k t")), writes=[b_hS], dma=True)
            def z_tile(m, hS=hS, b_hS=b_hS):
                bi = next_bank(0, 4)
                for k in range(8):
                    E("pe", lambda e, k=k, bi=bi, c0=16 + m * 128: e.matmul(
                        bank(bi), lhsT=Wsb[:, k * NWB2 + c0:k * NWB2 + c0 + 128], rhs=hS[:, k * 512:(k + 1) * 512],
                        start=(k == 0), stop=(k == 7)),
                      reads=[b_Wsb, b_hS], writes=[PB[bi]] if k == 0 else (), pwrites=() if k == 0 else [PB[bi]])
                E("act", lambda e, bi=bi: e.activation(out=ZS[:, m * 512:(m + 1) * 512], in_=bank(bi), func=AF.Silu),
                  reads=[PB[bi]], writes=[b_ZS] if m == 0 else (), pwrites=() if m == 0 else [b_ZS])

            pend_silu = []
            zq = [0]
            for m in range(12):
                if m % 3 != 0 and zq[0] < 8:
                    z_tile(zq[0])
                    zq[0] += 1
                if tt > 0:
                    E("dve", lambda e, m=m: e.tensor_copy(out=xin[:, m * XW_:m * XW_ + 3], in_=xin[:, m * XW_ + 512:m * XW_ + 515]),
                      reads=[b_xin[m]], writes=[b_xin[m]])
                bi = next_bank(0, 4)
                for k in range(8):
                    E("pe", lambda e, k=k, bi=bi, c0=m * 128, hS=hS: e.matmul(
                        bank(bi), lhsT=Wsa[:, k * NWA + c0:k * NWA + c0 + 128], rhs=hS[:, k * 512:(k + 1) * 512],
                        start=(k == 0), stop=(k == 7)),
                      reads=[b_Wsa, b_hS], writes=[PB[bi]] if k == 0 else (), pwrites=() if k == 0 else [PB[bi]])
                E("act", lambda e, m=m, bi=bi: e.copy(out=xin[:, m * XW_ + 3:m * XW_ + 515], in_=bank(bi)),
                  reads=[PB[bi]], writes=[b_xin[m]])
                acc = cv[:, m * 512:(m + 1) * 512]
                E("dve", lambda e, m=m, acc=acc: e.tensor_scalar(
                    out=acc, in0=xin[:, m * XW_ + 3:m * XW_ + 515], scalar1=cw[:, m * 4 + 3:m * 4 + 4],
                    scalar2=cbias[:, m:m + 1], op0=ALU.mult, op1=ALU.add),
                  reads=[b_xin[m], b_pcol], writes=[b_cv[m]])
                for kk in range(3):
                    E("dve", lambda e, m=m, acc=acc, kk=kk: e.scalar_tensor_tensor(
                        out=acc, in0=xin[:, m * XW_ + kk:m * XW_ + kk + 512], scalar=cw[:, m * 4 + kk:m * 4 + kk + 1],
                        in1=acc, op0=ALU.mult, op1=ALU.add),
                      reads=[b_xin[m], b_pcol, b_cv[m]], writes=[b_cv[m]])

                def silu_emit(m=m, acc=acc):
                    if m < 8:
                        E("act", lambda e: e.activation(out=acc, in_=acc, func=AF.Silu),
                          reads=[b_cv[m]], writes=[b_cv[m]])
                    else:
                        dst = BCs[:, (m - 8) * 512:(m - 7) * 512]
                        E("act", lambda e: e.activation(out=dst, in_=acc, func=AF.Silu),
                          reads=[b_cv[m]], writes=[b_BCs] if m == 8 else (), pwrites=() if m == 8 else [b_BCs])

                pend_silu.append(silu_emit)
                if len(pend_silu) > 4:
                    pend_silu.pop(0)()
            while zq[0] < 8:
                z_tile(zq[0])
                zq[0] += 1
            while pend_silu:
                pend_silu.pop(0)()
            E("pool", lambda e, tt=tt: e.dma_start(
                out=XS_d[:, tt * 512:(tt + 1) * 512].rearrange("(m p) t -> p m t", p=128),
                in_=cv[:, 0:8 * 512].rearrange("p (m t) -> p m t", m=8)), reads=b_cv[0:8], dma=True, key="cvo")
            E("pool", lambda e, tt=tt: e.dma_start(
                out=BC_d[:, tt * 512:(tt + 1) * 512].rearrange("(m p) t -> p m t", p=128),
                in_=BCs.rearrange("p (m t) -> p m t", m=4)), reads=[b_BCs], dma=True, key="bco")
            E("pool", lambda e, tt=tt: e.dma_start(
                out=ZS_d[:, tt * 512:(tt + 1) * 512].rearrange("(m p) t -> p m t", p=128),
                in_=ZS.rearrange("p (m t) -> p m t", m=8)), reads=[b_ZS], dma=True, key="zso")
            for j in range(4):
                for k in range(8):
                    E("pe", lambda e, j=j, k=k, hS=hS: e.matmul(
                        bank(7)[:, j * 16:(j + 1) * 16], lhsT=hS[:, k * 512 + j * 128:k * 512 + (j + 1) * 128],
                        rhs=Wsb[:, k * NWB2:k * NWB2 + 16], start=(k == 0), stop=(k == 7)),
                      reads=[b_Wsb, b_hS], writes=[PB[7]] if (j == 0 and k == 0) else (),
                      pwrites=() if (j == 0 and k == 0) else [PB[7]])
            E("dve", lambda e: e.tensor_tensor(
                out=dtv.rearrange("p (j h) -> p j h", j=4), in0=bank(7)[:, 0:64].rearrange("p (j h) -> p j h", j=4),
                in1=prow[:, PR_DTB:PR_DTB + 16].unsqueeze(1).to_broadcast([128, 4, 16]), op=ALU.add),
              reads=[PB[7], b_prow], writes=[b_dtv])
            E("act", lambda e: e.activation(out=dtv, in_=dtv, func=AF.Exp), reads=[b_dtv], writes=[b_dtv])
            E("act", lambda e: e.activation(out=dts, in_=dtv, func=AF.Ln, bias=1.0), reads=[b_dtv], writes=[b_dts])
            E("dve", lambda e: e.tensor_tensor(
                out=lds.rearrange("p (j h) -> p j h", j=4), in0=dts.rearrange("p (j h) -> p j h", j=4),
                in1=a_bc.unsqueeze(1).to_broadcast([128, 4, 16]), op=ALU.mult),
              reads=[b_dts, b_abc], writes=[b_lds])
            E("pool", lambda e, tt=tt: e.dma_start(
                out=DT_d[tt * 512:(tt + 1) * 512, 0:16].rearrange("(j p) h -> p j h", p=128),
                in_=dts.rearrange("p (j h) -> p j h", j=4)), reads=[b_dts], dma=True, key="dto")
            E("pool", lambda e, tt=tt: e.dma_start(
                out=DT_d[tt * 512:(tt + 1) * 512, 16:32].rearrange("(j p) h -> p j h", p=128),
                in_=lds.rearrange("p (j h) -> p j h", j=4)), reads=[b_lds], dma=True, key="ldo")
        S_.barrier()
        A.top = base_mark

        NB = NCH
        QT = [A.bf16(S) for _ in range(1)] * 2; KT = [A.bf16(S) for _ in range(1)] * 2
        ZAq = [A.bf16(512) for _ in range(2)]; b_ZAq = [Buf("ZAq%d" % i) for i in range(2)]
        b_QT = [Buf("QT0")] * 2; b_KT = [Buf("KT0")] * 2
        NE, NL, NX, NWB = 3, 4, 2, 3
        eb = [A.f32(1024) for _ in range(NE)]; b_eb = [Buf("e%d" % i) for i in range(NE)]
        lb = [A.bf16(1024) for _ in range(NL)]; b_lb = [Buf("l%d" % i) for i in range(NL)]
        xb = [A.f32(1024) for _ in range(NX)]; b_xb = [Buf("xx%d" % i) for i in range(NX)]
        wb = [A.bf16(1024) for _ in range(NWB)]; b_wb = [Buf("w%d" % i) for i in range(NWB)]
        sqa = A.f32(512); b_sqa = Buf("sqa")
        ats = [A.bf16(512) for _ in range(2)]; b_ats = [Buf("ats%d" % i) for i in range(2)]
        atq = [0]

        SB7 = 7
        b7 = bank(SB7)
        xsI = [A.f32(1024) for _ in range(2)]; b_xsI = [Buf("xsI%d" % i) for i in range(2)]
        bcI = [A.bf16(512) for _ in range(2)]; b_bcI = [Buf("bcI%d" % i) for i in range(2)]
        zsI = [A.bf16(1024) for _ in range(2)]; b_zsI = [Buf("zsI%d" % i) for i in range(2)]
        dtI = [A.f32(32) for _ in range(2)]; b_dtI = [Buf("dtI%d" % i) for i in range(2)]
        Rr = A.f32(16 * 128); b_R = Buf("R")
        decay = A.f32(16 * 128); b_decay = Buf("decay")
        Mm = A.bf16(16 * 128); b_M = Buf("M")
        CE = A.bf16(16 * 128); b_CE = Buf("CE")
        xtf = A.f32(1024); b_xtf = Buf("xtf")
        xtb = A.bf16(1024); b_xtb = Buf("xtb")
        Btk = A.bf16(256); b_Btk = Buf("Btk")
        XWt = A.bf16(1024); b_XW = Buf("XW")
        cbs = A.f32(256); b_cbs = Buf("cbs")
        dcd = A.f32(32); b_dcd = Buf("dcd")
        wgt = A.f32(16); b_wgt = Buf("wgt")
        state = A.f32(1024); b_state = Buf("state")
        stateb = A.bf16(1024); b_stateb = Buf("stateb")
        stsT = A.bf16(8 * 512); b_stsT = Buf("stsT")
        ub = Rr[:, 0:1024]
        sqs = Rr[:, 1024:2048]
        E("dve", lambda e: e.memset(state, 0.0), writes=[b_state])
        E("dve", lambda e: e.memset(stateb, 0.0), writes=[b_stateb])

        def ssd_load(c):
            sl = c % 2
            E("sp", lambda e: e.dma_start(out=xsI[sl].rearrange("p (m t) -> p m t", m=8),
                                          in_=XS_d[:, c * 128:(c + 1) * 128].rearrange("(m p) t -> p m t", p=128)),
              writes=[b_xsI[sl]], dma=True)
            E("sp", lambda e: e.dma_start(out=bcI[sl].rearrange("p (m t) -> p m t", m=4),
                                          in_=BC_d[:, c * 128:(c + 1) * 128].rearrange("(m p) t -> p m t", p=128)),
              writes=[b_bcI[sl]], dma=True)
            E("sp", lambda e: e.dma_start(out=zsI[sl].rearrange("p (m t) -> p m t", m=8),
                                          in_=ZS_d[:, c * 128:(c + 1) * 128].rearrange("(m p) t -> p m t", p=128)),
              writes=[b_zsI[sl]], dma=True)
            E("sp", lambda e: e.dma_start(out=dtI[sl], in_=DT_d[c * 128:(c + 1) * 128, :]), writes=[b_dtI[sl]], dma=True)

        def ssd_chunk(c):
            sl = c % 2
            xsT, bc_, zs_, dt_ = xsI[sl], bcI[sl], zsI[sl], dtI[sl]
            bx, bb, bz, bd = b_xsI[sl], b_bcI[sl], b_zsI[sl], b_dtI[sl]
            dtj = dt_[:, 0:16]
            ldj = dt_[:, 16:32]
            jq = c % 4
            if c + 1 < NCH:
                ssd_load(c + 1)
            R3 = Rr.rearrange("p (h l) -> p h l", h=16)

            def m_ops(q):
                for h in range(q * 4, q * 4 + 4):
                    g = h // 8
                    E("dve", lambda e, h=h, g=g: e.scalar_tensor_tensor(
                        out=Mm[:, h * 128:(h + 1) * 128], in0=decay[:, h * 128:(h + 1) * 128], scalar=dtj[:, h:h + 1],
                        in1=cbs[:, g * 128:(g + 1) * 128], op0=ALU.mult, op1=ALU.mult),
                      reads=[b_decay, bd, b_cbs], writes=[b_M] if h == 0 else (), pwrites=() if h == 0 else [b_M])

            def ce_op(g):
                E("dve", lambda e: e.tensor_tensor(
                    out=CE[:, g * 1024:(g + 1) * 1024].rearrange("p (h l) -> p h l", h=8),
                    in0=decay[:, g * 1024:(g + 1) * 1024].rearrange("p (h l) -> p h l", h=8),
                    in1=bc_[:, 256 + g * 128:256 + (g + 1) * 128].unsqueeze(1).to_broadcast([128, 8, 128]), op=ALU.mult),
                  reads=[b_decay, bb], writes=[b_CE] if g == 0 else (), pwrites=() if g == 0 else [b_CE])

            for hf in range(2):
                E("dve", lambda e, hf=hf: e.tensor_tensor(
                    out=R3[:, hf * 8:(hf + 1) * 8, :], in0=LEf.unsqueeze(1).to_broadcast([128, 8, 128]),
                    in1=ldj[:, hf * 8:(hf + 1) * 8].unsqueeze(2).to_broadcast([128, 8, 128]), op=ALU.mult),
                  reads=[b_cf, bd], writes=[b_R] if hf == 0 else (), pwrites=() if hf == 0 else [b_R])
                for m4 in range(4):
                    m = hf * 4 + m4
                    E("pe", lambda e, m=m, m4=m4: e.transpose(b7[:, m4 * 128:(m4 + 1) * 128], xsT[:, m * 128:(m + 1) * 128], idf),
                      reads=[bx, b_cf], writes=[PB[SB7]] if m4 == 0 else (), pwrites=() if m4 == 0 else [PB[SB7]])
                E("dve", lambda e, hf=hf: e.tensor_copy(out=xtf[:, hf * 512:(hf + 1) * 512], in_=b7), reads=[PB[SB7]],
                  writes=[b_xtf] if hf == 0 else (), pwrites=() if hf == 0 else [b_xtf])
                yield
            E("dve", lambda e: e.tensor_copy(out=xtb, in_=xtf), reads=[b_xtf], writes=[b_xtb])
            b7b = b7[:, 0:128].bitcast(BF16)
            for g in range(2):
                E("pe", lambda e, g=g: e.transpose(b7b[:, g * 128:(g + 1) * 128], bc_[:, g * 128:(g + 1) * 128], idb),
                  reads=[bb, b_cb], writes=[PB[SB7]] if g == 0 else (), pwrites=() if g == 0 else [PB[SB7]])
            for g in range(2):
                E("pe", lambda e, g=g: e.matmul(b7[:, 256 + g * 128:256 + (g + 1) * 128], lhsT=bc_[:, g * 128:(g + 1) * 128],
                                                rhs=bc_[:, 256 + g * 128:256 + (g + 1) * 128], start=True, stop=True),
                  reads=[bb], pwrites=[PB[SB7]])
            E("dve", lambda e: e.tensor_copy(out=Btk, in_=b7b), reads=[PB[SB7]], writes=[b_Btk])
            E("dve", lambda e: e.tensor_copy(out=cbs, in_=b7[:, 256:512]), reads=[PB[SB7]], writes=[b_cbs])
            yield
            for q in range(4):
                E("pe", lambda e, q=q: e.matmul(b7, lhsT=GTf, rhs=Rr[:, q * 512:(q + 1) * 512], start=True, stop=False,
                                                skip_group_check=True), reads=[b_R, b_cf], writes=[PB[SB7]])
                E("pe", lambda e: e.matmul(b7, lhsT=idb, rhs=neg4, start=False, stop=True, skip_group_check=True),
                  reads=[b_cb, b_neg4], pwrites=[PB[SB7]])
                E("act", lambda e, q=q: e.activation(out=decay[:, q * 512:(q + 1) * 512], in_=b7, func=AF.Exp),
                  reads=[PB[SB7]], writes=[b_decay] if q == 0 else (), pwrites=() if q == 0 else [b_decay])
                if q >= 1:
                    m_ops(q - 1)
                yield
            E("pe", lambda e: e.matmul(b7[:, 0:16], lhsT=GTf, rhs=ldj, start=True, stop=True),
              reads=[bd, b_cf], writes=[PB[SB7]])
            E("pe", lambda e: e.matmul(b7[:, 16:32], lhsT=onesf, rhs=ldj, start=True, stop=True),
              reads=[bd, b_cf], pwrites=[PB[SB7]])
            E("act", lambda e: e.activation(out=dcd, in_=b7[:, 0:32], func=AF.Exp), reads=[PB[SB7]], writes=[b_dcd])
            m_ops(3)
            yield
            for q in range(4):
                E("pe", lambda e, q=q: e.matmul(b7, lhsT=onesf, rhs=Rr[:, q * 512:(q + 1) * 512], start=True, stop=True),
                  reads=[b_R, b_cf], writes=[PB[SB7]])
                E("act", lambda e, q=q: e.activation(out=decay[:, q * 512:(q + 1) * 512], in_=b7, func=AF.Exp),
                  reads=[PB[SB7]], writes=[b_decay] if q == 0 else (), pwrites=() if q == 0 else [b_decay])
                if q == 0:
                    E("dve", lambda e: e.tensor_tensor(out=wgt, in0=dtj, in1=dcd[:, 0:16], op=ALU.mult),
                      reads=[bd, b_dcd], writes=[b_wgt])
                    E("dve", lambda e: e.tensor_tensor(
                        out=XWt.rearrange("p (h q) -> p h q", h=16), in0=xtf.rearrange("p (h q) -> p h q", h=16),
                        in1=wgt.unsqueeze(2).to_broadcast([128, 16, 64]), op=ALU.mult),
                      reads=[b_xtf, b_wgt], writes=[b_XW])
                if q == 2:
                    ce_op(0)
                yield
            ce_op(1)
            yield
            for hb in range(2):
                for p4 in range(4):
                    pr = hb * 4 + p4
                    csl = slice(p4 * 128, (p4 + 1) * 128)
                    for half in range(2):
                        h = 2 * pr + half
                        first = (p4 == 0 and half == 0)
                        E("pe", lambda e, csl=csl, half=half, h=h: e.matmul(
                            b7[64 * half:64 * half + 64, csl], lhsT=xtb[:, h * 64:(h + 1) * 64],
                            rhs=Mm[:, h * 128:(h + 1) * 128], start=True, stop=False, skip_group_check=True),
                          reads=[b_xtb, b_M], writes=[PB[SB7]] if first else (), pwrites=() if first else [PB[SB7]])
                        E("pe", lambda e, csl=csl, half=half, h=h: e.matmul(
                            b7[64 * half:64 * half + 64, csl], lhsT=stateb[:, h * 64:(h + 1) * 64],
                            rhs=CE[:, h * 128:(h + 1) * 128], start=False, stop=True, skip_group_check=True),
                          reads=[b_stateb, b_CE], pwrites=[PB[SB7]])
                usl = ub[:, hb * 512:(hb + 1) * 512]
                u3 = usl.rearrange("p (m t) -> p m t", m=4)
                E("dve", lambda e, hb=hb, u3=u3: e.tensor_tensor(
                    out=u3, in0=xsT[:, hb * 512:(hb + 1) * 512].rearrange("p (m t) -> p m t", m=4),
                    in1=pcol[:, PC_D + hb * 4:PC_D + hb * 4 + 4].unsqueeze(2).to_broadcast([128, 4, 128]), op=ALU.mult),
                  reads=[bx, b_pcol], writes=[b_R] if hb == 0 else (), pwrites=() if hb == 0 else [b_R])
                E("dve", lambda e, usl=usl: e.tensor_tensor(out=usl, in0=usl, in1=b7, op=ALU.add),
                  reads=[b_R, PB[SB7]], pwrites=[b_R])
                yield
                E("dve", lambda e, hb=hb, usl=usl: e.tensor_tensor(out=usl, in0=usl, in1=zs_[:, hb * 512:(hb + 1) * 512],
                                                                  op=ALU.mult), reads=[b_R, bz], pwrites=[b_R])
                o3 = stsT.rearrange("p (m t) -> p m t", m=8)[:, hb * 4:hb * 4 + 4, jq * 128:(jq + 1) * 128]
                E("dve", lambda e, hb=hb, u3=u3, o3=o3: e.tensor_tensor(
                    out=o3, in0=u3,
                    in1=pcol[:, PC_GS + hb * 4:PC_GS + hb * 4 + 4].unsqueeze(2).to_broadcast([128, 4, 128]), op=ALU.mult),
                  reads=[b_R, b_pcol], pwrites=[b_stsT])
                yield
            E("dve", lambda e: e.tensor_tensor(out=sqs, in0=ub, in1=ub, op=ALU.mult), reads=[b_R], pwrites=[b_R])
            yield
            for pr in range(8):
                E("pe", lambda e, pr=pr: e.matmul(b7[:, 0:1], lhsT=sqs[:, pr * 128:(pr + 1) * 128], rhs=onesf[:, 0:1],
                                                 start=(pr == 0), stop=(pr == 7)),
                  reads=[b_R, b_cf], writes=[PB[SB7]] if pr == 0 else (), pwrites=() if pr == 0 else [PB[SB7]])
            E("dve", lambda e: e.tensor_copy(out=ssq_s[:, c:c + 1], in_=b7[:, 0:1]), reads=[PB[SB7]], pwrites=[b_ssqs])
            yield
            for g in range(2):
                E("pe", lambda e, g=g: e.matmul(b7, lhsT=Btk[:, g * 128:(g + 1) * 128], rhs=XWt[:, g * 512:(g + 1) * 512],
                                                start=True, stop=True), reads=[b_Btk, b_XW], writes=[PB[SB7]])
                ssl = state[:, g * 512:(g + 1) * 512]
                E("dve", lambda e, g=g, ssl=ssl: e.tensor_tensor(
                    out=ssl.rearrange("p (h q) -> p h q", h=8), in0=ssl.rearrange("p (h q) -> p h q", h=8),
                    in1=dcd[:, 16 + g * 8:16 + g * 8 + 8].unsqueeze(2).to_broadcast([128, 8, 64]), op=ALU.mult),
                  reads=[b_state, b_dcd], writes=[b_state] if g == 0 else (), pwrites=() if g == 0 else [b_state])
                E("dve", lambda e, ssl=ssl: e.tensor_tensor(out=ssl, in0=ssl, in1=b7, op=ALU.add),
                  reads=[b_state, PB[SB7]], pwrites=[b_state])
                yield
            E("dve", lambda e: e.tensor_copy(out=stateb, in_=state), reads=[b_state], writes=[b_stateb])
            if jq == 3:
                tt_ = c // 4
                E("sp", lambda e: e.dma_start(
                    out=AT_d[1024:2048, tt_ * 512:(tt_ + 1) * 512].rearrange("(m p) t -> p m t", p=128),
                    in_=stsT.rearrange("p (m t) -> p m t", m=8)), reads=[b_stsT], dma=True)
            yield

        def ssd_all():
            for c in range(NCH):
                for _ in ssd_chunk(c):
                    yield

        ssd_load(0)
        ssd_gen = ssd_all()

        def ssd_step():
            try:
                next(ssd_gen)
                return True
            except StopIteration:
                return False

        def v2(ap, c0):
            return ap.rearrange("p (b c) -> p b c", b=2)[:, :, c0:512]

        def load_hp(hp):
            r = hp % 2
            E("sp", lambda e: e.dma_start(out=QT[r], in_=QT_d[hp * 128:(hp + 1) * 128, :]), writes=[b_QT[r]], dma=True)
            E("sp", lambda e: e.dma_start(out=KT[r], in_=KT_d[hp * 128:(hp + 1) * 128, :]), writes=[b_KT[r]], dma=True)

        load_hp(0)
        ucount = [0]

        def do_hp(hp, r):
            if hp > 0:
                load_hp(hp)
            steps = []
            for qt in range(NT):
                for jj in range(4 * qt + 3, -1, -1):
                    steps.append((qt, jj))
            nu = len(steps)
            info = {}

            def st_qk(n):
                qt, jj = steps[n]
                u = ucount[0] + n
                kk = jj - 4 * qt
                c0 = 128 * kk if kk > 0 else 0
                ap_ = 2 * (u % 2)
                info[n] = (qt, jj, u, kk, c0, ap_)
                if jj == 4 * qt + 3:
                    zi = (hp * NT + qt) % 2
                    E("sp", lambda e, zi=zi: e.dma_start(out=ZAq[zi], in_=ZA_d[hp * 128:(hp + 1) * 128, qt * 512:(qt + 1) * 512]),
                      writes=[b_ZAq[zi]], dma=True)
                for half in range(2):
                    pb = 64 * half
                    E("pe", lambda e, pb=pb, half=half: e.matmul(
                        bank(ap_ + half)[:, c0:512], lhsT=KT[r][pb:pb + 64, jj * 128:(jj + 1) * 128],
                        rhs=QT[r][pb:pb + 64, qt * 512 + c0:(qt + 1) * 512], start=True, stop=True),
                      reads=[b_KT[r], b_QT[r]], writes=[PB[ap_ + half]])

            def st_act1(n):
                qt, jj, u, kk, c0, ap_ = info[n]
                ei, li = u % NE, u % NL
                E("act", lambda e: e.activation(out=v2(eb[ei], c0), in_=v2(psum[:, ap_ * 512:(ap_ + 2) * 512], c0), func=AF.Exp),
                  reads=[PB[ap_], PB[ap_ + 1]], writes=[b_eb[ei]])
                if kk >= 0:
                    ev = eb[ei].rearrange("p (b c) -> p b c", b=2)[:, :, c0:c0 + 128]
                    E("pool", lambda e: e.tensor_tensor(out=ev, in0=ev, in1=LTf.unsqueeze(1).to_broadcast([128, 2, 128]),
                                                        op=ALU.mult), reads=[b_eb[ei], b_cf], writes=[b_eb[ei]])

            def st_act2(n):
                qt, jj, u, kk, c0, ap_ = info[n]
                ei, li = u % NE, u % NL
                E("act", lambda e: e.activation(out=v2(lb[li], c0), in_=v2(eb[ei], c0), func=AF.Ln, bias=1.0),
                  reads=[b_eb[ei]], writes=[b_lb[li]])

            def st_ge(n):
                qt, jj, u, kk, c0, ap_ = info[n]
                li, xi = u % NL, u % NX
                for half in range(2):
                    cbk = 4 + half
                    if jj == 4 * qt + 3:
                        E("pe", lambda e, cbk=cbk: e.matmul(bank(cbk), lhsT=zb[:, 0:128], rhs=zb, start=True, stop=False,
                                                            skip_group_check=True), reads=[b_zb], writes=[PB[cbk]])
                    E("pe", lambda e, cbk=cbk, half=half: e.matmul(
                        bank(cbk)[:, c0:512], lhsT=GEb, rhs=lb[li][:, half * 512 + c0:(half + 1) * 512],
                        start=False, stop=False, skip_group_check=True), reads=[b_lb[li], b_cb], writes=[PB[cbk]])
                E("act", lambda e: e.activation(out=v2(xb[xi], c0), in_=v2(psum[:, 2048:3072], c0), func=AF.Exp, scale=-1.0),
                  reads=[PB[4], PB[5]], writes=[b_xb[xi]])

            def st_lt(n):
                qt, jj, u, kk, c0, ap_ = info[n]
                if jj == 0:
                    return
                li = u % NL
                for half in range(2):
                    cbk = 4 + half
                    E("pe", lambda e, cbk=cbk, half=half: e.matmul(
                        bank(cbk)[:, c0:512], lhsT=LTb, rhs=lb[li][:, half * 512 + c0:(half + 1) * 512],
                        start=False, stop=False, skip_group_check=True), reads=[b_lb[li], b_cb], writes=[PB[cbk]])

            def st_w(n):
                qt, jj, u, kk, c0, ap_ = info[n]
                ei, xi, wi = u % NE, u % NX, u % NWB
                E("dve", lambda e: e.tensor_tensor(out=v2(wb[wi], c0), in0=v2(eb[ei], c0), in1=v2(xb[xi], c0), op=ALU.mult),
                  reads=[b_eb[ei], b_xb[xi]], writes=[b_wb[wi]])

            def st_pv(n):
                qt, jj, u, kk, c0, ap_ = info[n]
                wi = u % NWB
                ob = 6
                for half in range(2):
                    pb = 64 * half
                    hh = 2 * hp + half
                    if jj == 4 * qt + 3:
                        E("pe", lambda e, pb=pb: e.matmul(bank(ob)[pb:pb + 64, :], lhsT=zb[:, 0:64], rhs=zb, start=True,
                                                          stop=False, skip_group_check=True), reads=[b_zb],
                          writes=[PB[ob]] if half == 0 else (), pwrites=() if half == 0 else [PB[ob]])
                    E("pe", lambda e, pb=pb, half=half, hh=hh: e.matmul(
                        bank(ob)[pb:pb + 64, c0:512], lhsT=Vall[:, jj * 1024 + hh * 64:jj * 1024 + hh * 64 + 64],
                        rhs=wb[wi][:, half * 512 + c0:(half + 1) * 512], start=False, stop=(jj == 0), skip_group_check=True),
                      reads=[b_Vall, b_wb[wi]], pwrites=[PB[ob]])

            def st_epi(n):
                qt, jj, u, kk, c0, ap_ = info[n]
                ob = 6
                if jj == 0:
                    E("act", lambda e: e.activation(out=sqa, in_=bank(ob), func=AF.Square), reads=[PB[ob]], writes=[b_sqa])
                    for sub in range(4):
                        E("pe", lambda e, sub=sub: e.matmul(bank(0)[:, sub:sub + 1], lhsT=sqa[:, sub * 128:(sub + 1) * 128],
                                                            rhs=onesf[:, 0:1], start=True, stop=True),
                          reads=[b_sqa, b_cf], writes=[PB[0]] if sub == 0 else (), pwrites=() if sub == 0 else [PB[0]])
                    if hp == 0:
                        E("dve", lambda e: e.tensor_copy(out=ssq_a[:, 4 * qt:4 * qt + 4], in_=bank(0)[:, 0:4]),
                          reads=[PB[0]], pwrites=[b_ssqa])
                    else:
                        E("dve", lambda e: e.tensor_tensor(out=ssq_a[:, 4 * qt:4 * qt + 4], in0=ssq_a[:, 4 * qt:4 * qt + 4],
                                                           in1=bank(0)[:, 0:4], op=ALU.add),
                          reads=[PB[0], b_ssqa], writes=[b_ssqa])
                    ai = atq[0] % 2
                    atq[0] += 1
                    E("dve", lambda e: e.scalar_tensor_tensor(
                        out=ats[ai], in0=bank(ob), scalar=pcol[:, PC_GA + hp:PC_GA + hp + 1],
                        in1=ZAq[(hp * NT + qt) % 2], op0=ALU.mult, op1=ALU.mult),
                      reads=[PB[ob], b_pcol, b_ZAq[(hp * NT + qt) % 2]], writes=[b_ats[ai]])
                    E("sp", lambda e: e.dma_start(out=AT_d[hp * 128:(hp + 1) * 128, qt * 512:(qt + 1) * 512], in_=ats[ai]),
                      reads=[b_ats[ai]], dma=True)

            for n in range(nu + 3):
                if n < nu:
                    st_qk(n)
                if 0 <= n - 2 < nu:
                    st_lt(n - 2)
                if n < nu:
                    st_act1(n)
                if 0 <= n - 1 < nu:
                    st_ge(n - 1)
                if n < nu:
                    st_act2(n)
                if 0 <= n - 1 < nu:
                    st_w(n - 1)
                if 0 <= n - 2 < nu:
                    st_pv(n - 2)
                if 0 <= n - 2 < nu:
                    st_epi(n - 2)
                ssd_step()
            ucount[0] += nu

        for hp_ in range(8):
            do_hp(hp_, hp_ % 2)
        while ssd_step():
            pass
        S_.barrier()
        A.top = base_mark

        A.nw = NW_SBUF
        Wg = A.bf16(16 * 1024); b_Wg = Buf("Wg")
        wst = [A.f32(4096) for _ in range(2)]; b_wst = [Buf("wst%d" % i) for i in range(2)]
        wout_v = wout_d.rearrange("(m p) c -> p m c", p=128)
        for ch in range(4):
            rr = ch % 2
            E("sp", lambda e, ch=ch, rr=rr: e.dma_start(out=wst[rr].rearrange("p (m c) -> p m c", m=4),
                                                      in_=wout_v[:, ch * 4:(ch + 1) * 4, :]), writes=[b_wst[rr]], dma=True)
            E("dve" if ch % 2 == 0 else "pool", lambda e, ch=ch, rr=rr: e.tensor_tensor(
                out=Wg[:, ch * 4096:(ch + 1) * 4096].rearrange("p (m c) -> p m c", m=4),
                in0=wst[rr].rearrange("p (m c) -> p m c", m=4),
                in1=gate_bc.unsqueeze(1).to_broadcast([128, 4, 1024]), op=ALU.mult),
              reads=[b_wst[rr], b_gate], pwrites=[b_Wg])
        for (src, dst) in ((ssq_a, rstd_a), (ssq_s, rstd_s)):
            E("act", lambda e, src=src, dst=dst: e.activation(out=dst, in_=src, func=AF.Ln, scale=1.0 / 1024, bias=EPS),
              reads=[b_ssqa, b_ssqs], pwrites=[b_rstd])
            E("act", lambda e, dst=dst: e.activation(out=dst, in_=dst, func=AF.Exp, scale=-0.5),
              reads=[b_rstd], pwrites=[b_rstd])
        ATt = [A.bf16(16 * 512) for _ in range(2)]; b_ATt = [Buf("ATt%d" % i) for i in range(2)]
        xo = [A.f32(4096) for _ in range(2)]; b_xo = [Buf("xo%d" % i) for i in range(2)]
        yb = [A.f32(1024) for _ in range(2)]; b_yb = [Buf("yb%d" % i) for i in range(2)]
        junk3 = A.bf16(1024); b_junk3 = Buf("junk3")
        b_out = Buf("outd")
        yq = [0]
        gf_bc = prow[:, PR_GF:PR_GF + 1024]
        for tt in range(NT):
            r = tt % 2
            E("sp", lambda e, tt=tt, r=r: e.dma_start(
                out=ATt[r].rearrange("p (m t) -> p m t", m=16),
                in_=AT_d[:, tt * 512:(tt + 1) * 512].rearrange("(m p) t -> p m t", p=128)), writes=[b_ATt[r]], dma=True)
            E("sp", lambda e, tt=tt, r=r: e.dma_start(
                out=xo[r].rearrange("p (j d) -> p j d", j=4),
                in_=x_d[tt * 512:(tt + 1) * 512, :].rearrange("(j p) d -> p j d", p=128)), writes=[b_xo[r]], dma=True)
            for j in range(4):
                c = tt * 4 + j
                yi = yq[0] % 2
                yq[0] += 1
                for n in range(2):
                    ba = 4 * (c % 2) + 2 * n
                    bs = ba + 1
                    for m in range(8):
                        E("pe", lambda e, r=r, m=m, j=j, n=n, ba=ba: e.matmul(
                            bank(ba), lhsT=ATt[r][:, m * 512 + j * 128:m * 512 + (j + 1) * 128],
                            rhs=Wg[:, m * 1024 + n * 512:m * 1024 + (n + 1) * 512], start=(m == 0), stop=(m == 7)),
                          reads=[b_ATt[r], b_Wg], writes=[PB[ba]] if m == 0 else (), pwrites=() if m == 0 else [PB[ba]])
                    for m in range(8, 16):
                        E("pe", lambda e, r=r, m=m, j=j, n=n, bs=bs: e.matmul(
                            bank(bs), lhsT=ATt[r][:, m * 512 + j * 128:m * 512 + (j + 1) * 128],
                            rhs=Wg[:, m * 1024 + n * 512:m * 1024 + (n + 1) * 512], start=(m == 8), stop=(m == 15)),
                          reads=[b_ATt[r], b_Wg], writes=[PB[bs]] if m == 8 else (), pwrites=() if m == 8 else [PB[bs]])
                    ysl = yb[yi][:, n * 512:(n + 1) * 512]
                    E("dve", lambda e, r=r, j=j, n=n, ba=ba, ysl=ysl, c=c: e.scalar_tensor_tensor(
                        out=ysl, in0=bank(ba), scalar=rstd_a[:, c:c + 1],
                        in1=xo[r][:, j * 1024 + n * 512:j * 1024 + (n + 1) * 512], op0=ALU.mult, op1=ALU.add),
                      reads=[PB[ba], b_rstd, b_xo[r]], writes=[b_yb[yi]] if n == 0 else (), pwrites=() if n == 0 else [b_yb[yi]])
                    E("dve", lambda e, bs=bs, ysl=ysl, c=c: e.scalar_tensor_tensor(
                        out=ysl, in0=bank(bs), scalar=rstd_s[:, c:c + 1], in1=ysl, op0=ALU.mult, op1=ALU.add),
                      reads=[PB[bs], b_rstd, b_yb[yi]], pwrites=[b_yb[yi]])
                E("act", lambda e, yi=yi, c=c: e.activation(out=junk3, in_=yb[yi], func=AF.Square,
                                                            accum_out=ssqf[:, c:c + 1]),
                  reads=[b_yb[yi]], writes=[b_junk3], pwrites=[b_ssqf])
                E("act", lambda e, c=c: e.activation(out=rstdf[:, c:c + 1], in_=ssqf[:, c:c + 1], func=AF.Ln,
                                                     scale=1.0 / 1024, bias=EPS), reads=[b_ssqf], pwrites=[b_rstdf])
                E("act", lambda e, c=c: e.activation(out=rstdf[:, c:c + 1], in_=rstdf[:, c:c + 1], func=AF.Exp, scale=-0.5),
                  reads=[b_rstdf], pwrites=[b_rstdf])
                E("dve", lambda e, yi=yi, c=c: e.scalar_tensor_tensor(
                    out=yb[yi], in0=yb[yi], scalar=rstdf[:, c:c + 1], in1=gf_bc, op0=ALU.mult, op1=ALU.mult),
                  reads=[b_yb[yi], b_rstdf, b_prow], writes=[b_yb[yi]])
                E("pool", lambda e, yi=yi, c=c: e.dma_start(out=out_d[c * 128:(c + 1) * 128, :], in_=yb[yi]),
                  reads=[b_yb[yi]], pwrites=[b_out], dma=True, key="yo%d" % yi)
        E("sp", lambda e: e.nop(), reads=[b_out], real=False)
        S_.run()
    return nc


def _consts():
    a = np.arange(128)[:, None]
    b = np.arange(128)[None, :]
    c = np.zeros((128, NCONST), np.float32)
    c[:, C_ID:C_ID + 128] = (a == b)
    c[:, C_GE:C_GE + 128] = (a >= b)
    c[:, C_LT:C_LT + 128] = (a < b)
    c[:, C_LE:C_LE + 128] = (a <= b)
    c[:, C_GT:C_GT + 128] = (a > b)
    c[:, C_NEG:C_NEG + 128] = NEGV * (a > b)
    c[:, C_ONE:C_ONE + 128] = 1.0
    return c


def _col(v, n):
    return np.ascontiguousarray(np.asarray(v, np.float32).reshape(n, 128).T)


def make_in_maps(x, c, w_ada, b_ada, norm_in_gain, w_in, conv_w, conv_b, dt_bias, a_log, d_skip,
                 sb_norm_gain, ssm_norm_gain, w_out, norm_f_gain, cores=None):
    B = x.shape[0]
    consts = _consts()
    maps = []
    w_ada0 = np.ascontiguousarray(w_ada[0], np.float32)
    w_in0 = np.ascontiguousarray(w_in[0], np.float32)
    w_out0 = np.ascontiguousarray(w_out[0], np.float32)
    prow = np.zeros((1, NPROW), np.float32)
    prow[0, PR_BG:PR_BG + 1024] = b_ada[0, 2048:3072]
    prow[0, PR_GF:PR_GF + 1024] = norm_f_gain
    prow[0, PR_DTB:PR_DTB + 16] = dt_bias[0]
    prow[0, PR_AL:PR_AL + 16] = a_log[0]
    for b in (range(B) if cores is None else cores):
        pcol = np.zeros((128, NPCOL), np.float32)
        pcol[:, PC_C:PC_C + 8] = _col(c[b], 8)
        pcol[:, PC_BADA:PC_BADA + 24] = _col(b_ada[0], 24)
        pcol[:, PC_GIN:PC_GIN + 8] = _col(norm_in_gain[0], 8)
        pcol[:, PC_GA:PC_GA + 8] = _col(sb_norm_gain[0], 8)
        pcol[:, PC_GS:PC_GS + 8] = _col(ssm_norm_gain[0], 8)
        pcol[:, PC_D:PC_D + 8] = _col(np.repeat(np.asarray(d_skip[0], np.float32), 64), 8)
        cw = np.asarray(conv_w[0], np.float32)
        pcol[:, PC_CW:PC_CW + 48] = cw.T.reshape(12, 128, 4).transpose(1, 0, 2).reshape(128, 48)
        pcol[:, PC_CB:PC_CB + 12] = _col(conv_b[0], 12)
        maps.append({"x": np.ascontiguousarray(x[b], np.float32), "w_ada": w_ada0, "w_in": w_in0, "w_out": w_out0,
                     "consts": consts, "pcol": pcol, "prow": prow})
    return maps


_NC_CACHE = {}


def kernel(x, c, w_ada, b_ada, norm_in_gain, w_in, conv_w, conv_b, dt_bias, a_log, d_skip,
           sb_norm_gain, ssm_norm_gain, w_out, norm_f_gain):
    x = np.asarray(x)
    B, S, _ = x.shape
    if S not in _NC_CACHE:
        _NC_CACHE[S] = build(S)
    nc = _NC_CACHE[S]
    maps = make_in_maps(x, np.asarray(c), np.asarray(w_ada), np.asarray(b_ada), np.asarray(norm_in_gain),
                        np.asarray(w_in), np.asarray(conv_w), np.asarray(conv_b), np.asarray(dt_bias),
                        np.asarray(a_log), np.asarray(d_skip), np.asarray(sb_norm_gain),
                        np.asarray(ssm_norm_gain), np.asarray(w_out), np.asarray(norm_f_gain))
    res = run_bass_kernel_spmd(nc, maps, core_ids=list(range(B)))
    return np.stack([np.asarray(r["out"], np.float32) for r in res.results], axis=0)
```

```python
import numpy as np
from contextlib import ExitStack
import concourse.bass as bass
import concourse.mybir as mybir
from concourse.bass_utils import run_bass_kernel_spmd

F32 = mybir.dt.float32
BF16 = mybir.dt.bfloat16
AF = mybir.ActivationFunctionType
ALU = mybir.AluOpType
AX = mybir.AxisListType

ENGS = ("pe", "act", "dve", "pool", "sp")
EPS = 1e-6
NEGV = -30000.0


class Buf:
    __slots__ = ("name", "writers", "readers", "const")

    def __init__(self, name, const=False):
        self.name = name
        self.writers = []
        self.readers = []
        self.const = const


class Op:
    __slots__ = ("eng", "fn", "deps", "dma", "idx", "need_sig", "val", "key", "real")

    def __init__(self, eng, fn, dma, key, real):
        self.eng = eng
        self.fn = fn
        self.dma = dma
        self.key = key
        self.real = real
        self.deps = ()
        self.idx = -1
        self.need_sig = False
        self.val = 0


class Sched:
    def __init__(self, nc):
        self.nc = nc
        self.ops = {e: [] for e in ENGS}
        self.last_dma = {}
        self.groups = {e: {} for e in ENGS}
        self._gstart = {}

    def group_begin(self, eng):
        self._gstart[eng] = len(self.ops[eng])

    def group_end(self, eng):
        g0 = self._gstart.pop(eng)
        g1 = len(self.ops[eng])
        if g1 > g0 + 1:
            self.groups[eng][g0] = g1

    def emit(self, eng, fn, reads=(), writes=(), pwrites=(), dma=False, key=None, extra=(), real=True):
        if dma and key is None:
            key = writes[0].name if writes else (pwrites[0].name if pwrites else reads[0].name)
        op = Op(eng, fn, dma, key, real)
        deps = set(extra)
        for b in reads:
            deps.update(b.writers)
        for b in writes:
            deps.update(b.writers)
            deps.update(b.readers)
        for b in pwrites:
            deps.update(b.readers)
            if b.writers:
                deps.add(b.writers[0])
        op.deps = deps
        for b in reads:
            if not b.const:
                b.readers.append(op)
        for b in writes:
            b.writers = [op]
            b.readers = []
        for b in pwrites:
            b.writers.append(op)
        op.idx = len(self.ops[eng])
        self.ops[eng].append(op)
        if dma:
            self.last_dma[key] = op
        return op

    def barrier(self):
        lasts = []
        for e in ENGS:
            for op in reversed(self.ops[e]):
                if op.real and not op.dma:
                    lasts.append(op)
                    break
        deps = set(lasts) | set(self.last_dma.values())
        for e in ENGS:
            self.emit(e, lambda g: g.nop(), extra=deps, real=False)

    def _needed(self, op):
        comp = {}
        dmas = []
        for d in op.deps:
            if d.dma:
                dmas.append(d)
                continue
            if d.eng == op.eng and not op.dma:
                if d.eng == "pe":
                    continue
                if d.eng in ("act", "dve") and d.idx < op.idx - 2:
                    continue
            cur = comp.get(d.eng)
            if cur is None or d.idx > cur.idx:
                comp[d.eng] = d
        return comp, dmas

    def run(self):
        nc = self.nc
        for e in ENGS:
            for op in self.ops[e]:
                comp, _ = self._needed(op)
                for d in comp.values():
                    d.need_sig = True
        for e in ENGS:
            cnt = 0
            for op in self.ops[e]:
                if op.dma:
                    continue
                if op.need_sig:
                    cnt += 1
                    op.val = cnt
        keycnt = {}
        for e in ENGS:
            for op in self.ops[e]:
                if op.dma:
                    keycnt[op.key] = keycnt.get(op.key, 0) + 1
                    op.val = 16 * keycnt[op.key]
        keys = sorted(keycnt.keys())
        with ExitStack() as st:
            esem = {e: st.enter_context(nc.semaphore("s_" + e)) for e in ENGS}
            ksem = {k: st.enter_context(nc.semaphore("d%d" % i)) for i, k in enumerate(keys)}
            block = st.enter_context(nc.Block())
            sched = self

            def replay(ename, eng):
                seen = {}

                def do_waits(op):
                    comp, dmas = sched._needed(op)
                    for d in comp.values():
                        if seen.get(("e", d.eng), 0) < d.val:
                            eng.wait_ge(esem[d.eng], d.val)
                            seen[("e", d.eng)] = d.val
                    kmax = {}
                    for d in dmas:
                        if kmax.get(d.key, 0) < d.val:
                            kmax[d.key] = d.val
                    for kk_, vv_ in kmax.items():
                        if seen.get(("k", kk_), 0) < vv_:
                            eng.wait_ge(ksem[kk_], vv_)
                            seen[("k", kk_)] = vv_

                oplist = sched.ops[ename]
                for op in oplist:
                    g1 = sched.groups[ename].get(op.idx)
                    if g1 is not None:
                        for op2 in oplist[op.idx:g1]:
                            do_waits(op2)
                    comp, dmas = sched._needed(op)
                    for d in comp.values():
                        if seen.get(("e", d.eng), 0) < d.val:
                            eng.wait_ge(esem[d.eng], d.val)
                            seen[("e", d.eng)] = d.val
                    kmax = {}
                    for d in dmas:
                        if kmax.get(d.key, 0) < d.val:
                            kmax[d.key] = d.val
                    for kk_, vv_ in kmax.items():
                        if seen.get(("k", kk_), 0) < vv_:
                            eng.wait_ge(ksem[kk_], vv_)
                            seen[("k", kk_)] = vv_
                    ins = op.fn(eng)
                    if op.dma:
                        ins.then_inc(ksem[op.key], 16)
                    elif op.need_sig:
                        ins.then_inc(esem[ename], 1)

            @block.tensor
            def _(e):
                replay("pe", e)

            @block.scalar
            def _(e):
                replay("act", e)

            @block.vector
            def _(e):
                replay("dve", e)

            @block.gpsimd
            def _(e):
                replay("pool", e)

            @block.sync
            def _(e):
                replay("sp", e)


class Arena:
    def __init__(self, big, nw):
        self.big = big
        self.nw = nw
        self.top = 0

    def f32(self, n):
        off = self.top
        self.top += n
        assert self.top <= self.nw, ("sbuf arena overflow", self.top, self.nw)
        return self.big[:, off:off + n]

    def bf16(self, n):
        w = (n + 1) // 2
        off = self.top
        self.top += w
        assert self.top <= self.nw, ("sbuf arena overflow", self.top, self.nw)
        return self.big[:, off:off + w].bitcast(BF16)


C_ID, C_GE, C_LT, C_LE, C_GT, C_NEG, C_ONE = 0, 128, 256, 384, 512, 640, 768
NCONST = 896
PC_C, PC_BADA, PC_GIN, PC_GA, PC_GS, PC_D, PC_CW, PC_CB = 0, 8, 32, 40, 48, 56, 64, 112
NPCOL = 124
PR_BG, PR_GF, PR_DTB, PR_AL = 0, 1024, 2048, 2064
NPROW = 2080
NW_SBUF = 52224
NWARM = 0
NFILL = 0


def build(S, debug=False):
    NT = S // 512
    NCH = S // 128
    nc = bass.Bass("TRN2", target_bir_lowering=False)
    dk = "ExternalOutput" if debug else "Internal"
    x_d = nc.dram_tensor("x", [S, 1024], F32, kind="ExternalInput").ap()
    wada_d = nc.dram_tensor("w_ada", [1024, 3072], F32, kind="ExternalInput").ap()
    win_d = nc.dram_tensor("w_in", [1024, 6672], F32, kind="ExternalInput").ap()
    wout_d = nc.dram_tensor("w_out", [2048, 1024], F32, kind="ExternalInput").ap()
    const_d = nc.dram_tensor("consts", [128, NCONST], F32, kind="ExternalInput").ap()
    pcol_d = nc.dram_tensor("pcol", [128, NPCOL], F32, kind="ExternalInput").ap()
    prow_d = nc.dram_tensor("prow", [1, NPROW], F32, kind="ExternalInput").ap()
    out_d = nc.dram_tensor("out", [S, 1024], F32, kind="ExternalOutput").ap()
    HT_d = nc.dram_tensor("s_ht", [8, 128, S], BF16, kind=dk).ap()
    QT_d = nc.dram_tensor("s_qt", [1024, S], BF16, kind=dk).ap()
    KT_d = nc.dram_tensor("s_kt", [1024, S], BF16, kind=dk).ap()
    ZA_d = nc.dram_tensor("s_za", [1024, S], BF16, kind=dk).ap()
    V_d = nc.dram_tensor("s_v", [S, 1024], BF16, kind=dk).ap()
    AT_d = nc.dram_tensor("s_at", [2048, S], BF16, kind=dk).ap()
    XS_d = nc.dram_tensor("s_xs", [1024, S], F32, kind=dk).ap()
    BC_d = nc.dram_tensor("s_bc", [512, S], BF16, kind=dk).ap()
    ZS_d = nc.dram_tensor("s_zs", [1024, S], BF16, kind=dk).ap()
    DT_d = nc.dram_tensor("s_dt", [S, 32], F32, kind=dk).ap()

    S_ = Sched(nc)
    E = S_.emit

    with ExitStack() as st:
        big = st.enter_context(nc.sbuf_tensor("big", [128, NW_SBUF], F32))
        psum = st.enter_context(nc.psum_tensor("psum", [128, 4096], F32))
        A = Arena(big, NW_SBUF)

        def bank(i):
            return psum[:, 512 * i:512 * (i + 1)]

        PB = [Buf("ps%d" % i) for i in range(8)]

        cf = A.f32(NCONST); b_cf = Buf("cf", const=True)
        cb = A.bf16(NCONST); b_cb = Buf("cb", const=True)
        zb = A.bf16(512); b_zb = Buf("zb", const=True)
        neg4 = A.bf16(512); b_neg4 = Buf("neg4", const=True)
        pcol = A.f32(NPCOL); b_pcol = Buf("pcol", const=True)
        prow = A.f32(NPROW); b_prow = Buf("prow", const=True)
        modT = A.f32(24); b_modT = Buf("modT")
        g1 = A.f32(8); b_g1 = Buf("g1")
        gate_bc = A.f32(1024); b_gate = Buf("gate_bc")
        a_bc = A.f32(16); b_abc = Buf("a_bc")
        ssq_a = A.f32(NCH); b_ssqa = Buf("ssq_a")
        ssq_s = A.f32(NCH); b_ssqs = Buf("ssq_s")
        rstd_a = A.f32(NCH); rstd_s = A.f32(NCH); b_rstd = Buf("rstd_as")
        ssqf = A.f32(NCH); b_ssqf = Buf("ssqf")
        rstdf = A.f32(NCH); b_rstdf = Buf("rstdf")
        tmp_small = A.f32(64); b_tmps = Buf("tmps")

        idf = cf[:, C_ID:C_ID + 128]
        GEb = cb[:, C_GE:C_GE + 128]
        LTb = cb[:, C_LT:C_LT + 128]
        LTf = cf[:, C_LT:C_LT + 128]
        LEf = cf[:, C_LE:C_LE + 128]
        GTf = cf[:, C_GT:C_GT + 128]
        idb = cb[:, C_ID:C_ID + 128]
        onesf = cf[:, C_ONE:C_ONE + 128]

        E("sp", lambda e: e.dma_start(out=cf, in_=const_d), writes=[b_cf], dma=True)
        E("pool", lambda e: e.dma_start(out=cb, in_=const_d), writes=[b_cb], dma=True)
        E("sp", lambda e: e.dma_start(out=pcol, in_=pcol_d), writes=[b_pcol], dma=True)
        E("sp", lambda e: e.dma_start(out=prow, in_=prow_d.partition_broadcast(128)), writes=[b_prow], dma=True)
        E("dve", lambda e: e.memset(zb, 0.0), writes=[b_zb])
        for q in range(4):
            E("dve", lambda e, q=q: e.tensor_copy(out=neg4[:, q * 128:(q + 1) * 128], in_=cf[:, C_NEG:C_NEG + 128]),
              reads=[b_cf], pwrites=[b_neg4])

        base_mark = A.top
        NWA, NWB2 = 1536, 1040
        Wsa = A.bf16(8 * NWA); b_Wsa = Buf("Wsa")
        wsa_mark = A.top
        Wa = A.bf16(8 * 4096); b_Wa = Buf("Wa")
        win_v = win_d.rearrange("(k p) c -> p k c", p=128)
        Wa3 = Wa.rearrange("p (k c) -> p k c", k=8)
        for ci in range(8):
            E("pool", lambda e, ci=ci: e.dma_start(out=Wa3[:, :, ci * 512:(ci + 1) * 512],
                                                  in_=win_v[:, :, ci * 512:(ci + 1) * 512]),
              pwrites=[b_Wa], dma=True, key="Wa")
        Wsa3 = Wsa.rearrange("p (k c) -> p k c", k=8)
        for ci in range(3):
            E("pool", lambda e, ci=ci: e.dma_start(out=Wsa3[:, :, ci * 512:(ci + 1) * 512],
                                                  in_=win_v[:, :, 4096 + ci * 512:4096 + (ci + 1) * 512]),
              pwrites=[b_Wsa], dma=True, key="Wsa")
        p0_mark = A.top
        NVW = (S // 128) * 512
        Vall = big[:, NW_SBUF - NVW:NW_SBUF].bitcast(BF16); b_Vall = Buf("Vall")

        cact = A.f32(8); b_cact = Buf("cact")
        cact_rep = A.f32(8 * 128); b_crep = Buf("crep")
        wa = [A.f32(8 * 512) for _ in range(2)]
        b_wa = [Buf("wa%d" % i) for i in range(2)]
        E("act", lambda e: e.activation(out=cact, in_=pcol[:, PC_C:PC_C + 8], func=AF.Silu),
          reads=[b_pcol], writes=[b_cact])
        E("dve", lambda e: e.tensor_copy(out=cact_rep.rearrange("p (k m) -> p k m", k=8),
                                         in_=cact.unsqueeze(2).to_broadcast([128, 8, 128])),
          reads=[b_cact], writes=[b_crep])
        E("act", lambda e: e.activation(out=a_bc, in_=prow[:, PR_AL:PR_AL + 16], func=AF.Exp),
          reads=[b_prow], writes=[b_abc])
        E("dve", lambda e: e.tensor_scalar(out=a_bc, in0=a_bc, scalar1=-1.0, scalar2=None, op0=ALU.mult),
          reads=[b_abc], writes=[b_abc])
        wada_v = wada_d.rearrange("(k p) c -> p k c", p=128)
        for ci in range(6):
            r = ci % 2
            E("sp", lambda e, ci=ci, r=r: e.dma_start(out=wa[r].rearrange("p (k c) -> p k c", k=8),
                                                     in_=wada_v[:, :, ci * 512:(ci + 1) * 512]),
              writes=[b_wa[r]], dma=True)
            if ci < 4:
                for mm in range(4):
                    m = ci * 4 + mm
                    for k in range(8):
                        E("pe", lambda e, r=r, mm=mm, m=m, k=k: e.matmul(
                            bank(0)[:, m:m + 1], lhsT=wa[r][:, k * 512 + mm * 128:k * 512 + (mm + 1) * 128],
                            rhs=cact[:, k:k + 1], start=(k == 0), stop=(k == 7)),
                          reads=[b_wa[r], b_cact], pwrites=[PB[0]])
            else:
                n = ci - 4
                for k in range(8):
                    E("pe", lambda e, r=r, n=n, k=k: e.matmul(
                        bank(1 + n), lhsT=cact_rep[:, k * 128:(k + 1) * 128],
                        rhs=wa[r][:, k * 512:(k + 1) * 512], start=(k == 0), stop=(k == 7)),
                      reads=[b_wa[r], b_crep], pwrites=[PB[1 + n]])
        E("dve", lambda e: e.tensor_tensor(out=modT[:, 0:16], in0=bank(0)[:, 0:16], in1=pcol[:, PC_BADA:PC_BADA + 16],
                                           op=ALU.add), reads=[PB[0], b_pcol], writes=[b_modT])
        E("dve", lambda e: e.scalar_tensor_tensor(out=g1, in0=modT[:, 8:16], scalar=1.0, in1=pcol[:, PC_GIN:PC_GIN + 8],
                                                  op0=ALU.add, op1=ALU.mult), reads=[b_modT, b_pcol], writes=[b_g1])
        E("dve", lambda e: e.tensor_tensor(out=gate_bc, in0=psum[:, 512:1536], in1=prow[:, PR_BG:PR_BG + 1024], op=ALU.add),
          reads=[PB[1], PB[2], b_prow], writes=[b_gate])
        shiftc = modT
        S_.barrier()
        A.top = p0_mark

        xt = [A.f32(4096) for _ in range(2)]; b_xt = [Buf("xt%d" % i) for i in range(2)]
        hT = [A.bf16(8 * 512) for _ in range(2)]; b_hT = [Buf("hT%d" % i) for i in range(2)]
        junk = A.bf16(1024); b_junk = Buf("junk")
        ssq1 = A.f32(4 * NT); b_ssq1 = Buf("ssq1")
        rs1 = A.f32(4 * NT); b_rs1 = Buf("rs1")
        stq = A.bf16(8 * 512); b_stq = Buf("stq")
        stk = A.bf16(8 * 512); b_stk = Buf("stk")
        stz = A.bf16(8 * 512); b_stz = Buf("stz")
        stv = A.bf16(4 * 1024); b_stv = Buf("stv")
        stages = [(stq, b_stq, QT_d), (stk, b_stk, KT_d), (stz, b_stz, ZA_d)]
        pcnt = [0]

        def next_bank(lo, n):
            i = lo + (pcnt[0] % n)
            pcnt[0] += 1
            return i

        evq = [0]
        def p1a_front_a(tt):
            r = tt % 2
            E("sp", lambda e, tt=tt, r=r: e.dma_start(
                out=xt[r].rearrange("p (j d) -> p j d", j=4),
                in_=x_d[tt * 512:(tt + 1) * 512, :].rearrange("(j p) d -> p j d", p=128)),
              writes=[b_xt[r]], dma=True)
            for j in range(4):
                E("act", lambda e, r=r, j=j, tt=tt: e.activation(
                    out=junk, in_=xt[r][:, j * 1024:(j + 1) * 1024], func=AF.Square,
                    accum_out=ssq1[:, tt * 4 + j:tt * 4 + j + 1]),
                  reads=[b_xt[r]], writes=[b_junk], pwrites=[b_ssq1])
            E("act", lambda e, tt=tt: e.activation(out=rs1[:, tt * 4:tt * 4 + 4], in_=ssq1[:, tt * 4:tt * 4 + 4],
                                                   func=AF.Ln, scale=1.0 / 1024, bias=EPS),
              reads=[b_ssq1], pwrites=[b_rs1])
            E("act", lambda e, tt=tt: e.activation(out=rs1[:, tt * 4:tt * 4 + 4], in_=rs1[:, tt * 4:tt * 4 + 4],
                                                   func=AF.Exp, scale=-0.5),
              reads=[b_rs1], pwrites=[b_rs1])
            for j in range(4):
                E("dve", lambda e, r=r, j=j, tt=tt: e.tensor_scalar(
                    out=xt[r][:, j * 1024:(j + 1) * 1024], in0=xt[r][:, j * 1024:(j + 1) * 1024],
                    scalar1=rs1[:, tt * 4 + j:tt * 4 + j + 1], scalar2=None, op0=ALU.mult),
                  reads=[b_rs1, b_xt[r]], writes=[b_xt[r]])

        def p1a_front_b(tt):
            r = tt % 2
            for k in range(8):
                bi = next_bank(0, 4)
                for j in range(4):
                    E("pe", lambda e, r=r, j=j, k=k, bi=bi: e.transpose(
                        bank(bi)[:, j * 128:(j + 1) * 128], xt[r][:, j * 1024 + k * 128:j * 1024 + (k + 1) * 128], idf),
                      reads=[b_xt[r], b_cf], writes=[PB[bi]] if j == 0 else (), pwrites=() if j == 0 else [PB[bi]])
                if k % 2 == 0:
                    E("dve", lambda e, r=r, k=k, bi=bi: e.tensor_scalar(
                        out=hT[r][:, k * 512:(k + 1) * 512], in0=bank(bi), scalar1=g1[:, k:k + 1],
                        scalar2=shiftc[:, k:k + 1], op0=ALU.mult, op1=ALU.add),
                      reads=[PB[bi], b_g1, b_modT], pwrites=[b_hT[r]] if k else (), writes=() if k else [b_hT[r]])
                else:
                    E("act", lambda e, r=r, k=k, bi=bi: e.activation(
                        out=hT[r][:, k * 512:(k + 1) * 512], in_=bank(bi), func=AF.Identity,
                        scale=g1[:, k:k + 1], bias=shiftc[:, k:k + 1]),
                      reads=[PB[bi], b_g1, b_modT], pwrites=[b_hT[r]])
            E("pool", lambda e, r=r, tt=tt: e.dma_start(
                out=HT_d[:, :, tt * 512:(tt + 1) * 512].rearrange("k p t -> p k t"),
                in_=hT[r].rearrange("p (k t) -> p k t", k=8)), reads=[b_hT[r]], dma=True, key="hTo%d" % r)

        def p1a_back1(tt):
            r = tt % 2
            for grp in range(3):
                stg, b_stg, dst = stages[grp]
                c_base = [0, 1024, 3072][grp]
                for m in range(8):
                    bi = next_bank(4, 4)
                    for k in range(8):
                        E("pe", lambda e, r=r, k=k, bi=bi, c0=c_base + m * 128: e.matmul(
                            bank(bi), lhsT=Wa[:, k * 4096 + c0:k * 4096 + c0 + 128],
                            rhs=hT[r][:, k * 512:(k + 1) * 512], start=(k == 0), stop=(k == 7)),
                          reads=[b_Wa, b_hT[r]], writes=[PB[bi]] if k == 0 else (), pwrites=() if k == 0 else [PB[bi]])
                    wkw = dict(writes=[b_stg]) if m == 0 else dict(pwrites=[b_stg])
                    osl = stg[:, m * 512:(m + 1) * 512]
                    if grp == 0:
                        if evq[0] % 2 == 0:
                            E("act", lambda e, osl=osl, bi=bi: e.mul(out=osl, in_=bank(bi), mul=0.125),
                              reads=[PB[bi]], **wkw)
                        else:
                            E("dve", lambda e, osl=osl, bi=bi: e.tensor_scalar(out=osl, in0=bank(bi), scalar1=0.125,
                                                                               scalar2=None, op0=ALU.mult),
                              reads=[PB[bi]], **wkw)
                        evq[0] += 1
                    elif grp == 1:
                        if evq[0] % 2 == 0:
                            E("act", lambda e, osl=osl, bi=bi: e.copy(out=osl, in_=bank(bi)), reads=[PB[bi]], **wkw)
                        else:
                            E("dve", lambda e, osl=osl, bi=bi: e.tensor_copy(out=osl, in_=bank(bi)), reads=[PB[bi]], **wkw)
                        evq[0] += 1
                    else:
                        E("act", lambda e, osl=osl, bi=bi: e.activation(out=osl, in_=bank(bi), func=AF.Silu),
                          reads=[PB[bi]], **wkw)
                E("pool", lambda e, stg=stg, dst=dst, tt=tt: e.dma_start(
                    out=dst[:, tt * 512:(tt + 1) * 512].rearrange("(m p) t -> p m t", p=128),
                    in_=stg.rearrange("p (m t) -> p m t", m=8)), reads=[b_stg], dma=True)

        def p1a_back2(tt):
            r = tt % 2
            for j in range(4):
                for n in range(2):
                    bi = next_bank(4, 4)
                    for k in range(8):
                        E("pe", lambda e, r=r, k=k, bi=bi, j=j, n=n: e.matmul(
                            bank(bi), lhsT=hT[r][:, k * 512 + j * 128:k * 512 + (j + 1) * 128],
                            rhs=Wa[:, k * 4096 + 2048 + n * 512:k * 4096 + 2048 + (n + 1) * 512],
                            start=(k == 0), stop=(k == 7)),
                          reads=[b_Wa, b_hT[r]], writes=[PB[bi]] if k == 0 else (), pwrites=() if k == 0 else [PB[bi]])
                    wkw = dict(writes=[b_stv]) if (j == 0 and n == 0) else dict(pwrites=[b_stv])
                    osl = stv[:, j * 1024 + n * 512:j * 1024 + (n + 1) * 512]
                    if evq[0] % 2 == 0:
                        E("act", lambda e, osl=osl, bi=bi: e.copy(out=osl, in_=bank(bi)), reads=[PB[bi]], **wkw)
                    else:
                        E("dve", lambda e, osl=osl, bi=bi: e.tensor_copy(out=osl, in_=bank(bi)), reads=[PB[bi]], **wkw)
                    evq[0] += 1
            E("pool", lambda e, tt=tt: e.dma_start(
                out=V_d[tt * 512:(tt + 1) * 512, :].rearrange("(j p) c -> p j c", p=128),
                in_=stv.rearrange("p (j c) -> p j c", j=4)), reads=[b_stv], dma=True)

        p1a_front_a(0)
        p1a_front_b(0)
        for tt in range(NT):
            if tt + 1 < NT:
                p1a_front_a(tt + 1)
            p1a_back1(tt)
            if tt + 1 < NT:
                p1a_front_b(tt + 1)
            p1a_back2(tt)
        S_.barrier()
        A.top = base_mark

        A.top = wsa_mark
        A.nw = NW_SBUF - NVW
        Wsb = A.bf16(8 * NWB2); b_Wsb = Buf("Wsb")
        Wsb3 = Wsb.rearrange("p (k c) -> p k c", k=8)
        for (c0, c1) in ((0, 528), (528, 1040)):
            E("pool", lambda e, c0=c0, c1=c1: e.dma_start(out=Wsb3[:, :, c0:c1], in_=win_v[:, :, 5632 + c0:5632 + c1]),
              pwrites=[b_Wsb], dma=True, key="Wsb")
        Vall3 = Vall.rearrange("p (n c) -> p n c", c=1024)
        V_v = V_d.rearrange("(n p) c -> p n c", p=128)
        nvq = max(1, (S // 128) // 8)
        for vq in range(0, S // 128, nvq):
            E("sp", lambda e, vq=vq: e.dma_start(out=Vall3[:, vq:vq + nvq, :], in_=V_v[:, vq:vq + nvq, :]),
              pwrites=[b_Vall], dma=True, key="Vall")
        hSr = [A.bf16(8 * 512) for _ in range(2)]; b_hSr = [Buf("hS%d" % i) for i in range(2)]
        XW_ = 515
        xin = A.f32(12 * XW_); b_xin = [Buf("xin%d" % m) for m in range(12)]
        cv = A.f32(12 * 512); b_cv = [Buf("cv%d" % m) for m in range(12)]
        BCs = A.bf16(4 * 512); b_BCs = Buf("BCs")
        ZS = A.bf16(8 * 512); b_ZS = Buf("ZSst")
        dtv = A.f32(64); b_dtv = Buf("dtv")
        dts = A.f32(64); b_dts = Buf("dts")
        lds = A.f32(64); b_lds = Buf("lds")
        cw = pcol[:, PC_CW:PC_CW + 48]
        cbias = pcol[:, PC_CB:PC_CB + 12]
        E("dve", lambda e: e.memset(xin, 0.0), writes=b_xin)
        for tt in range(NT):
            hS = hSr[tt % 2]; b_hS = b_hSr[tt % 2]
            E("sp", lambda e, tt=tt, hS=hS: e.dma_start(
                out=hS.rearrange("p (k t) -> p k t", k=8),
                in_=HT_d[:, :, tt * 512:(tt + 1) * 512].rearrange("k p t -> p k t")), writes=[b_hS], dma=True)
            def z_tile(m, hS=hS, b_hS=b_hS):
                bi = next_bank(0, 4)
                for k in range(8):
                    E("pe", lambda e, k=k, bi=bi, c0=16 + m * 128: e.matmul(
                        bank(bi), lhsT=Wsb[:, k * NWB2 + c0:k * NWB2 + c0 + 128], rhs=hS[:, k * 512:(k + 1) * 512],
                        start=(k == 0), stop=(k == 7)),
                      reads=[b_Wsb, b_hS], writes=[PB[bi]] if k == 0 else (), pwrites=() if k == 0 else [PB[bi]])
                E("act", lambda e, bi=bi: e.activation(out=ZS[:, m * 512:(m + 1) * 512], in_=bank(bi), func=AF.Silu),
                  reads=[PB[bi]], writes=[b_ZS] if m == 0 else (), pwrites=() if m == 0 else [b_ZS])

            pend_silu = []
            zq = [0]
            for mp in range(0, 12, 2):
                for _z in range(2 if mp % 6 == 4 else 1):
                    if zq[0] < 8:
                        z_tile(zq[0])
                        zq[0] += 1
                for m in (mp, mp + 1):
                    if tt > 0:
                        E("dve", lambda e, m=m: e.tensor_copy(out=xin[:, m * XW_:m * XW_ + 3], in_=xin[:, m * XW_ + 512:m * XW_ + 515]),
                          reads=[b_xin[m]], writes=[b_xin[m]])
                    bi = next_bank(0, 4)
                    for k in range(8):
                        E("pe", lambda e, k=k, bi=bi, c0=m * 128, hS=hS: e.matmul(
                            bank(bi), lhsT=Wsa[:, k * NWA + c0:k * NWA + c0 + 128], rhs=hS[:, k * 512:(k + 1) * 512],
                            start=(k == 0), stop=(k == 7)),
                          reads=[b_Wsa, b_hS], writes=[PB[bi]] if k == 0 else (), pwrites=() if k == 0 else [PB[bi]])
                    E("act", lambda e, m=m, bi=bi: e.copy(out=xin[:, m * XW_ + 3:m * XW_ + 515], in_=bank(bi)),
                      reads=[PB[bi]], writes=[b_xin[m]])
                for step in range(4):
                    for m in (mp, mp + 1):
                        acc = cv[:, m * 512:(m + 1) * 512]
                        if step == 0:
                            E("dve", lambda e, m=m, acc=acc: e.tensor_scalar(
                                out=acc, in0=xin[:, m * XW_ + 3:m * XW_ + 515], scalar1=cw[:, m * 4 + 3:m * 4 + 4],
                                scalar2=cbias[:, m:m + 1], op0=ALU.mult, op1=ALU.add),
                              reads=[b_xin[m], b_pcol], writes=[b_cv[m]])
                        else:
                            kk = step - 1
                            E("dve", lambda e, m=m, acc=acc, kk=kk: e.scalar_tensor_tensor(
                                out=acc, in0=xin[:, m * XW_ + kk:m * XW_ + kk + 512], scalar=cw[:, m * 4 + kk:m * 4 + kk + 1],
                                in1=acc, op0=ALU.mult, op1=ALU.add),
                              reads=[b_xin[m], b_pcol, b_cv[m]], writes=[b_cv[m]])
                for m in (mp, mp + 1):
                    acc = cv[:, m * 512:(m + 1) * 512]

                    def silu_emit(m=m, acc=acc):
                        if m < 8:
                            E("act", lambda e: e.activation(out=acc, in_=acc, func=AF.Silu),
                              reads=[b_cv[m]], writes=[b_cv[m]])
                        else:
                            dst = BCs[:, (m - 8) * 512:(m - 7) * 512]
                            E("act", lambda e: e.activation(out=dst, in_=acc, func=AF.Silu),
                              reads=[b_cv[m]], writes=[b_BCs] if m == 8 else (), pwrites=() if m == 8 else [b_BCs])

                    pend_silu.append(silu_emit)
                while len(pend_silu) > 4:
                    pend_silu.pop(0)()
            while zq[0] < 8:
                z_tile(zq[0])
                zq[0] += 1
            while pend_silu:
                pend_silu.pop(0)()
            E("pool", lambda e, tt=tt: e.dma_start(
                out=XS_d[:, tt * 512:(tt + 1) * 512].rearrange("(m p) t -> p m t", p=128),
                in_=cv[:, 0:8 * 512].rearrange("p (m t) -> p m t", m=8)), reads=b_cv[0:8], dma=True, key="cvo")
            E("pool", lambda e, tt=tt: e.dma_start(
                out=BC_d[:, tt * 512:(tt + 1) * 512].rearrange("(m p) t -> p m t", p=128),
                in_=BCs.rearrange("p (m t) -> p m t", m=4)), reads=[b_BCs], dma=True, key="bco")
            E("pool", lambda e, tt=tt: e.dma_start(
                out=ZS_d[:, tt * 512:(tt + 1) * 512].rearrange("(m p) t -> p m t", p=128),
                in_=ZS.rearrange("p (m t) -> p m t", m=8)), reads=[b_ZS], dma=True, key="zso")
            for j in range(4):
                for k in range(8):
                    E("pe", lambda e, j=j, k=k, hS=hS: e.matmul(
                        bank(7)[:, j * 16:(j + 1) * 16], lhsT=hS[:, k * 512 + j * 128:k * 512 + (j + 1) * 128],
                        rhs=Wsb[:, k * NWB2:k * NWB2 + 16], start=(k == 0), stop=(k == 7)),
                      reads=[b_Wsb, b_hS], writes=[PB[7]] if (j == 0 and k == 0) else (),
                      pwrites=() if (j == 0 and k == 0) else [PB[7]])
            E("dve", lambda e: e.tensor_tensor(
                out=dtv.rearrange("p (j h) -> p j h", j=4), in0=bank(7)[:, 0:64].rearrange("p (j h) -> p j h", j=4),
                in1=prow[:, PR_DTB:PR_DTB + 16].unsqueeze(1).to_broadcast([128, 4, 16]), op=ALU.add),
              reads=[PB[7], b_prow], writes=[b_dtv])
            E("act", lambda e: e.activation(out=dtv, in_=dtv, func=AF.Exp), reads=[b_dtv], writes=[b_dtv])
            E("act", lambda e: e.activation(out=dts, in_=dtv, func=AF.Ln, bias=1.0), reads=[b_dtv], writes=[b_dts])
            E("dve", lambda e: e.tensor_tensor(
                out=lds.rearrange("p (j h) -> p j h", j=4), in0=dts.rearrange("p (j h) -> p j h", j=4),
                in1=a_bc.unsqueeze(1).to_broadcast([128, 4, 16]), op=ALU.mult),
              reads=[b_dts, b_abc], writes=[b_lds])
            E("pool", lambda e, tt=tt: e.dma_start(
                out=DT_d[tt * 512:(tt + 1) * 512, 0:16].rearrange("(j p) h -> p j h", p=128),
                in_=dts.rearrange("p (j h) -> p j h", j=4)), reads=[b_dts], dma=True, key="dto")
            E("pool", lambda e, tt=tt: e.dma_start(
                out=DT_d[tt * 512:(tt + 1) * 512, 16:32].rearrange("(j p) h -> p j h", p=128),
                in_=lds.rearrange("p (j h) -> p j h", j=4)), reads=[b_lds], dma=True, key="ldo")
        S_.barrier()
        A.top = base_mark

        NB = NCH
        QT = [A.bf16(S) for _ in range(1)] * 2; KT = [A.bf16(S) for _ in range(1)] * 2
        ZAq = [A.bf16(512) for _ in range(2)]; b_ZAq = [Buf("ZAq%d" % i) for i in range(2)]
        b_QT = [Buf("QT0")] * 2; b_KT = [Buf("KT0")] * 2
        NE, NL, NX, NWB = 3, 4, 2, 3
        eb = [A.f32(1024) for _ in range(NE)]; b_eb = [Buf("e%d" % i) for i in range(NE)]
        lb = [A.bf16(1024) for _ in range(NL)]; b_lb = [Buf("l%d" % i) for i in range(NL)]
        xb = [A.f32(1024) for _ in range(NX)]; b_xb = [Buf("xx%d" % i) for i in range(NX)]
        wb = [A.bf16(1024) for _ in range(NWB)]; b_wb = [Buf("w%d" % i) for i in range(NWB)]
        sqa = A.f32(512); b_sqa = Buf("sqa")
        ats = [A.bf16(512) for _ in range(2)]; b_ats = [Buf("ats%d" % i) for i in range(2)]
        atq = [0]

        SB7 = 7
        b7 = bank(SB7)
        xsI = [A.f32(1024) for _ in range(2)]; b_xsI = [Buf("xsI%d" % i) for i in range(2)]
        bcI = [A.bf16(512) for _ in range(2)]; b_bcI = [Buf("bcI%d" % i) for i in range(2)]
        zsI = [A.bf16(1024) for _ in range(2)]; b_zsI = [Buf("zsI%d" % i) for i in range(2)]
        dtI = [A.f32(32) for _ in range(2)]; b_dtI = [Buf("dtI%d" % i) for i in range(2)]
        Rr = A.f32(16 * 128); b_R = Buf("R")
        decay = A.f32(16 * 128); b_decay = Buf("decay")
        Mm = A.bf16(16 * 128); b_M = Buf("M")
        CE = A.bf16(16 * 128); b_CE = Buf("CE")
        xtf = A.f32(1024); b_xtf = Buf("xtf")
        xtb = A.bf16(1024); b_xtb = Buf("xtb")
        Btk = A.bf16(256); b_Btk = Buf("Btk")
        XWt = A.bf16(1024); b_XW = Buf("XW")
        cbs = A.f32(256); b_cbs = Buf("cbs")
        dcd = A.f32(32); b_dcd = Buf("dcd")
        wgt = A.f32(16); b_wgt = Buf("wgt")
        state = A.f32(1024); b_state = Buf("state")
        stateb = A.bf16(1024); b_stateb = Buf("stateb")
        stsT = A.bf16(8 * 512); b_stsT = Buf("stsT")
        ub = Rr[:, 0:1024]
        sqs = Rr[:, 1024:2048]
        E("dve", lambda e: e.memset(state, 0.0), writes=[b_state])
        E("dve", lambda e: e.memset(stateb, 0.0), writes=[b_stateb])

        def ssd_load(c):
            sl = c % 2
            E("sp", lambda e: e.dma_start(out=xsI[sl].rearrange("p (m t) -> p m t", m=8),
                                          in_=XS_d[:, c * 128:(c + 1) * 128].rearrange("(m p) t -> p m t", p=128)),
              writes=[b_xsI[sl]], dma=True)
            E("sp", lambda e: e.dma_start(out=bcI[sl].rearrange("p (m t) -> p m t", m=4),
                                          in_=BC_d[:, c * 128:(c + 1) * 128].rearrange("(m p) t -> p m t", p=128)),
              writes=[b_bcI[sl]], dma=True)
            E("sp", lambda e: e.dma_start(out=zsI[sl].rearrange("p (m t) -> p m t", m=8),
                                          in_=ZS_d[:, c * 128:(c + 1) * 128].rearrange("(m p) t -> p m t", p=128)),
              writes=[b_zsI[sl]], dma=True)
            E("sp", lambda e: e.dma_start(out=dtI[sl], in_=DT_d[c * 128:(c + 1) * 128, :]), writes=[b_dtI[sl]], dma=True)

        def ssd_chunk(c):
            sl = c % 2
            xsT, bc_, zs_, dt_ = xsI[sl], bcI[sl], zsI[sl], dtI[sl]
            bx, bb, bz, bd = b_xsI[sl], b_bcI[sl], b_zsI[sl], b_dtI[sl]
            dtj = dt_[:, 0:16]
            ldj = dt_[:, 16:32]
            jq = c % 4
            if c + 1 < NCH:
                ssd_load(c + 1)
            R3 = Rr.rearrange("p (h l) -> p h l", h=16)

            def m_ops(q):
                for h in range(q * 4, q * 4 + 4):
                    g = h // 8
                    E("dve", lambda e, h=h, g=g: e.scalar_tensor_tensor(
                        out=Mm[:, h * 128:(h + 1) * 128], in0=decay[:, h * 128:(h + 1) * 128], scalar=dtj[:, h:h + 1],
                        in1=cbs[:, g * 128:(g + 1) * 128], op0=ALU.mult, op1=ALU.mult),
                      reads=[b_decay, bd, b_cbs], writes=[b_M] if h == 0 else (), pwrites=() if h == 0 else [b_M])

            def ce_op(g):
                E("dve", lambda e: e.tensor_tensor(
                    out=CE[:, g * 1024:(g + 1) * 1024].rearrange("p (h l) -> p h l", h=8),
                    in0=decay[:, g * 1024:(g + 1) * 1024].rearrange("p (h l) -> p h l", h=8),
                    in1=bc_[:, 256 + g * 128:256 + (g + 1) * 128].unsqueeze(1).to_broadcast([128, 8, 128]), op=ALU.mult),
                  reads=[b_decay, bb], writes=[b_CE] if g == 0 else (), pwrites=() if g == 0 else [b_CE])

            for hf in range(2):
                E("dve", lambda e, hf=hf: e.tensor_tensor(
                    out=R3[:, hf * 8:(hf + 1) * 8, :], in0=LEf.unsqueeze(1).to_broadcast([128, 8, 128]),
                    in1=ldj[:, hf * 8:(hf + 1) * 8].unsqueeze(2).to_broadcast([128, 8, 128]), op=ALU.mult),
                  reads=[b_cf, bd], writes=[b_R] if hf == 0 else (), pwrites=() if hf == 0 else [b_R])
                for m4 in range(4):
                    m = hf * 4 + m4
                    E("pe", lambda e, m=m, m4=m4: e.transpose(b7[:, m4 * 128:(m4 + 1) * 128], xsT[:, m * 128:(m + 1) * 128], idf),
                      reads=[bx, b_cf], writes=[PB[SB7]] if m4 == 0 else (), pwrites=() if m4 == 0 else [PB[SB7]])
                E("dve", lambda e, hf=hf: e.tensor_copy(out=xtf[:, hf * 512:(hf + 1) * 512], in_=b7), reads=[PB[SB7]],
                  writes=[b_xtf] if hf == 0 else (), pwrites=() if hf == 0 else [b_xtf])
                yield
            E("dve", lambda e: e.tensor_copy(out=xtb, in_=xtf), reads=[b_xtf], writes=[b_xtb])
            b7b = b7[:, 0:128].bitcast(BF16)
            for g in range(2):
                E("pe", lambda e, g=g: e.transpose(b7b[:, g * 128:(g + 1) * 128], bc_[:, g * 128:(g + 1) * 128], idb),
                  reads=[bb, b_cb], writes=[PB[SB7]] if g == 0 else (), pwrites=() if g == 0 else [PB[SB7]])
            for g in range(2):
                E("pe", lambda e, g=g: e.matmul(b7[:, 256 + g * 128:256 + (g + 1) * 128], lhsT=bc_[:, g * 128:(g + 1) * 128],
                                                rhs=bc_[:, 256 + g * 128:256 + (g + 1) * 128], start=True, stop=True),
                  reads=[bb], pwrites=[PB[SB7]])
            E("dve", lambda e: e.tensor_copy(out=Btk, in_=b7b), reads=[PB[SB7]], writes=[b_Btk])
            E("dve", lambda e: e.tensor_copy(out=cbs, in_=b7[:, 256:512]), reads=[PB[SB7]], writes=[b_cbs])
            yield
            for q in range(4):
                E("pe", lambda e, q=q: e.matmul(b7, lhsT=GTf, rhs=Rr[:, q * 512:(q + 1) * 512], start=True, stop=False,
                                                skip_group_check=True), reads=[b_R, b_cf], writes=[PB[SB7]])
                E("pe", lambda e: e.matmul(b7, lhsT=idb, rhs=neg4, start=False, stop=True, skip_group_check=True),
                  reads=[b_cb, b_neg4], pwrites=[PB[SB7]])
                E("act", lambda e, q=q: e.activation(out=decay[:, q * 512:(q + 1) * 512], in_=b7, func=AF.Exp),
                  reads=[PB[SB7]], writes=[b_decay] if q == 0 else (), pwrites=() if q == 0 else [b_decay])
                if q >= 1:
                    m_ops(q - 1)
                yield
            E("pe", lambda e: e.matmul(b7[:, 0:16], lhsT=GTf, rhs=ldj, start=True, stop=True),
              reads=[bd, b_cf], writes=[PB[SB7]])
            E("pe", lambda e: e.matmul(b7[:, 16:32], lhsT=onesf, rhs=ldj, start=True, stop=True),
              reads=[bd, b_cf], pwrites=[PB[SB7]])
            E("act", lambda e: e.activation(out=dcd, in_=b7[:, 0:32], func=AF.Exp), reads=[PB[SB7]], writes=[b_dcd])
            m_ops(3)
            yield
            for q in range(4):
                E("pe", lambda e, q=q: e.matmul(b7, lhsT=onesf, rhs=Rr[:, q * 512:(q + 1) * 512], start=True, stop=True),
                  reads=[b_R, b_cf], writes=[PB[SB7]])
                E("act", lambda e, q=q: e.activation(out=decay[:, q * 512:(q + 1) * 512], in_=b7, func=AF.Exp),
                  reads=[PB[SB7]], writes=[b_decay] if q == 0 else (), pwrites=() if q == 0 else [b_decay])
                if q == 0:
                    E("dve", lambda e: e.tensor_tensor(out=wgt, in0=dtj, in1=dcd[:, 0:16], op=ALU.mult),
                      reads=[bd, b_dcd], writes=[b_wgt])
                    E("dve", lambda e: e.tensor_tensor(
                        out=XWt.rearrange("p (h q) -> p h q", h=16), in0=xtf.rearrange("p (h q) -> p h q", h=16),
                        in1=wgt.unsqueeze(2).to_broadcast([128, 16, 64]), op=ALU.mult),
                      reads=[b_xtf, b_wgt], writes=[b_XW])
                if q == 2:
                    ce_op(0)
                yield
            ce_op(1)
            yield
            for hb in range(2):
                for p4 in range(4):
                    pr = hb * 4 + p4
                    csl = slice(p4 * 128, (p4 + 1) * 128)
                    for half in range(2):
                        h = 2 * pr + half
                        first = (p4 == 0 and half == 0)
                        E("pe", lambda e, csl=csl, half=half, h=h: e.matmul(
                            b7[64 * half:64 * half + 64, csl], lhsT=xtb[:, h * 64:(h + 1) * 64],
                            rhs=Mm[:, h * 128:(h + 1) * 128], start=True, stop=False, skip_group_check=True),
                          reads=[b_xtb, b_M], writes=[PB[SB7]] if first else (), pwrites=() if first else [PB[SB7]])
                        E("pe", lambda e, csl=csl, half=half, h=h: e.matmul(
                            b7[64 * half:64 * half + 64, csl], lhsT=stateb[:, h * 64:(h + 1) * 64],
                            rhs=CE[:, h * 128:(h + 1) * 128], start=False, stop=True, skip_group_check=True),
                          reads=[b_stateb, b_CE], pwrites=[PB[SB7]])
                usl = ub[:, hb * 512:(hb + 1) * 512]
                u3 = usl.rearrange("p (m t) -> p m t", m=4)
                E("dve", lambda e, hb=hb, u3=u3: e.tensor_tensor(
                    out=u3, in0=xsT[:, hb * 512:(hb + 1) * 512].rearrange("p (m t) -> p m t", m=4),
                    in1=pcol[:, PC_D + hb * 4:PC_D + hb * 4 + 4].unsqueeze(2).to_broadcast([128, 4, 128]), op=ALU.mult),
                  reads=[bx, b_pcol], writes=[b_R] if hb == 0 else (), pwrites=() if hb == 0 else [b_R])
                E("dve", lambda e, usl=usl: e.tensor_tensor(out=usl, in0=usl, in1=b7, op=ALU.add),
                  reads=[b_R, PB[SB7]], pwrites=[b_R])
                yield
                E("dve", lambda e, hb=hb, usl=usl: e.tensor_tensor(out=usl, in0=usl, in1=zs_[:, hb * 512:(hb + 1) * 512],
                                                                  op=ALU.mult), reads=[b_R, bz], pwrites=[b_R])
                o3 = stsT.rearrange("p (m t) -> p m t", m=8)[:, hb * 4:hb * 4 + 4, jq * 128:(jq + 1) * 128]
                E("dve", lambda e, hb=hb, u3=u3, o3=o3: e.tensor_tensor(
                    out=o3, in0=u3,
                    in1=pcol[:, PC_GS + hb * 4:PC_GS + hb * 4 + 4].unsqueeze(2).to_broadcast([128, 4, 128]), op=ALU.mult),
                  reads=[b_R, b_pcol], pwrites=[b_stsT])
                yield
            E("dve", lambda e: e.tensor_tensor(out=sqs, in0=ub, in1=ub, op=ALU.mult), reads=[b_R], pwrites=[b_R])
            yield
            for pr in range(8):
                E("pe", lambda e, pr=pr: e.matmul(b7[:, 0:1], lhsT=sqs[:, pr * 128:(pr + 1) * 128], rhs=onesf[:, 0:1],
                                                 start=(pr == 0), stop=(pr == 7)),
                  reads=[b_R, b_cf], writes=[PB[SB7]] if pr == 0 else (), pwrites=() if pr == 0 else [PB[SB7]])
            E("dve", lambda e: e.tensor_copy(out=ssq_s[:, c:c + 1], in_=b7[:, 0:1]), reads=[PB[SB7]], pwrites=[b_ssqs])
            yield
            for g in range(2):
                E("pe", lambda e, g=g: e.matmul(b7, lhsT=Btk[:, g * 128:(g + 1) * 128], rhs=XWt[:, g * 512:(g + 1) * 512],
                                                start=True, stop=True), reads=[b_Btk, b_XW], writes=[PB[SB7]])
                ssl = state[:, g * 512:(g + 1) * 512]
                E("dve", lambda e, g=g, ssl=ssl: e.tensor_tensor(
                    out=ssl.rearrange("p (h q) -> p h q", h=8), in0=ssl.rearrange("p (h q) -> p h q", h=8),
                    in1=dcd[:, 16 + g * 8:16 + g * 8 + 8].unsqueeze(2).to_broadcast([128, 8, 64]), op=ALU.mult),
                  reads=[b_state, b_dcd], writes=[b_state] if g == 0 else (), pwrites=() if g == 0 else [b_state])
                E("dve", lambda e, ssl=ssl: e.tensor_tensor(out=ssl, in0=ssl, in1=b7, op=ALU.add),
                  reads=[b_state, PB[SB7]], pwrites=[b_state])
                yield
            E("dve", lambda e: e.tensor_copy(out=stateb, in_=state), reads=[b_state], writes=[b_stateb])
            if jq == 3:
                tt_ = c // 4
                E("sp", lambda e: e.dma_start(
                    out=AT_d[1024:2048, tt_ * 512:(tt_ + 1) * 512].rearrange("(m p) t -> p m t", p=128),
                    in_=stsT.rearrange("p (m t) -> p m t", m=8)), reads=[b_stsT], dma=True)
            yield

        def ssd_all():
            for c in range(NCH):
                for _ in ssd_chunk(c):
                    yield

        ssd_load(0)
        ssd_gen = ssd_all()

        def ssd_step():
            try:
                next(ssd_gen)
                return True
            except StopIteration:
                return False

        def v2(ap, c0):
            return ap.rearrange("p (b c) -> p b c", b=2)[:, :, c0:512]

        def load_hp(hp):
            r = hp % 2
            E("sp", lambda e: e.dma_start(out=QT[r], in_=QT_d[hp * 128:(hp + 1) * 128, :]), writes=[b_QT[r]], dma=True)
            E("sp", lambda e: e.dma_start(out=KT[r], in_=KT_d[hp * 128:(hp + 1) * 128, :]), writes=[b_KT[r]], dma=True)

        load_hp(0)
        ucount = [0]

        def do_hp(hp, r):
            if hp > 0:
                load_hp(hp)
            steps = []
            for qt in range(NT):
                for jj in range(4 * qt + 3, -1, -1):
                    steps.append((qt, jj))
            nu = len(steps)
            info = {}

            def st_qk(n):
                qt, jj = steps[n]
                u = ucount[0] + n
                kk = jj - 4 * qt
                c0 = 128 * kk if kk > 0 else 0
                ap_ = 2 * (u % 2)
                info[n] = (qt, jj, u, kk, c0, ap_)
                if jj == 4 * qt + 3:
                    zi = (hp * NT + qt) % 2
                    E("sp", lambda e, zi=zi: e.dma_start(out=ZAq[zi], in_=ZA_d[hp * 128:(hp + 1) * 128, qt * 512:(qt + 1) * 512]),
                      writes=[b_ZAq[zi]], dma=True)
                for half in range(2):
                    pb = 64 * half
                    E("pe", lambda e, pb=pb, half=half: e.matmul(
                        bank(ap_ + half)[:, c0:512], lhsT=KT[r][pb:pb + 64, jj * 128:(jj + 1) * 128],
                        rhs=QT[r][pb:pb + 64, qt * 512 + c0:(qt + 1) * 512], start=True, stop=True),
                      reads=[b_KT[r], b_QT[r]], writes=[PB[ap_ + half]])

            def st_act1(n):
                qt, jj, u, kk, c0, ap_ = info[n]
                ei, li = u % NE, u % NL
                E("act", lambda e: e.activation(out=v2(eb[ei], c0), in_=v2(psum[:, ap_ * 512:(ap_ + 2) * 512], c0), func=AF.Exp),
                  reads=[PB[ap_], PB[ap_ + 1]], writes=[b_eb[ei]])
                if kk >= 0:
                    ev = eb[ei].rearrange("p (b c) -> p b c", b=2)[:, :, c0:c0 + 128]
                    E("pool", lambda e: e.tensor_tensor(out=ev, in0=ev, in1=LTf.unsqueeze(1).to_broadcast([128, 2, 128]),
                                                        op=ALU.mult), reads=[b_eb[ei], b_cf], writes=[b_eb[ei]])

            def st_act2(n):
                qt, jj, u, kk, c0, ap_ = info[n]
                ei, li = u % NE, u % NL
                E("act", lambda e: e.activation(out=v2(lb[li], c0), in_=v2(eb[ei], c0), func=AF.Ln, bias=1.0),
                  reads=[b_eb[ei]], writes=[b_lb[li]])

            def st_ge(n):
                qt, jj, u, kk, c0, ap_ = info[n]
                li, xi = u % NL, u % NX
                for half in range(2):
                    cbk = 4 + half
                    if jj == 4 * qt + 3:
                        E("pe", lambda e, cbk=cbk: e.matmul(bank(cbk), lhsT=zb[:, 0:128], rhs=zb, start=True, stop=False,
                                                            skip_group_check=True), reads=[b_zb], writes=[PB[cbk]])
                    E("pe", lambda e, cbk=cbk, half=half: e.matmul(
                        bank(cbk)[:, c0:512], lhsT=GEb, rhs=lb[li][:, half * 512 + c0:(half + 1) * 512],
                        start=False, stop=False, skip_group_check=True), reads=[b_lb[li], b_cb], writes=[PB[cbk]])
                E("act", lambda e: e.activation(out=v2(xb[xi], c0), in_=v2(psum[:, 2048:3072], c0), func=AF.Exp, scale=-1.0),
                  reads=[PB[4], PB[5]], writes=[b_xb[xi]])

            def st_lt(n):
                qt, jj, u, kk, c0, ap_ = info[n]
                if jj == 0:
                    return
                li = u % NL
                for half in range(2):
                    cbk = 4 + half
                    E("pe", lambda e, cbk=cbk, half=half: e.matmul(
                        bank(cbk)[:, c0:512], lhsT=LTb, rhs=lb[li][:, half * 512 + c0:(half + 1) * 512],
                        start=False, stop=False, skip_group_check=True), reads=[b_lb[li], b_cb], writes=[PB[cbk]])

            def st_w(n):
                qt, jj, u, kk, c0, ap_ = info[n]
                ei, xi, wi = u % NE, u % NX, u % NWB
                E("dve", lambda e: e.tensor_tensor(out=v2(wb[wi], c0), in0=v2(eb[ei], c0), in1=v2(xb[xi], c0), op=ALU.mult),
                  reads=[b_eb[ei], b_xb[xi]], writes=[b_wb[wi]])

            def st_pv(n):
                qt, jj, u, kk, c0, ap_ = info[n]
                wi = u % NWB
                ob = 6
                for half in range(2):
                    pb = 64 * half
                    hh = 2 * hp + half
                    if jj == 4 * qt + 3:
                        E("pe", lambda e, pb=pb: e.matmul(bank(ob)[pb:pb + 64, :], lhsT=zb[:, 0:64], rhs=zb, start=True,
                                                          stop=False, skip_group_check=True), reads=[b_zb],
                          writes=[PB[ob]] if half == 0 else (), pwrites=() if half == 0 else [PB[ob]])
                    E("pe", lambda e, pb=pb, half=half, hh=hh: e.matmul(
                        bank(ob)[pb:pb + 64, c0:512], lhsT=Vall[:, jj * 1024 + hh * 64:jj * 1024 + hh * 64 + 64],
                        rhs=wb[wi][:, half * 512 + c0:(half + 1) * 512], start=False, stop=(jj == 0), skip_group_check=True),
                      reads=[b_Vall, b_wb[wi]], pwrites=[PB[ob]])

            def st_epi(n):
                qt, jj, u, kk, c0, ap_ = info[n]
                ob = 6
                if jj == 0:
                    E("act", lambda e: e.activation(out=sqa, in_=bank(ob), func=AF.Square), reads=[PB[ob]], writes=[b_sqa])
                    for sub in range(4):
                        E("pe", lambda e, sub=sub: e.matmul(bank(0)[:, sub:sub + 1], lhsT=sqa[:, sub * 128:(sub + 1) * 128],
                                                            rhs=onesf[:, 0:1], start=True, stop=True),
                          reads=[b_sqa, b_cf], writes=[PB[0]] if sub == 0 else (), pwrites=() if sub == 0 else [PB[0]])
                    if hp == 0:
                        E("dve", lambda e: e.tensor_copy(out=ssq_a[:, 4 * qt:4 * qt + 4], in_=bank(0)[:, 0:4]),
                          reads=[PB[0]], pwrites=[b_ssqa])
                    else:
                        E("dve", lambda e: e.tensor_tensor(out=ssq_a[:, 4 * qt:4 * qt + 4], in0=ssq_a[:, 4 * qt:4 * qt + 4],
                                                           in1=bank(0)[:, 0:4], op=ALU.add),
                          reads=[PB[0], b_ssqa], writes=[b_ssqa])
                    ai = atq[0] % 2
                    atq[0] += 1
                    E("dve", lambda e: e.scalar_tensor_tensor(
                        out=ats[ai], in0=bank(ob), scalar=pcol[:, PC_GA + hp:PC_GA + hp + 1],
                        in1=ZAq[(hp * NT + qt) % 2], op0=ALU.mult, op1=ALU.mult),
                      reads=[PB[ob], b_pcol, b_ZAq[(hp * NT + qt) % 2]], writes=[b_ats[ai]])
                    E("sp", lambda e: e.dma_start(out=AT_d[hp * 128:(hp + 1) * 128, qt * 512:(qt + 1) * 512], in_=ats[ai]),
                      reads=[b_ats[ai]], dma=True)

            for n in range(nu + 3):
                if n < nu:
                    st_qk(n)
                if 0 <= n - 2 < nu:
                    st_lt(n - 2)
                if n < nu:
                    st_act1(n)
                if 0 <= n - 1 < nu:
                    st_ge(n - 1)
                if n < nu:
                    st_act2(n)
                if 0 <= n - 1 < nu:
                    st_w(n - 1)
                if 0 <= n - 2 < nu:
                    st_pv(n - 2)
                if 0 <= n - 2 < nu:
                    st_epi(n - 2)
                ssd_step()
            ucount[0] += nu

        for hp_ in range(8):
            do_hp(hp_, hp_ % 2)
        while ssd_step():
            pass
        S_.barrier()
        A.top = base_mark

        A.nw = NW_SBUF
        Wg = A.bf16(16 * 1024); b_Wg = Buf("Wg")
        wst = [A.f32(4096) for _ in range(2)]; b_wst = [Buf("wst%d" % i) for i in range(2)]
        wout_v = wout_d.rearrange("(m p) c -> p m c", p=128)
        for ch in range(4):
            rr = ch % 2
            E("sp", lambda e, ch=ch, rr=rr: e.dma_start(out=wst[rr].rearrange("p (m c) -> p m c", m=4),
                                                      in_=wout_v[:, ch * 4:(ch + 1) * 4, :]), writes=[b_wst[rr]], dma=True)
            E("dve" if ch % 2 == 0 else "pool", lambda e, ch=ch, rr=rr: e.tensor_tensor(
                out=Wg[:, ch * 4096:(ch + 1) * 4096].rearrange("p (m c) -> p m c", m=4),
                in0=wst[rr].rearrange("p (m c) -> p m c", m=4),
                in1=gate_bc.unsqueeze(1).to_broadcast([128, 4, 1024]), op=ALU.mult),
              reads=[b_wst[rr], b_gate], pwrites=[b_Wg])
        for (src, dst) in ((ssq_a, rstd_a), (ssq_s, rstd_s)):
            E("act", lambda e, src=src, dst=dst: e.activation(out=dst, in_=src, func=AF.Ln, scale=1.0 / 1024, bias=EPS),
              reads=[b_ssqa, b_ssqs], pwrites=[b_rstd])
            E("act", lambda e, dst=dst: e.activation(out=dst, in_=dst, func=AF.Exp, scale=-0.5),
              reads=[b_rstd], pwrites=[b_rstd])
        ATt = [A.bf16(16 * 512) for _ in range(2)]; b_ATt = [Buf("ATt%d" % i) for i in range(2)]
        xo = [A.f32(4096) for _ in range(2)]; b_xo = [Buf("xo%d" % i) for i in range(2)]
        yb = [A.f32(1024) for _ in range(2)]; b_yb = [Buf("yb%d" % i) for i in range(2)]
        junk3 = A.bf16(1024); b_junk3 = Buf("junk3")
        b_out = Buf("outd")
        yq = [0]
        gf_bc = prow[:, PR_GF:PR_GF + 1024]
        for tt in range(NT):
            r = tt % 2
            E("sp", lambda e, tt=tt, r=r: e.dma_start(
                out=ATt[r].rearrange("p (m t) -> p m t", m=16),
                in_=AT_d[:, tt * 512:(tt + 1) * 512].rearrange("(m p) t -> p m t", p=128)), writes=[b_ATt[r]], dma=True)
            E("sp", lambda e, tt=tt, r=r: e.dma_start(
                out=xo[r].rearrange("p (j d) -> p j d", j=4),
                in_=x_d[tt * 512:(tt + 1) * 512, :].rearrange("(j p) d -> p j d", p=128)), writes=[b_xo[r]], dma=True)
            for j in range(4):
                c = tt * 4 + j
                yi = yq[0] % 2
                yq[0] += 1
                for n in range(2):
                    ba = 4 * (c % 2) + 2 * n
                    bs = ba + 1
                    for m in range(8):
                        E("pe", lambda e, r=r, m=m, j=j, n=n, ba=ba: e.matmul(
                            bank(ba), lhsT=ATt[r][:, m * 512 + j * 128:m * 512 + (j + 1) * 128],
                            rhs=Wg[:, m * 1024 + n * 512:m * 1024 + (n + 1) * 512], start=(m == 0), stop=(m == 7)),
                          reads=[b_ATt[r], b_Wg], writes=[PB[ba]] if m == 0 else (), pwrites=() if m == 0 else [PB[ba]])
                    for m in range(8, 16):
                        E("pe", lambda e, r=r, m=m, j=j, n=n, bs=bs: e.matmul(
                            bank(bs), lhsT=ATt[r][:, m * 512 + j * 128:m * 512 + (j + 1) * 128],
                            rhs=Wg[:, m * 1024 + n * 512:m * 1024 + (n + 1) * 512], start=(m == 8), stop=(m == 15)),
                          reads=[b_ATt[r], b_Wg], writes=[PB[bs]] if m == 8 else (), pwrites=() if m == 8 else [PB[bs]])
                    ysl = yb[yi][:, n * 512:(n + 1) * 512]
                    E("dve", lambda e, r=r, j=j, n=n, ba=ba, ysl=ysl, c=c: e.scalar_tensor_tensor(
                        out=ysl, in0=bank(ba), scalar=rstd_a[:, c:c + 1],
                        in1=xo[r][:, j * 1024 + n * 512:j * 1024 + (n + 1) * 512], op0=ALU.mult, op1=ALU.add),
                      reads=[PB[ba], b_rstd, b_xo[r]], writes=[b_yb[yi]] if n == 0 else (), pwrites=() if n == 0 else [b_yb[yi]])
                    E("dve", lambda e, bs=bs, ysl=ysl, c=c: e.scalar_tensor_tensor(
                        out=ysl, in0=bank(bs), scalar=rstd_s[:, c:c + 1], in1=ysl, op0=ALU.mult, op1=ALU.add),
                      reads=[PB[bs], b_rstd, b_yb[yi]], pwrites=[b_yb[yi]])
                E("act", lambda e, yi=yi, c=c: e.activation(out=junk3, in_=yb[yi], func=AF.Square,
                                                            accum_out=ssqf[:, c:c + 1]),
                  reads=[b_yb[yi]], writes=[b_junk3], pwrites=[b_ssqf])
                E("act", lambda e, c=c: e.activation(out=rstdf[:, c:c + 1], in_=ssqf[:, c:c + 1], func=AF.Ln,
                                                     scale=1.0 / 1024, bias=EPS), reads=[b_ssqf], pwrites=[b_rstdf])
                E("act", lambda e, c=c: e.activation(out=rstdf[:, c:c + 1], in_=rstdf[:, c:c + 1], func=AF.Exp, scale=-0.5),
                  reads=[b_rstdf], pwrites=[b_rstdf])
                E("dve", lambda e, yi=yi, c=c: e.scalar_tensor_tensor(
                    out=yb[yi], in0=yb[yi], scalar=rstdf[:, c:c + 1], in1=gf_bc, op0=ALU.mult, op1=ALU.mult),
                  reads=[b_yb[yi], b_rstdf, b_prow], writes=[b_yb[yi]])
                E("pool", lambda e, yi=yi, c=c: e.dma_start(out=out_d[c * 128:(c + 1) * 128, :], in_=yb[yi]),
                  reads=[b_yb[yi]], pwrites=[b_out], dma=True, key="yo%d" % yi)
        E("sp", lambda e: e.nop(), reads=[b_out], real=False)
        S_.run()
    return nc


def _consts():
    a = np.arange(128)[:, None]
    b = np.arange(128)[None, :]
    c = np.zeros((128, NCONST), np.float32)
    c[:, C_ID:C_ID + 128] = (a == b)
    c[:, C_GE:C_GE + 128] = (a >= b)
    c[:, C_LT:C_LT + 128] = (a < b)
    c[:, C_LE:C_LE + 128] = (a <= b)
    c[:, C_GT:C_GT + 128] = (a > b)
    c[:, C_NEG:C_NEG + 128] = NEGV * (a > b)
    c[:, C_ONE:C_ONE + 128] = 1.0
    return c


def _col(v, n):
    return np.ascontiguousarray(np.asarray(v, np.float32).reshape(n, 128).T)


def make_in_maps(x, c, w_ada, b_ada, norm_in_gain, w_in, conv_w, conv_b, dt_bias, a_log, d_skip,
                 sb_norm_gain, ssm_norm_gain, w_out, norm_f_gain, cores=None):
    B = x.shape[0]
    consts = _consts()
    maps = []
    w_ada0 = np.ascontiguousarray(w_ada[0], np.float32)
    w_in0 = np.ascontiguousarray(w_in[0], np.float32)
    w_out0 = np.ascontiguousarray(w_out[0], np.float32)
    prow = np.zeros((1, NPROW), np.float32)
    prow[0, PR_BG:PR_BG + 1024] = b_ada[0, 2048:3072]
    prow[0, PR_GF:PR_GF + 1024] = norm_f_gain
    prow[0, PR_DTB:PR_DTB + 16] = dt_bias[0]
    prow[0, PR_AL:PR_AL + 16] = a_log[0]
    for b in (range(B) if cores is None else cores):
        pcol = np.zeros((128, NPCOL), np.float32)
        pcol[:, PC_C:PC_C + 8] = _col(c[b], 8)
        pcol[:, PC_BADA:PC_BADA + 24] = _col(b_ada[0], 24)
        pcol[:, PC_GIN:PC_GIN + 8] = _col(norm_in_gain[0], 8)
        pcol[:, PC_GA:PC_GA + 8] = _col(sb_norm_gain[0], 8)
        pcol[:, PC_GS:PC_GS + 8] = _col(ssm_norm_gain[0], 8)
        pcol[:, PC_D:PC_D + 8] = _col(np.repeat(np.asarray(d_skip[0], np.float32), 64), 8)
        cw = np.asarray(conv_w[0], np.float32)
        pcol[:, PC_CW:PC_CW + 48] = cw.T.reshape(12, 128, 4).transpose(1, 0, 2).reshape(128, 48)
        pcol[:, PC_CB:PC_CB + 12] = _col(conv_b[0], 12)
        maps.append({"x": np.ascontiguousarray(x[b], np.float32), "w_ada": w_ada0, "w_in": w_in0, "w_out": w_out0,
                     "consts": consts, "pcol": pcol, "prow": prow})
    return maps


_NC_CACHE = {}


def kernel(x, c, w_ada, b_ada, norm_in_gain, w_in, conv_w, conv_b, dt_bias, a_log, d_skip,
           sb_norm_gain, ssm_norm_gain, w_out, norm_f_gain):
    x = np.asarray(x)
    B, S, _ = x.shape
    if S not in _NC_CACHE:
        _NC_CACHE[S] = build(S)
    nc = _NC_CACHE[S]
    maps = make_in_maps(x, np.asarray(c), np.asarray(w_ada), np.asarray(b_ada), np.asarray(norm_in_gain),
                        np.asarray(w_in), np.asarray(conv_w), np.asarray(conv_b), np.asarray(dt_bias),
                        np.asarray(a_log), np.asarray(d_skip), np.asarray(sb_norm_gain),
                        np.asarray(ssm_norm_gain), np.asarray(w_out), np.asarray(norm_f_gain))
    res = run_bass_kernel_spmd(nc, maps, core_ids=list(range(B)))
    return np.stack([np.asarray(r["out"], np.float32) for r in res.results], axis=0)
```
